# Optimizing a Trainium2 kernel written in Bass

```python
import math
import jax, jax.numpy as jnp
from jax import lax
import numpy as np

D_MODEL = 2048
BATCH = 1
SEQ = 8192
DEPTH = 2
DEC_BATCH = 128
DEC_SEQ = 1
PAST_LEN = 8192
PAGE_SIZE = 128

N_MIXERS = 2
N_POOL_LAYERS = (DEPTH + N_MIXERS - 1) // N_MIXERS
N_ATTN_LAYERS = DEPTH // N_MIXERS
POOL_WINDOWS = (2, 4, 8, 16)
POOL_GROUPS = len(POOL_WINDOWS)
POOL_GC = D_MODEL // POOL_GROUPS
POOL_MAXW = max(POOL_WINDOWS)
POOL_BUF = POOL_MAXW - 1
HEAD_DIM = 64
N_HEADS = D_MODEL // HEAD_DIM
N_KV_HEADS = N_HEADS // 4
GQA_GROUP = N_HEADS // N_KV_HEADS
WINDOW = 128
SWA_BLOCK = WINDOW
ROPE_THETA = 10000.0
N_MEM = 256
X_HEADS = 4
X_HEAD_DIM = 128
D_FF = 5632
RMS_EPS = 1e-6

kernel_name = "hybrid_pool_swa_sink_macaron_memxattn_step"


def rms_norm(x, g):
    xf = x.astype(jnp.float32)
    y = xf * lax.rsqrt(jnp.mean(xf * xf, axis=-1, keepdims=True) + RMS_EPS)
    return (y * g.astype(jnp.float32)).astype(x.dtype)


def swiglu(h, w_gu, w_dn):
    a, b = jnp.split(h @ w_gu, 2, axis=-1)
    return (jax.nn.silu(a) * b) @ w_dn


def rope(x, pos):
    half = x.shape[-1] // 2
    inv = ROPE_THETA ** (-jnp.arange(half, dtype=jnp.float32) / half)
    ang = pos.astype(jnp.float32)[:, None] * inv[None, :]
    cos = jnp.cos(ang)[None, :, None, :]
    sin = jnp.sin(ang)[None, :, None, :]
    xf = x.astype(jnp.float32)
    x1, x2 = xf[..., :half], xf[..., half:]
    return jnp.concatenate([x1 * cos - x2 * sin, x2 * cos + x1 * sin], axis=-1).astype(x.dtype)


def sink_softmax(s, sink):
    m = jnp.maximum(jnp.max(s, axis=-1, keepdims=True), sink)
    p = jnp.exp(s - m)
    return p / (jnp.sum(p, axis=-1, keepdims=True) + jnp.exp(sink - m))


def split_qkv(qkv):
    b, t, _ = qkv.shape
    nq, nk = N_HEADS * HEAD_DIM, N_KV_HEADS * HEAD_DIM
    q = qkv[..., :nq].reshape(b, t, N_HEADS, HEAD_DIM)
    k = qkv[..., nq:nq + nk].reshape(b, t, N_KV_HEADS, HEAD_DIM)
    v = qkv[..., nq + nk:].reshape(b, t, N_KV_HEADS, HEAD_DIM)
    return q, k, v


def pool_mix(h_ext, pos_ext, n_out, w_pool, scale):
    b, l, _ = h_ext.shape
    hf = h_ext.astype(jnp.float32)
    csp = jnp.concatenate([jnp.zeros((b, POOL_MAXW, D_MODEL), jnp.float32), jnp.cumsum(hf, axis=1)], axis=1)
    end = csp[:, POOL_MAXW + l - n_out:]
    pos1 = pos_ext[l - n_out:].astype(jnp.float32) + 1.0
    means = []
    for g, w in enumerate(POOL_WINDOWS):
        c0, c1 = g * POOL_GC, (g + 1) * POOL_GC
        start = csp[:, POOL_MAXW - w + l - n_out:POOL_MAXW - w + l, c0:c1]
        cnt = jnp.minimum(jnp.float32(w), pos1)[None, :, None]
        means.append((end[..., c0:c1] - start) / cnt)
    pooled = (jnp.concatenate(means, axis=-1) - hf[:, l - n_out:]).astype(h_ext.dtype)
    y = jnp.einsum('btgc,gcd->btgd', pooled.reshape(b, n_out, POOL_GROUPS, POOL_GC), w_pool)
    return y.reshape(b, n_out, D_MODEL) * scale


def swa_prompt(h, w_qkv, w_o, sinks):
    b, t, _ = h.shape
    q, k, v = split_qkv(h @ w_qkv)
    pos = jnp.arange(t)
    q, k = rope(q, pos), rope(k, pos)
    nb = t // SWA_BLOCK
    qb = q.reshape(b, nb, SWA_BLOCK, N_KV_HEADS, GQA_GROUP, HEAD_DIM)
    kb = k.reshape(b, nb, SWA_BLOCK, N_KV_HEADS, HEAD_DIM)
    vb = v.reshape(b, nb, SWA_BLOCK, N_KV_HEADS, HEAD_DIM)
    zero = jnp.zeros_like(kb[:, :1])
    kp = jnp.concatenate([zero, kb], axis=1)
    vp = jnp.concatenate([zero, vb], axis=1)
    kc = jnp.concatenate([kp[:, :-1], kp[:, 1:]], axis=2)
    vc = jnp.concatenate([vp[:, :-1], vp[:, 1:]], axis=2)
    s = jnp.einsum('bnqkgd,bnskd->bnkgqs', qb, kc).astype(jnp.float32) / math.sqrt(HEAD_DIM)
    blk = jnp.arange(nb)[:, None] * SWA_BLOCK
    qpos = blk + jnp.arange(SWA_BLOCK)[None, :]
    kpos = blk - SWA_BLOCK + jnp.arange(2 * SWA_BLOCK)[None, :]
    diff = qpos[:, :, None] - kpos[:, None, :]
    mask = (diff >= 0) & (diff < WINDOW) & (kpos[:, None, :] >= 0)
    s = jnp.where(mask[None, :, None, None], s, -jnp.inf)
    p = sink_softmax(s, sinks.astype(jnp.float32).reshape(1, 1, N_KV_HEADS, GQA_GROUP, 1, 1))
    o = jnp.einsum('bnkgqs,bnskd->bnqkgd', p.astype(vc.dtype), vc).reshape(b, t, N_HEADS * HEAD_DIM)
    keep = min(WINDOW, t)
    return o @ w_o, k[:, t - keep:], v[:, t - keep:]


def swa_sample(h, ck, cv, w_qkv, w_o, sinks):
    b, s_len, _ = h.shape
    buf = ck.shape[1]
    q, k, v = split_qkv(h @ w_qkv)
    qpos = PAST_LEN + jnp.arange(s_len)
    q, k = rope(q, qpos), rope(k, qpos)
    k_all = jnp.concatenate([ck, k], axis=1)
    v_all = jnp.concatenate([cv, v], axis=1)
    kpos = jnp.concatenate([PAST_LEN - buf + jnp.arange(buf), qpos])
    diff = qpos[:, None] - kpos[None, :]
    mask = (diff >= 0) & (diff < WINDOW)
    qg = q.reshape(b, s_len, N_KV_HEADS, GQA_GROUP, HEAD_DIM)
    s = jnp.einsum('bqkgd,bskd->bkgqs', qg, k_all).astype(jnp.float32) / math.sqrt(HEAD_DIM)
    s = jnp.where(mask, s, -jnp.inf)
    p = sink_softmax(s, sinks.astype(jnp.float32).reshape(1, N_KV_HEADS, GQA_GROUP, 1, 1))
    o = jnp.einsum('bkgqs,bskd->bqkgd', p.astype(v_all.dtype), v_all).reshape(b, s_len, N_HEADS * HEAD_DIM)
    return o @ w_o, k_all[:, -buf:], v_all[:, -buf:]


def mem_kv(mem, g, w_kv):
    b, m, _ = mem.shape
    k, v = jnp.split(rms_norm(mem, g) @ w_kv, 2, axis=-1)
    return k.reshape(b, m, X_HEADS, X_HEAD_DIM), v.reshape(b, m, X_HEADS, X_HEAD_DIM)


def cross_attn(h, mk, mv, w_q, w_o):
    b, t, _ = h.shape
    q = (h @ w_q).reshape(b, t, X_HEADS, X_HEAD_DIM)
    s = jnp.einsum('bthd,bmhd->bhtm', q, mk).astype(jnp.float32) / math.sqrt(X_HEAD_DIM)
    p = jax.nn.softmax(s, axis=-1)
    o = jnp.einsum('bhtm,bmhd->bthd', p.astype(mv.dtype), mv).reshape(b, t, X_HEADS * X_HEAD_DIM)
    return o @ w_o


def setup_inputs(seed: int = 0) -> dict:
    key = jax.random.key(seed)
    ks = iter(jax.random.split(key, 40))
    f32 = jnp.float32

    def nrm(shape, scale=1.0):
        return jax.random.normal(next(ks), shape, f32) * scale

    def gain(shape):
        return 1.0 + 0.05 * jax.random.normal(next(ks), shape, f32)

    swa_buf = min(WINDOW, PAST_LEN)
    qkv_w = (N_HEADS + 2 * N_KV_HEADS) * HEAD_DIM
    xw = X_HEADS * X_HEAD_DIM
    return {
        'x_prompt': nrm((BATCH, SEQ, D_MODEL)),
        'x_sample': nrm((DEC_BATCH, DEC_SEQ, D_MODEL)),
        'state_pool': nrm((N_POOL_LAYERS, DEC_BATCH, POOL_BUF, D_MODEL)),
        'cache_swa_k': nrm((N_ATTN_LAYERS, DEC_BATCH, swa_buf, N_KV_HEADS, HEAD_DIM)),
        'cache_swa_v': nrm((N_ATTN_LAYERS, DEC_BATCH, swa_buf, N_KV_HEADS, HEAD_DIM)),
        'cache_mem_k': nrm((DEPTH, DEC_BATCH, N_MEM, X_HEADS, X_HEAD_DIM)),
        'cache_mem_v': nrm((DEPTH, DEC_BATCH, N_MEM, X_HEADS, X_HEAD_DIM)),
        'mem_prompt': nrm((BATCH, N_MEM, D_MODEL)),
        'g_ffn1': gain((DEPTH, D_MODEL)),
        'w_ffn1_gu': nrm((DEPTH, D_MODEL, 2 * D_FF), D_MODEL ** -0.5),
        'w_ffn1_dn': nrm((DEPTH, D_FF, D_MODEL), D_FF ** -0.5),
        'g_mix': gain((DEPTH, D_MODEL)),
        'w_pool': nrm((N_POOL_LAYERS, POOL_GROUPS, POOL_GC, POOL_GC), POOL_GC ** -0.5),
        'pool_scale': 1.0 + 0.1 * nrm((N_POOL_LAYERS, D_MODEL)),
        'w_qkv': nrm((N_ATTN_LAYERS, D_MODEL, qkv_w), D_MODEL ** -0.5),
        'w_o': nrm((N_ATTN_LAYERS, N_HEADS * HEAD_DIM, D_MODEL), (N_HEADS * HEAD_DIM) ** -0.5),
        'sinks': nrm((N_ATTN_LAYERS, N_HEADS)),
        'g_xq': gain((DEPTH, D_MODEL)),
        'g_mem': gain((DEPTH, D_MODEL)),
        'w_xq': nrm((DEPTH, D_MODEL, xw), D_MODEL ** -0.5),
        'w_xkv': nrm((DEPTH, D_MODEL, 2 * xw), D_MODEL ** -0.5),
        'w_xo': nrm((DEPTH, xw, D_MODEL), xw ** -0.5),
        'g_ffn2': gain((DEPTH, D_MODEL)),
        'w_ffn2_gu': nrm((DEPTH, D_MODEL, 2 * D_FF), D_MODEL ** -0.5),
        'w_ffn2_dn': nrm((DEPTH, D_FF, D_MODEL), D_FF ** -0.5),
        'g_final': gain((D_MODEL,)),
    }


def reference(x_prompt, x_sample, state_pool, cache_swa_k, cache_swa_v, cache_mem_k, cache_mem_v, mem_prompt,
              g_ffn1, w_ffn1_gu, w_ffn1_dn, g_mix, w_pool, pool_scale, w_qkv, w_o, sinks,
              g_xq, g_mem, w_xq, w_xkv, w_xo, g_ffn2, w_ffn2_gu, w_ffn2_dn, g_final):
    xp, xs = x_prompt, x_sample
    t_p = xp.shape[1]
    pool_p, pool_s = [], []
    swa_kp, swa_vp, swa_ks, swa_vs = [], [], [], []
    mem_kp, mem_vp = [], []
    for layer in range(DEPTH):
        xp = xp + 0.5 * swiglu(rms_norm(xp, g_ffn1[layer]), w_ffn1_gu[layer], w_ffn1_dn[layer])
        xs = xs + 0.5 * swiglu(rms_norm(xs, g_ffn1[layer]), w_ffn1_gu[layer], w_ffn1_dn[layer])
        hp = rms_norm(xp, g_mix[layer])
        hs = rms_norm(xs, g_mix[layer])
        i = layer // N_MIXERS
        if layer % N_MIXERS == 0:
            xp = xp + pool_mix(hp, jnp.arange(t_p), t_p, w_pool[i], pool_scale[i])
            hs_ext = jnp.concatenate([state_pool[i], hs], axis=1)
            xs = xs + pool_mix(hs_ext, PAST_LEN - POOL_BUF + jnp.arange(hs_ext.shape[1]), hs.shape[1],
                               w_pool[i], pool_scale[i])
            pool_p.append(hp[:, t_p - POOL_BUF:])
            pool_s.append(hs_ext[:, -POOL_BUF:])
        else:
            yp, kp_new, vp_new = swa_prompt(hp, w_qkv[i], w_o[i], sinks[i])
            ys, ks_new, vs_new = swa_sample(hs, cache_swa_k[i], cache_swa_v[i], w_qkv[i], w_o[i], sinks[i])
            xp = xp + yp
            xs = xs + ys
            swa_kp.append(kp_new)
            swa_vp.append(vp_new)
            swa_ks.append(ks_new)
            swa_vs.append(vs_new)
        mk, mv = mem_kv(mem_prompt, g_mem[layer], w_xkv[layer])
        mem_kp.append(mk)
        mem_vp.append(mv)
        xp = xp + cross_attn(rms_norm(xp, g_xq[layer]), mk, mv, w_xq[layer], w_xo[layer])
        xs = xs + cross_attn(rms_norm(xs, g_xq[layer]), cache_mem_k[layer], cache_mem_v[layer],
                             w_xq[layer], w_xo[layer])
        xp = xp + 0.5 * swiglu(rms_norm(xp, g_ffn2[layer]), w_ffn2_gu[layer], w_ffn2_dn[layer])
        xs = xs + 0.5 * swiglu(rms_norm(xs, g_ffn2[layer]), w_ffn2_gu[layer], w_ffn2_dn[layer])
    y_prompt = rms_norm(xp, g_final)
    y_sample = rms_norm(xs, g_final)
    return (y_prompt, y_sample, jnp.stack(pool_p), jnp.stack(pool_s), jnp.stack(swa_kp), jnp.stack(swa_vp),
            jnp.stack(swa_ks), jnp.stack(swa_vs), jnp.stack(mem_kp), jnp.stack(mem_vp))
```

```python
import math
from contextlib import ExitStack

import ml_dtypes
import numpy as np

import concourse.bass as bass
import concourse.mybir as mybir
from concourse.bass_utils import run_bass_kernel_spmd

F32 = mybir.dt.float32
BF16 = mybir.dt.bfloat16
AF = mybir.ActivationFunctionType
ALU = mybir.AluOpType
AX = mybir.AxisListType
NPBF = ml_dtypes.bfloat16

D = 2048
NCH = 16
DFF = 5632
NF = 44
T = 1184
HALO = 144
NMAIN = 1024
NSAMP = 16
SAMP0 = 1168
NOUT = 1040
R = 9
NCORES = 8
CT_ALL = [(0, 512), (512, 512), (1024, 160)]
CT_MAIN = [(144, 512), (656, 512), (1168, 16)]
SEGB = [0, 144, 512, 656, 1024, 1168, 1184]
GV = {'ffn1_0': 0, 'mix_0': 1, 'xq_0': 2, 'ffn2_0': 3, 'ffn1_1': 4, 'mix_1': 5, 'xq_1': 6, 'ffn2_1': 7,
      'final': 8, 'mem_0': 9, 'mem_1': 10, 'pscale': 11}
NGV = 12
EPS = 1e-6
PAST = 8192
POOLW = (2, 4, 8, 16)


class Buf:
    __slots__ = ('w', 'r', 'excl')

    def __init__(self, snap=None):
        self.w = None
        self.r = dict(snap) if snap else {}
        self.excl = False


class Seg:
    def __init__(self, bounds, snap=None):
        self.b = bounds
        self.bufs = [Buf(snap) for _ in bounds[:-1]]

    def get(self, lo, hi):
        return [self.bufs[i] for i in range(len(self.b) - 1) if self.b[i] < hi and self.b[i + 1] > lo]


class Eng:
    def __init__(self, name, h, key):
        self.name, self.h, self.key = name, h, key
        self.cnt = 0
        self.known = {}


class Prog:
    def __init__(self):
        nc = bass.Bass("TRN2", target_bir_lowering=False)
        self.nc = nc
        self.es = ExitStack()
        self.sems = []
        self.dval = {}
        self.slots = []
        self.uid = 0
        self.PE = Eng('PE', nc.tensor, self.new_sem('s_pe'))
        self.ACT = Eng('ACT', nc.scalar, self.new_sem('s_act'))
        self.DVE = Eng('DVE', nc.vector, self.new_sem('s_dve'))
        self.POOL = Eng('POOL', nc.gpsimd, None)
        self.SP = Eng('SP', nc.sync, None)
        self.rsem = [self.new_sem(f's_ring{i}') for i in range(R)]
        self.spsem = [self.new_sem(f's_sp{i}') for i in range(16)]
        self.sp_rr = 0
        for k in self.rsem + self.spsem:
            self.dval[k] = 0

    def new_sem(self, name):
        h = self.es.enter_context(self.nc.semaphore(name))
        self.sems.append(h)
        return len(self.sems) - 1

    def dram(self, name, shape, dt, kind):
        return self.nc.dram_tensor(name, list(shape), dt, kind=kind).ap()

    def sb(self, name, shape, dt, es=None):
        self.uid += 1
        return (es or self.es).enter_context(self.nc.sbuf_tensor(f"{name}_{self.uid}", list(shape), dt))

    def snapshot(self):
        d = {}
        for E in (self.PE, self.ACT, self.DVE):
            if E.cnt > 0:
                d[E.key] = E.cnt
        for k in self.spsem:
            if self.dval[k] > 0:
                d[k] = self.dval[k]
        return d

    def _deps(self, E, Rb, Wb):
        need = {}

        def add(k, v):
            if need.get(k, 0) < v:
                need[k] = v
        for b in Rb:
            if b.w is not None:
                add(*b.w)
        for b in Wb:
            if b.w is not None:
                add(*b.w)
            for k, v in b.r.items():
                add(k, v)
        for k, v in need.items():
            if k == E.key:
                if E.name == 'PE':
                    continue
            if E.known.get(k, 0) >= v:
                continue
            E.h.wait_ge(self.sems[k], v)
            E.known[k] = v

    def _mark(self, tok, Rb, Wb):
        for b in Rb:
            if b.r.get(tok[0], 0) < tok[1]:
                b.r[tok[0]] = tok[1]
        for b in Wb:
            b.w = tok
            b.r = {}

    def op(self, E, fn, Rb=(), Wb=()):
        if E.name != 'PE':
            ex = [b for b in Rb if b.excl]
            if ex:
                Wb = list(Wb) + ex
                Rb = [b for b in Rb if not b.excl]
        self._deps(E, Rb, Wb)
        ins = fn()
        E.cnt += 1
        ins.then_inc(self.sems[E.key], 1)
        self._mark((E.key, E.cnt), Rb, Wb)

    def dma(self, Q, out, in_, Rb=(), Wb=(), semkey=None):
        self._deps(Q, Rb, Wb)
        if semkey is None:
            semkey = self.spsem[self.sp_rr % len(self.spsem)]
            self.sp_rr += 1
            if self.dval[semkey] > 0 and Q.known.get(semkey, 0) < self.dval[semkey]:
                Q.h.wait_ge(self.sems[semkey], self.dval[semkey])
                Q.known[semkey] = self.dval[semkey]
        self.dval[semkey] += 16
        Q.h.dma_start(out=out, in_=in_).then_inc(self.sems[semkey], 16)
        self._mark((semkey, self.dval[semkey]), Rb, Wb)

    def take_slot(self, desc):
        s = len(self.slots)
        self.slots.append(desc)
        r = s % R
        self.dma(self.POOL, out=self.ring[:, r, :], in_=self.wstream[s], Wb=(self.ringb[r],), semkey=self.rsem[r])
        return r

    def slot3(self, r, k):
        return self.ring[:, r, :].rearrange("p (k n) -> p k n", k=k)

    def PA(self, ci, n):
        if ci < 2:
            return self.pb[ci][:, 0:n], self.pbuf[ci].get(0, n)
        return self.pb[6][:, 0:n], self.pbuf[6].get(0, n)

    def PB(self, ci, n):
        if ci < 2:
            return self.pb[2 + ci][:, 0:n], self.pbuf[2 + ci].get(0, n)
        return self.pb[6][:, 160:160 + n], self.pbuf[6].get(0, 512)

    def bank(self, i, lo, hi):
        return self.pb[i][:, lo:hi], self.pbuf[i].get(lo, hi)

    def xb(self, k, c0, n):
        return self.xbufs[k].get(c0, c0 + n)

    def declare(self):
        nc = self.nc
        I, O = "ExternalInput", "ExternalOutput"
        self.d_xT = self.dram("xT", [128, NCH * T], F32, I)
        self.d_memT = self.dram("memT", [128, NCH * 256], F32, I)
        self.d_stT = self.dram("stT", [128, NCH * NSAMP * 15], F32, I)
        self.d_stN = self.dram("stN", [NSAMP, 15 * D], F32, I)
        self.d_ckN = self.dram("ckN", [NSAMP, 128 * 512], F32, I)
        self.d_cvN = self.dram("cvN", [NSAMP, 128 * 512], F32, I)
        self.d_gvec = self.dram("gvec", [128, NGV * NCH], F32, I)
        self.d_sinkT = self.dram("sinkT", [128, 16], F32, I)
        self.d_cs = self.dram("cs", [128, 2 * T], F32, I)
        self.d_icnt = self.dram("icnt", [128, 4 * T], F32, I)
        self.d_mask = self.dram("maskc", [128, 2 * 512], BF16, I)
        self.NS_decl = None
        self.o_yT = self.dram("yT", [128, NCH * NOUT], F32, O)
        self.o_pool_p = self.dram("pool_p", [128, NCH * 15], F32, O)
        self.o_pool_sn = self.dram("pool_s_new", [128, NCH * NSAMP], F32, O)
        self.o_pool_so = self.dram("pool_s_old", [NSAMP, 14 * D], F32, O)
        self.o_kp = self.dram("swa_k_p", [128, 4 * 128], F32, O)
        self.o_vp = self.dram("swa_v_p", [128, 4 * 128], F32, O)
        self.o_ksn = self.dram("swa_k_s_new", [128, 4 * NSAMP], F32, O)
        self.o_vsn = self.dram("swa_v_s_new", [NSAMP, 4 * 128], F32, O)
        self.o_kso = self.dram("swa_k_s_old", [NSAMP, 127 * 512], F32, O)
        self.o_vso = self.dram("swa_v_s_old", [NSAMP, 127 * 512], F32, O)
        self.o_memk = self.dram("mem_k", [128, 2 * 4 * 256], F32, O)
        self.o_memv = self.dram("mem_v", [128, 2 * 2 * 512], F32, O)

        self.xT = self.sb("xT", [128, NCH, T], F32)
        self.hT = self.sb("hT", [128, NCH, T], BF16)
        self.ring = self.sb("ring", [128, R, 2048], BF16)
        self.KmT = self.sb("KmT", [128, 2, 4, 256], BF16)
        self.Vm = self.sb("Vm", [128, 2, 2, 512], BF16)
        self.gv = self.sb("gv", [128, NGV, NCH], F32)
        self.ones = self.sb("ones", [128, 192], BF16)
        self.onesf = self.sb("onesf", [128, 128], BF16)
        self.esink = self.sb("esink", [128, 16], F32)
        self.pb = [self.es.enter_context(nc.psum_tensor(f"pb{i}", [128, 512], F32)) for i in range(8)]
        self.pbuf = [Seg([0, 512]) for i in range(8)]
        for sg in self.pbuf:
            for b in sg.bufs:
                b.excl = True
        self.xbufs = [Seg(SEGB) for _ in range(NCH)]
        self.hTb = [Buf() for _ in range(NCH)]
        self.ringb = [Buf() for _ in range(R)]
        self.b_KmT = [Buf(), Buf()]
        self.b_Vm = [Buf(), Buf()]
        self.b_gv = Buf()
        self.b_ones = Buf()
        self.b_esink = Buf()
        self.pd_rr = 0

    def pdslot(self, n):
        i = (4, 5, 7)[self.pd_rr % 3]
        self.pd_rr += 1
        return self.bank(i, 0, n)

    def proj(self, dst, r, ctiles, src=None, srcb=None):
        sv = self.slot3(r, 16)
        PE = self.PE.h
        src = self.hT if src is None else src
        srcb = self.hTb if srcb is None else srcb

        def fn():
            ins = None
            for k in range(NCH):
                for ci, (c0, n) in enumerate(ctiles):
                    ins = PE.matmul(dst[ci][0], lhsT=sv[:, k, :], rhs=src[:, k, c0:c0 + n],
                                    start=(k == 0), stop=(k == NCH - 1))
            return ins
        self.op(self.PE, fn, Rb=[self.ringb[r]] + list(srcb), Wb=[b for d in dst for b in d[1]])

    def norm(self, gidx, ctiles, es, make_h=True):
        lo = ctiles[0][0]
        hi = ctiles[-1][0] + ctiles[-1][1]
        sq = self.sb("sq", [128, 2, T], BF16, es)
        rbc = self.sb("rbc", [128, T], F32, es)
        snap = self.snapshot()
        sqb = [Buf(snap), Buf(snap)]
        rbcb = Seg(SEGB, snap)
        A, V, PE = self.ACT.h, self.DVE.h, self.PE.h
        dst = [self.PA(ci, n) for ci, (c0, n) in enumerate(ctiles)]
        for k in range(NCH):
            p = k % 2
            self.op(self.ACT, lambda: A.activation(out=sq[:, p, lo:hi], in_=self.xT[:, k, lo:hi], func=AF.Square),
                    Rb=self.xb(k, lo, hi - lo), Wb=[sqb[p]])

            def fn():
                ins = None
                for ci, (c0, n) in enumerate(ctiles):
                    ins = PE.matmul(dst[ci][0], lhsT=self.onesf[:, :], rhs=sq[:, p, c0:c0 + n],
                                    start=(k == 0), stop=(k == NCH - 1))
                return ins
            self.op(self.PE, fn, Rb=[sqb[p], self.b_ones], Wb=[b for d in dst for b in d[1]])
        for ci, (c0, n) in enumerate(ctiles):
            rb = rbcb.get(c0, c0 + n)
            self.op(self.ACT, lambda: A.activation(out=rbc[:, c0:c0 + n], in_=dst[ci][0], func=AF.Sqrt,
                                                   scale=1.0 / D, bias=EPS), Rb=dst[ci][1], Wb=rb)
            self.op(self.DVE, lambda: V.reciprocal(out=rbc[:, c0:c0 + n], in_=rbc[:, c0:c0 + n]), Rb=rb, Wb=rb)
        if make_h:
            for k in range(NCH):
                self.op(self.DVE, lambda: V.scalar_tensor_tensor(
                    out=self.hT[:, k, lo:hi], in0=self.xT[:, k, lo:hi], scalar=self.gv[:, gidx, k:k + 1],
                    in1=rbc[:, lo:hi], op0=ALU.mult, op1=ALU.mult),
                    Rb=self.xb(k, lo, hi - lo) + rbcb.get(lo, hi) + [self.b_gv], Wb=[self.hTb[k]])
        return rbc, rbcb

    def resid_add(self, d, c0, n, ps, scale=None, scale_ap=None):
        V = self.DVE.h
        xs = self.xT[:, d, c0:c0 + n]
        sc = scale_ap if scale_ap is not None else scale
        rb = list(ps[1]) + self.xb(d, c0, n) + ([self.b_gv] if scale_ap is not None else [])
        self.op(self.DVE, lambda: V.scalar_tensor_tensor(out=xs, in0=ps[0], scalar=sc, in1=xs,
                                                         op0=ALU.mult, op1=ALU.add),
                Rb=rb, Wb=self.xb(d, c0, n))

    def phase_init(self):
        V = self.DVE.h
        xv = self.d_xT.rearrange("p (k t) -> p k t", k=NCH)
        for k in range(NCH):
            self.dma(self.SP, out=self.xT[:, k, :], in_=xv[:, k, :], Wb=self.xbufs[k].bufs)
        self.dma(self.SP, out=self.gv[:, :, :], in_=self.d_gvec.rearrange("p (g k) -> p g k", g=NGV), Wb=[self.b_gv])
        self.dma(self.SP, out=self.esink[:, :], in_=self.d_sinkT[:, :], Wb=[self.b_esink])
        so = self.o_pool_so.rearrange("b (r d) -> b r d", r=14)
        si = self.d_stN.rearrange("b (r d) -> b r d", r=15)
        for r0 in range(0, 14, 4):
            r1 = min(r0 + 4, 14)
            self.dma(self.SP, out=so[:, r0:r1, :], in_=si[:, 1 + r0:1 + r1, :])
        for (o_, i_) in ((self.o_kso, self.d_ckN), (self.o_vso, self.d_cvN)):
            ov = o_.rearrange("b (r d) -> b r d", r=127)
            iv = i_.rearrange("b (r d) -> b r d", r=128)
            for r0 in range(0, 127, 16):
                r1 = min(r0 + 16, 127)
                self.dma(self.SP, out=ov[:, r0:r1, :], in_=iv[:, 1 + r0:1 + r1, :])

        def fn():
            V.memset(self.ones[:, 0:64], 1.0)
            V.memset(self.ones[:, 64:128], 0.0)
            V.memset(self.ones[:, 128:192], 1.0)
            return V.memset(self.onesf[:, :], 1.0)
        self.op(self.DVE, fn, Wb=[self.b_ones])
        self.op(self.ACT, lambda: self.ACT.h.activation(out=self.esink[:, :], in_=self.esink[:, :], func=AF.Exp),
                Rb=[self.b_esink], Wb=[self.b_esink])

    def phase_memkv(self):
        A, V, PE = self.ACT.h, self.DVE.h, self.PE.h
        with ExitStack() as es:
            snap = self.snapshot()
            memT = self.sb("memT", [128, NCH, 256], F32, es)
            hm = self.sb("hm", [128, NCH, 256], BF16, es)
            sq = self.sb("sqm", [128, 2, 256], BF16, es)
            rbc = self.sb("rbcm", [128, 256], F32, es)
            kst = self.sb("kst", [128, 2, 4, 256], F32, es)
            vst = self.sb("vst", [128, 2, 2, 512], F32, es)
            b_mem, b_rbc, b_kst, b_vst = Buf(snap), Buf(snap), Buf(snap), Buf(snap)
            sqb = [Buf(snap), Buf(snap)]
            hmb = [Buf(snap) for _ in range(NCH)]
            self.dma(self.SP, out=memT[:, :, :], in_=self.d_memT.rearrange("p (k m) -> p k m", k=NCH), Wb=[b_mem])
            ps = self.bank(0, 0, 256)
            for k in range(NCH):
                p = k % 2
                self.op(self.ACT, lambda: A.activation(out=sq[:, p, :], in_=memT[:, k, :], func=AF.Square),
                        Rb=[b_mem], Wb=[sqb[p]])
                self.op(self.PE, lambda: PE.matmul(ps[0], lhsT=self.onesf[:, :], rhs=sq[:, p, :],
                                                   start=(k == 0), stop=(k == NCH - 1)),
                        Rb=[sqb[p], self.b_ones], Wb=ps[1])
            self.op(self.ACT, lambda: A.activation(out=rbc[:, :], in_=ps[0], func=AF.Sqrt, scale=1.0 / D, bias=EPS),
                    Rb=ps[1], Wb=[b_rbc])
            self.op(self.DVE, lambda: V.reciprocal(out=rbc[:, :], in_=rbc[:, :]), Rb=[b_rbc], Wb=[b_rbc])
            if DBG == 1:
                return
            for l in range(2):
                gi = GV[f'mem_{l}']
                for k in range(NCH):
                    self.op(self.DVE, lambda: V.scalar_tensor_tensor(
                        out=hm[:, k, :], in0=memT[:, k, :], scalar=self.gv[:, gi, k:k + 1], in1=rbc[:, :],
                        op0=ALU.mult, op1=ALU.mult), Rb=[b_mem, b_rbc, self.b_gv], Wb=[hmb[k]])
                for h in range(4):
                    r = self.take_slot(('col', 'w_xkv', l, h * 128))
                    sv = self.slot3(r, 16)
                    pk = self.bank(1 + (h % 2), 0, 256)

                    def fn():
                        ins = None
                        for k in range(NCH):
                            ins = PE.matmul(pk[0], lhsT=sv[:, k, :], rhs=hm[:, k, :], start=(k == 0), stop=(k == NCH - 1))
                        return ins
                    self.op(self.PE, fn, Rb=[self.ringb[r]] + hmb, Wb=pk[1])
                    self.op(self.ACT, lambda: A.activation(out=self.KmT[:, l, h, :], in_=pk[0], func=AF.Copy),
                            Rb=pk[1], Wb=[self.b_KmT[l]])
                    self.op(self.DVE, lambda: V.tensor_copy(out=kst[:, l, h, :], in_=pk[0]), Rb=pk[1], Wb=[b_kst])
                    if DBG == 2:
                        return
                pv = [self.bank(3, 0, 512), self.bank(4, 0, 512)]
                for h in range(4):
                    r = self.take_slot(('col', 'w_xkv', l, 512 + h * 128))
                    sv = self.slot3(r, 16)
                    for mb in range(2):
                        def fn():
                            ins = None
                            for k in range(NCH):
                                ins = PE.matmul(pv[mb][0][:, h * 128:(h + 1) * 128], lhsT=hm[:, k, mb * 128:(mb + 1) * 128],
                                                rhs=sv[:, k, :], start=(k == 0), stop=(k == NCH - 1))
                            return ins
                        self.op(self.PE, fn, Rb=[self.ringb[r]] + hmb, Wb=pv[mb][1])
                for mb in range(2):
                    self.op(self.ACT, lambda: A.activation(out=self.Vm[:, l, mb, :], in_=pv[mb][0], func=AF.Copy),
                            Rb=pv[mb][1], Wb=[self.b_Vm[l]])
                    self.op(self.DVE, lambda: V.tensor_copy(out=vst[:, l, mb, :], in_=pv[mb][0]), Rb=pv[mb][1], Wb=[b_vst])
            self.dma(self.SP, out=self.o_memk.rearrange("p (l h m) -> p l h m", l=2, h=4), in_=kst[:, :, :, :], Rb=[b_kst])
            self.dma(self.SP, out=self.o_memv.rearrange("p (l b n) -> p l b n", l=2, b=2), in_=vst[:, :, :, :], Rb=[b_vst])

    def phase_ffn(self, l, which, ctiles):
        A, V, PE = self.ACT.h, self.DVE.h, self.PE.h
        wgu, wdn = f'w_ffn{which}_gu', f'w_ffn{which}_dn'
        lo = ctiles[0][0]
        hi = ctiles[-1][0] + ctiles[-1][1]
        with ExitStack() as es:
            with ExitStack() as es2:
                self.norm(GV[f'ffn{which}_{l}'], ctiles, es2)
            snap = self.snapshot()
            act = self.sb("act", [128, 2, 4, T], BF16, es)
            sil = self.sb("sil", [128, T], F32, es)
            actb = [[[Buf(snap) for _ in range(3)] for _ in range(4)] for _ in range(2)]
            silb = [Buf(snap) for _ in range(3)]
            order = [2, 0, 1]
            pa = [self.PA(ci, n) for ci, (c0, n) in enumerate(ctiles)]
            pbb = [self.PB(ci, n) for ci, (c0, n) in enumerate(ctiles)]

            def gu(j):
                par, f = (j // 4) % 2, j % 4
                r = self.take_slot(('col', wgu, l, j * 128))
                self.proj(pa, r, ctiles)

                for ci in order:
                    c0, n = ctiles[ci]
                    self.op(self.ACT, lambda: A.activation(out=sil[:, c0:c0 + n], in_=pa[ci][0], func=AF.Silu),
                            Rb=pa[ci][1], Wb=[silb[ci]])
                r2 = self.take_slot(('col', wgu, l, DFF + j * 128))
                self.proj(pbb, r2, ctiles)

                for ci in order:
                    c0, n = ctiles[ci]
                    self.op(self.DVE, lambda: V.tensor_tensor(out=act[:, par, f, c0:c0 + n], in0=pbb[ci][0], in1=sil[:, c0:c0 + n],
                                                              op=ALU.mult), Rb=[silb[ci]] + pbb[ci][1], Wb=[actb[par][f][ci]])

            def down(g):
                par = g % 2
                rs = [self.take_slot(('row', wdn, l, (4 * g + f) * 128)) for f in range(4)]
                for d in range(NCH):
                    for ci, (c0, n) in enumerate(ctiles):
                        ps = self.pdslot(n)

                        def fn():
                            ins = None
                            for f in range(4):
                                ins = PE.matmul(ps[0], lhsT=self.ring[:, rs[f], d * 128:(d + 1) * 128],
                                                rhs=act[:, par, f, c0:c0 + n], start=(f == 0), stop=(f == 3))
                            return ins
                        self.op(self.PE, fn, Rb=[self.ringb[x] for x in rs] + [actb[par][f][ci] for f in range(4)], Wb=ps[1])
                        self.resid_add(d, c0, n, ps, scale=0.5)
            NG = NF // 4
            for g in range(NG):
                for f in range(4):
                    gu(4 * g + f)
                if g >= 1:
                    down(g - 1)
            down(NG - 1)

    def phase_pool(self):
        A, V, PE = self.ACT.h, self.DVE.h, self.PE.h
        ct = CT_ALL
        with ExitStack() as es:
            rbc, rbcb = self.norm(GV['mix_0'], ct, es, make_h=False)
            snap = self.snapshot()
            tA = self.sb("tA", [128, T], F32, es)
            tB = self.sb("tB", [128, T], F32, es)
            tC = self.sb("tC", [128, T], F32, es)
            icn = self.sb("icn", [128, T], F32, es)
            stk = self.sb("stk", [128, 2, NSAMP, 15], F32, es)
            ssum = self.sb("ssum", [128, NSAMP], F32, es)
            pst = self.sb("pst", [128, NCH, 15], F32, es)
            sst = self.sb("sst", [128, NCH, NSAMP], F32, es)
            bA, bB, bC, b_icn, b_ss, b_pst, b_sst = (Buf(snap) for _ in range(7))
            b_stk = [Buf(snap), Buf(snap)]
            gi = GV['mix_0']
            stT = self.d_stT.rearrange("p (k b r) -> p k b r", k=NCH, b=NSAMP)
            icv = self.d_icnt.rearrange("p (g t) -> p g t", g=4)
            rall = rbcb.get(0, T)
            for k in range(NCH):
                g = k // 4
                w = POOLW[g]
                if k % 4 == 0:
                    self.dma(self.SP, out=icn[:, :], in_=icv[:, g, :], Wb=[b_icn])
                sp_ = k % 2
                self.dma(self.SP, out=stk[:, sp_, :, :], in_=stT[:, k, :, :], Wb=[b_stk[sp_]])
                self.op(self.DVE, lambda: V.scalar_tensor_tensor(
                    out=tA[:, :], in0=self.xT[:, k, :], scalar=self.gv[:, gi, k:k + 1], in1=rbc[:, :],
                    op0=ALU.mult, op1=ALU.mult), Rb=self.xb(k, 0, T) + rall + [self.b_gv], Wb=[bA])
                cur, curb = tA, bA
                oth = [(tB, bB), (tC, bC)]
                st, oi = 1, 0
                while st < w:
                    nt, nb = oth[oi]
                    lo_ = 2 * st - 1
                    self.op(self.DVE, lambda: V.tensor_tensor(out=nt[:, lo_:T], in0=cur[:, lo_:T], in1=cur[:, lo_ - st:T - st],
                                                              op=ALU.add), Rb=[curb], Wb=[nb])
                    cur, curb = nt, nb
                    oi ^= 1
                    st *= 2
                nt, nb = oth[oi]
                self.op(self.DVE, lambda: V.tensor_tensor(out=nt[:, 15:SAMP0], in0=cur[:, 15:SAMP0], in1=icn[:, 15:SAMP0],
                                                          op=ALU.mult), Rb=[curb, b_icn], Wb=[nb])
                self.op(self.DVE, lambda: V.tensor_tensor(out=self.hT[:, k, 15:SAMP0], in0=nt[:, 15:SAMP0], in1=tA[:, 15:SAMP0],
                                                          op=ALU.subtract), Rb=[nb, bA], Wb=[self.hTb[k]])
                self.op(self.DVE, lambda: V.tensor_reduce(out=ssum[:, :], in_=stk[:, sp_, :, 15 - (w - 1):15], axis=AX.X, op=ALU.add),
                        Rb=[b_stk[sp_]], Wb=[b_ss])
                self.op(self.DVE, lambda: V.tensor_tensor(out=ssum[:, :], in0=ssum[:, :], in1=tA[:, SAMP0:T], op=ALU.add),
                        Rb=[b_ss, bA], Wb=[b_ss])
                self.op(self.DVE, lambda: V.scalar_tensor_tensor(out=self.hT[:, k, SAMP0:T], in0=ssum[:, :], scalar=1.0 / w,
                                                                 in1=tA[:, SAMP0:T], op0=ALU.mult, op1=ALU.subtract),
                        Rb=[b_ss, bA], Wb=[self.hTb[k]])
                self.op(self.ACT, lambda: A.activation(out=pst[:, k, :], in_=tA[:, SAMP0 - 15:SAMP0], func=AF.Copy), Rb=[bA], Wb=[b_pst])
                self.op(self.ACT, lambda: A.activation(out=sst[:, k, :], in_=tA[:, SAMP0:T], func=AF.Copy), Rb=[bA], Wb=[b_sst])
            self.dma(self.SP, out=self.o_pool_p.rearrange("p (k r) -> p k r", k=NCH), in_=pst[:, :, :], Rb=[b_pst])
            self.dma(self.SP, out=self.o_pool_sn.rearrange("p (k b) -> p k b", k=NCH), in_=sst[:, :, :], Rb=[b_sst])
            gs = GV['pscale']
            for g in range(4):
                r = self.take_slot(('pool', g))
                sv = self.slot3(r, 4)
                for oc in range(4):
                    d = 4 * g + oc
                    for ci, (c0, n) in enumerate(ct):
                        ps = self.pdslot(n)

                        def fn():
                            ins = None
                            for kc in range(4):
                                ins = PE.matmul(ps[0], lhsT=sv[:, kc, oc * 128:(oc + 1) * 128], rhs=self.hT[:, 4 * g + kc, c0:c0 + n],
                                                start=(kc == 0), stop=(kc == 3))
                            return ins
                        self.op(self.PE, fn, Rb=[self.ringb[r]] + self.hTb[4 * g:4 * g + 4], Wb=ps[1])
                        self.resid_add(d, c0, n, ps, scale_ap=self.gv[:, gs, d:d + 1])

    def phase_xattn(self, l, ctiles):
        A, V, PE = self.ACT.h, self.DVE.h, self.PE.h
        sc = 1.0 / math.sqrt(128.0)
        with ExitStack() as es:
            with ExitStack() as es2:
                self.norm(GV[f'xq_{l}'], ctiles, es2)
            snap = self.snapshot()
            qT = self.sb("qT", [128, 4, T], BF16, es)
            oT = self.sb("oT", [128, 4, T], BF16, es)
            PT = self.sb("PT", [128, 2, 2, 512], BF16, es)
            rec = self.sb("rec", [128, 2, 512], F32, es)
            PTs = self.sb("PTs", [128, 128], BF16, es)
            recs = self.sb("recs", [128, 64], F32, es)
            qTb = [Buf(snap) for _ in range(4)]
            oTb = [Seg(SEGB, snap) for _ in range(4)]
            PTb = [[Buf(snap), Buf(snap)], [Buf(snap), Buf(snap)]]
            recb = [Buf(snap), Buf(snap)]
            b_PTs, b_recs = Buf(snap), Buf(snap)
            for h in range(4):
                r = self.take_slot(('col', 'w_xq', l, h * 128))
                dst = [(self.PA if h % 2 == 0 else self.PB)(ci, n) for ci, (c0, n) in enumerate(ctiles)]
                self.proj(dst, r, ctiles)

                def fa():
                    ins = None
                    for ci, (c0, n) in enumerate(ctiles):
                        ins = A.activation(out=qT[:, h, c0:c0 + n], in_=dst[ci][0], func=AF.Copy)
                    return ins
                self.op(self.ACT, fa, Rb=[b for d in dst for b in d[1]], Wb=[qTb[h]])
            units = [(h, c0, n) for h in range(4) for (c0, n) in ctiles if c0 < SAMP0]

            def xs1(u):
                h, c0, n = units[u]
                par = u % 2
                pS = [self.bank(2 * par, 0, n), self.bank(2 * par + 1, 0, n)]

                def fs():
                    ins = None
                    for mb in range(2):
                        ins = PE.matmul(pS[mb][0], lhsT=self.KmT[:, l, h, mb * 128:(mb + 1) * 128], rhs=qT[:, h, c0:c0 + n],
                                        start=True, stop=True)
                    return ins
                self.op(self.PE, fs, Rb=[self.b_KmT[l], qTb[h]], Wb=pS[0][1] + pS[1][1])
                for mb in range(2):
                    self.op(self.ACT, lambda: A.activation(out=PT[:, par, mb, 0:n], in_=pS[mb][0], func=AF.Exp, scale=sc),
                            Rb=pS[mb][1], Wb=[PTb[par][mb]])

            def xs2(u):
                h, c0, n = units[u]
                par = u % 2
                po = self.bank(4 + par, 0, n)
                pd = self.bank(7 - par, 0, n)

                def fo():
                    ins = None
                    for mb in range(2):
                        ins = PE.matmul(po[0], lhsT=self.Vm[:, l, mb, h * 128:(h + 1) * 128], rhs=PT[:, par, mb, 0:n],
                                        start=(mb == 0), stop=(mb == 1))
                    for mb in range(2):
                        ins = PE.matmul(pd[0], lhsT=self.onesf[:, :], rhs=PT[:, par, mb, 0:n], start=(mb == 0), stop=(mb == 1))
                    return ins
                self.op(self.PE, fo, Rb=[self.b_Vm[l], self.b_ones] + PTb[par], Wb=po[1] + pd[1])
                self.op(self.DVE, lambda: V.reciprocal(out=rec[:, par, 0:n], in_=pd[0]), Rb=pd[1], Wb=[recb[par]])
                self.op(self.DVE, lambda: V.tensor_tensor(out=oT[:, h, c0:c0 + n], in0=po[0], in1=rec[:, par, 0:n], op=ALU.mult),
                        Rb=po[1] + [recb[par]], Wb=oTb[h].get(c0, c0 + n))
            xs1(0)
            for u in range(len(units)):
                if u + 1 < len(units):
                    xs1(u + 1)
                xs2(u)
            pSs = self.bank(0, 0, 128)
            krs = [self.take_slot(('xk', l, bp)) for bp in range(8)]
            for bp in range(8):
                kv = self.ring[:, krs[bp], :].rearrange("p (b h m) -> p b h m", b=2, h=4)

                def fk():
                    ins = None
                    for bb in range(2):
                        b = 2 * bp + bb
                        for h in range(4):
                            for mb in range(2):
                                col = mb * 64 + h * 16 + b
                                ins = PE.matmul(pSs[0][:, col:col + 1], lhsT=kv[:, bb, h, mb * 128:(mb + 1) * 128],
                                                rhs=qT[:, h, SAMP0 + b:SAMP0 + b + 1], start=True, stop=True)
                    return ins
                self.op(self.PE, fk, Rb=[self.ringb[krs[bp]]] + qTb, Wb=pSs[1])
            self.op(self.ACT, lambda: A.activation(out=PTs[:, :], in_=pSs[0], func=AF.Exp, scale=sc), Rb=pSs[1], Wb=[b_PTs])
            pos_ = self.bank(1, 0, 64)
            pds_ = self.bank(2, 0, 64)
            vrs = [self.take_slot(('xv', l, bp)) for bp in range(8)]
            for bp in range(8):
                vv = self.ring[:, vrs[bp], :].rearrange("p (b m n) -> p b m n", b=2, m=2)

                def fv():
                    ins = None
                    for bb in range(2):
                        b = 2 * bp + bb
                        for h in range(4):
                            for mb in range(2):
                                col = mb * 64 + h * 16 + b
                                ins = PE.matmul(pos_[0][:, h * 16 + b:h * 16 + b + 1], lhsT=vv[:, bb, mb, h * 128:(h + 1) * 128],
                                                rhs=PTs[:, col:col + 1], start=(mb == 0), stop=(mb == 1))
                    return ins
                self.op(self.PE, fv, Rb=[self.ringb[vrs[bp]], b_PTs], Wb=pos_[1])

            def fd():
                ins = None
                for mb in range(2):
                    ins = PE.matmul(pds_[0], lhsT=self.onesf[:, :], rhs=PTs[:, mb * 64:(mb + 1) * 64], start=(mb == 0), stop=(mb == 1))
                return ins
            self.op(self.PE, fd, Rb=[b_PTs, self.b_ones], Wb=pds_[1])
            self.op(self.DVE, lambda: V.reciprocal(out=recs[:, :], in_=pds_[0]), Rb=pds_[1], Wb=[b_recs])
            self.op(self.DVE, lambda: V.tensor_tensor(out=oT[:, :, SAMP0:T], in0=pos_[0].rearrange("p (h b) -> p h b", h=4),
                                                      in1=recs[:, :].rearrange("p (h b) -> p h b", h=4), op=ALU.mult),
                    Rb=pos_[1] + [b_recs], Wb=[b for s in oTb for b in s.get(SAMP0, T)])
            rs = [self.take_slot(('row', 'w_xo', l, h * 128)) for h in range(4)]
            for d in range(NCH):
                for ci, (c0, n) in enumerate(ctiles):
                    ps = self.pdslot(n)

                    def fn():
                        ins = None
                        for h in range(4):
                            ins = PE.matmul(ps[0], lhsT=self.ring[:, rs[h], d * 128:(d + 1) * 128], rhs=oT[:, h, c0:c0 + n],
                                            start=(h == 0), stop=(h == 3))
                        return ins
                    self.op(self.PE, fn, Rb=[self.ringb[x] for x in rs] + [b for s in oTb for b in s.get(c0, c0 + n)], Wb=ps[1])
                    self.resid_add(d, c0, n, ps, scale=1.0)

    def phase_swa(self):
        A, V, PE = self.ACT.h, self.DVE.h, self.PE.h
        ct = CT_ALL
        sc = 0.125
        LAST0 = SAMP0 - 128
        with ExitStack() as es:
            with ExitStack() as es2:
                self.norm(GV['mix_1'], ct, es2)
            snap = self.snapshot()
            cs = self.sb("cs", [128, 2, T], F32, es)
            msk = self.sb("msk", [128, 2, 512], BF16, es)
            kT = self.sb("kT", [128, T], BF16, es)
            Vp = self.sb("Vp", [128, 9, 192], BF16, es)
            qc = self.sb("qc", [128, 2, T], BF16, es)
            qs = self.sb("qs", [128, 4, NSAMP], BF16, es)
            oT = self.sb("oTs", [128, 4, T], BF16, es)
            r1 = self.sb("r1", [128, 2, 512], F32, es)
            r2 = self.sb("r2", [128, 2, 512], F32, es)
            PT = self.sb("PTw", [128, 2, 512], BF16, es)
            dn = self.sb("dn", [128, 2, 128], F32, es)
            PTs = self.sb("PTss", [128, 128], BF16, es)
            dns = self.sb("dns", [128, 64], F32, es)
            kpst = self.sb("kpst", [128, 4, 128], F32, es)
            vpst = self.sb("vpst", [128, 4, 128], F32, es)
            ksst = self.sb("ksst", [128, 4, NSAMP], F32, es)
            vsst = self.sb("vsst", [NSAMP, 4, 128], F32, es)
            vnb = self.sb("vnb", [NSAMP, 128], BF16, es)
            b_cs, b_msk, b_kT, b_Vp, b_qs, b_PTs, b_dns, b_kpst, b_vpst, b_ksst, b_vsst, b_vnb = (Buf(snap) for _ in range(12))
            qcb = [Buf(snap), Buf(snap)]
            oTb = [Seg(SEGB, snap) for _ in range(4)]
            r1b = [Buf(snap), Buf(snap)]
            r2b = [Buf(snap), Buf(snap)]
            PTb = [[Buf(snap), Buf(snap)], [Buf(snap), Buf(snap)]]
            dnb = [Buf(snap), Buf(snap)]
            self.dma(self.SP, out=cs[:, :, :], in_=self.d_cs.rearrange("p (a t) -> p a t", a=2), Wb=[b_cs])
            self.dma(self.SP, out=msk[:, :, :], in_=self.d_mask.rearrange("p (a t) -> p a t", a=2), Wb=[b_msk])

            def fz():
                V.memset(Vp[:, :, :], 0.0)
                return V.memset(oT[:, :, :], 0.0)
            self.op(self.DVE, fz, Wb=[b_Vp] + [b for s in oTb for b in s.bufs])
            if DBG == 3:
                raise _Stop()
            self.rope_n = 0

            def rope(pn, ps_, outs):
                for ci, (c0, n) in enumerate(ct):
                    p = self.rope_n % 2
                    self.rope_n += 1
                    self.op(self.DVE, lambda: V.tensor_tensor(out=r1[:, p, 0:n], in0=pn[ci][0], in1=cs[:, 0, c0:c0 + n], op=ALU.mult),
                            Rb=pn[ci][1] + [b_cs], Wb=[r1b[p]])
                    self.op(self.DVE, lambda: V.tensor_tensor(out=r2[:, p, 0:n], in0=ps_[ci][0], in1=cs[:, 1, c0:c0 + n], op=ALU.mult),
                            Rb=ps_[ci][1] + [b_cs], Wb=[r2b[p]])
                    self.op(self.DVE, lambda: V.tensor_tensor(out=r1[:, p, 0:n], in0=r1[:, p, 0:n], in1=r2[:, p, 0:n], op=ALU.add),
                            Rb=[r1b[p], r2b[p]], Wb=[r1b[p]])
                    for (dst, bufs, lo_, hi_) in outs:
                        a, b = max(lo_, c0), min(hi_, c0 + n)
                        if a >= b:
                            continue
                        self.op(self.ACT, lambda: A.activation(out=dst(a, b), in_=r1[:, p, a - c0:b - c0], func=AF.Copy),
                                Rb=[r1b[p]], Wb=bufs)

            pa = [self.PA(ci, n) for ci, (c0, n) in enumerate(ct)]
            pbb = [self.PB(ci, n) for ci, (c0, n) in enumerate(ct)]
            unit = 0
            for jj in range(4):
                r = self.take_slot(('col', 'w_qkv', 0, ('k', jj, 0)))
                self.proj(pa, r, ct)
                r = self.take_slot(('col', 'w_qkv', 0, ('k', jj, 1)))
                self.proj(pbb, r, ct)
                rope(pa, pbb, [
                    (lambda a, b: kT[:, a:b], [b_kT], 0, T),
                    (lambda a, b: kpst[:, jj, a - LAST0:b - LAST0], [b_kpst], LAST0, SAMP0),
                    (lambda a, b: ksst[:, jj, a - SAMP0:b - SAMP0], [b_ksst], SAMP0, T),
                ])
                if DBG == 4:
                    raise _Stop()
                r = self.take_slot(('col', 'w_qkv', 0, ('v', jj)))
                sv = self.slot3(r, 16)
                pv = [self.bank(4, 0, 512), self.bank(5, 0, 512), self.bank(7, 0, 256)]

                def vdst(blk):
                    return pv[blk // 4][0][:, (blk % 4) * 128:(blk % 4) * 128 + 128]
                for blk in range(10):
                    c0 = 16 + 128 * blk if blk < 9 else SAMP0
                    m = 128 if blk < 9 else NSAMP

                    def fn():
                        ins = None
                        for k in range(NCH):
                            ins = PE.matmul(vdst(blk)[0:m, :], lhsT=self.hT[:, k, c0:c0 + m], rhs=sv[:, k, :],
                                            start=(k == 0), stop=(k == NCH - 1))
                        return ins
                    self.op(self.PE, fn, Rb=[self.ringb[r]] + self.hTb, Wb=pv[blk // 4][1])
                for grp in range(3):
                    nb = 4 if grp < 2 else 1
                    src = pv[grp][0][:, 0:nb * 128].rearrange("p (b j d) -> p b j d", b=nb, j=2)
                    dstv = Vp[:, 4 * grp:4 * grp + nb, :].rearrange("p b (j d) -> p b j d", j=3)[:, :, 0:3:2, :]
                    self.op(self.ACT, lambda: A.activation(out=dstv, in_=src, func=AF.Copy), Rb=pv[grp][1], Wb=[b_Vp])
                self.op(self.DVE, lambda: V.tensor_copy(out=vpst[:, jj, :], in_=vdst(8)), Rb=pv[2][1], Wb=[b_vpst])
                self.op(self.DVE, lambda: V.tensor_copy(out=vsst[:, jj, :], in_=vdst(9)[0:NSAMP, :]), Rb=pv[2][1], Wb=[b_vsst])
                self.op(self.DVE, lambda: V.tensor_copy(out=vnb[:, :], in_=vdst(9)[0:NSAMP, :]), Rb=pv[2][1], Wb=[b_vnb])
                if DBG == 5:
                    raise _Stop()
                for g in range(4):
                    qp = g % 2
                    r = self.take_slot(('col', 'w_qkv', 0, ('q', jj, g, 0)))
                    self.proj(pa, r, ct)
                    r = self.take_slot(('col', 'w_qkv', 0, ('q', jj, g, 1)))
                    self.proj(pbb, r, ct)
                    rope(pa, pbb, [
                        (lambda a, b: qc[:, qp, a:b], [qcb[qp]], 0, T),
                        (lambda a, b: qs[:, g, a - SAMP0:b - SAMP0], [b_qs], SAMP0, T),
                    ])
                    if DBG == 61:
                        raise _Stop()
                    cidx = jj * 4 + g

                    def stage1(n_):
                        par = n_ % 2
                        q0 = HALO + 128 * n_
                        pS2 = [self.bank(2 * par, 0, 256), self.bank(2 * par + 1, 0, 256)]
                        mv = 1 if n_ == 0 else 0

                        def fs():
                            ins = None
                            for kb in range(2):
                                for hd in range(2):
                                    b0 = 64 * hd
                                    k0 = 16 + 128 * (n_ + kb)
                                    ins = PE.matmul(pS2[hd][0][:, kb * 128:kb * 128 + 128], lhsT=kT[b0:b0 + 64, k0:k0 + 128],
                                                    rhs=qc[b0:b0 + 64, qp, q0:q0 + 128], start=True, stop=True)
                            return ins
                        self.op(self.PE, fs, Rb=[b_kT, qcb[qp]], Wb=pS2[0][1] + pS2[1][1])
                        for hd in range(2):
                            self.op(self.ACT, lambda: A.activation(out=PT[:, par, hd * 256:hd * 256 + 256], in_=pS2[hd][0], func=AF.Exp, scale=sc),
                                    Rb=pS2[hd][1], Wb=[PTb[par][hd]])
                            self.op(self.DVE, lambda: V.tensor_tensor(out=PT[:, par, hd * 256:hd * 256 + 256], in0=PT[:, par, hd * 256:hd * 256 + 256],
                                                                      in1=msk[:, mv, hd * 256:hd * 256 + 256], op=ALU.mult),
                                    Rb=[PTb[par][hd], b_msk], Wb=[PTb[par][hd]])

                    def stage2(n_):
                        par = n_ % 2
                        q0 = HALO + 128 * n_
                        pod = self.bank(4 + par, 0, 256)

                        def fo():
                            ins = None
                            i = 0
                            for hd in range(2):
                                for kb in range(2):
                                    o0 = (hd * 2 + kb) * 128
                                    ins = PE.matmul(pod[0][:, 0:128], lhsT=Vp[:, n_ + kb, 64 * hd:64 * hd + 128], rhs=PT[:, par, o0:o0 + 128],
                                                    start=(i == 0), stop=(i == 3))
                                    i += 1
                            i = 0
                            for hd in range(2):
                                for kb in range(2):
                                    o0 = (hd * 2 + kb) * 128
                                    ins = PE.matmul(pod[0][:, 128:256], lhsT=self.ones[:, 64 * hd:64 * hd + 128], rhs=PT[:, par, o0:o0 + 128],
                                                    start=(i == 0), stop=(i == 3))
                                    i += 1
                            return ins
                        self.op(self.PE, fo, Rb=[b_Vp, self.b_ones] + PTb[par], Wb=pod[1])
                        self.op(self.DVE, lambda: V.tensor_scalar(out=dn[:, par, :], in0=pod[0][:, 128:256], scalar1=self.esink[:, cidx:cidx + 1],
                                                                  scalar2=None, op0=ALU.add), Rb=pod[1] + [self.b_esink], Wb=[dnb[par]])
                        self.op(self.DVE, lambda: V.reciprocal(out=dn[:, par, :], in_=dn[:, par, :]), Rb=[dnb[par]], Wb=[dnb[par]])
                        self.op(self.DVE, lambda: V.tensor_tensor(out=oT[:, g, q0:q0 + 128], in0=pod[0][:, 0:128], in1=dn[:, par, :], op=ALU.mult),
                                Rb=pod[1] + [dnb[par]], Wb=oTb[g].get(q0, q0 + 128))
                    stage1(0)
                    for n_ in range(8):
                        if n_ + 1 < 8:
                            stage1(n_ + 1)
                        stage2(n_)
                    if DBG == 6:
                        raise _Stop()
                rk = self.take_slot(('sk', jj))
                kv = self.ring[:, rk, :].rearrange("p (b s) -> p b s", b=NSAMP)
                self.op(self.DVE, lambda: V.tensor_copy(out=kv[:, :, 0], in_=kT[:, SAMP0:T]), Rb=[b_kT, self.ringb[rk]], Wb=[self.ringb[rk]])
                pSs2 = [self.bank(6, 0, 64), self.bank(7, 0, 64)]

                def fk():
                    ins = None
                    for g in range(4):
                        for b in range(NSAMP):
                            for hd in range(2):
                                b0 = 64 * hd
                                col = g * NSAMP + b
                                ins = PE.matmul(pSs2[hd][0][:, col:col + 1], lhsT=kv[b0:b0 + 64, b, :], rhs=qs[b0:b0 + 64, g, b:b + 1],
                                                start=True, stop=True)
                    return ins
                self.op(self.PE, fk, Rb=[self.ringb[rk], b_qs], Wb=pSs2[0][1] + pSs2[1][1])

                def fes():
                    ins = None
                    for hd in range(2):
                        ins = A.activation(out=PTs[:, hd * 64:hd * 64 + 64], in_=pSs2[hd][0], func=AF.Exp, scale=sc)
                    return ins
                self.op(self.ACT, fes, Rb=pSs2[0][1] + pSs2[1][1], Wb=[b_PTs])
                rvs = [self.take_slot(('sv', jj, hf)) for hf in range(2)]
                pos_ = self.bank(5, 0, 64)
                pds_ = self.bank(4, 0, 64)
                for hf in range(2):
                    vv = self.ring[:, rvs[hf], 0:8 * 192].rearrange("p (b n) -> p b n", b=8)
                    dst = self.ring[0:1, rvs[hf], 0:8 * 192].rearrange("p (b n) -> p b n", b=8)
                    for j2 in range(2):
                        self.dma(self.SP, out=dst[:, :, 128 * j2:128 * j2 + 64], in_=vnb[8 * hf:8 * hf + 8, 64 * j2:64 * j2 + 64],
                                 Rb=[b_vnb], Wb=[self.ringb[rvs[hf]]])

                    def fv():
                        ins = None
                        for g in range(4):
                            for b8 in range(8):
                                b = 8 * hf + b8
                                for hd in range(2):
                                    col = hd * 64 + g * NSAMP + b
                                    ins = PE.matmul(pos_[0][:, g * NSAMP + b:g * NSAMP + b + 1], lhsT=vv[:, b8, 64 * hd:64 * hd + 128],
                                                    rhs=PTs[:, col:col + 1], start=(hd == 0), stop=(hd == 1))
                        return ins
                    self.op(self.PE, fv, Rb=[self.ringb[rvs[hf]], b_PTs], Wb=pos_[1])

                def fd():
                    ins = None
                    for g in range(4):
                        for hd in range(2):
                            c_ = hd * 64 + g * NSAMP
                            ins = PE.matmul(pds_[0][:, g * NSAMP:(g + 1) * NSAMP], lhsT=self.ones[:, 64 * hd:64 * hd + 128],
                                            rhs=PTs[:, c_:c_ + NSAMP], start=(hd == 0), stop=(hd == 1))
                    return ins
                self.op(self.PE, fd, Rb=[b_PTs, self.b_ones], Wb=pds_[1])

                def fdn():
                    ins = None
                    for g in range(4):
                        cidx = jj * 4 + g
                        ins = V.tensor_scalar(out=dns[:, g * NSAMP:(g + 1) * NSAMP], in0=pds_[0][:, g * NSAMP:(g + 1) * NSAMP],
                                              scalar1=self.esink[:, cidx:cidx + 1], scalar2=None, op0=ALU.add)
                    return ins
                self.op(self.DVE, fdn, Rb=pds_[1] + [self.b_esink], Wb=[b_dns])
                self.op(self.DVE, lambda: V.reciprocal(out=dns[:, :], in_=dns[:, :]), Rb=[b_dns], Wb=[b_dns])
                self.op(self.DVE, lambda: V.tensor_tensor(out=oT[:, :, SAMP0:T], in0=pos_[0].rearrange("p (g b) -> p g b", g=4),
                                                          in1=dns[:, :].rearrange("p (g b) -> p g b", g=4), op=ALU.mult),
                        Rb=pos_[1] + [b_dns], Wb=[b for s in oTb for b in s.get(SAMP0, T)])
                if DBG == 7:
                    raise _Stop()
                rs = [self.take_slot(('rowsel', 'w_o', 0, (jj, g))) for g in range(4)]
                for d in range(NCH):
                    for ci, (c0, n) in enumerate(CT_MAIN):
                        ps = self.pdslot(n) if False else self.bank(6, 0, n) if False else self.pdslot_swa(n)

                        def fn():
                            ins = None
                            for g in range(4):
                                ins = PE.matmul(ps[0], lhsT=self.ring[:, rs[g], d * 128:(d + 1) * 128], rhs=oT[:, g, c0:c0 + n],
                                                start=(g == 0), stop=(g == 3))
                            return ins
                        self.op(self.PE, fn, Rb=[self.ringb[x] for x in rs] + [b for s in oTb for b in s.get(c0, c0 + n)], Wb=ps[1])
                        self.resid_add(d, c0, n, ps, scale=1.0)
            self.dma(self.SP, out=self.o_kp.rearrange("p (j t) -> p j t", j=4), in_=kpst[:, :, :], Rb=[b_kpst])
            self.dma(self.SP, out=self.o_vp.rearrange("p (j t) -> p j t", j=4), in_=vpst[:, :, :], Rb=[b_vpst])
            self.dma(self.SP, out=self.o_ksn.rearrange("p (j t) -> p j t", j=4), in_=ksst[:, :, :], Rb=[b_ksst])
            self.dma(self.SP, out=self.o_vsn.rearrange("p (j t) -> p j t", j=4), in_=vsst[:, :, :], Rb=[b_vsst])

    def pdslot_swa(self, n):
        i = (4, 5)[self.pd_rr % 2]
        self.pd_rr += 1
        return self.bank(i, 0, n)

    def phase_final(self):
        V = self.DVE.h
        ct = CT_MAIN
        with ExitStack() as es:
            rbc, rbcb = self.norm(GV['final'], ct, es, make_h=False)
            snap = self.snapshot()
            yst = self.sb("yst", [128, 2, NOUT], F32, es)
            yb = [Buf(snap), Buf(snap)]
            gi = GV['final']
            yv = self.o_yT.rearrange("p (k t) -> p k t", k=NCH)
            for k in range(NCH):
                p = k % 2
                self.op(self.DVE, lambda: V.scalar_tensor_tensor(
                    out=yst[:, p, :], in0=self.xT[:, k, HALO:T], scalar=self.gv[:, gi, k:k + 1], in1=rbc[:, HALO:T],
                    op0=ALU.mult, op1=ALU.mult), Rb=self.xb(k, HALO, NOUT) + rbcb.get(HALO, T) + [self.b_gv], Wb=[yb[p]])
                self.dma(self.SP, out=yv[:, k, :], in_=yst[:, p, :], Rb=[yb[p]])

    def finish(self):
        SP = self.SP
        for k in self.spsem:
            if self.dval[k] > 0 and SP.known.get(k, 0) < self.dval[k]:
                SP.h.wait_ge(self.sems[k], self.dval[k])
        for E in (self.PE, self.ACT, self.DVE):
            SP.h.wait_ge(self.sems[E.key], E.cnt)
        for k in self.rsem:
            if self.dval[k] > 0:
                self.POOL.h.wait_ge(self.sems[k], self.dval[k])


NS_MAX = 668
DBG = 0


class _Stop(Exception):
    pass
_CACHE = {}


def build_program(stop=None, ns=None):
    key = (stop, ns)
    if key in _CACHE:
        return _CACHE[key]
    P = Prog()
    P.declare()
    P.NS = ns if ns is not None else NS_MAX
    P.wstream = P.dram("wstream", [P.NS, 128, 2048], F32, "ExternalInput")
    phases = [P.phase_init, P.phase_memkv, lambda: P.phase_ffn(0, 1, CT_ALL), P.phase_pool,
              lambda: P.phase_xattn(0, CT_ALL), lambda: P.phase_ffn(0, 2, CT_ALL), lambda: P.phase_ffn(1, 1, CT_ALL),
              P.phase_swa, lambda: P.phase_xattn(1, CT_MAIN), lambda: P.phase_ffn(1, 2, CT_MAIN), P.phase_final]
    for i, ph in enumerate(phases):
        if stop is not None and i >= stop:
            break
        try:
            ph()
        except _Stop:
            break
    P.finish()
    assert len(P.slots) <= P.NS or stop is not None, len(P.slots)
    _CACHE[key] = P
    return P


def _col_tile(W, cols):
    return np.ascontiguousarray(W[:, cols].reshape(NCH, 128, 128).transpose(1, 0, 2)).reshape(128, 2048)


def _qkv_cols(spec):
    kind = spec[0]
    if kind == 'k':
        _, jj, sw = spec
        base = 2048 + jj * 128
        heads = [base, base + 64]
    elif kind == 'v':
        _, jj = spec
        return np.arange(2560 + jj * 128, 2560 + jj * 128 + 128)
    else:
        _, jj, g, sw = spec
        heads = [(8 * jj + g) * 64, (8 * jj + 4 + g) * 64]
    cols = []
    for hb in heads:
        idx = np.arange(64)
        if sw:
            idx = (idx + 32) % 64
        cols.append(hb + idx)
    return np.concatenate(cols)


def _common_slots(slots, inp):
    ws = np.zeros((max(len(slots), 1), 128, 2048), np.float32)
    percore = []
    for s, d in enumerate(slots):
        kind = d[0]
        if kind == 'col':
            _, name, l, c = d
            W = inp[name][l]
            cols = _qkv_cols(c) if name == 'w_qkv' else np.arange(c, c + 128)
            ws[s] = _col_tile(W, cols)
        elif kind == 'row':
            _, name, l, r0 = d
            ws[s] = inp[name][l][r0:r0 + 128, :]
        elif kind == 'rowsel':
            _, name, l, (jj, g) = d
            rows = np.concatenate([np.arange((8 * jj + g) * 64, (8 * jj + g) * 64 + 64),
                                   np.arange((8 * jj + 4 + g) * 64, (8 * jj + 4 + g) * 64 + 64)])
            ws[s] = inp[name][l][rows, :]
        elif kind == 'pool':
            g = d[1]
            ws[s] = inp['w_pool'][0][g].reshape(4, 128, 512).transpose(1, 0, 2).reshape(128, 2048)
        else:
            percore.append((s, d))
    return ws, percore


def _fill_percore(ws, percore, inp, c):
    b0 = NSAMP * c
    for s, d in percore:
        kind = d[0]
        if kind == 'xk':
            _, l, bp = d
            a = inp['cache_mem_k'][l, b0 + 2 * bp:b0 + 2 * bp + 2]
            ws[s] = a.transpose(3, 0, 2, 1).reshape(128, 2048)
        elif kind == 'xv':
            _, l, bp = d
            a = inp['cache_mem_v'][l, b0 + 2 * bp:b0 + 2 * bp + 2]
            ws[s] = a.reshape(2, 2, 128, 512).transpose(2, 0, 1, 3).reshape(128, 2048)
        elif kind == 'sk':
            _, jj = d
            a = inp['cache_swa_k'][0, b0:b0 + NSAMP, :, 2 * jj:2 * jj + 2, :]
            ws[s] = a.transpose(2, 3, 0, 1).reshape(128, 2048)
        elif kind == 'sv':
            _, jj, hf = d
            a = inp['cache_swa_v'][0, b0 + 8 * hf:b0 + 8 * hf + 8, :, 2 * jj:2 * jj + 2, :]
            t = np.zeros((128, 8, 192), np.float32)
            t[:, :, 0:64] = a[:, :, 0, :].transpose(1, 0, 2)
            t[:, :, 128:192] = a[:, :, 1, :].transpose(1, 0, 2)
            ws[s] = 0.0
            ws[s][:, 0:8 * 192] = t.reshape(128, 8 * 192)


def _fm(a):
    return np.ascontiguousarray(a.reshape(a.shape[0], NCH, 128).transpose(2, 1, 0))


def _const_tables(c):
    pos = np.zeros(T, np.float64)
    pos[:SAMP0] = c * NMAIN - HALO + np.arange(SAMP0)
    pos[SAMP0:] = PAST
    p = np.arange(128)
    dd = p % 64
    inv = (10000.0 ** (-(dd % 32).astype(np.float32) / 32.0)).astype(np.float32)
    ang = pos.astype(np.float32)[None, :] * inv[:, None]
    cs = np.zeros((128, 2, T), np.float32)
    cs[:, 0, :] = np.cos(ang)
    sn = np.sin(ang)
    cs[:, 1, :] = np.where((dd < 32)[:, None], -sn, sn)
    ic = np.ones((4, T), np.float32)
    for g, w in enumerate(POOLW):
        cnt = np.minimum(float(w), np.maximum(pos[:SAMP0] + 1.0, 1.0))
        ic[g, :SAMP0] = (1.0 / cnt).astype(np.float32)
    icnt = np.broadcast_to(ic[None], (128, 4, T)).reshape(128, 4 * T)
    k = np.arange(128)[:, None]
    q = np.arange(128)[None, :]
    m = np.zeros((128, 2, 2, 2, 128), np.float32)
    for hd in range(2):
        m[:, 0, hd, 0, :] = (k > q)
        m[:, 0, hd, 1, :] = (k <= q)
    m[:, 1] = m[:, 0]
    if c == 0:
        m[:, 1, :, 0, :] = 0.0
    return cs.reshape(128, 2 * T), np.ascontiguousarray(icnt), m.reshape(128, 1024).astype(NPBF)


def prep_common(inp, P):
    ws_common, percore = _common_slots(P.slots, inp)
    gv = np.zeros((128, NGV, NCH), np.float32)
    vecs = {'ffn1_0': inp['g_ffn1'][0], 'mix_0': inp['g_mix'][0], 'xq_0': inp['g_xq'][0], 'ffn2_0': inp['g_ffn2'][0],
            'ffn1_1': inp['g_ffn1'][1], 'mix_1': inp['g_mix'][1], 'xq_1': inp['g_xq'][1], 'ffn2_1': inp['g_ffn2'][1],
            'final': inp['g_final'], 'mem_0': inp['g_mem'][0], 'mem_1': inp['g_mem'][1], 'pscale': inp['pool_scale'][0]}
    for name, v in vecs.items():
        gv[:, GV[name], :] = v.reshape(NCH, 128).T
    sinks = inp['sinks'][0]
    sinkT = np.zeros((128, 16), np.float32)
    for jj in range(4):
        for g in range(4):
            sinkT[0:64, jj * 4 + g] = sinks[8 * jj + g]
            sinkT[64:128, jj * 4 + g] = sinks[8 * jj + 4 + g]
    memT = _fm(inp['mem_prompt'][0]).reshape(128, NCH * 256)
    return dict(ws=ws_common, percore=percore, gv=gv, sinkT=sinkT, memT=memT)


def core_inputs(inp, cm, c):
    xp = inp['x_prompt'][0]
    xs = inp['x_sample'][:, 0, :]
    tok = np.zeros((T, D), np.float32)
    lo = c * NMAIN - HALO
    a = max(lo, 0)
    tok[a - lo:SAMP0] = xp[a:(c + 1) * NMAIN]
    tok[SAMP0:] = xs[NSAMP * c:NSAMP * (c + 1)]
    cs, icnt, mask = _const_tables(c)
    st = inp['state_pool'][0, NSAMP * c:NSAMP * (c + 1)]
    stT = np.ascontiguousarray(st.reshape(NSAMP, 15, NCH, 128).transpose(3, 2, 0, 1)).reshape(128, NCH * NSAMP * 15)
    ws = cm['ws'].copy()
    _fill_percore(ws, cm['percore'], inp, c)
    return {
        'xT': _fm(tok).reshape(128, NCH * T), 'memT': cm['memT'], 'stT': stT,
        'stN': np.ascontiguousarray(st).reshape(NSAMP, 15 * D),
        'ckN': np.ascontiguousarray(inp['cache_swa_k'][0, NSAMP * c:NSAMP * (c + 1)]).reshape(NSAMP, 128 * 512),
        'cvN': np.ascontiguousarray(inp['cache_swa_v'][0, NSAMP * c:NSAMP * (c + 1)]).reshape(NSAMP, 128 * 512),
        'gvec': cm['gv'].reshape(128, NGV * NCH), 'sinkT': cm['sinkT'], 'cs': cs, 'icnt': icnt, 'maskc': mask, 'wstream': ws,
    }


def kernel(**inp):
    inp = {k: np.asarray(v) for k, v in inp.items()}
    P = build_program()
    cm = prep_common(inp, P)
    in_maps = [core_inputs(inp, cm, c) for c in range(NCORES)]
    res = run_bass_kernel_spmd(P.nc, in_maps, core_ids=list(range(NCORES)))
    out = res.results
    f32 = np.float32
    y_prompt = np.zeros((1, 8192, D), f32)
    y_sample = np.zeros((128, 1, D), f32)
    pool_s = np.zeros((1, 128, 15, D), f32)
    k_s = np.zeros((1, 128, 128, 8, 64), f32)
    v_s = np.zeros((1, 128, 128, 8, 64), f32)
    for c in range(NCORES):
        o = out[c]
        yT = o['yT'].reshape(128, NCH, NOUT)
        yy = yT.transpose(2, 1, 0).reshape(NOUT, D)
        y_prompt[0, c * NMAIN:(c + 1) * NMAIN] = yy[:NMAIN]
        y_sample[NSAMP * c:NSAMP * (c + 1), 0] = yy[NMAIN:]
        sl = slice(NSAMP * c, NSAMP * (c + 1))
        pool_s[0, sl, 0:14] = o['pool_s_old'].reshape(NSAMP, 14, D)
        pool_s[0, sl, 14] = o['pool_s_new'].reshape(128, NCH, NSAMP).transpose(2, 1, 0).reshape(NSAMP, D)
        k_s[0, sl, 0:127] = o['swa_k_s_old'].reshape(NSAMP, 127, 8, 64)
        v_s[0, sl, 0:127] = o['swa_v_s_old'].reshape(NSAMP, 127, 8, 64)
        kn = o['swa_k_s_new'].reshape(2, 64, 4, NSAMP)
        k_s[0, sl, 127] = kn.transpose(3, 2, 0, 1).reshape(NSAMP, 8, 64)
        v_s[0, sl, 127] = o['swa_v_s_new'].reshape(NSAMP, 8, 64)
    o7 = out[NCORES - 1]
    pool_p = o7['pool_p'].reshape(128, NCH, 15).transpose(2, 1, 0).reshape(1, 1, 15, D).astype(f32)
    k_p = o7['swa_k_p'].reshape(2, 64, 4, 128).transpose(3, 2, 0, 1).reshape(1, 1, 128, 8, 64).astype(f32)
    v_p = o7['swa_v_p'].reshape(1, 1, 128, 8, 64).astype(f32)
    o0 = out[0]
    mem_k = o0['mem_k'].reshape(128, 2, 4, 256).transpose(1, 3, 2, 0).reshape(2, 1, 256, 4, 128).astype(f32)
    mem_v = o0['mem_v'].reshape(128, 2, 2, 512).transpose(1, 2, 0, 3).reshape(2, 1, 256, 4, 128).astype(f32)
    return (y_prompt, y_sample, np.ascontiguousarray(pool_p), pool_s, np.ascontiguousarray(k_p), np.ascontiguousarray(v_p),
            k_s, v_s, np.ascontiguousarray(mem_k), np.ascontiguousarray(mem_v))
```

```python
import math
from contextlib import ExitStack

import ml_dtypes
import numpy as np

import concourse.bass as bass
import concourse.mybir as mybir
from concourse.bass_utils import run_bass_kernel_spmd

F32 = mybir.dt.float32
BF16 = mybir.dt.bfloat16
AF = mybir.ActivationFunctionType
ALU = mybir.AluOpType
AX = mybir.AxisListType
NPBF = ml_dtypes.bfloat16

D = 2048
NCH = 16
DFF = 5632
NF = 44
T = 1184
HALO = 144
NMAIN = 1024
NSAMP = 16
SAMP0 = 1168
NOUT = 1040
R = 9
NCORES = 8
CT_ALL = [(0, 512), (512, 512), (1024, 160)]
CT_MAIN = [(144, 512), (656, 512), (1168, 16)]
SEGB = [0, 144, 512, 656, 1024, 1168, 1184]
GV = {'ffn1_0': 0, 'mix_0': 1, 'xq_0': 2, 'ffn2_0': 3, 'ffn1_1': 4, 'mix_1': 5, 'xq_1': 6, 'ffn2_1': 7,
      'final': 8, 'mem_0': 9, 'mem_1': 10, 'pscale': 11}
NGV = 12
EPS = 1e-6
PAST = 8192
POOLW = (2, 4, 8, 16)


class Buf:
    __slots__ = ('w', 'r', 'excl')

    def __init__(self, snap=None):
        self.w = None
        self.r = dict(snap) if snap else {}
        self.excl = False


class Seg:
    def __init__(self, bounds, snap=None):
        self.b = bounds
        self.bufs = [Buf(snap) for _ in bounds[:-1]]

    def get(self, lo, hi):
        return [self.bufs[i] for i in range(len(self.b) - 1) if self.b[i] < hi and self.b[i + 1] > lo]


class Eng:
    def __init__(self, name, h, key):
        self.name, self.h, self.key = name, h, key
        self.cnt = 0
        self.known = {}


class Prog:
    def __init__(self):
        nc = bass.Bass("TRN2", target_bir_lowering=False)
        self.nc = nc
        self.es = ExitStack()
        self.sems = []
        self.dval = {}
        self.slots = []
        self.uid = 0
        self.PE = Eng('PE', nc.tensor, self.new_sem('s_pe'))
        self.ACT = Eng('ACT', nc.scalar, self.new_sem('s_act'))
        self.DVE = Eng('DVE', nc.vector, self.new_sem('s_dve'))
        self.POOL = Eng('POOL', nc.gpsimd, None)
        self.SP = Eng('SP', nc.sync, None)
        self.rsem = [self.new_sem(f's_ring{i}') for i in range(R)]
        self.spsem = [self.new_sem(f's_sp{i}') for i in range(16)]
        self.sp_rr = 0
        self.ring_ptr = 0
        self.held = set()
        for k in self.rsem + self.spsem:
            self.dval[k] = 0

    def new_sem(self, name):
        h = self.es.enter_context(self.nc.semaphore(name))
        self.sems.append(h)
        return len(self.sems) - 1

    def dram(self, name, shape, dt, kind):
        return self.nc.dram_tensor(name, list(shape), dt, kind=kind).ap()

    def sb(self, name, shape, dt, es=None):
        self.uid += 1
        return (es or self.es).enter_context(self.nc.sbuf_tensor(f"{name}_{self.uid}", list(shape), dt))

    def snapshot(self):
        d = {}
        for E in (self.PE, self.ACT, self.DVE):
            if E.cnt > 0:
                d[E.key] = E.cnt
        for k in self.spsem:
            if self.dval[k] > 0:
                d[k] = self.dval[k]
        return d

    def _deps(self, E, Rb, Wb):
        need = {}

        def add(k, v):
            if need.get(k, 0) < v:
                need[k] = v
        for b in Rb:
            if b.w is not None:
                add(*b.w)
        for b in Wb:
            if b.w is not None:
                add(*b.w)
            for k, v in b.r.items():
                add(k, v)
        for k, v in need.items():
            if k == E.key:
                if E.name == 'PE':
                    continue
            if E.known.get(k, 0) >= v:
                continue
            E.h.wait_ge(self.sems[k], v)
            E.known[k] = v

    def _mark(self, tok, Rb, Wb):
        for b in Rb:
            if b.r.get(tok[0], 0) < tok[1]:
                b.r[tok[0]] = tok[1]
        for b in Wb:
            b.w = tok
            b.r = {}

    def op(self, E, fn, Rb=(), Wb=()):
        if E.name != 'PE':
            ex = [b for b in Rb if b.excl]
            if ex:
                Wb = list(Wb) + ex
                Rb = [b for b in Rb if not b.excl]
        self._deps(E, Rb, Wb)
        ins = fn()
        E.cnt += 1
        ins.then_inc(self.sems[E.key], 1)
        self._mark((E.key, E.cnt), Rb, Wb)

    def dma(self, Q, out, in_, Rb=(), Wb=(), semkey=None):
        self._deps(Q, Rb, Wb)
        if semkey is None:
            semkey = self.spsem[self.sp_rr % len(self.spsem)]
            self.sp_rr += 1
            if self.dval[semkey] > 0 and Q.known.get(semkey, 0) < self.dval[semkey]:
                Q.h.wait_ge(self.sems[semkey], self.dval[semkey])
                Q.known[semkey] = self.dval[semkey]
        self.dval[semkey] += 16
        Q.h.dma_start(out=out, in_=in_).then_inc(self.sems[semkey], 16)
        self._mark((semkey, self.dval[semkey]), Rb, Wb)

    def take_slot(self, desc, hold=False):
        s = len(self.slots)
        self.slots.append(desc)
        r = self.ring_ptr % R
        while r in self.held:
            r = (r + 1) % R
        self.ring_ptr = r + 1
        if hold:
            self.held.add(r)
        self.dma(self.POOL, out=self.ring[:, r, :], in_=self.wstream[s], Wb=(self.ringb[r],), semkey=self.rsem[r])
        return r

    def slot3(self, r, k):
        return self.ring[:, r, :].rearrange("p (k n) -> p k n", k=k)

    def PA(self, ci, n):
        if ci < 2:
            return self.pb[ci][:, 0:n], self.pbuf[ci].get(0, n)
        return self.pb[6][:, 0:n], self.pbuf[6].get(0, n)

    def PB(self, ci, n):
        if ci < 2:
            return self.pb[2 + ci][:, 0:n], self.pbuf[2 + ci].get(0, n)
        return self.pb[6][:, 160:160 + n], self.pbuf[6].get(0, 512)

    def bank(self, i, lo, hi):
        return self.pb[i][:, lo:hi], self.pbuf[i].get(lo, hi)

    def xb(self, k, c0, n):
        return self.xbufs[k].get(c0, c0 + n)

    def declare(self):
        nc = self.nc
        I, O = "ExternalInput", "ExternalOutput"
        self.d_xT = self.dram("xT", [128, NCH * T], F32, I)
        self.d_memT = self.dram("memT", [128, NCH * 256], F32, I)
        self.d_stT = self.dram("stT", [128, NCH * NSAMP * 15], F32, I)
        self.d_stN = self.dram("stN", [NSAMP, 15 * D], F32, I)
        self.d_ckN = self.dram("ckN", [NSAMP, 128 * 512], F32, I)
        self.d_cvN = self.dram("cvN", [NSAMP, 128 * 512], F32, I)
        self.d_gvec = self.dram("gvec", [128, NGV * NCH], F32, I)
        self.d_sinkT = self.dram("sinkT", [128, 16], F32, I)
        self.d_cs = self.dram("cs", [128, 2 * T], F32, I)
        self.d_icnt = self.dram("icnt", [128, 4 * T], F32, I)
        self.d_mask = self.dram("maskc", [128, 2 * 512], BF16, I)
        self.NS_decl = None
        self.o_yT = self.dram("yT", [128, NCH * NOUT], F32, O)
        self.o_pool_p = self.dram("pool_p", [128, NCH * 15], F32, O)
        self.o_pool_sn = self.dram("pool_s_new", [128, NCH * NSAMP], F32, O)
        self.o_pool_so = self.dram("pool_s_old", [NSAMP, 14 * D], F32, O)
        self.o_kp = self.dram("swa_k_p", [128, 4 * 128], F32, O)
        self.o_vp = self.dram("swa_v_p", [128, 4 * 128], F32, O)
        self.o_ksn = self.dram("swa_k_s_new", [128, 4 * NSAMP], F32, O)
        self.o_vsn = self.dram("swa_v_s_new", [NSAMP, 4 * 128], F32, O)
        self.o_kso = self.dram("swa_k_s_old", [NSAMP, 127 * 512], F32, O)
        self.o_vso = self.dram("swa_v_s_old", [NSAMP, 127 * 512], F32, O)
        self.o_memk = self.dram("mem_k", [128, 2 * 4 * 256], F32, O)
        self.o_memv = self.dram("mem_v", [128, 2 * 2 * 512], F32, O)

        self.xT = self.sb("xT", [128, NCH, T], F32)
        self.hT = self.sb("hT", [128, NCH, T], BF16)
        self.ring = self.sb("ring", [128, R, 2048], BF16)
        self.KmT = self.sb("KmT", [128, 2, 4, 256], BF16)
        self.Vm = self.sb("Vm", [128, 2, 2, 512], BF16)
        self.gv = self.sb("gv", [128, NGV, NCH], F32)
        self.ones = self.sb("ones", [128, 192], BF16)
        self.onesf = self.sb("onesf", [128, 128], BF16)
        self.esink = self.sb("esink", [128, 16], F32)
        self.pb = [self.es.enter_context(nc.psum_tensor(f"pb{i}", [128, 512], F32)) for i in range(8)]
        self.pbuf = [Seg([0, 512]) for i in range(8)]
        for sg in self.pbuf:
            for b in sg.bufs:
                b.excl = True
        self.xbufs = [Seg(SEGB) for _ in range(NCH)]
        self.hTb = [Buf() for _ in range(NCH)]
        self.ringb = [Buf() for _ in range(R)]
        self.b_KmT = [Buf(), Buf()]
        self.b_Vm = [Buf(), Buf()]
        self.b_gv = Buf()
        self.b_ones = Buf()
        self.b_esink = Buf()
        self.pd_rr = 0

    def pdslot(self, n):
        i = (4, 5, 7)[self.pd_rr % 3]
        self.pd_rr += 1
        return self.bank(i, 0, n)

    def proj(self, dst, r, ctiles, src=None, srcb=None):
        sv = self.slot3(r, 16)
        PE = self.PE.h
        src = self.hT if src is None else src
        srcb = self.hTb if srcb is None else srcb

        def fn():
            ins = None
            for k in range(NCH):
                for ci, (c0, n) in enumerate(ctiles):
                    ins = PE.matmul(dst[ci][0], lhsT=sv[:, k, :], rhs=src[:, k, c0:c0 + n],
                                    start=(k == 0), stop=(k == NCH - 1))
            return ins
        self.op(self.PE, fn, Rb=[self.ringb[r]] + list(srcb), Wb=[b for d in dst for b in d[1]])

    def norm(self, gidx, ctiles, es, make_h=True):
        lo = ctiles[0][0]
        hi = ctiles[-1][0] + ctiles[-1][1]
        sq = self.sb("sq", [128, 2, T], BF16, es)
        rbc = self.sb("rbc", [128, T], F32, es)
        snap = self.snapshot()
        sqb = [Buf(snap), Buf(snap)]
        rbcb = Seg(SEGB, snap)
        A, V, PE = self.ACT.h, self.DVE.h, self.PE.h
        dst = [self.PA(ci, n) for ci, (c0, n) in enumerate(ctiles)]
        for k in range(NCH):
            p = k % 2
            if k % 2 == 0:
                self.op(self.ACT, lambda: A.activation(out=sq[:, p, lo:hi], in_=self.xT[:, k, lo:hi], func=AF.Square),
                        Rb=self.xb(k, lo, hi - lo), Wb=[sqb[p]])
            else:
                self.op(self.DVE, lambda: V.tensor_tensor(out=sq[:, p, lo:hi], in0=self.xT[:, k, lo:hi], in1=self.xT[:, k, lo:hi],
                                                          op=ALU.mult), Rb=self.xb(k, lo, hi - lo), Wb=[sqb[p]])

            def fn():
                ins = None
                for ci, (c0, n) in enumerate(ctiles):
                    ins = PE.matmul(dst[ci][0], lhsT=self.onesf[:, :], rhs=sq[:, p, c0:c0 + n],
                                    start=(k == 0), stop=(k == NCH - 1))
                return ins
            self.op(self.PE, fn, Rb=[sqb[p], self.b_ones], Wb=[b for d in dst for b in d[1]])
        for ci, (c0, n) in enumerate(ctiles):
            rb = rbcb.get(c0, c0 + n)
            self.op(self.ACT, lambda: A.activation(out=rbc[:, c0:c0 + n], in_=dst[ci][0], func=AF.Sqrt,
                                                   scale=1.0 / D, bias=EPS), Rb=dst[ci][1], Wb=rb)
            self.op(self.DVE, lambda: V.reciprocal(out=rbc[:, c0:c0 + n], in_=rbc[:, c0:c0 + n]), Rb=rb, Wb=rb)
        if make_h:
            for k in range(NCH):
                self.op(self.DVE, lambda: V.scalar_tensor_tensor(
                    out=self.hT[:, k, lo:hi], in0=self.xT[:, k, lo:hi], scalar=self.gv[:, gidx, k:k + 1],
                    in1=rbc[:, lo:hi], op0=ALU.mult, op1=ALU.mult),
                    Rb=self.xb(k, lo, hi - lo) + rbcb.get(lo, hi) + [self.b_gv], Wb=[self.hTb[k]])
        return rbc, rbcb

    def resid_add(self, d, c0, n, ps, scale=None, scale_ap=None):
        V = self.DVE.h
        xs = self.xT[:, d, c0:c0 + n]
        sc = scale_ap if scale_ap is not None else scale
        rb = list(ps[1]) + self.xb(d, c0, n) + ([self.b_gv] if scale_ap is not None else [])
        self.op(self.DVE, lambda: V.scalar_tensor_tensor(out=xs, in0=ps[0], scalar=sc, in1=xs,
                                                         op0=ALU.mult, op1=ALU.add),
                Rb=rb, Wb=self.xb(d, c0, n))

    def phase_init(self):
        V = self.DVE.h
        xv = self.d_xT.rearrange("p (k t) -> p k t", k=NCH)
        for k in range(NCH):
            self.dma(self.SP, out=self.xT[:, k, :], in_=xv[:, k, :], Wb=self.xbufs[k].bufs)
        self.dma(self.SP, out=self.gv[:, :, :], in_=self.d_gvec.rearrange("p (g k) -> p g k", g=NGV), Wb=[self.b_gv])
        self.dma(self.SP, out=self.esink[:, :], in_=self.d_sinkT[:, :], Wb=[self.b_esink])
        so = self.o_pool_so.rearrange("b (r d) -> b r d", r=14)
        si = self.d_stN.rearrange("b (r d) -> b r d", r=15)
        for r0 in range(0, 14, 4):
            r1 = min(r0 + 4, 14)
            self.dma(self.SP, out=so[:, r0:r1, :], in_=si[:, 1 + r0:1 + r1, :])
        for (o_, i_) in ((self.o_kso, self.d_ckN), (self.o_vso, self.d_cvN)):
            ov = o_.rearrange("b (r d) -> b r d", r=127)
            iv = i_.rearrange("b (r d) -> b r d", r=128)
            for r0 in range(0, 127, 16):
                r1 = min(r0 + 16, 127)
                self.dma(self.SP, out=ov[:, r0:r1, :], in_=iv[:, 1 + r0:1 + r1, :])

        def fn():
            V.memset(self.ones[:, 0:64], 1.0)
            V.memset(self.ones[:, 64:128], 0.0)
            V.memset(self.ones[:, 128:192], 1.0)
            return V.memset(self.onesf[:, :], 1.0)
        self.op(self.DVE, fn, Wb=[self.b_ones])
        self.op(self.ACT, lambda: self.ACT.h.activation(out=self.esink[:, :], in_=self.esink[:, :], func=AF.Exp),
                Rb=[self.b_esink], Wb=[self.b_esink])

    def phase_memkv(self):
        A, V, PE = self.ACT.h, self.DVE.h, self.PE.h
        with ExitStack() as es:
            snap = self.snapshot()
            memT = self.sb("memT", [128, NCH, 256], F32, es)
            hm = self.sb("hm", [128, NCH, 256], BF16, es)
            sq = self.sb("sqm", [128, 2, 256], BF16, es)
            rbc = self.sb("rbcm", [128, 256], F32, es)
            kst = self.sb("kst", [128, 2, 4, 256], F32, es)
            vst = self.sb("vst", [128, 2, 2, 512], F32, es)
            b_mem, b_rbc, b_kst, b_vst = Buf(snap), Buf(snap), Buf(snap), Buf(snap)
            sqb = [Buf(snap), Buf(snap)]
            hmb = [Buf(snap) for _ in range(NCH)]
            self.dma(self.SP, out=memT[:, :, :], in_=self.d_memT.rearrange("p (k m) -> p k m", k=NCH), Wb=[b_mem])
            ps = self.bank(0, 0, 256)
            for k in range(NCH):
                p = k % 2
                self.op(self.ACT, lambda: A.activation(out=sq[:, p, :], in_=memT[:, k, :], func=AF.Square),
                        Rb=[b_mem], Wb=[sqb[p]])
                self.op(self.PE, lambda: PE.matmul(ps[0], lhsT=self.onesf[:, :], rhs=sq[:, p, :],
                                                   start=(k == 0), stop=(k == NCH - 1)),
                        Rb=[sqb[p], self.b_ones], Wb=ps[1])
            self.op(self.ACT, lambda: A.activation(out=rbc[:, :], in_=ps[0], func=AF.Sqrt, scale=1.0 / D, bias=EPS),
                    Rb=ps[1], Wb=[b_rbc])
            self.op(self.DVE, lambda: V.reciprocal(out=rbc[:, :], in_=rbc[:, :]), Rb=[b_rbc], Wb=[b_rbc])
            if DBG == 1:
                return
            for l in range(2):
                gi = GV[f'mem_{l}']
                for k in range(NCH):
                    self.op(self.DVE, lambda: V.scalar_tensor_tensor(
                        out=hm[:, k, :], in0=memT[:, k, :], scalar=self.gv[:, gi, k:k + 1], in1=rbc[:, :],
                        op0=ALU.mult, op1=ALU.mult), Rb=[b_mem, b_rbc, self.b_gv], Wb=[hmb[k]])
                for h in range(4):
                    r = self.take_slot(('col', 'w_xkv', l, h * 128))
                    sv = self.slot3(r, 16)
                    pk = self.bank(1 + (h % 2), 0, 256)

                    def fn():
                        ins = None
                        for k in range(NCH):
                            ins = PE.matmul(pk[0], lhsT=sv[:, k, :], rhs=hm[:, k, :], start=(k == 0), stop=(k == NCH - 1))
                        return ins
                    self.op(self.PE, fn, Rb=[self.ringb[r]] + hmb, Wb=pk[1])
                    self.op(self.ACT, lambda: A.activation(out=self.KmT[:, l, h, :], in_=pk[0], func=AF.Copy),
                            Rb=pk[1], Wb=[self.b_KmT[l]])
                    self.op(self.DVE, lambda: V.tensor_copy(out=kst[:, l, h, :], in_=pk[0]), Rb=pk[1], Wb=[b_kst])
                    if DBG == 2:
                        return
                pv = [self.bank(3, 0, 512), self.bank(4, 0, 512)]
                for h in range(4):
                    r = self.take_slot(('col', 'w_xkv', l, 512 + h * 128))
                    sv = self.slot3(r, 16)
                    for mb in range(2):
                        def fn():
                            ins = None
                            for k in range(NCH):
                                ins = PE.matmul(pv[mb][0][:, h * 128:(h + 1) * 128], lhsT=hm[:, k, mb * 128:(mb + 1) * 128],
                                                rhs=sv[:, k, :], start=(k == 0), stop=(k == NCH - 1))
                            return ins
                        self.op(self.PE, fn, Rb=[self.ringb[r]] + hmb, Wb=pv[mb][1])
                for mb in range(2):
                    self.op(self.ACT, lambda: A.activation(out=self.Vm[:, l, mb, :], in_=pv[mb][0], func=AF.Copy),
                            Rb=pv[mb][1], Wb=[self.b_Vm[l]])
                    self.op(self.DVE, lambda: V.tensor_copy(out=vst[:, l, mb, :], in_=pv[mb][0]), Rb=pv[mb][1], Wb=[b_vst])
            self.dma(self.SP, out=self.o_memk.rearrange("p (l h m) -> p l h m", l=2, h=4), in_=kst[:, :, :, :], Rb=[b_kst])
            self.dma(self.SP, out=self.o_memv.rearrange("p (l b n) -> p l b n", l=2, b=2), in_=vst[:, :, :, :], Rb=[b_vst])

    def phase_ffn(self, l, which, ctiles):
        A, V, PE = self.ACT.h, self.DVE.h, self.PE.h
        wgu, wdn = f'w_ffn{which}_gu', f'w_ffn{which}_dn'
        lo = ctiles[0][0]
        hi = ctiles[-1][0] + ctiles[-1][1]
        with ExitStack() as es:
            with ExitStack() as es2:
                self.norm(GV[f'ffn{which}_{l}'], ctiles, es2)
            snap = self.snapshot()
            act = self.sb("act", [128, 2, 4, T], BF16, es)
            sil = self.sb("sil", [128, T], F32, es)
            actb = [[[Buf(snap) for _ in range(3)] for _ in range(4)] for _ in range(2)]
            silb = [Buf(snap) for _ in range(3)]
            order = [2, 0, 1]
            pa = [self.PA(ci, n) for ci, (c0, n) in enumerate(ctiles)]
            pbb = [self.PB(ci, n) for ci, (c0, n) in enumerate(ctiles)]

            pending = []

            def drain(k):
                for _ in range(min(k, len(pending))):
                    pending.pop(0)()

            def gu(j):
                par, f = (j // 4) % 2, j % 4
                r = self.take_slot(('col', wgu, l, j * 128))
                self.proj(pa, r, ctiles)

                for ci in order:
                    c0, n = ctiles[ci]
                    self.op(self.ACT, lambda: A.activation(out=sil[:, c0:c0 + n], in_=pa[ci][0], func=AF.Silu),
                            Rb=pa[ci][1], Wb=[silb[ci]])
                drain(6)
                r2 = self.take_slot(('col', wgu, l, DFF + j * 128))
                self.proj(pbb, r2, ctiles)

                for ci in order:
                    c0, n = ctiles[ci]
                    self.op(self.DVE, lambda: V.tensor_tensor(out=act[:, par, f, c0:c0 + n], in0=pbb[ci][0], in1=sil[:, c0:c0 + n],
                                                              op=ALU.mult), Rb=[silb[ci]] + pbb[ci][1], Wb=[actb[par][f][ci]])
                drain(6)

            def down_tasks(g):
                par = g % 2
                st = {}
                tasks = []
                ntask = NCH * len(ctiles)

                def mk(idx, d, ci):
                    def task():
                        if 'rs' not in st:
                            st['rs'] = [self.take_slot(('row', wdn, l, (4 * g + f) * 128), hold=True) for f in range(4)]
                        rs = st['rs']
                        c0, n = ctiles[ci]
                        ps = self.pdslot(n)

                        def fn():
                            ins = None
                            for f in range(4):
                                ins = PE.matmul(ps[0], lhsT=self.ring[:, rs[f], d * 128:(d + 1) * 128],
                                                rhs=act[:, par, f, c0:c0 + n], start=(f == 0), stop=(f == 3))
                            return ins
                        self.op(self.PE, fn, Rb=[self.ringb[x] for x in rs] + [actb[par][f][ci] for f in range(4)], Wb=ps[1])
                        self.resid_add(d, c0, n, ps, scale=0.5)
                        if idx == ntask - 1:
                            for x in rs:
                                self.held.discard(x)
                    return task
                i = 0
                for d in range(NCH):
                    for ci in range(len(ctiles)):
                        tasks.append(mk(i, d, ci))
                        i += 1
                return tasks
            NG = NF // 4
            for g in range(NG):
                for f in range(4):
                    gu(4 * g + f)
                drain(len(pending))
                pending.extend(down_tasks(g))
            drain(len(pending))

    def phase_pool(self):
        A, V, PE = self.ACT.h, self.DVE.h, self.PE.h
        ct = CT_ALL
        with ExitStack() as es:
            rbc, rbcb = self.norm(GV['mix_0'], ct, es, make_h=False)
            snap = self.snapshot()
            tA = self.sb("tA", [128, T], F32, es)
            tB = self.sb("tB", [128, T], F32, es)
            tC = self.sb("tC", [128, T], F32, es)
            icn = self.sb("icn", [128, T], F32, es)
            stk = self.sb("stk", [128, 2, NSAMP, 15], F32, es)
            ssum = self.sb("ssum", [128, NSAMP], F32, es)
            pst = self.sb("pst", [128, NCH, 15], F32, es)
            sst = self.sb("sst", [128, NCH, NSAMP], F32, es)
            bA, bB, bC, b_icn, b_ss, b_pst, b_sst = (Buf(snap) for _ in range(7))
            b_stk = [Buf(snap), Buf(snap)]
            gi = GV['mix_0']
            stT = self.d_stT.rearrange("p (k b r) -> p k b r", k=NCH, b=NSAMP)
            icv = self.d_icnt.rearrange("p (g t) -> p g t", g=4)
            rall = rbcb.get(0, T)
            for k in range(NCH):
                g = k // 4
                w = POOLW[g]
                if k % 4 == 0:
                    self.dma(self.SP, out=icn[:, :], in_=icv[:, g, :], Wb=[b_icn])
                sp_ = k % 2
                self.dma(self.SP, out=stk[:, sp_, :, :], in_=stT[:, k, :, :], Wb=[b_stk[sp_]])
                self.op(self.DVE, lambda: V.scalar_tensor_tensor(
                    out=tA[:, :], in0=self.xT[:, k, :], scalar=self.gv[:, gi, k:k + 1], in1=rbc[:, :],
                    op0=ALU.mult, op1=ALU.mult), Rb=self.xb(k, 0, T) + rall + [self.b_gv], Wb=[bA])
                cur, curb = tA, bA
                oth = [(tB, bB), (tC, bC)]
                st, oi = 1, 0
                while st < w:
                    nt, nb = oth[oi]
                    lo_ = 2 * st - 1
                    self.op(self.DVE, lambda: V.tensor_tensor(out=nt[:, lo_:T], in0=cur[:, lo_:T], in1=cur[:, lo_ - st:T - st],
                                                              op=ALU.add), Rb=[curb], Wb=[nb])
                    cur, curb = nt, nb
                    oi ^= 1
                    st *= 2
                nt, nb = oth[oi]
                self.op(self.DVE, lambda: V.tensor_tensor(out=nt[:, 15:SAMP0], in0=cur[:, 15:SAMP0], in1=icn[:, 15:SAMP0],
                                                          op=ALU.mult), Rb=[curb, b_icn], Wb=[nb])
                self.op(self.DVE, lambda: V.tensor_tensor(out=self.hT[:, k, 15:SAMP0], in0=nt[:, 15:SAMP0], in1=tA[:, 15:SAMP0],
                                                          op=ALU.subtract), Rb=[nb, bA], Wb=[self.hTb[k]])
                self.op(self.DVE, lambda: V.tensor_reduce(out=ssum[:, :], in_=stk[:, sp_, :, 15 - (w - 1):15], axis=AX.X, op=ALU.add),
                        Rb=[b_stk[sp_]], Wb=[b_ss])
                self.op(self.DVE, lambda: V.tensor_tensor(out=ssum[:, :], in0=ssum[:, :], in1=tA[:, SAMP0:T], op=ALU.add),
                        Rb=[b_ss, bA], Wb=[b_ss])
                self.op(self.DVE, lambda: V.scalar_tensor_tensor(out=self.hT[:, k, SAMP0:T], in0=ssum[:, :], scalar=1.0 / w,
                                                                 in1=tA[:, SAMP0:T], op0=ALU.mult, op1=ALU.subtract),
                        Rb=[b_ss, bA], Wb=[self.hTb[k]])
                self.op(self.ACT, lambda: A.activation(out=pst[:, k, :], in_=tA[:, SAMP0 - 15:SAMP0], func=AF.Copy), Rb=[bA], Wb=[b_pst])
                self.op(self.ACT, lambda: A.activation(out=sst[:, k, :], in_=tA[:, SAMP0:T], func=AF.Copy), Rb=[bA], Wb=[b_sst])
            self.dma(self.SP, out=self.o_pool_p.rearrange("p (k r) -> p k r", k=NCH), in_=pst[:, :, :], Rb=[b_pst])
            self.dma(self.SP, out=self.o_pool_sn.rearrange("p (k b) -> p k b", k=NCH), in_=sst[:, :, :], Rb=[b_sst])
            gs = GV['pscale']
            for g in range(4):
                r = self.take_slot(('pool', g))
                sv = self.slot3(r, 4)
                for oc in range(4):
                    d = 4 * g + oc
                    for ci, (c0, n) in enumerate(ct):
                        ps = self.pdslot(n)

                        def fn():
                            ins = None
                            for kc in range(4):
                                ins = PE.matmul(ps[0], lhsT=sv[:, kc, oc * 128:(oc + 1) * 128], rhs=self.hT[:, 4 * g + kc, c0:c0 + n],
                                                start=(kc == 0), stop=(kc == 3))
                            return ins
                        self.op(self.PE, fn, Rb=[self.ringb[r]] + self.hTb[4 * g:4 * g + 4], Wb=ps[1])
                        self.resid_add(d, c0, n, ps, scale_ap=self.gv[:, gs, d:d + 1])

    def phase_xattn(self, l, ctiles):
        A, V, PE = self.ACT.h, self.DVE.h, self.PE.h
        sc = 1.0 / math.sqrt(128.0)
        with ExitStack() as es:
            with ExitStack() as es2:
                self.norm(GV[f'xq_{l}'], ctiles, es2)
            snap = self.snapshot()
            qT = self.sb("qT", [128, 4, T], BF16, es)
            oT = self.sb("oT", [128, 4, T], BF16, es)
            PT = self.sb("PT", [128, 2, 2, 512], BF16, es)
            rec = self.sb("rec", [128, 2, 512], F32, es)
            PTs = self.sb("PTs", [128, 128], BF16, es)
            recs = self.sb("recs", [128, 64], F32, es)
            qTb = [Buf(snap) for _ in range(4)]
            oTb = [Seg(SEGB, snap) for _ in range(4)]
            PTb = [[Buf(snap), Buf(snap)], [Buf(snap), Buf(snap)]]
            recb = [Buf(snap), Buf(snap)]
            b_PTs, b_recs = Buf(snap), Buf(snap)
            for h in range(4):
                r = self.take_slot(('col', 'w_xq', l, h * 128))
                dst = [(self.PA if h % 2 == 0 else self.PB)(ci, n) for ci, (c0, n) in enumerate(ctiles)]
                self.proj(dst, r, ctiles)

                def fa():
                    ins = None
                    for ci, (c0, n) in enumerate(ctiles):
                        ins = A.activation(out=qT[:, h, c0:c0 + n], in_=dst[ci][0], func=AF.Copy)
                    return ins
                self.op(self.ACT, fa, Rb=[b for d in dst for b in d[1]], Wb=[qTb[h]])
            units = [(h, c0, n) for h in range(4) for (c0, n) in ctiles if c0 < SAMP0]

            def xs1(u):
                h, c0, n = units[u]
                par = u % 2
                pS = [self.bank(2 * par, 0, n), self.bank(2 * par + 1, 0, n)]

                def fs():
                    ins = None
                    for mb in range(2):
                        ins = PE.matmul(pS[mb][0], lhsT=self.KmT[:, l, h, mb * 128:(mb + 1) * 128], rhs=qT[:, h, c0:c0 + n],
                                        start=True, stop=True)
                    return ins
                self.op(self.PE, fs, Rb=[self.b_KmT[l], qTb[h]], Wb=pS[0][1] + pS[1][1])
                for mb in range(2):
                    self.op(self.ACT, lambda: A.activation(out=PT[:, par, mb, 0:n], in_=pS[mb][0], func=AF.Exp, scale=sc),
                            Rb=pS[mb][1], Wb=[PTb[par][mb]])

            def xs2(u):
                h, c0, n = units[u]
                par = u % 2
                po = self.bank(4 + par, 0, n)
                pd = self.bank(7 - par, 0, n)

                def fo():
                    ins = None
                    for mb in range(2):
                        ins = PE.matmul(po[0], lhsT=self.Vm[:, l, mb, h * 128:(h + 1) * 128], rhs=PT[:, par, mb, 0:n],
                                        start=(mb == 0), stop=(mb == 1))
                    for mb in range(2):
                        ins = PE.matmul(pd[0], lhsT=self.onesf[:, :], rhs=PT[:, par, mb, 0:n], start=(mb == 0), stop=(mb == 1))
                    return ins
                self.op(self.PE, fo, Rb=[self.b_Vm[l], self.b_ones] + PTb[par], Wb=po[1] + pd[1])
                self.op(self.DVE, lambda: V.reciprocal(out=rec[:, par, 0:n], in_=pd[0]), Rb=pd[1], Wb=[recb[par]])
                self.op(self.DVE, lambda: V.tensor_tensor(out=oT[:, h, c0:c0 + n], in0=po[0], in1=rec[:, par, 0:n], op=ALU.mult),
                        Rb=po[1] + [recb[par]], Wb=oTb[h].get(c0, c0 + n))
            xs1(0)
            for u in range(len(units)):
                if u + 1 < len(units):
                    xs1(u + 1)
                xs2(u)
            pSs = self.bank(0, 0, 128)
            krs = [self.take_slot(('xk', l, bp)) for bp in range(8)]
            for bp in range(8):
                kv = self.ring[:, krs[bp], :].rearrange("p (b h m) -> p b h m", b=2, h=4)

                def fk():
                    ins = None
                    for bb in range(2):
                        b = 2 * bp + bb
                        for h in range(4):
                            for mb in range(2):
                                col = mb * 64 + h * 16 + b
                                ins = PE.matmul(pSs[0][:, col:col + 1], lhsT=kv[:, bb, h, mb * 128:(mb + 1) * 128],
                                                rhs=qT[:, h, SAMP0 + b:SAMP0 + b + 1], start=True, stop=True)
                    return ins
                self.op(self.PE, fk, Rb=[self.ringb[krs[bp]]] + qTb, Wb=pSs[1])
            self.op(self.ACT, lambda: A.activation(out=PTs[:, :], in_=pSs[0], func=AF.Exp, scale=sc), Rb=pSs[1], Wb=[b_PTs])
            pos_ = self.bank(1, 0, 64)
            pds_ = self.bank(2, 0, 64)
            vrs = [self.take_slot(('xv', l, bp)) for bp in range(8)]
            for bp in range(8):
                vv = self.ring[:, vrs[bp], :].rearrange("p (b m n) -> p b m n", b=2, m=2)

                def fv():
                    ins = None
                    for bb in range(2):
                        b = 2 * bp + bb
                        for h in range(4):
                            for mb in range(2):
                                col = mb * 64 + h * 16 + b
                                ins = PE.matmul(pos_[0][:, h * 16 + b:h * 16 + b + 1], lhsT=vv[:, bb, mb, h * 128:(h + 1) * 128],
                                                rhs=PTs[:, col:col + 1], start=(mb == 0), stop=(mb == 1))
                    return ins
                self.op(self.PE, fv, Rb=[self.ringb[vrs[bp]], b_PTs], Wb=pos_[1])

            def fd():
                ins = None
                for mb in range(2):
                    ins = PE.matmul(pds_[0], lhsT=self.onesf[:, :], rhs=PTs[:, mb * 64:(mb + 1) * 64], start=(mb == 0), stop=(mb == 1))
                return ins
            self.op(self.PE, fd, Rb=[b_PTs, self.b_ones], Wb=pds_[1])
            self.op(self.DVE, lambda: V.reciprocal(out=recs[:, :], in_=pds_[0]), Rb=pds_[1], Wb=[b_recs])
            self.op(self.DVE, lambda: V.tensor_tensor(out=oT[:, :, SAMP0:T], in0=pos_[0].rearrange("p (h b) -> p h b", h=4),
                                                      in1=recs[:, :].rearrange("p (h b) -> p h b", h=4), op=ALU.mult),
                    Rb=pos_[1] + [b_recs], Wb=[b for s in oTb for b in s.get(SAMP0, T)])
            rs = [self.take_slot(('row', 'w_xo', l, h * 128)) for h in range(4)]
            for d in range(NCH):
                for ci, (c0, n) in enumerate(ctiles):
                    ps = self.pdslot(n)

                    def fn():
                        ins = None
                        for h in range(4):
                            ins = PE.matmul(ps[0], lhsT=self.ring[:, rs[h], d * 128:(d + 1) * 128], rhs=oT[:, h, c0:c0 + n],
                                            start=(h == 0), stop=(h == 3))
                        return ins
                    self.op(self.PE, fn, Rb=[self.ringb[x] for x in rs] + [b for s in oTb for b in s.get(c0, c0 + n)], Wb=ps[1])
                    self.resid_add(d, c0, n, ps, scale=1.0)

    def phase_swa(self):
        A, V, PE = self.ACT.h, self.DVE.h, self.PE.h
        ct = CT_ALL
        sc = 0.125
        LAST0 = SAMP0 - 128
        with ExitStack() as es:
            with ExitStack() as es2:
                self.norm(GV['mix_1'], ct, es2)
            snap = self.snapshot()
            cs = self.sb("cs", [128, 2, T], F32, es)
            msk = self.sb("msk", [128, 2, 512], BF16, es)
            kT = self.sb("kT", [128, T], BF16, es)
            Vp = self.sb("Vp", [128, 9, 192], BF16, es)
            qc = self.sb("qc", [128, 2, T], BF16, es)
            qs = self.sb("qs", [128, 4, NSAMP], BF16, es)
            oT = self.sb("oTs", [128, 4, T], BF16, es)
            r1 = self.sb("r1", [128, 2, 512], F32, es)
            r2 = self.sb("r2", [128, 2, 512], F32, es)
            PT = self.sb("PTw", [128, 2, 512], BF16, es)
            dn = self.sb("dn", [128, 2, 128], F32, es)
            PTs = self.sb("PTss", [128, 128], BF16, es)
            dns = self.sb("dns", [128, 64], F32, es)
            kpst = self.sb("kpst", [128, 4, 128], F32, es)
            vpst = self.sb("vpst", [128, 4, 128], F32, es)
            ksst = self.sb("ksst", [128, 4, NSAMP], F32, es)
            vsst = self.sb("vsst", [NSAMP, 4, 128], F32, es)
            vnb = self.sb("vnb", [NSAMP, 128], BF16, es)
            b_cs, b_msk, b_kT, b_Vp, b_qs, b_PTs, b_dns, b_kpst, b_vpst, b_ksst, b_vsst, b_vnb = (Buf(snap) for _ in range(12))
            qcb = [Buf(snap), Buf(snap)]
            oTb = [Seg(SEGB, snap) for _ in range(4)]
            r1b = [Buf(snap), Buf(snap)]
            r2b = [Buf(snap), Buf(snap)]
            PTb = [[Buf(snap), Buf(snap)], [Buf(snap), Buf(snap)]]
            dnb = [Buf(snap), Buf(snap)]
            self.dma(self.SP, out=cs[:, :, :], in_=self.d_cs.rearrange("p (a t) -> p a t", a=2), Wb=[b_cs])
            self.dma(self.SP, out=msk[:, :, :], in_=self.d_mask.rearrange("p (a t) -> p a t", a=2), Wb=[b_msk])

            def fz():
                V.memset(Vp[:, :, :], 0.0)
                return V.memset(oT[:, :, :], 0.0)
            self.op(self.DVE, fz, Wb=[b_Vp] + [b for s in oTb for b in s.bufs])
            if DBG == 3:
                raise _Stop()
            self.rope_n = 0

            def rope(pn, ps_, outs):
                for ci, (c0, n) in enumerate(ct):
                    p = self.rope_n % 2
                    self.rope_n += 1
                    self.op(self.DVE, lambda: V.tensor_tensor(out=r1[:, p, 0:n], in0=pn[ci][0], in1=cs[:, 0, c0:c0 + n], op=ALU.mult),
                            Rb=pn[ci][1] + [b_cs], Wb=[r1b[p]])
                    self.op(self.DVE, lambda: V.tensor_tensor(out=r2[:, p, 0:n], in0=ps_[ci][0], in1=cs[:, 1, c0:c0 + n], op=ALU.mult),
                            Rb=ps_[ci][1] + [b_cs], Wb=[r2b[p]])
                    self.op(self.DVE, lambda: V.tensor_tensor(out=r1[:, p, 0:n], in0=r1[:, p, 0:n], in1=r2[:, p, 0:n], op=ALU.add),
                            Rb=[r1b[p], r2b[p]], Wb=[r1b[p]])
                    for (dst, bufs, lo_, hi_) in outs:
                        a, b = max(lo_, c0), min(hi_, c0 + n)
                        if a >= b:
                            continue
                        self.op(self.ACT, lambda: A.activation(out=dst(a, b), in_=r1[:, p, a - c0:b - c0], func=AF.Copy),
                                Rb=[r1b[p]], Wb=bufs)

            pa = [self.PA(ci, n) for ci, (c0, n) in enumerate(ct)]
            pbb = [self.PB(ci, n) for ci, (c0, n) in enumerate(ct)]
            unit = 0
            for jj in range(4):
                r = self.take_slot(('col', 'w_qkv', 0, ('k', jj, 0)))
                self.proj(pa, r, ct)
                r = self.take_slot(('col', 'w_qkv', 0, ('k', jj, 1)))
                self.proj(pbb, r, ct)
                rope(pa, pbb, [
                    (lambda a, b: kT[:, a:b], [b_kT], 0, T),
                    (lambda a, b: kpst[:, jj, a - LAST0:b - LAST0], [b_kpst], LAST0, SAMP0),
                    (lambda a, b: ksst[:, jj, a - SAMP0:b - SAMP0], [b_ksst], SAMP0, T),
                ])
                if DBG == 4:
                    raise _Stop()
                r = self.take_slot(('col', 'w_qkv', 0, ('v', jj)))
                sv = self.slot3(r, 16)
                pv = [self.bank(4, 0, 512), self.bank(5, 0, 512), self.bank(7, 0, 256)]

                def vdst(blk):
                    return pv[blk // 4][0][:, (blk % 4) * 128:(blk % 4) * 128 + 128]
                for blk in range(10):
                    c0 = 16 + 128 * blk if blk < 9 else SAMP0
                    m = 128 if blk < 9 else NSAMP

                    def fn():
                        ins = None
                        for k in range(NCH):
                            ins = PE.matmul(vdst(blk)[0:m, :], lhsT=self.hT[:, k, c0:c0 + m], rhs=sv[:, k, :],
                                            start=(k == 0), stop=(k == NCH - 1))
                        return ins
                    self.op(self.PE, fn, Rb=[self.ringb[r]] + self.hTb, Wb=pv[blk // 4][1])
                for grp in range(3):
                    nb = 4 if grp < 2 else 1
                    src = pv[grp][0][:, 0:nb * 128].rearrange("p (b j d) -> p b j d", b=nb, j=2)
                    dstv = Vp[:, 4 * grp:4 * grp + nb, :].rearrange("p b (j d) -> p b j d", j=3)[:, :, 0:3:2, :]
                    self.op(self.ACT, lambda: A.activation(out=dstv, in_=src, func=AF.Copy), Rb=pv[grp][1], Wb=[b_Vp])
                self.op(self.DVE, lambda: V.tensor_copy(out=vpst[:, jj, :], in_=vdst(8)), Rb=pv[2][1], Wb=[b_vpst])
                self.op(self.DVE, lambda: V.tensor_copy(out=vsst[:, jj, :], in_=vdst(9)[0:NSAMP, :]), Rb=pv[2][1], Wb=[b_vsst])
                self.op(self.DVE, lambda: V.tensor_copy(out=vnb[:, :], in_=vdst(9)[0:NSAMP, :]), Rb=pv[2][1], Wb=[b_vnb])
                if DBG == 5:
                    raise _Stop()
                for g in range(4):
                    qp = g % 2
                    r = self.take_slot(('col', 'w_qkv', 0, ('q', jj, g, 0)))
                    self.proj(pa, r, ct)
                    r = self.take_slot(('col', 'w_qkv', 0, ('q', jj, g, 1)))
                    self.proj(pbb, r, ct)
                    rope(pa, pbb, [
                        (lambda a, b: qc[:, qp, a:b], [qcb[qp]], 0, T),
                        (lambda a, b: qs[:, g, a - SAMP0:b - SAMP0], [b_qs], SAMP0, T),
                    ])
                    if DBG == 61:
                        raise _Stop()
                    cidx = jj * 4 + g

                    def stage1(n_):
                        par = n_ % 2
                        q0 = HALO + 128 * n_
                        pS2 = [self.bank(2 * par, 0, 256), self.bank(2 * par + 1, 0, 256)]
                        mv = 1 if n_ == 0 else 0

                        def fs():
                            ins = None
                            for kb in range(2):
                                for hd in range(2):
                                    b0 = 64 * hd
                                    k0 = 16 + 128 * (n_ + kb)
                                    ins = PE.matmul(pS2[hd][0][:, kb * 128:kb * 128 + 128], lhsT=kT[b0:b0 + 64, k0:k0 + 128],
                                                    rhs=qc[b0:b0 + 64, qp, q0:q0 + 128], start=True, stop=True)
                            return ins
                        self.op(self.PE, fs, Rb=[b_kT, qcb[qp]], Wb=pS2[0][1] + pS2[1][1])
                        for hd in range(2):
                            self.op(self.ACT, lambda: A.activation(out=PT[:, par, hd * 256:hd * 256 + 256], in_=pS2[hd][0], func=AF.Exp, scale=sc),
                                    Rb=pS2[hd][1], Wb=[PTb[par][hd]])
                            self.op(self.DVE, lambda: V.tensor_tensor(out=PT[:, par, hd * 256:hd * 256 + 256], in0=PT[:, par, hd * 256:hd * 256 + 256],
                                                                      in1=msk[:, mv, hd * 256:hd * 256 + 256], op=ALU.mult),
                                    Rb=[PTb[par][hd], b_msk], Wb=[PTb[par][hd]])

                    def stage2(n_):
                        par = n_ % 2
                        q0 = HALO + 128 * n_
                        pod = self.bank(4 + par, 0, 256)

                        def fo():
                            ins = None
                            i = 0
                            for hd in range(2):
                                for kb in range(2):
                                    o0 = (hd * 2 + kb) * 128
                                    ins = PE.matmul(pod[0][:, 0:128], lhsT=Vp[:, n_ + kb, 64 * hd:64 * hd + 128], rhs=PT[:, par, o0:o0 + 128],
                                                    start=(i == 0), stop=(i == 3))
                                    i += 1
                            i = 0
                            for hd in range(2):
                                for kb in range(2):
                                    o0 = (hd * 2 + kb) * 128
                                    ins = PE.matmul(pod[0][:, 128:256], lhsT=self.ones[:, 64 * hd:64 * hd + 128], rhs=PT[:, par, o0:o0 + 128],
                                                    start=(i == 0), stop=(i == 3))
                                    i += 1
                            return ins
                        self.op(self.PE, fo, Rb=[b_Vp, self.b_ones] + PTb[par], Wb=pod[1])
                        self.op(self.DVE, lambda: V.tensor_scalar(out=dn[:, par, :], in0=pod[0][:, 128:256], scalar1=self.esink[:, cidx:cidx + 1],
                                                                  scalar2=None, op0=ALU.add), Rb=pod[1] + [self.b_esink], Wb=[dnb[par]])
                        self.op(self.DVE, lambda: V.reciprocal(out=dn[:, par, :], in_=dn[:, par, :]), Rb=[dnb[par]], Wb=[dnb[par]])
                        self.op(self.DVE, lambda: V.tensor_tensor(out=oT[:, g, q0:q0 + 128], in0=pod[0][:, 0:128], in1=dn[:, par, :], op=ALU.mult),
                                Rb=pod[1] + [dnb[par]], Wb=oTb[g].get(q0, q0 + 128))
                    stage1(0)
                    for n_ in range(8):
                        if n_ + 1 < 8:
                            stage1(n_ + 1)
                        stage2(n_)
                    if DBG == 6:
                        raise _Stop()
                rk = self.take_slot(('sk', jj))
                kv = self.ring[:, rk, :].rearrange("p (b s) -> p b s", b=NSAMP)
                self.op(self.DVE, lambda: V.tensor_copy(out=kv[:, :, 0], in_=kT[:, SAMP0:T]), Rb=[b_kT, self.ringb[rk]], Wb=[self.ringb[rk]])
                pSs2 = [self.bank(6, 0, 64), self.bank(7, 0, 64)]

                def fk():
                    ins = None
                    for g in range(4):
                        for b in range(NSAMP):
                            for hd in range(2):
                                b0 = 64 * hd
                                col = g * NSAMP + b
                                ins = PE.matmul(pSs2[hd][0][:, col:col + 1], lhsT=kv[b0:b0 + 64, b, :], rhs=qs[b0:b0 + 64, g, b:b + 1],
                                                start=True, stop=True)
                    return ins
                self.op(self.PE, fk, Rb=[self.ringb[rk], b_qs], Wb=pSs2[0][1] + pSs2[1][1])

                def fes():
                    ins = None
                    for hd in range(2):
                        ins = A.activation(out=PTs[:, hd * 64:hd * 64 + 64], in_=pSs2[hd][0], func=AF.Exp, scale=sc)
                    return ins
                self.op(self.ACT, fes, Rb=pSs2[0][1] + pSs2[1][1], Wb=[b_PTs])
                rvs = [self.take_slot(('sv', jj, hf)) for hf in range(2)]
                pos_ = self.bank(5, 0, 64)
                pds_ = self.bank(4, 0, 64)
                for hf in range(2):
                    vv = self.ring[:, rvs[hf], 0:8 * 192].rearrange("p (b n) -> p b n", b=8)
                    dst = self.ring[0:1, rvs[hf], 0:8 * 192].rearrange("p (b n) -> p b n", b=8)
                    for j2 in range(2):
                        self.dma(self.SP, out=dst[:, :, 128 * j2:128 * j2 + 64], in_=vnb[8 * hf:8 * hf + 8, 64 * j2:64 * j2 + 64],
                                 Rb=[b_vnb], Wb=[self.ringb[rvs[hf]]])

                    def fv():
                        ins = None
                        for g in range(4):
                            for b8 in range(8):
                                b = 8 * hf + b8
                                for hd in range(2):
                                    col = hd * 64 + g * NSAMP + b
                                    ins = PE.matmul(pos_[0][:, g * NSAMP + b:g * NSAMP + b + 1], lhsT=vv[:, b8, 64 * hd:64 * hd + 128],
                                                    rhs=PTs[:, col:col + 1], start=(hd == 0), stop=(hd == 1))
                        return ins
                    self.op(self.PE, fv, Rb=[self.ringb[rvs[hf]], b_PTs], Wb=pos_[1])

                def fd():
                    ins = None
                    for g in range(4):
                        for hd in range(2):
                            c_ = hd * 64 + g * NSAMP
                            ins = PE.matmul(pds_[0][:, g * NSAMP:(g + 1) * NSAMP], lhsT=self.ones[:, 64 * hd:64 * hd + 128],
                                            rhs=PTs[:, c_:c_ + NSAMP], start=(hd == 0), stop=(hd == 1))
                    return ins
                self.op(self.PE, fd, Rb=[b_PTs, self.b_ones], Wb=pds_[1])

                def fdn():
                    ins = None
                    for g in range(4):
                        cidx = jj * 4 + g
                        ins = V.tensor_scalar(out=dns[:, g * NSAMP:(g + 1) * NSAMP], in0=pds_[0][:, g * NSAMP:(g + 1) * NSAMP],
                                              scalar1=self.esink[:, cidx:cidx + 1], scalar2=None, op0=ALU.add)
                    return ins
                self.op(self.DVE, fdn, Rb=pds_[1] + [self.b_esink], Wb=[b_dns])
                self.op(self.DVE, lambda: V.reciprocal(out=dns[:, :], in_=dns[:, :]), Rb=[b_dns], Wb=[b_dns])
                self.op(self.DVE, lambda: V.tensor_tensor(out=oT[:, :, SAMP0:T], in0=pos_[0].rearrange("p (g b) -> p g b", g=4),
                                                          in1=dns[:, :].rearrange("p (g b) -> p g b", g=4), op=ALU.mult),
                        Rb=pos_[1] + [b_dns], Wb=[b for s in oTb for b in s.get(SAMP0, T)])
                if DBG == 7:
                    raise _Stop()
                rs = [self.take_slot(('rowsel', 'w_o', 0, (jj, g))) for g in range(4)]
                for d in range(NCH):
                    for ci, (c0, n) in enumerate(CT_MAIN):
                        ps = self.pdslot(n) if False else self.bank(6, 0, n) if False else self.pdslot_swa(n)

                        def fn():
                            ins = None
                            for g in range(4):
                                ins = PE.matmul(ps[0], lhsT=self.ring[:, rs[g], d * 128:(d + 1) * 128], rhs=oT[:, g, c0:c0 + n],
                                                start=(g == 0), stop=(g == 3))
                            return ins
                        self.op(self.PE, fn, Rb=[self.ringb[x] for x in rs] + [b for s in oTb for b in s.get(c0, c0 + n)], Wb=ps[1])
                        self.resid_add(d, c0, n, ps, scale=1.0)
            self.dma(self.SP, out=self.o_kp.rearrange("p (j t) -> p j t", j=4), in_=kpst[:, :, :], Rb=[b_kpst])
            self.dma(self.SP, out=self.o_vp.rearrange("p (j t) -> p j t", j=4), in_=vpst[:, :, :], Rb=[b_vpst])
            self.dma(self.SP, out=self.o_ksn.rearrange("p (j t) -> p j t", j=4), in_=ksst[:, :, :], Rb=[b_ksst])
            self.dma(self.SP, out=self.o_vsn.rearrange("p (j t) -> p j t", j=4), in_=vsst[:, :, :], Rb=[b_vsst])

    def pdslot_swa(self, n):
        i = (4, 5)[self.pd_rr % 2]
        self.pd_rr += 1
        return self.bank(i, 0, n)

    def phase_final(self):
        V = self.DVE.h
        ct = CT_MAIN
        with ExitStack() as es:
            rbc, rbcb = self.norm(GV['final'], ct, es, make_h=False)
            snap = self.snapshot()
            yst = self.sb("yst", [128, 2, NOUT], F32, es)
            yb = [Buf(snap), Buf(snap)]
            gi = GV['final']
            yv = self.o_yT.rearrange("p (k t) -> p k t", k=NCH)
            for k in range(NCH):
                p = k % 2
                self.op(self.DVE, lambda: V.scalar_tensor_tensor(
                    out=yst[:, p, :], in0=self.xT[:, k, HALO:T], scalar=self.gv[:, gi, k:k + 1], in1=rbc[:, HALO:T],
                    op0=ALU.mult, op1=ALU.mult), Rb=self.xb(k, HALO, NOUT) + rbcb.get(HALO, T) + [self.b_gv], Wb=[yb[p]])
                self.dma(self.SP, out=yv[:, k, :], in_=yst[:, p, :], Rb=[yb[p]])

    def finish(self):
        SP = self.SP
        for k in self.spsem:
            if self.dval[k] > 0 and SP.known.get(k, 0) < self.dval[k]:
                SP.h.wait_ge(self.sems[k], self.dval[k])
        for E in (self.PE, self.ACT, self.DVE):
            SP.h.wait_ge(self.sems[E.key], E.cnt)
        for k in self.rsem:
            if self.dval[k] > 0:
                self.POOL.h.wait_ge(self.sems[k], self.dval[k])


NS_MAX = 668
DBG = 0


class _Stop(Exception):
    pass
_CACHE = {}


def build_program(stop=None, ns=None):
    key = (stop, ns)
    if key in _CACHE:
        return _CACHE[key]
    P = Prog()
    P.declare()
    P.NS = ns if ns is not None else NS_MAX
    P.wstream = P.dram("wstream", [P.NS, 128, 2048], F32, "ExternalInput")
    phases = [P.phase_init, P.phase_memkv, lambda: P.phase_ffn(0, 1, CT_ALL), P.phase_pool,
              lambda: P.phase_xattn(0, CT_ALL), lambda: P.phase_ffn(0, 2, CT_ALL), lambda: P.phase_ffn(1, 1, CT_ALL),
              P.phase_swa, lambda: P.phase_xattn(1, CT_MAIN), lambda: P.phase_ffn(1, 2, CT_MAIN), P.phase_final]
    for i, ph in enumerate(phases):
        if stop is not None and i >= stop:
            break
        try:
            ph()
        except _Stop:
            break
    P.finish()
    assert len(P.slots) <= P.NS or stop is not None, len(P.slots)
    _CACHE[key] = P
    return P


def _col_tile(W, cols):
    return np.ascontiguousarray(W[:, cols].reshape(NCH, 128, 128).transpose(1, 0, 2)).reshape(128, 2048)


def _qkv_cols(spec):
    kind = spec[0]
    if kind == 'k':
        _, jj, sw = spec
        base = 2048 + jj * 128
        heads = [base, base + 64]
    elif kind == 'v':
        _, jj = spec
        return np.arange(2560 + jj * 128, 2560 + jj * 128 + 128)
    else:
        _, jj, g, sw = spec
        heads = [(8 * jj + g) * 64, (8 * jj + 4 + g) * 64]
    cols = []
    for hb in heads:
        idx = np.arange(64)
        if sw:
            idx = (idx + 32) % 64
        cols.append(hb + idx)
    return np.concatenate(cols)


def _common_slots(slots, inp):
    ws = np.zeros((max(len(slots), 1), 128, 2048), np.float32)
    percore = []
    for s, d in enumerate(slots):
        kind = d[0]
        if kind == 'col':
            _, name, l, c = d
            W = inp[name][l]
            cols = _qkv_cols(c) if name == 'w_qkv' else np.arange(c, c + 128)
            ws[s] = _col_tile(W, cols)
        elif kind == 'row':
            _, name, l, r0 = d
            ws[s] = inp[name][l][r0:r0 + 128, :]
        elif kind == 'rowsel':
            _, name, l, (jj, g) = d
            rows = np.concatenate([np.arange((8 * jj + g) * 64, (8 * jj + g) * 64 + 64),
                                   np.arange((8 * jj + 4 + g) * 64, (8 * jj + 4 + g) * 64 + 64)])
            ws[s] = inp[name][l][rows, :]
        elif kind == 'pool':
            g = d[1]
            ws[s] = inp['w_pool'][0][g].reshape(4, 128, 512).transpose(1, 0, 2).reshape(128, 2048)
        else:
            percore.append((s, d))
    return ws, percore


def _fill_percore(ws, percore, inp, c):
    b0 = NSAMP * c
    for s, d in percore:
        kind = d[0]
        if kind == 'xk':
            _, l, bp = d
            a = inp['cache_mem_k'][l, b0 + 2 * bp:b0 + 2 * bp + 2]
            ws[s] = a.transpose(3, 0, 2, 1).reshape(128, 2048)
        elif kind == 'xv':
            _, l, bp = d
            a = inp['cache_mem_v'][l, b0 + 2 * bp:b0 + 2 * bp + 2]
            ws[s] = a.reshape(2, 2, 128, 512).transpose(2, 0, 1, 3).reshape(128, 2048)
        elif kind == 'sk':
            _, jj = d
            a = inp['cache_swa_k'][0, b0:b0 + NSAMP, :, 2 * jj:2 * jj + 2, :]
            ws[s] = a.transpose(2, 3, 0, 1).reshape(128, 2048)
        elif kind == 'sv':
            _, jj, hf = d
            a = inp['cache_swa_v'][0, b0 + 8 * hf:b0 + 8 * hf + 8, :, 2 * jj:2 * jj + 2, :]
            t = np.zeros((128, 8, 192), np.float32)
            t[:, :, 0:64] = a[:, :, 0, :].transpose(1, 0, 2)
            t[:, :, 128:192] = a[:, :, 1, :].transpose(1, 0, 2)
            ws[s] = 0.0
            ws[s][:, 0:8 * 192] = t.reshape(128, 8 * 192)


def _fm(a):
    return np.ascontiguousarray(a.reshape(a.shape[0], NCH, 128).transpose(2, 1, 0))


def _const_tables(c):
    pos = np.zeros(T, np.float64)
    pos[:SAMP0] = c * NMAIN - HALO + np.arange(SAMP0)
    pos[SAMP0:] = PAST
    p = np.arange(128)
    dd = p % 64
    inv = (10000.0 ** (-(dd % 32).astype(np.float32) / 32.0)).astype(np.float32)
    ang = pos.astype(np.float32)[None, :] * inv[:, None]
    cs = np.zeros((128, 2, T), np.float32)
    cs[:, 0, :] = np.cos(ang)
    sn = np.sin(ang)
    cs[:, 1, :] = np.where((dd < 32)[:, None], -sn, sn)
    ic = np.ones((4, T), np.float32)
    for g, w in enumerate(POOLW):
        cnt = np.minimum(float(w), np.maximum(pos[:SAMP0] + 1.0, 1.0))
        ic[g, :SAMP0] = (1.0 / cnt).astype(np.float32)
    icnt = np.broadcast_to(ic[None], (128, 4, T)).reshape(128, 4 * T)
    k = np.arange(128)[:, None]
    q = np.arange(128)[None, :]
    m = np.zeros((128, 2, 2, 2, 128), np.float32)
    for hd in range(2):
        m[:, 0, hd, 0, :] = (k > q)
        m[:, 0, hd, 1, :] = (k <= q)
    m[:, 1] = m[:, 0]
    if c == 0:
        m[:, 1, :, 0, :] = 0.0
    return cs.reshape(128, 2 * T), np.ascontiguousarray(icnt), m.reshape(128, 1024).astype(NPBF)


def prep_common(inp, P):
    ws_common, percore = _common_slots(P.slots, inp)
    gv = np.zeros((128, NGV, NCH), np.float32)
    vecs = {'ffn1_0': inp['g_ffn1'][0], 'mix_0': inp['g_mix'][0], 'xq_0': inp['g_xq'][0], 'ffn2_0': inp['g_ffn2'][0],
            'ffn1_1': inp['g_ffn1'][1], 'mix_1': inp['g_mix'][1], 'xq_1': inp['g_xq'][1], 'ffn2_1': inp['g_ffn2'][1],
            'final': inp['g_final'], 'mem_0': inp['g_mem'][0], 'mem_1': inp['g_mem'][1], 'pscale': inp['pool_scale'][0]}
    for name, v in vecs.items():
        gv[:, GV[name], :] = v.reshape(NCH, 128).T
    sinks = inp['sinks'][0]
    sinkT = np.zeros((128, 16), np.float32)
    for jj in range(4):
        for g in range(4):
            sinkT[0:64, jj * 4 + g] = sinks[8 * jj + g]
            sinkT[64:128, jj * 4 + g] = sinks[8 * jj + 4 + g]
    memT = _fm(inp['mem_prompt'][0]).reshape(128, NCH * 256)
    return dict(ws=ws_common, percore=percore, gv=gv, sinkT=sinkT, memT=memT)


def core_inputs(inp, cm, c):
    xp = inp['x_prompt'][0]
    xs = inp['x_sample'][:, 0, :]
    tok = np.zeros((T, D), np.float32)
    lo = c * NMAIN - HALO
    a = max(lo, 0)
    tok[a - lo:SAMP0] = xp[a:(c + 1) * NMAIN]
    tok[SAMP0:] = xs[NSAMP * c:NSAMP * (c + 1)]
    cs, icnt, mask = _const_tables(c)
    st = inp['state_pool'][0, NSAMP * c:NSAMP * (c + 1)]
    stT = np.ascontiguousarray(st.reshape(NSAMP, 15, NCH, 128).transpose(3, 2, 0, 1)).reshape(128, NCH * NSAMP * 15)
    ws = cm['ws'].copy()
    _fill_percore(ws, cm['percore'], inp, c)
    return {
        'xT': _fm(tok).reshape(128, NCH * T), 'memT': cm['memT'], 'stT': stT,
        'stN': np.ascontiguousarray(st).reshape(NSAMP, 15 * D),
        'ckN': np.ascontiguousarray(inp['cache_swa_k'][0, NSAMP * c:NSAMP * (c + 1)]).reshape(NSAMP, 128 * 512),
        'cvN': np.ascontiguousarray(inp['cache_swa_v'][0, NSAMP * c:NSAMP * (c + 1)]).reshape(NSAMP, 128 * 512),
        'gvec': cm['gv'].reshape(128, NGV * NCH), 'sinkT': cm['sinkT'], 'cs': cs, 'icnt': icnt, 'maskc': mask, 'wstream': ws,
    }


def kernel(**inp):
    inp = {k: np.asarray(v) for k, v in inp.items()}
    P = build_program()
    cm = prep_common(inp, P)
    in_maps = [core_inputs(inp, cm, c) for c in range(NCORES)]
    res = run_bass_kernel_spmd(P.nc, in_maps, core_ids=list(range(NCORES)))
    out = res.results
    f32 = np.float32
    y_prompt = np.zeros((1, 8192, D), f32)
    y_sample = np.zeros((128, 1, D), f32)
    pool_s = np.zeros((1, 128, 15, D), f32)
    k_s = np.zeros((1, 128, 128, 8, 64), f32)
    v_s = np.zeros((1, 128, 128, 8, 64), f32)
    for c in range(NCORES):
        o = out[c]
        yT = o['yT'].reshape(128, NCH, NOUT)
        yy = yT.transpose(2, 1, 0).reshape(NOUT, D)
        y_prompt[0, c * NMAIN:(c + 1) * NMAIN] = yy[:NMAIN]
        y_sample[NSAMP * c:NSAMP * (c + 1), 0] = yy[NMAIN:]
        sl = slice(NSAMP * c, NSAMP * (c + 1))
        pool_s[0, sl, 0:14] = o['pool_s_old'].reshape(NSAMP, 14, D)
        pool_s[0, sl, 14] = o['pool_s_new'].reshape(128, NCH, NSAMP).transpose(2, 1, 0).reshape(NSAMP, D)
        k_s[0, sl, 0:127] = o['swa_k_s_old'].reshape(NSAMP, 127, 8, 64)
        v_s[0, sl, 0:127] = o['swa_v_s_old'].reshape(NSAMP, 127, 8, 64)
        kn = o['swa_k_s_new'].reshape(2, 64, 4, NSAMP)
        k_s[0, sl, 127] = kn.transpose(3, 2, 0, 1).reshape(NSAMP, 8, 64)
        v_s[0, sl, 127] = o['swa_v_s_new'].reshape(NSAMP, 8, 64)
    o7 = out[NCORES - 1]
    pool_p = o7['pool_p'].reshape(128, NCH, 15).transpose(2, 1, 0).reshape(1, 1, 15, D).astype(f32)
    k_p = o7['swa_k_p'].reshape(2, 64, 4, 128).transpose(3, 2, 0, 1).reshape(1, 1, 128, 8, 64).astype(f32)
    v_p = o7['swa_v_p'].reshape(1, 1, 128, 8, 64).astype(f32)
    o0 = out[0]
    mem_k = o0['mem_k'].reshape(128, 2, 4, 256).transpose(1, 3, 2, 0).reshape(2, 1, 256, 4, 128).astype(f32)
    mem_v = o0['mem_v'].reshape(128, 2, 2, 512).transpose(1, 2, 0, 3).reshape(2, 1, 256, 4, 128).astype(f32)
    return (y_prompt, y_sample, np.ascontiguousarray(pool_p), pool_s, np.ascontiguousarray(k_p), np.ascontiguousarray(v_p),
            k_s, v_s, np.ascontiguousarray(mem_k), np.ascontiguousarray(mem_v))
```

```python
import math
from contextlib import ExitStack

import ml_dtypes
import numpy as np

import concourse.bass as bass
import concourse.mybir as mybir
from concourse.bass_utils import run_bass_kernel_spmd

F32 = mybir.dt.float32
BF16 = mybir.dt.bfloat16
AF = mybir.ActivationFunctionType
ALU = mybir.AluOpType
AX = mybir.AxisListType
NPBF = ml_dtypes.bfloat16

D = 2048
NCH = 16
DFF = 5632
NF = 44
T = 1184
HALO = 144
NMAIN = 1024
NSAMP = 16
SAMP0 = 1168
NOUT = 1040
R = 9
NCORES = 8
CT_ALL = [(0, 512), (512, 512), (1024, 160)]
CT_MAIN = [(144, 448), (592, 448), (1040, 144)]
SEGB = [0, 144, 512, 592, 1024, 1040, 1168, 1184]
GV = {'ffn1_0': 0, 'mix_0': 1, 'xq_0': 2, 'ffn2_0': 3, 'ffn1_1': 4, 'mix_1': 5, 'xq_1': 6, 'ffn2_1': 7,
      'final': 8, 'mem_0': 9, 'mem_1': 10, 'pscale': 11}
NGV = 12
EPS = 1e-6
PAST = 8192
POOLW = (2, 4, 8, 16)


class Buf:
    __slots__ = ('w', 'r', 'excl')

    def __init__(self, snap=None):
        self.w = None
        self.r = dict(snap) if snap else {}
        self.excl = False


class Seg:
    def __init__(self, bounds, snap=None):
        self.b = bounds
        self.bufs = [Buf(snap) for _ in bounds[:-1]]

    def get(self, lo, hi):
        return [self.bufs[i] for i in range(len(self.b) - 1) if self.b[i] < hi and self.b[i + 1] > lo]


class Eng:
    def __init__(self, name, h, key):
        self.name, self.h, self.key = name, h, key
        self.cnt = 0
        self.known = {}


class Prog:
    def __init__(self):
        nc = bass.Bass("TRN2", target_bir_lowering=False)
        self.nc = nc
        self.es = ExitStack()
        self.sems = []
        self.dval = {}
        self.slots = []
        self.uid = 0
        self.PE = Eng('PE', nc.tensor, self.new_sem('s_pe'))
        self.ACT = Eng('ACT', nc.scalar, self.new_sem('s_act'))
        self.DVE = Eng('DVE', nc.vector, self.new_sem('s_dve'))
        self.POOL = Eng('POOL', nc.gpsimd, None)
        self.SP = Eng('SP', nc.sync, None)
        self.rsem = [self.new_sem(f's_ring{i}') for i in range(R)]
        self.spsem = [self.new_sem(f's_sp{i}') for i in range(16)]
        self.sp_rr = 0
        self.ring_ptr = 0
        self.held = set()
        for k in self.rsem + self.spsem:
            self.dval[k] = 0

    def new_sem(self, name):
        h = self.es.enter_context(self.nc.semaphore(name))
        self.sems.append(h)
        return len(self.sems) - 1

    def dram(self, name, shape, dt, kind):
        return self.nc.dram_tensor(name, list(shape), dt, kind=kind).ap()

    def sb(self, name, shape, dt, es=None):
        self.uid += 1
        return (es or self.es).enter_context(self.nc.sbuf_tensor(f"{name}_{self.uid}", list(shape), dt))

    def snapshot(self):
        d = {}
        for E in (self.PE, self.ACT, self.DVE):
            if E.cnt > 0:
                d[E.key] = E.cnt
        for k in self.spsem:
            if self.dval[k] > 0:
                d[k] = self.dval[k]
        return d

    def _deps(self, E, Rb, Wb):
        need = {}

        def add(k, v):
            if need.get(k, 0) < v:
                need[k] = v
        for b in Rb:
            if b.w is not None:
                add(*b.w)
        for b in Wb:
            if b.w is not None:
                add(*b.w)
            for k, v in b.r.items():
                add(k, v)
        for k, v in need.items():
            if k == E.key:
                if E.name == 'PE':
                    continue
            if E.known.get(k, 0) >= v:
                continue
            E.h.wait_ge(self.sems[k], v)
            E.known[k] = v

    def _mark(self, tok, Rb, Wb):
        for b in Rb:
            if b.r.get(tok[0], 0) < tok[1]:
                b.r[tok[0]] = tok[1]
        for b in Wb:
            b.w = tok
            b.r = {}

    def op(self, E, fn, Rb=(), Wb=()):
        if E.name != 'PE':
            ex = [b for b in Rb if b.excl]
            if ex:
                Wb = list(Wb) + ex
                Rb = [b for b in Rb if not b.excl]
        self._deps(E, Rb, Wb)
        ins = fn()
        E.cnt += 1
        ins.then_inc(self.sems[E.key], 1)
        self._mark((E.key, E.cnt), Rb, Wb)

    def dma(self, Q, out, in_, Rb=(), Wb=(), semkey=None):
        self._deps(Q, Rb, Wb)
        if semkey is None:
            semkey = self.spsem[self.sp_rr % len(self.spsem)]
            self.sp_rr += 1
            if self.dval[semkey] > 0 and Q.known.get(semkey, 0) < self.dval[semkey]:
                Q.h.wait_ge(self.sems[semkey], self.dval[semkey])
                Q.known[semkey] = self.dval[semkey]
        self.dval[semkey] += 16
        Q.h.dma_start(out=out, in_=in_).then_inc(self.sems[semkey], 16)
        self._mark((semkey, self.dval[semkey]), Rb, Wb)

    def take_slot(self, desc, hold=False):
        s = len(self.slots)
        self.slots.append(desc)
        r = self.ring_ptr % R
        while r in self.held:
            r = (r + 1) % R
        self.ring_ptr = r + 1
        if hold:
            self.held.add(r)
        self.dma(self.POOL, out=self.ring[:, r, :], in_=self.wstream[s], Wb=(self.ringb[r],), semkey=self.rsem[r])
        return r

    def slot3(self, r, k):
        return self.ring[:, r, :].rearrange("p (k n) -> p k n", k=k)

    def PA(self, ci, n):
        if ci < 2:
            return self.pb[ci][:, 0:n], self.pbuf[ci].get(0, n)
        return self.pb[6][:, 0:n], self.pbuf[6].get(0, n)

    def PB(self, ci, n):
        if ci < 2:
            return self.pb[2 + ci][:, 0:n], self.pbuf[2 + ci].get(0, n)
        return self.pb[6][:, 160:160 + n], self.pbuf[6].get(0, 512)

    def bank(self, i, lo, hi):
        return self.pb[i][:, lo:hi], self.pbuf[i].get(lo, hi)

    def xb(self, k, c0, n):
        return self.xbufs[k].get(c0, c0 + n)

    def declare(self):
        nc = self.nc
        I, O = "ExternalInput", "ExternalOutput"
        self.d_xT = self.dram("xT", [128, NCH * T], F32, I)
        self.d_memT = self.dram("memT", [128, NCH * 256], F32, I)
        self.d_stT = self.dram("stT", [128, NCH * NSAMP * 15], F32, I)
        self.d_stN = self.dram("stN", [NSAMP, 15 * D], F32, I)
        self.d_ckN = self.dram("ckN", [NSAMP, 128 * 512], F32, I)
        self.d_cvN = self.dram("cvN", [NSAMP, 128 * 512], F32, I)
        self.d_gvec = self.dram("gvec", [128, NGV * NCH], F32, I)
        self.d_sinkT = self.dram("sinkT", [128, 16], F32, I)
        self.d_cs = self.dram("cs", [128, 2 * T], F32, I)
        self.d_icnt = self.dram("icnt", [128, 4 * T], F32, I)
        self.d_mask = self.dram("maskc", [128, 2 * 512], BF16, I)
        self.NS_decl = None
        self.o_yT = self.dram("yT", [128, NCH * NOUT], F32, O)
        self.o_pool_p = self.dram("pool_p", [128, NCH * 15], F32, O)
        self.o_pool_sn = self.dram("pool_s_new", [128, NCH * NSAMP], F32, O)
        self.o_pool_so = self.dram("pool_s_old", [NSAMP, 14 * D], F32, O)
        self.o_kp = self.dram("swa_k_p", [128, 4 * 128], F32, O)
        self.o_vp = self.dram("swa_v_p", [128, 4 * 128], F32, O)
        self.o_ksn = self.dram("swa_k_s_new", [128, 4 * NSAMP], F32, O)
        self.o_vsn = self.dram("swa_v_s_new", [NSAMP, 4 * 128], F32, O)
        self.o_kso = self.dram("swa_k_s_old", [NSAMP, 127 * 512], F32, O)
        self.o_vso = self.dram("swa_v_s_old", [NSAMP, 127 * 512], F32, O)
        self.o_memk = self.dram("mem_k", [128, 2 * 4 * 256], F32, O)
        self.o_memv = self.dram("mem_v", [128, 2 * 2 * 512], F32, O)

        self.xT = self.sb("xT", [128, NCH, T], F32)
        self.hT = self.sb("hT", [128, NCH, T], BF16)
        self.ring = self.sb("ring", [128, R, 2048], BF16)
        self.KmT = self.sb("KmT", [128, 2, 4, 256], BF16)
        self.Vm = self.sb("Vm", [128, 2, 2, 512], BF16)
        self.gv = self.sb("gv", [128, NGV, NCH], F32)
        self.ones = self.sb("ones", [128, 192], BF16)
        self.onesf = self.sb("onesf", [128, 128], BF16)
        self.esink = self.sb("esink", [128, 16], F32)
        self.pb = [self.es.enter_context(nc.psum_tensor(f"pb{i}", [128, 512], F32)) for i in range(8)]
        self.pbuf = [Seg([0, 512]) for i in range(8)]
        for sg in self.pbuf:
            for b in sg.bufs:
                b.excl = True
        self.xbufs = [Seg(SEGB) for _ in range(NCH)]
        self.hTb = [Buf() for _ in range(NCH)]
        self.ringb = [Buf() for _ in range(R)]
        self.b_KmT = [Buf(), Buf()]
        self.b_Vm = [Buf(), Buf()]
        self.b_gv = Buf()
        self.b_ones = Buf()
        self.b_esink = Buf()
        self.pd_rr = 0

    def pdslot(self, n):
        i = (4, 5, 7)[self.pd_rr % 3]
        self.pd_rr += 1
        return self.bank(i, 0, n)

    def proj(self, dst, r, ctiles, src=None, srcb=None, fine=False):
        sv = self.slot3(r, 16)
        PE = self.PE.h
        src = self.hT if src is None else src
        srcb = self.hTb if srcb is None else srcb

        step = 2 if fine else NCH
        for k0 in range(0, NCH, step):
            def fn():
                ins = None
                for k in range(k0, k0 + step):
                    for ci, (c0, n) in enumerate(ctiles):
                        ins = PE.matmul(dst[ci][0], lhsT=sv[:, k, :], rhs=src[:, k, c0:c0 + n],
                                        start=(k == 0), stop=(k == NCH - 1))
                return ins
            self.op(self.PE, fn, Rb=[self.ringb[r]] + list(srcb[k0:k0 + step]), Wb=[b for d in dst for b in d[1]])

    def norm(self, gidx, ctiles, es, make_h=True):
        lo = ctiles[0][0]
        hi = ctiles[-1][0] + ctiles[-1][1]
        sq = self.sb("sq", [128, 2, T], BF16, es)
        rbc = self.sb("rbc", [128, T], F32, es)
        snap = self.snapshot()
        sqb = [Buf(snap), Buf(snap)]
        rbcb = Seg(SEGB, snap)
        A, V, PE = self.ACT.h, self.DVE.h, self.PE.h
        dst = [self.PA(ci, n) for ci, (c0, n) in enumerate(ctiles)]
        for k in range(NCH):
            p = k % 2
            if k % 2 == 0:
                self.op(self.ACT, lambda: A.activation(out=sq[:, p, lo:hi], in_=self.xT[:, k, lo:hi], func=AF.Square),
                        Rb=self.xb(k, lo, hi - lo), Wb=[sqb[p]])
            else:
                self.op(self.DVE, lambda: V.tensor_tensor(out=sq[:, p, lo:hi], in0=self.xT[:, k, lo:hi], in1=self.xT[:, k, lo:hi],
                                                          op=ALU.mult), Rb=self.xb(k, lo, hi - lo), Wb=[sqb[p]])

            def fn():
                ins = None
                for ci, (c0, n) in enumerate(ctiles):
                    ins = PE.matmul(dst[ci][0], lhsT=self.onesf[:, :], rhs=sq[:, p, c0:c0 + n],
                                    start=(k == 0), stop=(k == NCH - 1))
                return ins
            self.op(self.PE, fn, Rb=[sqb[p], self.b_ones], Wb=[b for d in dst for b in d[1]])
        for ci, (c0, n) in enumerate(ctiles):
            rb = rbcb.get(c0, c0 + n)
            self.op(self.ACT, lambda: A.activation(out=rbc[:, c0:c0 + n], in_=dst[ci][0], func=AF.Sqrt,
                                                   scale=1.0 / D, bias=EPS), Rb=dst[ci][1], Wb=rb)
            self.op(self.DVE, lambda: V.reciprocal(out=rbc[:, c0:c0 + n], in_=rbc[:, c0:c0 + n]), Rb=rb, Wb=rb)
        if make_h:
            for k in range(NCH):
                self.op(self.DVE, lambda: V.scalar_tensor_tensor(
                    out=self.hT[:, k, lo:hi], in0=self.xT[:, k, lo:hi], scalar=self.gv[:, gidx, k:k + 1],
                    in1=rbc[:, lo:hi], op0=ALU.mult, op1=ALU.mult),
                    Rb=self.xb(k, lo, hi - lo) + rbcb.get(lo, hi) + [self.b_gv], Wb=[self.hTb[k]])
        return rbc, rbcb

    def resid_add(self, d, c0, n, ps, scale=None, scale_ap=None):
        V = self.DVE.h
        xs = self.xT[:, d, c0:c0 + n]
        sc = scale_ap if scale_ap is not None else scale
        rb = list(ps[1]) + self.xb(d, c0, n) + ([self.b_gv] if scale_ap is not None else [])
        self.op(self.DVE, lambda: V.scalar_tensor_tensor(out=xs, in0=ps[0], scalar=sc, in1=xs,
                                                         op0=ALU.mult, op1=ALU.add),
                Rb=rb, Wb=self.xb(d, c0, n))

    def phase_init(self):
        V = self.DVE.h
        self.dma(self.SP, out=self.gv[:, :, :], in_=self.d_gvec.rearrange("p (g k) -> p g k", g=NGV), Wb=[self.b_gv])
        self.dma(self.SP, out=self.esink[:, :], in_=self.d_sinkT[:, :], Wb=[self.b_esink])
        def fn():
            V.memset(self.ones[:, 0:64], 1.0)
            V.memset(self.ones[:, 64:128], 0.0)
            V.memset(self.ones[:, 128:192], 1.0)
            return V.memset(self.onesf[:, :], 1.0)
        self.op(self.DVE, fn, Wb=[self.b_ones])
        self.op(self.ACT, lambda: self.ACT.h.activation(out=self.esink[:, :], in_=self.esink[:, :], func=AF.Exp),
                Rb=[self.b_esink], Wb=[self.b_esink])

    def phase_memkv(self):
        A, V, PE = self.ACT.h, self.DVE.h, self.PE.h
        with ExitStack() as es:
            snap = self.snapshot()
            memT = self.sb("memT", [128, NCH, 256], F32, es)
            hm = self.sb("hm", [128, NCH, 256], BF16, es)
            sq = self.sb("sqm", [128, 2, 256], BF16, es)
            rbc = self.sb("rbcm", [128, 256], F32, es)
            kst = self.sb("kst", [128, 2, 4, 256], F32, es)
            vst = self.sb("vst", [128, 2, 2, 512], F32, es)
            b_mem, b_rbc, b_kst, b_vst = Buf(snap), Buf(snap), Buf(snap), Buf(snap)
            sqb = [Buf(snap), Buf(snap)]
            hmb = [Buf(snap) for _ in range(NCH)]
            self.dma(self.SP, out=memT[:, :, :], in_=self.d_memT.rearrange("p (k m) -> p k m", k=NCH), Wb=[b_mem])
            xv = self.d_xT.rearrange("p (k t) -> p k t", k=NCH)
            for k in range(NCH):
                self.dma(self.SP, out=self.xT[:, k, :], in_=xv[:, k, :], Wb=self.xbufs[k].bufs)
            so = self.o_pool_so.rearrange("b (r d) -> b r d", r=14)
            si = self.d_stN.rearrange("b (r d) -> b r d", r=15)
            for r0 in range(0, 14, 4):
                r1 = min(r0 + 4, 14)
                self.dma(self.SP, out=so[:, r0:r1, :], in_=si[:, 1 + r0:1 + r1, :])
            for (o_, i_) in ((self.o_kso, self.d_ckN), (self.o_vso, self.d_cvN)):
                ov = o_.rearrange("b (r d) -> b r d", r=127)
                iv = i_.rearrange("b (r d) -> b r d", r=128)
                for r0 in range(0, 127, 16):
                    r1 = min(r0 + 16, 127)
                    self.dma(self.SP, out=ov[:, r0:r1, :], in_=iv[:, 1 + r0:1 + r1, :])

            ps = self.bank(0, 0, 256)
            for k in range(NCH):
                p = k % 2
                self.op(self.ACT, lambda: A.activation(out=sq[:, p, :], in_=memT[:, k, :], func=AF.Square),
                        Rb=[b_mem], Wb=[sqb[p]])
                self.op(self.PE, lambda: PE.matmul(ps[0], lhsT=self.onesf[:, :], rhs=sq[:, p, :],
                                                   start=(k == 0), stop=(k == NCH - 1)),
                        Rb=[sqb[p], self.b_ones], Wb=ps[1])
            self.op(self.ACT, lambda: A.activation(out=rbc[:, :], in_=ps[0], func=AF.Sqrt, scale=1.0 / D, bias=EPS),
                    Rb=ps[1], Wb=[b_rbc])
            self.op(self.DVE, lambda: V.reciprocal(out=rbc[:, :], in_=rbc[:, :]), Rb=[b_rbc], Wb=[b_rbc])
            if DBG == 1:
                return
            for l in range(2):
                gi = GV[f'mem_{l}']
                for k in range(NCH):
                    self.op(self.DVE, lambda: V.scalar_tensor_tensor(
                        out=hm[:, k, :], in0=memT[:, k, :], scalar=self.gv[:, gi, k:k + 1], in1=rbc[:, :],
                        op0=ALU.mult, op1=ALU.mult), Rb=[b_mem, b_rbc, self.b_gv], Wb=[hmb[k]])
                for h in range(4):
                    r = self.take_slot(('col', 'w_xkv', l, h * 128))
                    sv = self.slot3(r, 16)
                    pk = self.bank(1 + (h % 2), 0, 256)

                    def fn():
                        ins = None
                        for k in range(NCH):
                            ins = PE.matmul(pk[0], lhsT=sv[:, k, :], rhs=hm[:, k, :], start=(k == 0), stop=(k == NCH - 1))
                        return ins
                    self.op(self.PE, fn, Rb=[self.ringb[r]] + hmb, Wb=pk[1])
                    self.op(self.ACT, lambda: A.activation(out=self.KmT[:, l, h, :], in_=pk[0], func=AF.Copy),
                            Rb=pk[1], Wb=[self.b_KmT[l]])
                    self.op(self.DVE, lambda: V.tensor_copy(out=kst[:, l, h, :], in_=pk[0]), Rb=pk[1], Wb=[b_kst])
                    if DBG == 2:
                        return
                pv = [self.bank(3, 0, 512), self.bank(4, 0, 512)]
                for h in range(4):
                    r = self.take_slot(('col', 'w_xkv', l, 512 + h * 128))
                    sv = self.slot3(r, 16)
                    for mb in range(2):
                        def fn():
                            ins = None
                            for k in range(NCH):
                                ins = PE.matmul(pv[mb][0][:, h * 128:(h + 1) * 128], lhsT=hm[:, k, mb * 128:(mb + 1) * 128],
                                                rhs=sv[:, k, :], start=(k == 0), stop=(k == NCH - 1))
                            return ins
                        self.op(self.PE, fn, Rb=[self.ringb[r]] + hmb, Wb=pv[mb][1])
                for mb in range(2):
                    self.op(self.ACT, lambda: A.activation(out=self.Vm[:, l, mb, :], in_=pv[mb][0], func=AF.Copy),
                            Rb=pv[mb][1], Wb=[self.b_Vm[l]])
                    self.op(self.DVE, lambda: V.tensor_copy(out=vst[:, l, mb, :], in_=pv[mb][0]), Rb=pv[mb][1], Wb=[b_vst])
            self.dma(self.SP, out=self.o_memk.rearrange("p (l h m) -> p l h m", l=2, h=4), in_=kst[:, :, :, :], Rb=[b_kst])
            self.dma(self.SP, out=self.o_memv.rearrange("p (l b n) -> p l b n", l=2, b=2), in_=vst[:, :, :, :], Rb=[b_vst])

    def phase_ffn(self, l, which, ctiles):
        A, V, PE = self.ACT.h, self.DVE.h, self.PE.h
        wgu, wdn = f'w_ffn{which}_gu', f'w_ffn{which}_dn'
        lo = ctiles[0][0]
        hi = ctiles[-1][0] + ctiles[-1][1]
        with ExitStack() as es:
            with ExitStack() as es2:
                self.norm(GV[f'ffn{which}_{l}'], ctiles, es2)
            snap = self.snapshot()
            act = self.sb("act", [128, 2, 4, T], BF16, es)
            sil = self.sb("sil", [128, T], F32, es)
            actb = [[[Buf(snap) for _ in range(3)] for _ in range(4)] for _ in range(2)]
            silb = [Buf(snap) for _ in range(3)]
            order = [2, 0, 1]
            pa = [self.PA(ci, n) for ci, (c0, n) in enumerate(ctiles)]
            pbb = [self.PB(ci, n) for ci, (c0, n) in enumerate(ctiles)]

            pending = []

            def drain(k):
                for _ in range(min(k, len(pending))):
                    pending.pop(0)()

            def gu(j):
                par, f = (j // 4) % 2, j % 4
                r = self.take_slot(('col', wgu, l, j * 128))
                self.proj(pa, r, ctiles, fine=(j == 0))

                for ci in order:
                    c0, n = ctiles[ci]
                    self.op(self.ACT, lambda: A.activation(out=sil[:, c0:c0 + n], in_=pa[ci][0], func=AF.Silu),
                            Rb=pa[ci][1], Wb=[silb[ci]])
                drain(6)
                r2 = self.take_slot(('col', wgu, l, DFF + j * 128))
                self.proj(pbb, r2, ctiles)

                for ci in order:
                    c0, n = ctiles[ci]
                    self.op(self.DVE, lambda: V.tensor_tensor(out=act[:, par, f, c0:c0 + n], in0=pbb[ci][0], in1=sil[:, c0:c0 + n],
                                                              op=ALU.mult), Rb=[silb[ci]] + pbb[ci][1], Wb=[actb[par][f][ci]])
                drain(6)

            def down_tasks(g):
                par = g % 2
                st = {}
                tasks = []
                ntask = NCH * len(ctiles)

                def mk(idx, d, ci):
                    def task():
                        if 'rs' not in st:
                            st['rs'] = [self.take_slot(('row', wdn, l, (4 * g + f) * 128), hold=True) for f in range(4)]
                        rs = st['rs']
                        c0, n = ctiles[ci]
                        ps = self.pdslot(n)

                        def fn():
                            ins = None
                            for f in range(4):
                                ins = PE.matmul(ps[0], lhsT=self.ring[:, rs[f], d * 128:(d + 1) * 128],
                                                rhs=act[:, par, f, c0:c0 + n], start=(f == 0), stop=(f == 3))
                            return ins
                        self.op(self.PE, fn, Rb=[self.ringb[x] for x in rs] + [actb[par][f][ci] for f in range(4)], Wb=ps[1])
                        self.resid_add(d, c0, n, ps, scale=0.5)
                        if idx == ntask - 1:
                            for x in rs:
                                self.held.discard(x)
                    return task
                i = 0
                for d in range(NCH):
                    for ci in range(len(ctiles)):
                        tasks.append(mk(i, d, ci))
                        i += 1
                return tasks
            NG = NF // 4
            for g in range(NG):
                for f in range(4):
                    gu(4 * g + f)
                drain(len(pending))
                pending.extend(down_tasks(g))
            drain(len(pending))

    def phase_pool(self):
        A, V, PE = self.ACT.h, self.DVE.h, self.PE.h
        ct = CT_ALL
        with ExitStack() as es:
            rbc, rbcb = self.norm(GV['mix_0'], ct, es, make_h=False)
            snap = self.snapshot()
            tA = self.sb("tA", [128, T], F32, es)
            tB = self.sb("tB", [128, T], F32, es)
            tC = self.sb("tC", [128, T], F32, es)
            icn = self.sb("icn", [128, T], F32, es)
            stk = self.sb("stk", [128, 2, NSAMP, 15], F32, es)
            ssum = self.sb("ssum", [128, NSAMP], F32, es)
            pst = self.sb("pst", [128, NCH, 15], F32, es)
            sst = self.sb("sst", [128, NCH, NSAMP], F32, es)
            bA, bB, bC, b_icn, b_ss, b_pst, b_sst = (Buf(snap) for _ in range(7))
            b_stk = [Buf(snap), Buf(snap)]
            gi = GV['mix_0']
            stT = self.d_stT.rearrange("p (k b r) -> p k b r", k=NCH, b=NSAMP)
            icv = self.d_icnt.rearrange("p (g t) -> p g t", g=4)
            rall = rbcb.get(0, T)
            for k in range(NCH):
                g = k // 4
                w = POOLW[g]
                if k % 4 == 0:
                    self.dma(self.SP, out=icn[:, :], in_=icv[:, g, :], Wb=[b_icn])
                sp_ = k % 2
                self.dma(self.SP, out=stk[:, sp_, :, :], in_=stT[:, k, :, :], Wb=[b_stk[sp_]])
                self.op(self.DVE, lambda: V.scalar_tensor_tensor(
                    out=tA[:, :], in0=self.xT[:, k, :], scalar=self.gv[:, gi, k:k + 1], in1=rbc[:, :],
                    op0=ALU.mult, op1=ALU.mult), Rb=self.xb(k, 0, T) + rall + [self.b_gv], Wb=[bA])
                cur, curb = tA, bA
                oth = [(tB, bB), (tC, bC)]
                st, oi = 1, 0
                while st < w:
                    nt, nb = oth[oi]
                    lo_ = 2 * st - 1
                    self.op(self.DVE, lambda: V.tensor_tensor(out=nt[:, lo_:T], in0=cur[:, lo_:T], in1=cur[:, lo_ - st:T - st],
                                                              op=ALU.add), Rb=[curb], Wb=[nb])
                    cur, curb = nt, nb
                    oi ^= 1
                    st *= 2
                nt, nb = oth[oi]
                self.op(self.DVE, lambda: V.tensor_tensor(out=nt[:, 15:SAMP0], in0=cur[:, 15:SAMP0], in1=icn[:, 15:SAMP0],
                                                          op=ALU.mult), Rb=[curb, b_icn], Wb=[nb])
                self.op(self.DVE, lambda: V.tensor_tensor(out=self.hT[:, k, 15:SAMP0], in0=nt[:, 15:SAMP0], in1=tA[:, 15:SAMP0],
                                                          op=ALU.subtract), Rb=[nb, bA], Wb=[self.hTb[k]])
                self.op(self.DVE, lambda: V.tensor_reduce(out=ssum[:, :], in_=stk[:, sp_, :, 15 - (w - 1):15], axis=AX.X, op=ALU.add),
                        Rb=[b_stk[sp_]], Wb=[b_ss])
                self.op(self.DVE, lambda: V.tensor_tensor(out=ssum[:, :], in0=ssum[:, :], in1=tA[:, SAMP0:T], op=ALU.add),
                        Rb=[b_ss, bA], Wb=[b_ss])
                self.op(self.DVE, lambda: V.scalar_tensor_tensor(out=self.hT[:, k, SAMP0:T], in0=ssum[:, :], scalar=1.0 / w,
                                                                 in1=tA[:, SAMP0:T], op0=ALU.mult, op1=ALU.subtract),
                        Rb=[b_ss, bA], Wb=[self.hTb[k]])
                self.op(self.ACT, lambda: A.activation(out=pst[:, k, :], in_=tA[:, SAMP0 - 15:SAMP0], func=AF.Copy), Rb=[bA], Wb=[b_pst])
                self.op(self.ACT, lambda: A.activation(out=sst[:, k, :], in_=tA[:, SAMP0:T], func=AF.Copy), Rb=[bA], Wb=[b_sst])
            self.dma(self.SP, out=self.o_pool_p.rearrange("p (k r) -> p k r", k=NCH), in_=pst[:, :, :], Rb=[b_pst])
            self.dma(self.SP, out=self.o_pool_sn.rearrange("p (k b) -> p k b", k=NCH), in_=sst[:, :, :], Rb=[b_sst])
            gs = GV['pscale']
            for g in range(4):
                r = self.take_slot(('pool', g))
                sv = self.slot3(r, 4)
                for oc in range(4):
                    d = 4 * g + oc
                    for ci, (c0, n) in enumerate(ct):
                        ps = self.pdslot(n)

                        def fn():
                            ins = None
                            for kc in range(4):
                                ins = PE.matmul(ps[0], lhsT=sv[:, kc, oc * 128:(oc + 1) * 128], rhs=self.hT[:, 4 * g + kc, c0:c0 + n],
                                                start=(kc == 0), stop=(kc == 3))
                            return ins
                        self.op(self.PE, fn, Rb=[self.ringb[r]] + self.hTb[4 * g:4 * g + 4], Wb=ps[1])
                        self.resid_add(d, c0, n, ps, scale_ap=self.gv[:, gs, d:d + 1])

    def phase_xattn(self, l, ctiles):
        A, V, PE = self.ACT.h, self.DVE.h, self.PE.h
        sc = 1.0 / math.sqrt(128.0)
        with ExitStack() as es:
            with ExitStack() as es2:
                self.norm(GV[f'xq_{l}'], ctiles, es2)
            snap = self.snapshot()
            qT = self.sb("qT", [128, 4, T], BF16, es)
            oT = self.sb("oT", [128, 4, T], BF16, es)
            PT = self.sb("PT", [128, 2, 2, 512], BF16, es)
            rec = self.sb("rec", [128, 2, 512], F32, es)
            PTs = self.sb("PTs", [128, 128], BF16, es)
            recs = self.sb("recs", [128, 64], F32, es)
            qTb = [Buf(snap) for _ in range(4)]
            oTb = [Seg(SEGB, snap) for _ in range(4)]
            PTb = [[Buf(snap), Buf(snap)], [Buf(snap), Buf(snap)]]
            recb = [Buf(snap), Buf(snap)]
            b_PTs, b_recs = Buf(snap), Buf(snap)
            for h in range(4):
                r = self.take_slot(('col', 'w_xq', l, h * 128))
                dst = [(self.PA if h % 2 == 0 else self.PB)(ci, n) for ci, (c0, n) in enumerate(ctiles)]
                self.proj(dst, r, ctiles, fine=(h == 0))

                def fa():
                    ins = None
                    for ci, (c0, n) in enumerate(ctiles):
                        ins = A.activation(out=qT[:, h, c0:c0 + n], in_=dst[ci][0], func=AF.Copy)
                    return ins
                self.op(self.ACT, fa, Rb=[b for d in dst for b in d[1]], Wb=[qTb[h]])
            units = [(h, c0, n) for h in range(4) for (c0, n) in ctiles if c0 < SAMP0]

            def xs1(u):
                h, c0, n = units[u]
                par = u % 2
                pS = [self.bank(2 * par, 0, n), self.bank(2 * par + 1, 0, n)]

                def fs():
                    ins = None
                    for mb in range(2):
                        ins = PE.matmul(pS[mb][0], lhsT=self.KmT[:, l, h, mb * 128:(mb + 1) * 128], rhs=qT[:, h, c0:c0 + n],
                                        start=True, stop=True)
                    return ins
                self.op(self.PE, fs, Rb=[self.b_KmT[l], qTb[h]], Wb=pS[0][1] + pS[1][1])
                for mb in range(2):
                    self.op(self.ACT, lambda: A.activation(out=PT[:, par, mb, 0:n], in_=pS[mb][0], func=AF.Exp, scale=sc),
                            Rb=pS[mb][1], Wb=[PTb[par][mb]])

            def xs2(u):
                h, c0, n = units[u]
                par = u % 2
                po = self.bank(4 + par, 0, n)
                pd = self.bank(7 - par, 0, n)

                def fo():
                    ins = None
                    for mb in range(2):
                        ins = PE.matmul(po[0], lhsT=self.Vm[:, l, mb, h * 128:(h + 1) * 128], rhs=PT[:, par, mb, 0:n],
                                        start=(mb == 0), stop=(mb == 1))
                    for mb in range(2):
                        ins = PE.matmul(pd[0], lhsT=self.onesf[:, :], rhs=PT[:, par, mb, 0:n], start=(mb == 0), stop=(mb == 1))
                    return ins
                self.op(self.PE, fo, Rb=[self.b_Vm[l], self.b_ones] + PTb[par], Wb=po[1] + pd[1])
                self.op(self.DVE, lambda: V.reciprocal(out=rec[:, par, 0:n], in_=pd[0]), Rb=pd[1], Wb=[recb[par]])
                self.op(self.DVE, lambda: V.tensor_tensor(out=oT[:, h, c0:c0 + n], in0=po[0], in1=rec[:, par, 0:n], op=ALU.mult),
                        Rb=po[1] + [recb[par]], Wb=oTb[h].get(c0, c0 + n))
            xs1(0)
            for u in range(len(units)):
                if u + 1 < len(units):
                    xs1(u + 1)
                xs2(u)
            pSs = self.bank(0, 0, 128)
            krs = [self.take_slot(('xk', l, bp)) for bp in range(8)]
            for bp in range(8):
                kv = self.ring[:, krs[bp], :].rearrange("p (b h m) -> p b h m", b=2, h=4)

                def fk():
                    ins = None
                    for bb in range(2):
                        b = 2 * bp + bb
                        for h in range(4):
                            for mb in range(2):
                                col = mb * 64 + h * 16 + b
                                ins = PE.matmul(pSs[0][:, col:col + 1], lhsT=kv[:, bb, h, mb * 128:(mb + 1) * 128],
                                                rhs=qT[:, h, SAMP0 + b:SAMP0 + b + 1], start=True, stop=True)
                    return ins
                self.op(self.PE, fk, Rb=[self.ringb[krs[bp]]] + qTb, Wb=pSs[1])
            self.op(self.ACT, lambda: A.activation(out=PTs[:, :], in_=pSs[0], func=AF.Exp, scale=sc), Rb=pSs[1], Wb=[b_PTs])
            pos_ = self.bank(1, 0, 64)
            pds_ = self.bank(2, 0, 64)
            vrs = [self.take_slot(('xv', l, bp)) for bp in range(8)]
            for bp in range(8):
                vv = self.ring[:, vrs[bp], :].rearrange("p (b m n) -> p b m n", b=2, m=2)

                def fv():
                    ins = None
                    for bb in range(2):
                        b = 2 * bp + bb
                        for h in range(4):
                            for mb in range(2):
                                col = mb * 64 + h * 16 + b
                                ins = PE.matmul(pos_[0][:, h * 16 + b:h * 16 + b + 1], lhsT=vv[:, bb, mb, h * 128:(h + 1) * 128],
                                                rhs=PTs[:, col:col + 1], start=(mb == 0), stop=(mb == 1))
                    return ins
                self.op(self.PE, fv, Rb=[self.ringb[vrs[bp]], b_PTs], Wb=pos_[1])

            def fd():
                ins = None
                for mb in range(2):
                    ins = PE.matmul(pds_[0], lhsT=self.onesf[:, :], rhs=PTs[:, mb * 64:(mb + 1) * 64], start=(mb == 0), stop=(mb == 1))
                return ins
            self.op(self.PE, fd, Rb=[b_PTs, self.b_ones], Wb=pds_[1])
            self.op(self.DVE, lambda: V.reciprocal(out=recs[:, :], in_=pds_[0]), Rb=pds_[1], Wb=[b_recs])
            self.op(self.DVE, lambda: V.tensor_tensor(out=oT[:, :, SAMP0:T], in0=pos_[0].rearrange("p (h b) -> p h b", h=4),
                                                      in1=recs[:, :].rearrange("p (h b) -> p h b", h=4), op=ALU.mult),
                    Rb=pos_[1] + [b_recs], Wb=[b for s in oTb for b in s.get(SAMP0, T)])
            rs = [self.take_slot(('row', 'w_xo', l, h * 128)) for h in range(4)]
            for d in range(NCH):
                for ci, (c0, n) in enumerate(ctiles):
                    ps = self.pdslot(n)

                    def fn():
                        ins = None
                        for h in range(4):
                            ins = PE.matmul(ps[0], lhsT=self.ring[:, rs[h], d * 128:(d + 1) * 128], rhs=oT[:, h, c0:c0 + n],
                                            start=(h == 0), stop=(h == 3))
                        return ins
                    self.op(self.PE, fn, Rb=[self.ringb[x] for x in rs] + [b for s in oTb for b in s.get(c0, c0 + n)], Wb=ps[1])
                    self.resid_add(d, c0, n, ps, scale=1.0)

    def phase_swa(self):
        A, V, PE = self.ACT.h, self.DVE.h, self.PE.h
        ct = CT_ALL
        sc = 0.125
        LAST0 = SAMP0 - 128
        with ExitStack() as es:
            with ExitStack() as es2:
                self.norm(GV['mix_1'], ct, es2)
            snap = self.snapshot()
            cs = self.sb("cs", [128, 2, T], F32, es)
            msk = self.sb("msk", [128, 2, 512], BF16, es)
            kT = self.sb("kT", [128, T], BF16, es)
            Vp = self.sb("Vp", [128, 9, 192], BF16, es)
            qc = self.sb("qc", [128, 2, T], BF16, es)
            qs = self.sb("qs", [128, 4, NSAMP], BF16, es)
            oT = self.sb("oTs", [128, 4, T], BF16, es)
            r1 = self.sb("r1", [128, 2, 512], F32, es)
            r2 = self.sb("r2", [128, 2, 512], F32, es)
            PT = self.sb("PTw", [128, 2, 512], BF16, es)
            dn = self.sb("dn", [128, 2, 128], F32, es)
            PTs = self.sb("PTss", [128, 128], BF16, es)
            dns = self.sb("dns", [128, 64], F32, es)
            kpst = self.sb("kpst", [128, 4, 128], F32, es)
            vpst = self.sb("vpst", [128, 4, 128], F32, es)
            ksst = self.sb("ksst", [128, 4, NSAMP], F32, es)
            vsst = self.sb("vsst", [NSAMP, 4, 128], F32, es)
            vnb = self.sb("vnb", [NSAMP, 128], BF16, es)
            b_cs, b_msk, b_kT, b_Vp, b_qs, b_PTs, b_dns, b_kpst, b_vpst, b_ksst, b_vsst, b_vnb = (Buf(snap) for _ in range(12))
            qcb = [Buf(snap), Buf(snap)]
            oTb = [Seg(SEGB, snap) for _ in range(4)]
            r1b = [Buf(snap), Buf(snap)]
            r2b = [Buf(snap), Buf(snap)]
            PTb = [[Buf(snap), Buf(snap)], [Buf(snap), Buf(snap)]]
            dnb = [Buf(snap), Buf(snap)]
            self.dma(self.SP, out=cs[:, :, :], in_=self.d_cs.rearrange("p (a t) -> p a t", a=2), Wb=[b_cs])
            self.dma(self.SP, out=msk[:, :, :], in_=self.d_mask.rearrange("p (a t) -> p a t", a=2), Wb=[b_msk])

            def fz():
                V.memset(Vp[:, :, :], 0.0)
                return V.memset(oT[:, :, :], 0.0)
            self.op(self.DVE, fz, Wb=[b_Vp] + [b for s in oTb for b in s.bufs])
            if DBG == 3:
                raise _Stop()
            self.rope_n = 0

            def rope(pn, ps_, outs):
                for ci, (c0, n) in enumerate(ct):
                    p = self.rope_n % 2
                    self.rope_n += 1
                    self.op(self.DVE, lambda: V.tensor_tensor(out=r1[:, p, 0:n], in0=pn[ci][0], in1=cs[:, 0, c0:c0 + n], op=ALU.mult),
                            Rb=pn[ci][1] + [b_cs], Wb=[r1b[p]])
                    self.op(self.DVE, lambda: V.tensor_tensor(out=r2[:, p, 0:n], in0=ps_[ci][0], in1=cs[:, 1, c0:c0 + n], op=ALU.mult),
                            Rb=ps_[ci][1] + [b_cs], Wb=[r2b[p]])
                    self.op(self.DVE, lambda: V.tensor_tensor(out=r1[:, p, 0:n], in0=r1[:, p, 0:n], in1=r2[:, p, 0:n], op=ALU.add),
                            Rb=[r1b[p], r2b[p]], Wb=[r1b[p]])
                    for (dst, bufs, lo_, hi_) in outs:
                        a, b = max(lo_, c0), min(hi_, c0 + n)
                        if a >= b:
                            continue
                        self.op(self.ACT, lambda: A.activation(out=dst(a, b), in_=r1[:, p, a - c0:b - c0], func=AF.Copy),
                                Rb=[r1b[p]], Wb=bufs)

            pa = [self.PA(ci, n) for ci, (c0, n) in enumerate(ct)]
            pbb = [self.PB(ci, n) for ci, (c0, n) in enumerate(ct)]
            unit = 0
            for jj in range(4):
                r = self.take_slot(('col', 'w_qkv', 0, ('k', jj, 0)))
                self.proj(pa, r, ct, fine=(jj == 0))
                r = self.take_slot(('col', 'w_qkv', 0, ('k', jj, 1)))
                self.proj(pbb, r, ct)
                rope(pa, pbb, [
                    (lambda a, b: kT[:, a:b], [b_kT], 0, T),
                    (lambda a, b: kpst[:, jj, a - LAST0:b - LAST0], [b_kpst], LAST0, SAMP0),
                    (lambda a, b: ksst[:, jj, a - SAMP0:b - SAMP0], [b_ksst], SAMP0, T),
                ])
                if DBG == 4:
                    raise _Stop()
                r = self.take_slot(('col', 'w_qkv', 0, ('v', jj)))
                sv = self.slot3(r, 16)
                pv = [self.bank(4, 0, 512), self.bank(5, 0, 512), self.bank(7, 0, 256)]

                def vdst(blk):
                    return pv[blk // 4][0][:, (blk % 4) * 128:(blk % 4) * 128 + 128]
                for blk in range(10):
                    c0 = 16 + 128 * blk if blk < 9 else SAMP0
                    m = 128 if blk < 9 else NSAMP

                    def fn():
                        ins = None
                        for k in range(NCH):
                            ins = PE.matmul(vdst(blk)[0:m, :], lhsT=self.hT[:, k, c0:c0 + m], rhs=sv[:, k, :],
                                            start=(k == 0), stop=(k == NCH - 1))
                        return ins
                    self.op(self.PE, fn, Rb=[self.ringb[r]] + self.hTb, Wb=pv[blk // 4][1])
                for grp in range(3):
                    nb = 4 if grp < 2 else 1
                    src = pv[grp][0][:, 0:nb * 128].rearrange("p (b j d) -> p b j d", b=nb, j=2)
                    dstv = Vp[:, 4 * grp:4 * grp + nb, :].rearrange("p b (j d) -> p b j d", j=3)[:, :, 0:3:2, :]
                    self.op(self.ACT, lambda: A.activation(out=dstv, in_=src, func=AF.Copy), Rb=pv[grp][1], Wb=[b_Vp])
                self.op(self.DVE, lambda: V.tensor_copy(out=vpst[:, jj, :], in_=vdst(8)), Rb=pv[2][1], Wb=[b_vpst])
                self.op(self.DVE, lambda: V.tensor_copy(out=vsst[:, jj, :], in_=vdst(9)[0:NSAMP, :]), Rb=pv[2][1], Wb=[b_vsst])
                self.op(self.DVE, lambda: V.tensor_copy(out=vnb[:, :], in_=vdst(9)[0:NSAMP, :]), Rb=pv[2][1], Wb=[b_vnb])
                if DBG == 5:
                    raise _Stop()
                for g in range(4):
                    qp = g % 2
                    r = self.take_slot(('col', 'w_qkv', 0, ('q', jj, g, 0)))
                    self.proj(pa, r, ct)
                    r = self.take_slot(('col', 'w_qkv', 0, ('q', jj, g, 1)))
                    self.proj(pbb, r, ct)
                    rope(pa, pbb, [
                        (lambda a, b: qc[:, qp, a:b], [qcb[qp]], 0, T),
                        (lambda a, b: qs[:, g, a - SAMP0:b - SAMP0], [b_qs], SAMP0, T),
                    ])
                    if DBG == 61:
                        raise _Stop()
                    cidx = jj * 4 + g

                    def stage1(n_):
                        par = n_ % 2
                        q0 = HALO + 128 * n_
                        pS2 = [self.bank(2 * par, 0, 256), self.bank(2 * par + 1, 0, 256)]
                        mv = 1 if n_ == 0 else 0

                        def fs():
                            ins = None
                            for kb in range(2):
                                for hd in range(2):
                                    b0 = 64 * hd
                                    k0 = 16 + 128 * (n_ + kb)
                                    ins = PE.matmul(pS2[hd][0][:, kb * 128:kb * 128 + 128], lhsT=kT[b0:b0 + 64, k0:k0 + 128],
                                                    rhs=qc[b0:b0 + 64, qp, q0:q0 + 128], start=True, stop=True)
                            return ins
                        self.op(self.PE, fs, Rb=[b_kT, qcb[qp]], Wb=pS2[0][1] + pS2[1][1])
                        for hd in range(2):
                            self.op(self.ACT, lambda: A.activation(out=PT[:, par, hd * 256:hd * 256 + 256], in_=pS2[hd][0], func=AF.Exp, scale=sc),
                                    Rb=pS2[hd][1], Wb=[PTb[par][hd]])
                            self.op(self.DVE, lambda: V.tensor_tensor(out=PT[:, par, hd * 256:hd * 256 + 256], in0=PT[:, par, hd * 256:hd * 256 + 256],
                                                                      in1=msk[:, mv, hd * 256:hd * 256 + 256], op=ALU.mult),
                                    Rb=[PTb[par][hd], b_msk], Wb=[PTb[par][hd]])

                    def stage2(n_):
                        par = n_ % 2
                        q0 = HALO + 128 * n_
                        pod = self.bank(4 + par, 0, 256)

                        def fo():
                            ins = None
                            i = 0
                            for hd in range(2):
                                for kb in range(2):
                                    o0 = (hd * 2 + kb) * 128
                                    ins = PE.matmul(pod[0][:, 0:128], lhsT=Vp[:, n_ + kb, 64 * hd:64 * hd + 128], rhs=PT[:, par, o0:o0 + 128],
                                                    start=(i == 0), stop=(i == 3))
                                    i += 1
                            i = 0
                            for hd in range(2):
                                for kb in range(2):
                                    o0 = (hd * 2 + kb) * 128
                                    ins = PE.matmul(pod[0][:, 128:256], lhsT=self.ones[:, 64 * hd:64 * hd + 128], rhs=PT[:, par, o0:o0 + 128],
                                                    start=(i == 0), stop=(i == 3))
                                    i += 1
                            return ins
                        self.op(self.PE, fo, Rb=[b_Vp, self.b_ones] + PTb[par], Wb=pod[1])
                        self.op(self.DVE, lambda: V.tensor_scalar(out=dn[:, par, :], in0=pod[0][:, 128:256], scalar1=self.esink[:, cidx:cidx + 1],
                                                                  scalar2=None, op0=ALU.add), Rb=pod[1] + [self.b_esink], Wb=[dnb[par]])
                        self.op(self.DVE, lambda: V.reciprocal(out=dn[:, par, :], in_=dn[:, par, :]), Rb=[dnb[par]], Wb=[dnb[par]])
                        self.op(self.DVE, lambda: V.tensor_tensor(out=oT[:, g, q0:q0 + 128], in0=pod[0][:, 0:128], in1=dn[:, par, :], op=ALU.mult),
                                Rb=pod[1] + [dnb[par]], Wb=oTb[g].get(q0, q0 + 128))
                    stage1(0)
                    for n_ in range(8):
                        if n_ + 1 < 8:
                            stage1(n_ + 1)
                        stage2(n_)
                    if DBG == 6:
                        raise _Stop()
                rk = self.take_slot(('sk', jj))
                kv = self.ring[:, rk, :].rearrange("p (b s) -> p b s", b=NSAMP)
                self.op(self.DVE, lambda: V.tensor_copy(out=kv[:, :, 0], in_=kT[:, SAMP0:T]), Rb=[b_kT, self.ringb[rk]], Wb=[self.ringb[rk]])
                pSs2 = [self.bank(6, 0, 64), self.bank(7, 0, 64)]

                def fk():
                    ins = None
                    for g in range(4):
                        for b in range(NSAMP):
                            for hd in range(2):
                                b0 = 64 * hd
                                col = g * NSAMP + b
                                ins = PE.matmul(pSs2[hd][0][:, col:col + 1], lhsT=kv[b0:b0 + 64, b, :], rhs=qs[b0:b0 + 64, g, b:b + 1],
                                                start=True, stop=True)
                    return ins
                self.op(self.PE, fk, Rb=[self.ringb[rk], b_qs], Wb=pSs2[0][1] + pSs2[1][1])

                def fes():
                    ins = None
                    for hd in range(2):
                        ins = A.activation(out=PTs[:, hd * 64:hd * 64 + 64], in_=pSs2[hd][0], func=AF.Exp, scale=sc)
                    return ins
                self.op(self.ACT, fes, Rb=pSs2[0][1] + pSs2[1][1], Wb=[b_PTs])
                rvs = [self.take_slot(('sv', jj, hf)) for hf in range(2)]
                pos_ = self.bank(5, 0, 64)
                pds_ = self.bank(4, 0, 64)
                for hf in range(2):
                    vv = self.ring[:, rvs[hf], 0:8 * 192].rearrange("p (b n) -> p b n", b=8)
                    dst = self.ring[0:1, rvs[hf], 0:8 * 192].rearrange("p (b n) -> p b n", b=8)
                    for j2 in range(2):
                        self.dma(self.SP, out=dst[:, :, 128 * j2:128 * j2 + 64], in_=vnb[8 * hf:8 * hf + 8, 64 * j2:64 * j2 + 64],
                                 Rb=[b_vnb], Wb=[self.ringb[rvs[hf]]])

                    def fv():
                        ins = None
                        for g in range(4):
                            for b8 in range(8):
                                b = 8 * hf + b8
                                for hd in range(2):
                                    col = hd * 64 + g * NSAMP + b
                                    ins = PE.matmul(pos_[0][:, g * NSAMP + b:g * NSAMP + b + 1], lhsT=vv[:, b8, 64 * hd:64 * hd + 128],
                                                    rhs=PTs[:, col:col + 1], start=(hd == 0), stop=(hd == 1))
                        return ins
                    self.op(self.PE, fv, Rb=[self.ringb[rvs[hf]], b_PTs], Wb=pos_[1])

                def fd():
                    ins = None
                    for g in range(4):
                        for hd in range(2):
                            c_ = hd * 64 + g * NSAMP
                            ins = PE.matmul(pds_[0][:, g * NSAMP:(g + 1) * NSAMP], lhsT=self.ones[:, 64 * hd:64 * hd + 128],
                                            rhs=PTs[:, c_:c_ + NSAMP], start=(hd == 0), stop=(hd == 1))
                    return ins
                self.op(self.PE, fd, Rb=[b_PTs, self.b_ones], Wb=pds_[1])

                def fdn():
                    ins = None
                    for g in range(4):
                        cidx = jj * 4 + g
                        ins = V.tensor_scalar(out=dns[:, g * NSAMP:(g + 1) * NSAMP], in0=pds_[0][:, g * NSAMP:(g + 1) * NSAMP],
                                              scalar1=self.esink[:, cidx:cidx + 1], scalar2=None, op0=ALU.add)
                    return ins
                self.op(self.DVE, fdn, Rb=pds_[1] + [self.b_esink], Wb=[b_dns])
                self.op(self.DVE, lambda: V.reciprocal(out=dns[:, :], in_=dns[:, :]), Rb=[b_dns], Wb=[b_dns])
                self.op(self.DVE, lambda: V.tensor_tensor(out=oT[:, :, SAMP0:T], in0=pos_[0].rearrange("p (g b) -> p g b", g=4),
                                                          in1=dns[:, :].rearrange("p (g b) -> p g b", g=4), op=ALU.mult),
                        Rb=pos_[1] + [b_dns], Wb=[b for s in oTb for b in s.get(SAMP0, T)])
                if DBG == 7:
                    raise _Stop()
                rs = [self.take_slot(('rowsel', 'w_o', 0, (jj, g))) for g in range(4)]
                for d in range(NCH):
                    for ci, (c0, n) in enumerate(CT_MAIN):
                        ps = self.pdslot(n) if False else self.bank(6, 0, n) if False else self.pdslot_swa(n)

                        def fn():
                            ins = None
                            for g in range(4):
                                ins = PE.matmul(ps[0], lhsT=self.ring[:, rs[g], d * 128:(d + 1) * 128], rhs=oT[:, g, c0:c0 + n],
                                                start=(g == 0), stop=(g == 3))
                            return ins
                        self.op(self.PE, fn, Rb=[self.ringb[x] for x in rs] + [b for s in oTb for b in s.get(c0, c0 + n)], Wb=ps[1])
                        self.resid_add(d, c0, n, ps, scale=1.0)
            self.dma(self.SP, out=self.o_kp.rearrange("p (j t) -> p j t", j=4), in_=kpst[:, :, :], Rb=[b_kpst])
            self.dma(self.SP, out=self.o_vp.rearrange("p (j t) -> p j t", j=4), in_=vpst[:, :, :], Rb=[b_vpst])
            self.dma(self.SP, out=self.o_ksn.rearrange("p (j t) -> p j t", j=4), in_=ksst[:, :, :], Rb=[b_ksst])
            self.dma(self.SP, out=self.o_vsn.rearrange("p (j t) -> p j t", j=4), in_=vsst[:, :, :], Rb=[b_vsst])

    def pdslot_swa(self, n):
        i = (4, 5)[self.pd_rr % 2]
        self.pd_rr += 1
        return self.bank(i, 0, n)

    def phase_final(self):
        V = self.DVE.h
        ct = CT_MAIN
        with ExitStack() as es:
            rbc, rbcb = self.norm(GV['final'], ct, es, make_h=False)
            snap = self.snapshot()
            yst = self.sb("yst", [128, 2, NOUT], F32, es)
            yb = [Buf(snap), Buf(snap)]
            gi = GV['final']
            yv = self.o_yT.rearrange("p (k t) -> p k t", k=NCH)
            for k in range(NCH):
                p = k % 2
                self.op(self.DVE, lambda: V.scalar_tensor_tensor(
                    out=yst[:, p, :], in0=self.xT[:, k, HALO:T], scalar=self.gv[:, gi, k:k + 1], in1=rbc[:, HALO:T],
                    op0=ALU.mult, op1=ALU.mult), Rb=self.xb(k, HALO, NOUT) + rbcb.get(HALO, T) + [self.b_gv], Wb=[yb[p]])
                self.dma(self.SP, out=yv[:, k, :], in_=yst[:, p, :], Rb=[yb[p]])

    def finish(self):
        SP = self.SP
        for k in self.spsem:
            if self.dval[k] > 0 and SP.known.get(k, 0) < self.dval[k]:
                SP.h.wait_ge(self.sems[k], self.dval[k])
        for E in (self.PE, self.ACT, self.DVE):
            SP.h.wait_ge(self.sems[E.key], E.cnt)
        for k in self.rsem:
            if self.dval[k] > 0:
                self.POOL.h.wait_ge(self.sems[k], self.dval[k])


NS_MAX = 668
DBG = 0


class _Stop(Exception):
    pass
_CACHE = {}


def build_program(stop=None, ns=None):
    key = (stop, ns)
    if key in _CACHE:
        return _CACHE[key]
    P = Prog()
    P.declare()
    P.NS = ns if ns is not None else NS_MAX
    P.wstream = P.dram("wstream", [P.NS, 128, 2048], F32, "ExternalInput")
    phases = [P.phase_init, P.phase_memkv, lambda: P.phase_ffn(0, 1, CT_ALL), P.phase_pool,
              lambda: P.phase_xattn(0, CT_ALL), lambda: P.phase_ffn(0, 2, CT_ALL), lambda: P.phase_ffn(1, 1, CT_ALL),
              P.phase_swa, lambda: P.phase_xattn(1, CT_MAIN), lambda: P.phase_ffn(1, 2, CT_MAIN), P.phase_final]
    for i, ph in enumerate(phases):
        if stop is not None and i >= stop:
            break
        try:
            ph()
        except _Stop:
            break
    P.finish()
    assert len(P.slots) <= P.NS or stop is not None, len(P.slots)
    _CACHE[key] = P
    return P


def _col_tile(W, cols):
    return np.ascontiguousarray(W[:, cols].reshape(NCH, 128, 128).transpose(1, 0, 2)).reshape(128, 2048)


def _qkv_cols(spec):
    kind = spec[0]
    if kind == 'k':
        _, jj, sw = spec
        base = 2048 + jj * 128
        heads = [base, base + 64]
    elif kind == 'v':
        _, jj = spec
        return np.arange(2560 + jj * 128, 2560 + jj * 128 + 128)
    else:
        _, jj, g, sw = spec
        heads = [(8 * jj + g) * 64, (8 * jj + 4 + g) * 64]
    cols = []
    for hb in heads:
        idx = np.arange(64)
        if sw:
            idx = (idx + 32) % 64
        cols.append(hb + idx)
    return np.concatenate(cols)


def _common_slots(slots, inp):
    ws = np.zeros((max(len(slots), 1), 128, 2048), np.float32)
    percore = []
    for s, d in enumerate(slots):
        kind = d[0]
        if kind == 'col':
            _, name, l, c = d
            W = inp[name][l]
            cols = _qkv_cols(c) if name == 'w_qkv' else np.arange(c, c + 128)
            ws[s] = _col_tile(W, cols)
        elif kind == 'row':
            _, name, l, r0 = d
            ws[s] = inp[name][l][r0:r0 + 128, :]
        elif kind == 'rowsel':
            _, name, l, (jj, g) = d
            rows = np.concatenate([np.arange((8 * jj + g) * 64, (8 * jj + g) * 64 + 64),
                                   np.arange((8 * jj + 4 + g) * 64, (8 * jj + 4 + g) * 64 + 64)])
            ws[s] = inp[name][l][rows, :]
        elif kind == 'pool':
            g = d[1]
            ws[s] = inp['w_pool'][0][g].reshape(4, 128, 512).transpose(1, 0, 2).reshape(128, 2048)
        else:
            percore.append((s, d))
    return ws, percore


def _fill_percore(ws, percore, inp, c):
    b0 = NSAMP * c
    for s, d in percore:
        kind = d[0]
        if kind == 'xk':
            _, l, bp = d
            a = inp['cache_mem_k'][l, b0 + 2 * bp:b0 + 2 * bp + 2]
            ws[s] = a.transpose(3, 0, 2, 1).reshape(128, 2048)
        elif kind == 'xv':
            _, l, bp = d
            a = inp['cache_mem_v'][l, b0 + 2 * bp:b0 + 2 * bp + 2]
            ws[s] = a.reshape(2, 2, 128, 512).transpose(2, 0, 1, 3).reshape(128, 2048)
        elif kind == 'sk':
            _, jj = d
            a = inp['cache_swa_k'][0, b0:b0 + NSAMP, :, 2 * jj:2 * jj + 2, :]
            ws[s] = a.transpose(2, 3, 0, 1).reshape(128, 2048)
        elif kind == 'sv':
            _, jj, hf = d
            a = inp['cache_swa_v'][0, b0 + 8 * hf:b0 + 8 * hf + 8, :, 2 * jj:2 * jj + 2, :]
            t = np.zeros((128, 8, 192), np.float32)
            t[:, :, 0:64] = a[:, :, 0, :].transpose(1, 0, 2)
            t[:, :, 128:192] = a[:, :, 1, :].transpose(1, 0, 2)
            ws[s] = 0.0
            ws[s][:, 0:8 * 192] = t.reshape(128, 8 * 192)


def _fm(a):
    return np.ascontiguousarray(a.reshape(a.shape[0], NCH, 128).transpose(2, 1, 0))


def _const_tables(c):
    pos = np.zeros(T, np.float64)
    pos[:SAMP0] = c * NMAIN - HALO + np.arange(SAMP0)
    pos[SAMP0:] = PAST
    p = np.arange(128)
    dd = p % 64
    inv = (10000.0 ** (-(dd % 32).astype(np.float32) / 32.0)).astype(np.float32)
    ang = pos.astype(np.float32)[None, :] * inv[:, None]
    cs = np.zeros((128, 2, T), np.float32)
    cs[:, 0, :] = np.cos(ang)
    sn = np.sin(ang)
    cs[:, 1, :] = np.where((dd < 32)[:, None], -sn, sn)
    ic = np.ones((4, T), np.float32)
    for g, w in enumerate(POOLW):
        cnt = np.minimum(float(w), np.maximum(pos[:SAMP0] + 1.0, 1.0))
        ic[g, :SAMP0] = (1.0 / cnt).astype(np.float32)
    icnt = np.broadcast_to(ic[None], (128, 4, T)).reshape(128, 4 * T)
    k = np.arange(128)[:, None]
    q = np.arange(128)[None, :]
    m = np.zeros((128, 2, 2, 2, 128), np.float32)
    for hd in range(2):
        m[:, 0, hd, 0, :] = (k > q)
        m[:, 0, hd, 1, :] = (k <= q)
    m[:, 1] = m[:, 0]
    if c == 0:
        m[:, 1, :, 0, :] = 0.0
    return cs.reshape(128, 2 * T), np.ascontiguousarray(icnt), m.reshape(128, 1024).astype(NPBF)


def prep_common(inp, P):
    ws_common, percore = _common_slots(P.slots, inp)
    gv = np.zeros((128, NGV, NCH), np.float32)
    vecs = {'ffn1_0': inp['g_ffn1'][0], 'mix_0': inp['g_mix'][0], 'xq_0': inp['g_xq'][0], 'ffn2_0': inp['g_ffn2'][0],
            'ffn1_1': inp['g_ffn1'][1], 'mix_1': inp['g_mix'][1], 'xq_1': inp['g_xq'][1], 'ffn2_1': inp['g_ffn2'][1],
            'final': inp['g_final'], 'mem_0': inp['g_mem'][0], 'mem_1': inp['g_mem'][1], 'pscale': inp['pool_scale'][0]}
    for name, v in vecs.items():
        gv[:, GV[name], :] = v.reshape(NCH, 128).T
    sinks = inp['sinks'][0]
    sinkT = np.zeros((128, 16), np.float32)
    for jj in range(4):
        for g in range(4):
            sinkT[0:64, jj * 4 + g] = sinks[8 * jj + g]
            sinkT[64:128, jj * 4 + g] = sinks[8 * jj + 4 + g]
    memT = _fm(inp['mem_prompt'][0]).reshape(128, NCH * 256)
    return dict(ws=ws_common, percore=percore, gv=gv, sinkT=sinkT, memT=memT)


def core_inputs(inp, cm, c):
    xp = inp['x_prompt'][0]
    xs = inp['x_sample'][:, 0, :]
    tok = np.zeros((T, D), np.float32)
    lo = c * NMAIN - HALO
    a = max(lo, 0)
    tok[a - lo:SAMP0] = xp[a:(c + 1) * NMAIN]
    tok[SAMP0:] = xs[NSAMP * c:NSAMP * (c + 1)]
    cs, icnt, mask = _const_tables(c)
    st = inp['state_pool'][0, NSAMP * c:NSAMP * (c + 1)]
    stT = np.ascontiguousarray(st.reshape(NSAMP, 15, NCH, 128).transpose(3, 2, 0, 1)).reshape(128, NCH * NSAMP * 15)
    ws = cm['ws'].copy()
    _fill_percore(ws, cm['percore'], inp, c)
    return {
        'xT': _fm(tok).reshape(128, NCH * T), 'memT': cm['memT'], 'stT': stT,
        'stN': np.ascontiguousarray(st).reshape(NSAMP, 15 * D),
        'ckN': np.ascontiguousarray(inp['cache_swa_k'][0, NSAMP * c:NSAMP * (c + 1)]).reshape(NSAMP, 128 * 512),
        'cvN': np.ascontiguousarray(inp['cache_swa_v'][0, NSAMP * c:NSAMP * (c + 1)]).reshape(NSAMP, 128 * 512),
        'gvec': cm['gv'].reshape(128, NGV * NCH), 'sinkT': cm['sinkT'], 'cs': cs, 'icnt': icnt, 'maskc': mask, 'wstream': ws,
    }


def kernel(**inp):
    inp = {k: np.asarray(v) for k, v in inp.items()}
    P = build_program()
    cm = prep_common(inp, P)
    in_maps = [core_inputs(inp, cm, c) for c in range(NCORES)]
    res = run_bass_kernel_spmd(P.nc, in_maps, core_ids=list(range(NCORES)))
    out = res.results
    f32 = np.float32
    y_prompt = np.zeros((1, 8192, D), f32)
    y_sample = np.zeros((128, 1, D), f32)
    pool_s = np.zeros((1, 128, 15, D), f32)
    k_s = np.zeros((1, 128, 128, 8, 64), f32)
    v_s = np.zeros((1, 128, 128, 8, 64), f32)
    for c in range(NCORES):
        o = out[c]
        yT = o['yT'].reshape(128, NCH, NOUT)
        yy = yT.transpose(2, 1, 0).reshape(NOUT, D)
        y_prompt[0, c * NMAIN:(c + 1) * NMAIN] = yy[:NMAIN]
        y_sample[NSAMP * c:NSAMP * (c + 1), 0] = yy[NMAIN:]
        sl = slice(NSAMP * c, NSAMP * (c + 1))
        pool_s[0, sl, 0:14] = o['pool_s_old'].reshape(NSAMP, 14, D)
        pool_s[0, sl, 14] = o['pool_s_new'].reshape(128, NCH, NSAMP).transpose(2, 1, 0).reshape(NSAMP, D)
        k_s[0, sl, 0:127] = o['swa_k_s_old'].reshape(NSAMP, 127, 8, 64)
        v_s[0, sl, 0:127] = o['swa_v_s_old'].reshape(NSAMP, 127, 8, 64)
        kn = o['swa_k_s_new'].reshape(2, 64, 4, NSAMP)
        k_s[0, sl, 127] = kn.transpose(3, 2, 0, 1).reshape(NSAMP, 8, 64)
        v_s[0, sl, 127] = o['swa_v_s_new'].reshape(NSAMP, 8, 64)
    o7 = out[NCORES - 1]
    pool_p = o7['pool_p'].reshape(128, NCH, 15).transpose(2, 1, 0).reshape(1, 1, 15, D).astype(f32)
    k_p = o7['swa_k_p'].reshape(2, 64, 4, 128).transpose(3, 2, 0, 1).reshape(1, 1, 128, 8, 64).astype(f32)
    v_p = o7['swa_v_p'].reshape(1, 1, 128, 8, 64).astype(f32)
    o0 = out[0]
    mem_k = o0['mem_k'].reshape(128, 2, 4, 256).transpose(1, 3, 2, 0).reshape(2, 1, 256, 4, 128).astype(f32)
    mem_v = o0['mem_v'].reshape(128, 2, 2, 512).transpose(1, 2, 0, 3).reshape(2, 1, 256, 4, 128).astype(f32)
    return (y_prompt, y_sample, np.ascontiguousarray(pool_p), pool_s, np.ascontiguousarray(k_p), np.ascontiguousarray(v_p),
            k_s, v_s, np.ascontiguousarray(mem_k), np.ascontiguousarray(mem_v))
```

```python
import math
from contextlib import ExitStack

import ml_dtypes
import numpy as np

import concourse.bass as bass
import concourse.mybir as mybir
from concourse.bass_utils import run_bass_kernel_spmd

F32 = mybir.dt.float32
BF16 = mybir.dt.bfloat16
AF = mybir.ActivationFunctionType
ALU = mybir.AluOpType
AX = mybir.AxisListType
NPBF = ml_dtypes.bfloat16

D = 2048
NCH = 16
DFF = 5632
NF = 44
T = 1184
HALO = 144
NMAIN = 1024
NSAMP = 16
SAMP0 = 1168
NOUT = 1040
R = 9
NCORES = 8
CT_ALL = [(0, 512), (512, 512), (1024, 160)]
CT_MAIN = [(144, 448), (592, 448), (1040, 144)]
SEGB = [0, 144, 512, 592, 1024, 1040, 1168, 1184]
GV = {'ffn1_0': 0, 'mix_0': 1, 'xq_0': 2, 'ffn2_0': 3, 'ffn1_1': 4, 'mix_1': 5, 'xq_1': 6, 'ffn2_1': 7,
      'final': 8, 'mem_0': 9, 'mem_1': 10, 'pscale': 11}
NGV = 12
EPS = 1e-6
PAST = 8192
POOLW = (2, 4, 8, 16)


class Buf:
    __slots__ = ('w', 'r', 'excl')

    def __init__(self, snap=None):
        self.w = None
        self.r = dict(snap) if snap else {}
        self.excl = False


class Seg:
    def __init__(self, bounds, snap=None):
        self.b = bounds
        self.bufs = [Buf(snap) for _ in bounds[:-1]]

    def get(self, lo, hi):
        return [self.bufs[i] for i in range(len(self.b) - 1) if self.b[i] < hi and self.b[i + 1] > lo]


class Eng:
    def __init__(self, name, h, key):
        self.name, self.h, self.key = name, h, key
        self.cnt = 0
        self.known = {}


class Prog:
    def __init__(self):
        nc = bass.Bass("TRN2", target_bir_lowering=False)
        self.nc = nc
        self.es = ExitStack()
        self.sems = []
        self.dval = {}
        self.slots = []
        self.uid = 0
        self.PE = Eng('PE', nc.tensor, self.new_sem('s_pe'))
        self.ACT = Eng('ACT', nc.scalar, self.new_sem('s_act'))
        self.DVE = Eng('DVE', nc.vector, self.new_sem('s_dve'))
        self.POOL = Eng('POOL', nc.gpsimd, None)
        self.SP = Eng('SP', nc.sync, None)
        self.rsem = [self.new_sem(f's_ring{i}') for i in range(R)]
        self.spsem = [self.new_sem(f's_sp{i}') for i in range(16)]
        self.sp_rr = 0
        self.ring_ptr = 0
        self.held = set()
        for k in self.rsem + self.spsem:
            self.dval[k] = 0

    def new_sem(self, name):
        h = self.es.enter_context(self.nc.semaphore(name))
        self.sems.append(h)
        return len(self.sems) - 1

    def dram(self, name, shape, dt, kind):
        return self.nc.dram_tensor(name, list(shape), dt, kind=kind).ap()

    def sb(self, name, shape, dt, es=None):
        self.uid += 1
        return (es or self.es).enter_context(self.nc.sbuf_tensor(f"{name}_{self.uid}", list(shape), dt))

    def snapshot(self):
        d = {}
        for E in (self.PE, self.ACT, self.DVE):
            if E.cnt > 0:
                d[E.key] = E.cnt
        for k in self.spsem:
            if self.dval[k] > 0:
                d[k] = self.dval[k]
        return d

    def _deps(self, E, Rb, Wb):
        need = {}

        def add(k, v):
            if need.get(k, 0) < v:
                need[k] = v
        for b in Rb:
            if b.w is not None:
                add(*b.w)
        for b in Wb:
            if b.w is not None:
                add(*b.w)
            for k, v in b.r.items():
                add(k, v)
        for k, v in need.items():
            if k == E.key:
                if E.name == 'PE':
                    continue
            if E.known.get(k, 0) >= v:
                continue
            E.h.wait_ge(self.sems[k], v)
            E.known[k] = v

    def _mark(self, tok, Rb, Wb):
        for b in Rb:
            if b.r.get(tok[0], 0) < tok[1]:
                b.r[tok[0]] = tok[1]
        for b in Wb:
            b.w = tok
            b.r = {}

    def op(self, E, fn, Rb=(), Wb=()):
        if E.name != 'PE':
            ex = [b for b in Rb if b.excl]
            if ex:
                Wb = list(Wb) + ex
                Rb = [b for b in Rb if not b.excl]
        self._deps(E, Rb, Wb)
        ins = fn()
        E.cnt += 1
        ins.then_inc(self.sems[E.key], 1)
        self._mark((E.key, E.cnt), Rb, Wb)

    def dma(self, Q, out, in_, Rb=(), Wb=(), semkey=None):
        self._deps(Q, Rb, Wb)
        if semkey is None:
            semkey = self.spsem[self.sp_rr % len(self.spsem)]
            self.sp_rr += 1
            if self.dval[semkey] > 0 and Q.known.get(semkey, 0) < self.dval[semkey]:
                Q.h.wait_ge(self.sems[semkey], self.dval[semkey])
                Q.known[semkey] = self.dval[semkey]
        self.dval[semkey] += 16
        Q.h.dma_start(out=out, in_=in_).then_inc(self.sems[semkey], 16)
        self._mark((semkey, self.dval[semkey]), Rb, Wb)

    def take_slot(self, desc, hold=False):
        s = len(self.slots)
        self.slots.append(desc)
        r = self.ring_ptr % R
        while r in self.held:
            r = (r + 1) % R
        self.ring_ptr = r + 1
        if hold:
            self.held.add(r)
        self.dma(self.POOL, out=self.ring[:, r, :], in_=self.wstream[s], Wb=(self.ringb[r],), semkey=self.rsem[r])
        return r

    def slot3(self, r, k):
        return self.ring[:, r, :].rearrange("p (k n) -> p k n", k=k)

    def PA(self, ci, n):
        if ci < 2:
            return self.pb[ci][:, 0:n], self.pbuf[ci].get(0, n)
        return self.pb[6][:, 0:n], self.pbuf[6].get(0, n)

    def PB(self, ci, n):
        if ci < 2:
            return self.pb[2 + ci][:, 0:n], self.pbuf[2 + ci].get(0, n)
        return self.pb[6][:, 160:160 + n], self.pbuf[6].get(0, 512)

    def bank(self, i, lo, hi):
        return self.pb[i][:, lo:hi], self.pbuf[i].get(lo, hi)

    def xb(self, k, c0, n):
        return self.xbufs[k].get(c0, c0 + n)

    def declare(self):
        nc = self.nc
        I, O = "ExternalInput", "ExternalOutput"
        self.d_xT = self.dram("xT", [128, NCH * T], F32, I)
        self.d_memT = self.dram("memT", [128, NCH * 256], F32, I)
        self.d_stT = self.dram("stT", [128, NCH * NSAMP * 15], F32, I)
        self.d_stN = self.dram("stN", [NSAMP, 15 * D], F32, I)
        self.d_ckN = self.dram("ckN", [NSAMP, 128 * 512], F32, I)
        self.d_cvN = self.dram("cvN", [NSAMP, 128 * 512], F32, I)
        self.d_gvec = self.dram("gvec", [128, NGV * NCH], F32, I)
        self.d_sinkT = self.dram("sinkT", [128, 16], F32, I)
        self.d_cs = self.dram("cs", [128, 2 * T], F32, I)
        self.d_icnt = self.dram("icnt", [128, 4 * T], F32, I)
        self.d_mask = self.dram("maskc", [128, 2 * 256], BF16, I)
        self.d_ident = self.dram("identc", [128, 128], BF16, I)
        self.NS_decl = None
        self.o_yT = self.dram("yT", [128, NCH * NOUT], F32, O)
        self.o_pool_p = self.dram("pool_p", [128, NCH * 15], F32, O)
        self.o_pool_sn = self.dram("pool_s_new", [128, NCH * NSAMP], F32, O)
        self.o_pool_so = self.dram("pool_s_old", [NSAMP, 14 * D], F32, O)
        self.o_kp = self.dram("swa_k_p", [128, 4 * 128], F32, O)
        self.o_vp = self.dram("swa_v_p", [128, 4 * 128], F32, O)
        self.o_ksn = self.dram("swa_k_s_new", [128, 4 * NSAMP], F32, O)
        self.o_vsn = self.dram("swa_v_s_new", [NSAMP, 4 * 128], F32, O)
        self.o_kso = self.dram("swa_k_s_old", [NSAMP, 127 * 512], F32, O)
        self.o_vso = self.dram("swa_v_s_old", [NSAMP, 127 * 512], F32, O)
        self.o_memk = self.dram("mem_k", [128, 2 * 4 * 256], F32, O)
        self.o_memv = self.dram("mem_v", [128, 2 * 2 * 512], F32, O)

        self.xT = self.sb("xT", [128, NCH, T], F32)
        self.hT = self.sb("hT", [128, NCH, T], BF16)
        self.ring = self.sb("ring", [128, R, 2048], BF16)
        self.KmT = self.sb("KmT", [128, 2, 4, 256], BF16)
        self.Vm = self.sb("Vm", [128, 2, 2, 512], BF16)
        self.gv = self.sb("gv", [128, NGV, NCH], F32)
        self.ones = self.sb("ones", [128, 192], BF16)
        self.onesf = self.sb("onesf", [128, 128], BF16)
        self.esink = self.sb("esink", [128, 16], F32)
        self.pb = [self.es.enter_context(nc.psum_tensor(f"pb{i}", [128, 512], F32)) for i in range(8)]
        self.pbuf = [Seg([0, 512]) for i in range(8)]
        for sg in self.pbuf:
            for b in sg.bufs:
                b.excl = True
        self.xbufs = [Seg(SEGB) for _ in range(NCH)]
        self.hTb = [Buf() for _ in range(NCH)]
        self.ringb = [Buf() for _ in range(R)]
        self.b_KmT = [Buf(), Buf()]
        self.b_Vm = [Buf(), Buf()]
        self.b_gv = Buf()
        self.b_ones = Buf()
        self.b_esink = Buf()
        self.pd_rr = 0

    def pdslot(self, n):
        i = (4, 5, 7)[self.pd_rr % 3]
        self.pd_rr += 1
        return self.bank(i, 0, n)

    def proj(self, dst, r, ctiles, src=None, srcb=None, fine=False):
        sv = self.slot3(r, 16)
        PE = self.PE.h
        src = self.hT if src is None else src
        srcb = self.hTb if srcb is None else srcb

        step = 2 if fine else NCH
        for k0 in range(0, NCH, step):
            def fn():
                ins = None
                for k in range(k0, k0 + step):
                    for ci, (c0, n) in enumerate(ctiles):
                        ins = PE.matmul(dst[ci][0], lhsT=sv[:, k, :], rhs=src[:, k, c0:c0 + n],
                                        start=(k == 0), stop=(k == NCH - 1))
                return ins
            self.op(self.PE, fn, Rb=[self.ringb[r]] + list(srcb[k0:k0 + step]), Wb=[b for d in dst for b in d[1]])

    def norm(self, gidx, ctiles, es, make_h=True):
        lo = ctiles[0][0]
        hi = ctiles[-1][0] + ctiles[-1][1]
        sq = self.sb("sq", [128, 2, T], BF16, es)
        rbc = self.sb("rbc", [128, T], F32, es)
        snap = self.snapshot()
        sqb = [Buf(snap), Buf(snap)]
        rbcb = Seg(SEGB, snap)
        A, V, PE = self.ACT.h, self.DVE.h, self.PE.h
        dst = [self.PA(ci, n) for ci, (c0, n) in enumerate(ctiles)]
        for k in range(NCH):
            p = k % 2
            if k % 2 == 0:
                self.op(self.ACT, lambda: A.activation(out=sq[:, p, lo:hi], in_=self.xT[:, k, lo:hi], func=AF.Square),
                        Rb=self.xb(k, lo, hi - lo), Wb=[sqb[p]])
            else:
                self.op(self.DVE, lambda: V.tensor_tensor(out=sq[:, p, lo:hi], in0=self.xT[:, k, lo:hi], in1=self.xT[:, k, lo:hi],
                                                          op=ALU.mult), Rb=self.xb(k, lo, hi - lo), Wb=[sqb[p]])

            def fn():
                ins = None
                for ci, (c0, n) in enumerate(ctiles):
                    ins = PE.matmul(dst[ci][0], lhsT=self.onesf[:, :], rhs=sq[:, p, c0:c0 + n],
                                    start=(k == 0), stop=(k == NCH - 1))
                return ins
            self.op(self.PE, fn, Rb=[sqb[p], self.b_ones], Wb=[b for d in dst for b in d[1]])
        for ci, (c0, n) in enumerate(ctiles):
            rb = rbcb.get(c0, c0 + n)
            self.op(self.ACT, lambda: A.activation(out=rbc[:, c0:c0 + n], in_=dst[ci][0], func=AF.Sqrt,
                                                   scale=1.0 / D, bias=EPS), Rb=dst[ci][1], Wb=rb)
            self.op(self.DVE, lambda: V.reciprocal(out=rbc[:, c0:c0 + n], in_=rbc[:, c0:c0 + n]), Rb=rb, Wb=rb)
        if make_h:
            for k in range(NCH):
                self.op(self.DVE, lambda: V.scalar_tensor_tensor(
                    out=self.hT[:, k, lo:hi], in0=self.xT[:, k, lo:hi], scalar=self.gv[:, gidx, k:k + 1],
                    in1=rbc[:, lo:hi], op0=ALU.mult, op1=ALU.mult),
                    Rb=self.xb(k, lo, hi - lo) + rbcb.get(lo, hi) + [self.b_gv], Wb=[self.hTb[k]])
        return rbc, rbcb

    def resid_add(self, d, c0, n, ps, scale=None, scale_ap=None):
        V = self.DVE.h
        xs = self.xT[:, d, c0:c0 + n]
        sc = scale_ap if scale_ap is not None else scale
        rb = list(ps[1]) + self.xb(d, c0, n) + ([self.b_gv] if scale_ap is not None else [])
        self.op(self.DVE, lambda: V.scalar_tensor_tensor(out=xs, in0=ps[0], scalar=sc, in1=xs,
                                                         op0=ALU.mult, op1=ALU.add),
                Rb=rb, Wb=self.xb(d, c0, n))

    def phase_init(self):
        V = self.DVE.h
        self.dma(self.SP, out=self.gv[:, :, :], in_=self.d_gvec.rearrange("p (g k) -> p g k", g=NGV), Wb=[self.b_gv])
        self.dma(self.SP, out=self.esink[:, :], in_=self.d_sinkT[:, :], Wb=[self.b_esink])
        def fn():
            V.memset(self.ones[:, 0:64], 1.0)
            V.memset(self.ones[:, 64:128], 0.0)
            V.memset(self.ones[:, 128:192], 1.0)
            return V.memset(self.onesf[:, :], 1.0)
        self.op(self.DVE, fn, Wb=[self.b_ones])
        self.op(self.ACT, lambda: self.ACT.h.activation(out=self.esink[:, :], in_=self.esink[:, :], func=AF.Exp),
                Rb=[self.b_esink], Wb=[self.b_esink])

    def phase_memkv(self):
        A, V, PE = self.ACT.h, self.DVE.h, self.PE.h
        with ExitStack() as es:
            snap = self.snapshot()
            memT = self.sb("memT", [128, NCH, 256], F32, es)
            hm = self.sb("hm", [128, NCH, 256], BF16, es)
            sq = self.sb("sqm", [128, 2, 256], BF16, es)
            rbc = self.sb("rbcm", [128, 256], F32, es)
            kst = self.sb("kst", [128, 2, 4, 256], F32, es)
            vst = self.sb("vst", [128, 2, 2, 512], F32, es)
            b_mem, b_rbc, b_kst, b_vst = Buf(snap), Buf(snap), Buf(snap), Buf(snap)
            sqb = [Buf(snap), Buf(snap)]
            hmb = [Buf(snap) for _ in range(NCH)]
            self.dma(self.SP, out=memT[:, :, :], in_=self.d_memT.rearrange("p (k m) -> p k m", k=NCH), Wb=[b_mem])
            xv = self.d_xT.rearrange("p (k t) -> p k t", k=NCH)
            for k in range(NCH):
                self.dma(self.SP, out=self.xT[:, k, :], in_=xv[:, k, :], Wb=self.xbufs[k].bufs)
            ps = self.bank(0, 0, 256)
            for k in range(NCH):
                p = k % 2
                self.op(self.ACT, lambda: A.activation(out=sq[:, p, :], in_=memT[:, k, :], func=AF.Square),
                        Rb=[b_mem], Wb=[sqb[p]])
                self.op(self.PE, lambda: PE.matmul(ps[0], lhsT=self.onesf[:, :], rhs=sq[:, p, :],
                                                   start=(k == 0), stop=(k == NCH - 1)),
                        Rb=[sqb[p], self.b_ones], Wb=ps[1])
            self.op(self.ACT, lambda: A.activation(out=rbc[:, :], in_=ps[0], func=AF.Sqrt, scale=1.0 / D, bias=EPS),
                    Rb=ps[1], Wb=[b_rbc])
            self.op(self.DVE, lambda: V.reciprocal(out=rbc[:, :], in_=rbc[:, :]), Rb=[b_rbc], Wb=[b_rbc])
            if DBG == 1:
                return
            for l in range(2):
                gi = GV[f'mem_{l}']
                for k in range(NCH):
                    self.op(self.DVE, lambda: V.scalar_tensor_tensor(
                        out=hm[:, k, :], in0=memT[:, k, :], scalar=self.gv[:, gi, k:k + 1], in1=rbc[:, :],
                        op0=ALU.mult, op1=ALU.mult), Rb=[b_mem, b_rbc, self.b_gv], Wb=[hmb[k]])
                for h in range(4):
                    r = self.take_slot(('col', 'w_xkv', l, h * 128))
                    sv = self.slot3(r, 16)
                    pk = self.bank(1 + (h % 2), 0, 256)

                    def fn():
                        ins = None
                        for k in range(NCH):
                            ins = PE.matmul(pk[0], lhsT=sv[:, k, :], rhs=hm[:, k, :], start=(k == 0), stop=(k == NCH - 1))
                        return ins
                    self.op(self.PE, fn, Rb=[self.ringb[r]] + hmb, Wb=pk[1])
                    self.op(self.ACT, lambda: A.activation(out=self.KmT[:, l, h, :], in_=pk[0], func=AF.Copy),
                            Rb=pk[1], Wb=[self.b_KmT[l]])
                    self.op(self.DVE, lambda: V.tensor_copy(out=kst[:, l, h, :], in_=pk[0]), Rb=pk[1], Wb=[b_kst])
                    if DBG == 2:
                        return
                pv = [self.bank(3, 0, 512), self.bank(4, 0, 512)]
                for h in range(4):
                    r = self.take_slot(('col', 'w_xkv', l, 512 + h * 128))
                    sv = self.slot3(r, 16)
                    for mb in range(2):
                        def fn():
                            ins = None
                            for k in range(NCH):
                                ins = PE.matmul(pv[mb][0][:, h * 128:(h + 1) * 128], lhsT=hm[:, k, mb * 128:(mb + 1) * 128],
                                                rhs=sv[:, k, :], start=(k == 0), stop=(k == NCH - 1))
                            return ins
                        self.op(self.PE, fn, Rb=[self.ringb[r]] + hmb, Wb=pv[mb][1])
                for mb in range(2):
                    self.op(self.ACT, lambda: A.activation(out=self.Vm[:, l, mb, :], in_=pv[mb][0], func=AF.Copy),
                            Rb=pv[mb][1], Wb=[self.b_Vm[l]])
                    self.op(self.DVE, lambda: V.tensor_copy(out=vst[:, l, mb, :], in_=pv[mb][0]), Rb=pv[mb][1], Wb=[b_vst])
            self.dma(self.SP, out=self.o_memk.rearrange("p (l h m) -> p l h m", l=2, h=4), in_=kst[:, :, :, :], Rb=[b_kst])
            self.dma(self.SP, out=self.o_memv.rearrange("p (l b n) -> p l b n", l=2, b=2), in_=vst[:, :, :, :], Rb=[b_vst])

    def phase_ffn(self, l, which, ctiles):
        A, V, PE = self.ACT.h, self.DVE.h, self.PE.h
        wgu, wdn = f'w_ffn{which}_gu', f'w_ffn{which}_dn'
        lo = ctiles[0][0]
        hi = ctiles[-1][0] + ctiles[-1][1]
        with ExitStack() as es:
            with ExitStack() as es2:
                self.norm(GV[f'ffn{which}_{l}'], ctiles, es2)
            snap = self.snapshot()
            act = self.sb("act", [128, 2, 4, T], BF16, es)
            sil = self.sb("sil", [128, T], F32, es)
            actb = [[[Buf(snap) for _ in range(3)] for _ in range(4)] for _ in range(2)]
            silb = [Buf(snap) for _ in range(3)]
            order = [2, 0, 1]
            pa = [self.PA(ci, n) for ci, (c0, n) in enumerate(ctiles)]
            pbb = [self.PB(ci, n) for ci, (c0, n) in enumerate(ctiles)]

            pending = []

            def drain(k):
                for _ in range(min(k, len(pending))):
                    pending.pop(0)()

            def gu(j):
                par, f = (j // 4) % 2, j % 4
                r = self.take_slot(('col', wgu, l, j * 128))
                self.proj(pa, r, ctiles, fine=(j == 0))

                for ci in order:
                    c0, n = ctiles[ci]
                    self.op(self.ACT, lambda: A.activation(out=sil[:, c0:c0 + n], in_=pa[ci][0], func=AF.Silu),
                            Rb=pa[ci][1], Wb=[silb[ci]])
                drain(6)
                r2 = self.take_slot(('col', wgu, l, DFF + j * 128))
                self.proj(pbb, r2, ctiles)

                for ci in order:
                    c0, n = ctiles[ci]
                    self.op(self.DVE, lambda: V.tensor_tensor(out=act[:, par, f, c0:c0 + n], in0=pbb[ci][0], in1=sil[:, c0:c0 + n],
                                                              op=ALU.mult), Rb=[silb[ci]] + pbb[ci][1], Wb=[actb[par][f][ci]])
                drain(6)

            def down_tasks(g):
                par = g % 2
                st = {}
                tasks = []
                ntask = NCH * len(ctiles)

                def mk(idx, d, ci):
                    def task():
                        if 'rs' not in st:
                            st['rs'] = [self.take_slot(('row', wdn, l, (4 * g + f) * 128), hold=True) for f in range(4)]
                        rs = st['rs']
                        c0, n = ctiles[ci]
                        ps = self.pdslot(n)

                        def fn():
                            ins = None
                            for f in range(4):
                                ins = PE.matmul(ps[0], lhsT=self.ring[:, rs[f], d * 128:(d + 1) * 128],
                                                rhs=act[:, par, f, c0:c0 + n], start=(f == 0), stop=(f == 3))
                            return ins
                        self.op(self.PE, fn, Rb=[self.ringb[x] for x in rs] + [actb[par][f][ci] for f in range(4)], Wb=ps[1])
                        self.resid_add(d, c0, n, ps, scale=0.5)
                        if idx == ntask - 1:
                            for x in rs:
                                self.held.discard(x)
                    return task
                i = 0
                for d in range(NCH):
                    for ci in range(len(ctiles)):
                        tasks.append(mk(i, d, ci))
                        i += 1
                return tasks
            NG = NF // 4
            for g in range(NG):
                for f in range(4):
                    gu(4 * g + f)
                drain(len(pending))
                pending.extend(down_tasks(g))
            drain(len(pending))

    def phase_pool(self):
        A, V, PE = self.ACT.h, self.DVE.h, self.PE.h
        ct = CT_ALL
        with ExitStack() as es:
            so = self.o_pool_so.rearrange("b (r d) -> b r d", r=14)
            si = self.d_stN.rearrange("b (r d) -> b r d", r=15)
            for r0 in range(0, 14, 4):
                r1 = min(r0 + 4, 14)
                self.dma(self.SP, out=so[:, r0:r1, :], in_=si[:, 1 + r0:1 + r1, :])
            for (o_, i_) in ((self.o_kso, self.d_ckN), (self.o_vso, self.d_cvN)):
                ov = o_.rearrange("b (r d) -> b r d", r=127)
                iv = i_.rearrange("b (r d) -> b r d", r=128)
                for r0 in range(0, 127, 16):
                    r1 = min(r0 + 16, 127)
                    self.dma(self.SP, out=ov[:, r0:r1, :], in_=iv[:, 1 + r0:1 + r1, :])

            rbc, rbcb = self.norm(GV['mix_0'], ct, es, make_h=False)
            snap = self.snapshot()
            tA = self.sb("tA", [128, T], F32, es)
            tB = self.sb("tB", [128, T], F32, es)
            tC = self.sb("tC", [128, T], F32, es)
            icn = self.sb("icn", [128, T], F32, es)
            stk = self.sb("stk", [128, 2, NSAMP, 15], F32, es)
            ssum = self.sb("ssum", [128, NSAMP], F32, es)
            pst = self.sb("pst", [128, NCH, 15], F32, es)
            sst = self.sb("sst", [128, NCH, NSAMP], F32, es)
            bA, bB, bC, b_icn, b_ss, b_pst, b_sst = (Buf(snap) for _ in range(7))
            b_stk = [Buf(snap), Buf(snap)]
            gi = GV['mix_0']
            stT = self.d_stT.rearrange("p (k b r) -> p k b r", k=NCH, b=NSAMP)
            icv = self.d_icnt.rearrange("p (g t) -> p g t", g=4)
            rall = rbcb.get(0, T)
            for k in range(NCH):
                g = k // 4
                w = POOLW[g]
                if k % 4 == 0:
                    self.dma(self.SP, out=icn[:, :], in_=icv[:, g, :], Wb=[b_icn])
                sp_ = k % 2
                self.dma(self.SP, out=stk[:, sp_, :, :], in_=stT[:, k, :, :], Wb=[b_stk[sp_]])
                self.op(self.DVE, lambda: V.scalar_tensor_tensor(
                    out=tA[:, :], in0=self.xT[:, k, :], scalar=self.gv[:, gi, k:k + 1], in1=rbc[:, :],
                    op0=ALU.mult, op1=ALU.mult), Rb=self.xb(k, 0, T) + rall + [self.b_gv], Wb=[bA])
                cur, curb = tA, bA
                oth = [(tB, bB), (tC, bC)]
                st, oi = 1, 0
                while st < w:
                    nt, nb = oth[oi]
                    lo_ = 2 * st - 1
                    self.op(self.DVE, lambda: V.tensor_tensor(out=nt[:, lo_:T], in0=cur[:, lo_:T], in1=cur[:, lo_ - st:T - st],
                                                              op=ALU.add), Rb=[curb], Wb=[nb])
                    cur, curb = nt, nb
                    oi ^= 1
                    st *= 2
                nt, nb = oth[oi]
                self.op(self.DVE, lambda: V.tensor_tensor(out=nt[:, 15:SAMP0], in0=cur[:, 15:SAMP0], in1=icn[:, 15:SAMP0],
                                                          op=ALU.mult), Rb=[curb, b_icn], Wb=[nb])
                self.op(self.DVE, lambda: V.tensor_tensor(out=self.hT[:, k, 15:SAMP0], in0=nt[:, 15:SAMP0], in1=tA[:, 15:SAMP0],
                                                          op=ALU.subtract), Rb=[nb, bA], Wb=[self.hTb[k]])
                self.op(self.DVE, lambda: V.tensor_reduce(out=ssum[:, :], in_=stk[:, sp_, :, 15 - (w - 1):15], axis=AX.X, op=ALU.add),
                        Rb=[b_stk[sp_]], Wb=[b_ss])
                self.op(self.DVE, lambda: V.tensor_tensor(out=ssum[:, :], in0=ssum[:, :], in1=tA[:, SAMP0:T], op=ALU.add),
                        Rb=[b_ss, bA], Wb=[b_ss])
                self.op(self.DVE, lambda: V.scalar_tensor_tensor(out=self.hT[:, k, SAMP0:T], in0=ssum[:, :], scalar=1.0 / w,
                                                                 in1=tA[:, SAMP0:T], op0=ALU.mult, op1=ALU.subtract),
                        Rb=[b_ss, bA], Wb=[self.hTb[k]])
                self.op(self.ACT, lambda: A.activation(out=pst[:, k, :], in_=tA[:, SAMP0 - 15:SAMP0], func=AF.Copy), Rb=[bA], Wb=[b_pst])
                self.op(self.ACT, lambda: A.activation(out=sst[:, k, :], in_=tA[:, SAMP0:T], func=AF.Copy), Rb=[bA], Wb=[b_sst])
            self.dma(self.SP, out=self.o_pool_p.rearrange("p (k r) -> p k r", k=NCH), in_=pst[:, :, :], Rb=[b_pst])
            self.dma(self.SP, out=self.o_pool_sn.rearrange("p (k b) -> p k b", k=NCH), in_=sst[:, :, :], Rb=[b_sst])
            gs = GV['pscale']
            for g in range(4):
                r = self.take_slot(('pool', g))
                sv = self.slot3(r, 4)
                for oc in range(4):
                    d = 4 * g + oc
                    for ci, (c0, n) in enumerate(ct):
                        ps = self.pdslot(n)

                        def fn():
                            ins = None
                            for kc in range(4):
                                ins = PE.matmul(ps[0], lhsT=sv[:, kc, oc * 128:(oc + 1) * 128], rhs=self.hT[:, 4 * g + kc, c0:c0 + n],
                                                start=(kc == 0), stop=(kc == 3))
                            return ins
                        self.op(self.PE, fn, Rb=[self.ringb[r]] + self.hTb[4 * g:4 * g + 4], Wb=ps[1])
                        self.resid_add(d, c0, n, ps, scale_ap=self.gv[:, gs, d:d + 1])

    def phase_xattn(self, l, ctiles):
        A, V, PE = self.ACT.h, self.DVE.h, self.PE.h
        sc = 1.0 / math.sqrt(128.0)
        with ExitStack() as es:
            with ExitStack() as es2:
                self.norm(GV[f'xq_{l}'], ctiles, es2)
            snap = self.snapshot()
            qT = self.sb("qT", [128, 4, T], BF16, es)
            oT = self.sb("oT", [128, 4, T], BF16, es)
            PT = self.sb("PT", [128, 2, 2, 512], BF16, es)
            rec = self.sb("rec", [128, 2, 512], F32, es)
            PTs = self.sb("PTs", [128, 128], BF16, es)
            recs = self.sb("recs", [128, 64], F32, es)
            qTb = [Buf(snap) for _ in range(4)]
            oTb = [Seg(SEGB, snap) for _ in range(4)]
            PTb = [[Buf(snap), Buf(snap)], [Buf(snap), Buf(snap)]]
            recb = [Buf(snap), Buf(snap)]
            b_PTs, b_recs = Buf(snap), Buf(snap)
            for h in range(4):
                r = self.take_slot(('col', 'w_xq', l, h * 128))
                dst = [(self.PA if h % 2 == 0 else self.PB)(ci, n) for ci, (c0, n) in enumerate(ctiles)]
                self.proj(dst, r, ctiles, fine=(h == 0))

                def fa():
                    ins = None
                    for ci, (c0, n) in enumerate(ctiles):
                        ins = A.activation(out=qT[:, h, c0:c0 + n], in_=dst[ci][0], func=AF.Copy)
                    return ins
                self.op(self.ACT, fa, Rb=[b for d in dst for b in d[1]], Wb=[qTb[h]])
            units = [(h, c0, n) for h in range(4) for (c0, n) in ctiles if c0 < SAMP0]

            def xs1(u):
                h, c0, n = units[u]
                par = u % 2
                pS = [self.bank(2 * par, 0, n), self.bank(2 * par + 1, 0, n)]

                def fs():
                    ins = None
                    for mb in range(2):
                        ins = PE.matmul(pS[mb][0], lhsT=self.KmT[:, l, h, mb * 128:(mb + 1) * 128], rhs=qT[:, h, c0:c0 + n],
                                        start=True, stop=True)
                    return ins
                self.op(self.PE, fs, Rb=[self.b_KmT[l], qTb[h]], Wb=pS[0][1] + pS[1][1])
                for mb in range(2):
                    self.op(self.ACT, lambda: A.activation(out=PT[:, par, mb, 0:n], in_=pS[mb][0], func=AF.Exp, scale=sc),
                            Rb=pS[mb][1], Wb=[PTb[par][mb]])

            def xs2(u):
                h, c0, n = units[u]
                par = u % 2
                po = self.bank(4 + par, 0, n)
                pd = self.bank(7 - par, 0, n)

                def fo():
                    ins = None
                    for mb in range(2):
                        ins = PE.matmul(po[0], lhsT=self.Vm[:, l, mb, h * 128:(h + 1) * 128], rhs=PT[:, par, mb, 0:n],
                                        start=(mb == 0), stop=(mb == 1))
                    for mb in range(2):
                        ins = PE.matmul(pd[0], lhsT=self.onesf[:, :], rhs=PT[:, par, mb, 0:n], start=(mb == 0), stop=(mb == 1))
                    return ins
                self.op(self.PE, fo, Rb=[self.b_Vm[l], self.b_ones] + PTb[par], Wb=po[1] + pd[1])
                self.op(self.DVE, lambda: V.reciprocal(out=rec[:, par, 0:n], in_=pd[0]), Rb=pd[1], Wb=[recb[par]])
                self.op(self.DVE, lambda: V.tensor_tensor(out=oT[:, h, c0:c0 + n], in0=po[0], in1=rec[:, par, 0:n], op=ALU.mult),
                        Rb=po[1] + [recb[par]], Wb=oTb[h].get(c0, c0 + n))
            xs1(0)
            for u in range(len(units)):
                if u + 1 < len(units):
                    xs1(u + 1)
                xs2(u)
            pSs = self.bank(0, 0, 128)
            krs = [self.take_slot(('xk', l, bp)) for bp in range(8)]
            for bp in range(8):
                kv = self.ring[:, krs[bp], :].rearrange("p (b h m) -> p b h m", b=2, h=4)

                def fk():
                    ins = None
                    for bb in range(2):
                        b = 2 * bp + bb
                        for h in range(4):
                            for mb in range(2):
                                col = mb * 64 + h * 16 + b
                                ins = PE.matmul(pSs[0][:, col:col + 1], lhsT=kv[:, bb, h, mb * 128:(mb + 1) * 128],
                                                rhs=qT[:, h, SAMP0 + b:SAMP0 + b + 1], start=True, stop=True)
                    return ins
                self.op(self.PE, fk, Rb=[self.ringb[krs[bp]]] + qTb, Wb=pSs[1])
            self.op(self.ACT, lambda: A.activation(out=PTs[:, :], in_=pSs[0], func=AF.Exp, scale=sc), Rb=pSs[1], Wb=[b_PTs])
            pos_ = self.bank(1, 0, 64)
            pds_ = self.bank(2, 0, 64)
            vrs = [self.take_slot(('xv', l, bp)) for bp in range(8)]
            for bp in range(8):
                vv = self.ring[:, vrs[bp], :].rearrange("p (b m n) -> p b m n", b=2, m=2)

                def fv():
                    ins = None
                    for bb in range(2):
                        b = 2 * bp + bb
                        for h in range(4):
                            for mb in range(2):
                                col = mb * 64 + h * 16 + b
                                ins = PE.matmul(pos_[0][:, h * 16 + b:h * 16 + b + 1], lhsT=vv[:, bb, mb, h * 128:(h + 1) * 128],
                                                rhs=PTs[:, col:col + 1], start=(mb == 0), stop=(mb == 1))
                    return ins
                self.op(self.PE, fv, Rb=[self.ringb[vrs[bp]], b_PTs], Wb=pos_[1])

            def fd():
                ins = None
                for mb in range(2):
                    ins = PE.matmul(pds_[0], lhsT=self.onesf[:, :], rhs=PTs[:, mb * 64:(mb + 1) * 64], start=(mb == 0), stop=(mb == 1))
                return ins
            self.op(self.PE, fd, Rb=[b_PTs, self.b_ones], Wb=pds_[1])
            self.op(self.DVE, lambda: V.reciprocal(out=recs[:, :], in_=pds_[0]), Rb=pds_[1], Wb=[b_recs])
            self.op(self.DVE, lambda: V.tensor_tensor(out=oT[:, :, SAMP0:T], in0=pos_[0].rearrange("p (h b) -> p h b", h=4),
                                                      in1=recs[:, :].rearrange("p (h b) -> p h b", h=4), op=ALU.mult),
                    Rb=pos_[1] + [b_recs], Wb=[b for s in oTb for b in s.get(SAMP0, T)])
            rs = [self.take_slot(('row', 'w_xo', l, h * 128)) for h in range(4)]
            for d in range(NCH):
                for ci, (c0, n) in enumerate(ctiles):
                    ps = self.pdslot(n)

                    def fn():
                        ins = None
                        for h in range(4):
                            ins = PE.matmul(ps[0], lhsT=self.ring[:, rs[h], d * 128:(d + 1) * 128], rhs=oT[:, h, c0:c0 + n],
                                            start=(h == 0), stop=(h == 3))
                        return ins
                    self.op(self.PE, fn, Rb=[self.ringb[x] for x in rs] + [b for s in oTb for b in s.get(c0, c0 + n)], Wb=ps[1])
                    self.resid_add(d, c0, n, ps, scale=1.0)

    def phase_swa(self):
        A, V, PE = self.ACT.h, self.DVE.h, self.PE.h
        ct = CT_ALL
        sc = 0.125
        LAST0 = SAMP0 - 128
        with ExitStack() as es:
            with ExitStack() as es2:
                self.norm(GV['mix_1'], ct, es2)
            snap = self.snapshot()
            cs = self.sb("cs", [128, 2, T], F32, es)
            msk = self.sb("msk", [128, 2, 256], BF16, es)
            idn = self.sb("idn", [128, 128], BF16, es)
            kT = self.sb("kT", [128, T], BF16, es)
            Vp = self.sb("Vp", [128, 9, 192], BF16, es)
            qc = self.sb("qc", [128, 2, T], BF16, es)
            qs = self.sb("qs", [128, 4, NSAMP], BF16, es)
            oT = self.sb("oTs", [128, 4, T], BF16, es)
            r1 = self.sb("r1", [128, 2, 512], F32, es)
            r2 = self.sb("r2", [128, 2, 512], F32, es)
            PT = self.sb("PTw", [128, 2, 512], BF16, es)
            dn = self.sb("dn", [128, 2, 128], F32, es)
            PTs = self.sb("PTss", [128, 128], BF16, es)
            dns = self.sb("dns", [128, 64], F32, es)
            kpst = self.sb("kpst", [128, 4, 128], F32, es)
            vpst = self.sb("vpst", [128, 4, 128], F32, es)
            ksst = self.sb("ksst", [128, 4, NSAMP], F32, es)
            vsst = self.sb("vsst", [NSAMP, 4, 128], F32, es)
            vnb = self.sb("vnb", [NSAMP, 128], BF16, es)
            b_cs, b_msk, b_kT, b_Vp, b_qs, b_PTs, b_dns, b_kpst, b_vpst, b_ksst, b_vsst, b_vnb = (Buf(snap) for _ in range(12))
            qcb = [Buf(snap), Buf(snap)]
            oTb = [Seg(SEGB, snap) for _ in range(4)]
            r1b = [Buf(snap), Buf(snap)]
            r2b = [Buf(snap), Buf(snap)]
            PTb = [[Buf(snap), Buf(snap)], [Buf(snap), Buf(snap)]]
            dnb = [Buf(snap), Buf(snap)]
            self.dma(self.SP, out=cs[:, :, :], in_=self.d_cs.rearrange("p (a t) -> p a t", a=2), Wb=[b_cs])
            self.dma(self.SP, out=msk[:, :, :], in_=self.d_mask.rearrange("p (a t) -> p a t", a=2), Wb=[b_msk])
            self.dma(self.SP, out=idn[:, :], in_=self.d_ident[:, :], Wb=[b_msk])

            def fz():
                V.memset(Vp[:, :, :], 0.0)
                return V.memset(oT[:, :, :], 0.0)
            self.op(self.DVE, fz, Wb=[b_Vp] + [b for s in oTb for b in s.bufs])
            if DBG == 3:
                raise _Stop()
            self.rope_n = 0

            def rope(pn, ps_, outs):
                for ci, (c0, n) in enumerate(ct):
                    p = self.rope_n % 2
                    self.rope_n += 1
                    self.op(self.DVE, lambda: V.tensor_tensor(out=r1[:, p, 0:n], in0=pn[ci][0], in1=cs[:, 0, c0:c0 + n], op=ALU.mult),
                            Rb=pn[ci][1] + [b_cs], Wb=[r1b[p]])
                    self.op(self.DVE, lambda: V.tensor_tensor(out=r2[:, p, 0:n], in0=ps_[ci][0], in1=cs[:, 1, c0:c0 + n], op=ALU.mult),
                            Rb=ps_[ci][1] + [b_cs], Wb=[r2b[p]])
                    self.op(self.DVE, lambda: V.tensor_tensor(out=r1[:, p, 0:n], in0=r1[:, p, 0:n], in1=r2[:, p, 0:n], op=ALU.add),
                            Rb=[r1b[p], r2b[p]], Wb=[r1b[p]])
                    for (dst, bufs, lo_, hi_) in outs:
                        a, b = max(lo_, c0), min(hi_, c0 + n)
                        if a >= b:
                            continue
                        self.op(self.ACT, lambda: A.activation(out=dst(a, b), in_=r1[:, p, a - c0:b - c0], func=AF.Copy),
                                Rb=[r1b[p]], Wb=bufs)

            pa = [self.PA(ci, n) for ci, (c0, n) in enumerate(ct)]
            pbb = [self.PB(ci, n) for ci, (c0, n) in enumerate(ct)]
            unit = 0
            for jj in range(4):
                r = self.take_slot(('col', 'w_qkv', 0, ('k', jj, 0)))
                self.proj(pa, r, ct, fine=(jj == 0))
                r = self.take_slot(('col', 'w_qkv', 0, ('k', jj, 1)))
                self.proj(pbb, r, ct)
                rope(pa, pbb, [
                    (lambda a, b: kT[:, a:b], [b_kT], 0, T),
                    (lambda a, b: kpst[:, jj, a - LAST0:b - LAST0], [b_kpst], LAST0, SAMP0),
                    (lambda a, b: ksst[:, jj, a - SAMP0:b - SAMP0], [b_ksst], SAMP0, T),
                ])
                if DBG == 4:
                    raise _Stop()
                r = self.take_slot(('col', 'w_qkv', 0, ('v', jj)))
                sv = self.slot3(r, 16)
                pv = [self.bank(4, 0, 512), self.bank(5, 0, 512), self.bank(7, 0, 256)]

                def vdst(blk):
                    return pv[blk // 4][0][:, (blk % 4) * 128:(blk % 4) * 128 + 128]
                for blk in range(10):
                    c0 = 16 + 128 * blk if blk < 9 else SAMP0
                    m = 128 if blk < 9 else NSAMP

                    def fn():
                        ins = None
                        for k in range(NCH):
                            ins = PE.matmul(vdst(blk)[0:m, :], lhsT=self.hT[:, k, c0:c0 + m], rhs=sv[:, k, :],
                                            start=(k == 0), stop=(k == NCH - 1))
                        return ins
                    self.op(self.PE, fn, Rb=[self.ringb[r]] + self.hTb, Wb=pv[blk // 4][1])
                for grp in range(3):
                    nb = 4 if grp < 2 else 1
                    src = pv[grp][0][:, 0:nb * 128].rearrange("p (b j d) -> p b j d", b=nb, j=2)
                    dstv = Vp[:, 4 * grp:4 * grp + nb, :].rearrange("p b (j d) -> p b j d", j=3)[:, :, 0:3:2, :]
                    self.op(self.ACT, lambda: A.activation(out=dstv, in_=src, func=AF.Copy), Rb=pv[grp][1], Wb=[b_Vp])
                self.op(self.DVE, lambda: V.tensor_copy(out=vpst[:, jj, :], in_=vdst(8)), Rb=pv[2][1], Wb=[b_vpst])
                self.op(self.DVE, lambda: V.tensor_copy(out=vsst[:, jj, :], in_=vdst(9)[0:NSAMP, :]), Rb=pv[2][1], Wb=[b_vsst])
                self.op(self.DVE, lambda: V.tensor_copy(out=vnb[:, :], in_=vdst(9)[0:NSAMP, :]), Rb=pv[2][1], Wb=[b_vnb])
                if DBG == 5:
                    raise _Stop()
                for g in range(4):
                    qp = g % 2
                    r = self.take_slot(('col', 'w_qkv', 0, ('q', jj, g, 0)))
                    self.proj(pa, r, ct)
                    r = self.take_slot(('col', 'w_qkv', 0, ('q', jj, g, 1)))
                    self.proj(pbb, r, ct)
                    rope(pa, pbb, [
                        (lambda a, b: qc[:, qp, a:b], [qcb[qp]], 0, T),
                        (lambda a, b: qs[:, g, a - SAMP0:b - SAMP0], [b_qs], SAMP0, T),
                    ])
                    if DBG == 61:
                        raise _Stop()
                    cidx = jj * 4 + g

                    def stage1(n_):
                        par = n_ % 2
                        q0 = HALO + 128 * n_
                        pS2 = [self.bank(2 * par, 0, 256), self.bank(2 * par + 1, 0, 256)]
                        mv = 1 if n_ == 0 else 0

                        def fs():
                            ins = None
                            for hd in range(2):
                                ins = PE.matmul(pS2[hd][0], lhsT=idn[:, :], rhs=msk[:, mv, :], start=True, stop=False)
                            for kb in range(2):
                                for hd in range(2):
                                    b0 = 64 * hd
                                    k0 = 16 + 128 * (n_ + kb)
                                    ins = PE.matmul(pS2[hd][0][:, kb * 128:kb * 128 + 128], lhsT=kT[b0:b0 + 64, k0:k0 + 128],
                                                    rhs=qc[b0:b0 + 64, qp, q0:q0 + 128], start=False, stop=(kb == 1))
                            return ins
                        self.op(self.PE, fs, Rb=[b_kT, qcb[qp], b_msk], Wb=pS2[0][1] + pS2[1][1])
                        for hd in range(2):
                            self.op(self.ACT, lambda: A.activation(out=PT[:, par, hd * 256:hd * 256 + 256], in_=pS2[hd][0], func=AF.Exp, scale=sc),
                                    Rb=pS2[hd][1], Wb=[PTb[par][hd]])

                    def stage2(n_):
                        par = n_ % 2
                        q0 = HALO + 128 * n_
                        pod = self.bank(4 + par, 0, 256)

                        def fo():
                            ins = None
                            i = 0
                            for hd in range(2):
                                for kb in range(2):
                                    o0 = (hd * 2 + kb) * 128
                                    ins = PE.matmul(pod[0][:, 0:128], lhsT=Vp[:, n_ + kb, 64 * hd:64 * hd + 128], rhs=PT[:, par, o0:o0 + 128],
                                                    start=(i == 0), stop=(i == 3))
                                    i += 1
                            i = 0
                            for hd in range(2):
                                for kb in range(2):
                                    o0 = (hd * 2 + kb) * 128
                                    ins = PE.matmul(pod[0][:, 128:256], lhsT=self.ones[:, 64 * hd:64 * hd + 128], rhs=PT[:, par, o0:o0 + 128],
                                                    start=(i == 0), stop=(i == 3))
                                    i += 1
                            return ins
                        self.op(self.PE, fo, Rb=[b_Vp, self.b_ones] + PTb[par], Wb=pod[1])
                        self.op(self.DVE, lambda: V.tensor_scalar(out=dn[:, par, :], in0=pod[0][:, 128:256], scalar1=self.esink[:, cidx:cidx + 1],
                                                                  scalar2=None, op0=ALU.add), Rb=pod[1] + [self.b_esink], Wb=[dnb[par]])
                        self.op(self.DVE, lambda: V.reciprocal(out=dn[:, par, :], in_=dn[:, par, :]), Rb=[dnb[par]], Wb=[dnb[par]])
                        self.op(self.DVE, lambda: V.tensor_tensor(out=oT[:, g, q0:q0 + 128], in0=pod[0][:, 0:128], in1=dn[:, par, :], op=ALU.mult),
                                Rb=pod[1] + [dnb[par]], Wb=oTb[g].get(q0, q0 + 128))
                    stage1(0)
                    for n_ in range(8):
                        if n_ + 1 < 8:
                            stage1(n_ + 1)
                        stage2(n_)
                    if DBG == 6:
                        raise _Stop()
                rk = self.take_slot(('sk', jj))
                kv = self.ring[:, rk, :].rearrange("p (b s) -> p b s", b=NSAMP)
                self.op(self.DVE, lambda: V.tensor_copy(out=kv[:, :, 0], in_=kT[:, SAMP0:T]), Rb=[b_kT, self.ringb[rk]], Wb=[self.ringb[rk]])
                pSs2 = [self.bank(6, 0, 64), self.bank(7, 0, 64)]

                def fk():
                    ins = None
                    for g in range(4):
                        for b in range(NSAMP):
                            for hd in range(2):
                                b0 = 64 * hd
                                col = g * NSAMP + b
                                ins = PE.matmul(pSs2[hd][0][:, col:col + 1], lhsT=kv[b0:b0 + 64, b, :], rhs=qs[b0:b0 + 64, g, b:b + 1],
                                                start=True, stop=True)
                    return ins
                self.op(self.PE, fk, Rb=[self.ringb[rk], b_qs], Wb=pSs2[0][1] + pSs2[1][1])

                def fes():
                    ins = None
                    for hd in range(2):
                        ins = A.activation(out=PTs[:, hd * 64:hd * 64 + 64], in_=pSs2[hd][0], func=AF.Exp, scale=sc)
                    return ins
                self.op(self.ACT, fes, Rb=pSs2[0][1] + pSs2[1][1], Wb=[b_PTs])
                rvs = [self.take_slot(('sv', jj, hf)) for hf in range(2)]
                pos_ = self.bank(5, 0, 64)
                pds_ = self.bank(4, 0, 64)
                for hf in range(2):
                    vv = self.ring[:, rvs[hf], 0:8 * 192].rearrange("p (b n) -> p b n", b=8)
                    dst = self.ring[0:1, rvs[hf], 0:8 * 192].rearrange("p (b n) -> p b n", b=8)
                    for j2 in range(2):
                        self.dma(self.SP, out=dst[:, :, 128 * j2:128 * j2 + 64], in_=vnb[8 * hf:8 * hf + 8, 64 * j2:64 * j2 + 64],
                                 Rb=[b_vnb], Wb=[self.ringb[rvs[hf]]])

                    def fv():
                        ins = None
                        for g in range(4):
                            for b8 in range(8):
                                b = 8 * hf + b8
                                for hd in range(2):
                                    col = hd * 64 + g * NSAMP + b
                                    ins = PE.matmul(pos_[0][:, g * NSAMP + b:g * NSAMP + b + 1], lhsT=vv[:, b8, 64 * hd:64 * hd + 128],
                                                    rhs=PTs[:, col:col + 1], start=(hd == 0), stop=(hd == 1))
                        return ins
                    self.op(self.PE, fv, Rb=[self.ringb[rvs[hf]], b_PTs], Wb=pos_[1])

                def fd():
                    ins = None
                    for g in range(4):
                        for hd in range(2):
                            c_ = hd * 64 + g * NSAMP
                            ins = PE.matmul(pds_[0][:, g * NSAMP:(g + 1) * NSAMP], lhsT=self.ones[:, 64 * hd:64 * hd + 128],
                                            rhs=PTs[:, c_:c_ + NSAMP], start=(hd == 0), stop=(hd == 1))
                    return ins
                self.op(self.PE, fd, Rb=[b_PTs, self.b_ones], Wb=pds_[1])

                def fdn():
                    ins = None
                    for g in range(4):
                        cidx = jj * 4 + g
                        ins = V.tensor_scalar(out=dns[:, g * NSAMP:(g + 1) * NSAMP], in0=pds_[0][:, g * NSAMP:(g + 1) * NSAMP],
                                              scalar1=self.esink[:, cidx:cidx + 1], scalar2=None, op0=ALU.add)
                    return ins
                self.op(self.DVE, fdn, Rb=pds_[1] + [self.b_esink], Wb=[b_dns])
                self.op(self.DVE, lambda: V.reciprocal(out=dns[:, :], in_=dns[:, :]), Rb=[b_dns], Wb=[b_dns])
                self.op(self.DVE, lambda: V.tensor_tensor(out=oT[:, :, SAMP0:T], in0=pos_[0].rearrange("p (g b) -> p g b", g=4),
                                                          in1=dns[:, :].rearrange("p (g b) -> p g b", g=4), op=ALU.mult),
                        Rb=pos_[1] + [b_dns], Wb=[b for s in oTb for b in s.get(SAMP0, T)])
                if DBG == 7:
                    raise _Stop()
                rs = [self.take_slot(('rowsel', 'w_o', 0, (jj, g))) for g in range(4)]
                for d in range(NCH):
                    for ci, (c0, n) in enumerate(CT_MAIN):
                        ps = self.pdslot(n) if False else self.bank(6, 0, n) if False else self.pdslot_swa(n)

                        def fn():
                            ins = None
                            for g in range(4):
                                ins = PE.matmul(ps[0], lhsT=self.ring[:, rs[g], d * 128:(d + 1) * 128], rhs=oT[:, g, c0:c0 + n],
                                                start=(g == 0), stop=(g == 3))
                            return ins
                        self.op(self.PE, fn, Rb=[self.ringb[x] for x in rs] + [b for s in oTb for b in s.get(c0, c0 + n)], Wb=ps[1])
                        self.resid_add(d, c0, n, ps, scale=1.0)
            self.dma(self.SP, out=self.o_kp.rearrange("p (j t) -> p j t", j=4), in_=kpst[:, :, :], Rb=[b_kpst])
            self.dma(self.SP, out=self.o_vp.rearrange("p (j t) -> p j t", j=4), in_=vpst[:, :, :], Rb=[b_vpst])
            self.dma(self.SP, out=self.o_ksn.rearrange("p (j t) -> p j t", j=4), in_=ksst[:, :, :], Rb=[b_ksst])
            self.dma(self.SP, out=self.o_vsn.rearrange("p (j t) -> p j t", j=4), in_=vsst[:, :, :], Rb=[b_vsst])

    def pdslot_swa(self, n):
        i = (4, 5)[self.pd_rr % 2]
        self.pd_rr += 1
        return self.bank(i, 0, n)

    def phase_final(self):
        V = self.DVE.h
        ct = CT_MAIN
        with ExitStack() as es:
            rbc, rbcb = self.norm(GV['final'], ct, es, make_h=False)
            snap = self.snapshot()
            yst = self.sb("yst", [128, 2, NOUT], F32, es)
            yb = [Buf(snap), Buf(snap)]
            gi = GV['final']
            yv = self.o_yT.rearrange("p (k t) -> p k t", k=NCH)
            for k in range(NCH):
                p = k % 2
                self.op(self.DVE, lambda: V.scalar_tensor_tensor(
                    out=yst[:, p, :], in0=self.xT[:, k, HALO:T], scalar=self.gv[:, gi, k:k + 1], in1=rbc[:, HALO:T],
                    op0=ALU.mult, op1=ALU.mult), Rb=self.xb(k, HALO, NOUT) + rbcb.get(HALO, T) + [self.b_gv], Wb=[yb[p]])
                self.dma(self.SP, out=yv[:, k, :], in_=yst[:, p, :], Rb=[yb[p]])

    def finish(self):
        SP = self.SP
        for k in self.spsem:
            if self.dval[k] > 0 and SP.known.get(k, 0) < self.dval[k]:
                SP.h.wait_ge(self.sems[k], self.dval[k])
        for E in (self.PE, self.ACT, self.DVE):
            SP.h.wait_ge(self.sems[E.key], E.cnt)
        for k in self.rsem:
            if self.dval[k] > 0:
                self.POOL.h.wait_ge(self.sems[k], self.dval[k])


NS_MAX = 668
DBG = 0


class _Stop(Exception):
    pass
_CACHE = {}


def build_program(stop=None, ns=None):
    key = (stop, ns)
    if key in _CACHE:
        return _CACHE[key]
    P = Prog()
    P.declare()
    P.NS = ns if ns is not None else NS_MAX
    P.wstream = P.dram("wstream", [P.NS, 128, 2048], F32, "ExternalInput")
    phases = [P.phase_init, P.phase_memkv, lambda: P.phase_ffn(0, 1, CT_ALL), P.phase_pool,
              lambda: P.phase_xattn(0, CT_ALL), lambda: P.phase_ffn(0, 2, CT_ALL), lambda: P.phase_ffn(1, 1, CT_ALL),
              P.phase_swa, lambda: P.phase_xattn(1, CT_MAIN), lambda: P.phase_ffn(1, 2, CT_MAIN), P.phase_final]
    for i, ph in enumerate(phases):
        if stop is not None and i >= stop:
            break
        try:
            ph()
        except _Stop:
            break
    P.finish()
    assert len(P.slots) <= P.NS or stop is not None, len(P.slots)
    _CACHE[key] = P
    return P


def _col_tile(W, cols):
    return np.ascontiguousarray(W[:, cols].reshape(NCH, 128, 128).transpose(1, 0, 2)).reshape(128, 2048)


def _qkv_cols(spec):
    kind = spec[0]
    if kind == 'k':
        _, jj, sw = spec
        base = 2048 + jj * 128
        heads = [base, base + 64]
    elif kind == 'v':
        _, jj = spec
        return np.arange(2560 + jj * 128, 2560 + jj * 128 + 128)
    else:
        _, jj, g, sw = spec
        heads = [(8 * jj + g) * 64, (8 * jj + 4 + g) * 64]
    cols = []
    for hb in heads:
        idx = np.arange(64)
        if sw:
            idx = (idx + 32) % 64
        cols.append(hb + idx)
    return np.concatenate(cols)


def _common_slots(slots, inp):
    ws = np.zeros((max(len(slots), 1), 128, 2048), np.float32)
    percore = []
    for s, d in enumerate(slots):
        kind = d[0]
        if kind == 'col':
            _, name, l, c = d
            W = inp[name][l]
            cols = _qkv_cols(c) if name == 'w_qkv' else np.arange(c, c + 128)
            ws[s] = _col_tile(W, cols)
        elif kind == 'row':
            _, name, l, r0 = d
            ws[s] = inp[name][l][r0:r0 + 128, :]
        elif kind == 'rowsel':
            _, name, l, (jj, g) = d
            rows = np.concatenate([np.arange((8 * jj + g) * 64, (8 * jj + g) * 64 + 64),
                                   np.arange((8 * jj + 4 + g) * 64, (8 * jj + 4 + g) * 64 + 64)])
            ws[s] = inp[name][l][rows, :]
        elif kind == 'pool':
            g = d[1]
            ws[s] = inp['w_pool'][0][g].reshape(4, 128, 512).transpose(1, 0, 2).reshape(128, 2048)
        else:
            percore.append((s, d))
    return ws, percore


def _fill_percore(ws, percore, inp, c):
    b0 = NSAMP * c
    for s, d in percore:
        kind = d[0]
        if kind == 'xk':
            _, l, bp = d
            a = inp['cache_mem_k'][l, b0 + 2 * bp:b0 + 2 * bp + 2]
            ws[s] = a.transpose(3, 0, 2, 1).reshape(128, 2048)
        elif kind == 'xv':
            _, l, bp = d
            a = inp['cache_mem_v'][l, b0 + 2 * bp:b0 + 2 * bp + 2]
            ws[s] = a.reshape(2, 2, 128, 512).transpose(2, 0, 1, 3).reshape(128, 2048)
        elif kind == 'sk':
            _, jj = d
            a = inp['cache_swa_k'][0, b0:b0 + NSAMP, :, 2 * jj:2 * jj + 2, :]
            ws[s] = a.transpose(2, 3, 0, 1).reshape(128, 2048)
        elif kind == 'sv':
            _, jj, hf = d
            a = inp['cache_swa_v'][0, b0 + 8 * hf:b0 + 8 * hf + 8, :, 2 * jj:2 * jj + 2, :]
            t = np.zeros((128, 8, 192), np.float32)
            t[:, :, 0:64] = a[:, :, 0, :].transpose(1, 0, 2)
            t[:, :, 128:192] = a[:, :, 1, :].transpose(1, 0, 2)
            ws[s] = 0.0
            ws[s][:, 0:8 * 192] = t.reshape(128, 8 * 192)


def _fm(a):
    return np.ascontiguousarray(a.reshape(a.shape[0], NCH, 128).transpose(2, 1, 0))


def _const_tables(c):
    pos = np.zeros(T, np.float64)
    pos[:SAMP0] = c * NMAIN - HALO + np.arange(SAMP0)
    pos[SAMP0:] = PAST
    p = np.arange(128)
    dd = p % 64
    inv = (10000.0 ** (-(dd % 32).astype(np.float32) / 32.0)).astype(np.float32)
    ang = pos.astype(np.float32)[None, :] * inv[:, None]
    cs = np.zeros((128, 2, T), np.float32)
    cs[:, 0, :] = np.cos(ang)
    sn = np.sin(ang)
    cs[:, 1, :] = np.where((dd < 32)[:, None], -sn, sn)
    ic = np.ones((4, T), np.float32)
    for g, w in enumerate(POOLW):
        cnt = np.minimum(float(w), np.maximum(pos[:SAMP0] + 1.0, 1.0))
        ic[g, :SAMP0] = (1.0 / cnt).astype(np.float32)
    icnt = np.broadcast_to(ic[None], (128, 4, T)).reshape(128, 4 * T)
    k = np.arange(128)[:, None]
    q = np.arange(128)[None, :]
    m = np.zeros((128, 2, 2, 128), np.float32)
    m[:, 0, 0, :] = (k > q)
    m[:, 0, 1, :] = (k <= q)
    m[:, 1] = m[:, 0]
    if c == 0:
        m[:, 1, 0, :] = 0.0
    madd = np.where(m > 0.5, 0.0, -30000.0).astype(np.float32)
    return cs.reshape(128, 2 * T), np.ascontiguousarray(icnt), madd.reshape(128, 512).astype(NPBF)


def prep_common(inp, P):
    ws_common, percore = _common_slots(P.slots, inp)
    gv = np.zeros((128, NGV, NCH), np.float32)
    vecs = {'ffn1_0': inp['g_ffn1'][0], 'mix_0': inp['g_mix'][0], 'xq_0': inp['g_xq'][0], 'ffn2_0': inp['g_ffn2'][0],
            'ffn1_1': inp['g_ffn1'][1], 'mix_1': inp['g_mix'][1], 'xq_1': inp['g_xq'][1], 'ffn2_1': inp['g_ffn2'][1],
            'final': inp['g_final'], 'mem_0': inp['g_mem'][0], 'mem_1': inp['g_mem'][1], 'pscale': inp['pool_scale'][0]}
    for name, v in vecs.items():
        gv[:, GV[name], :] = v.reshape(NCH, 128).T
    sinks = inp['sinks'][0]
    sinkT = np.zeros((128, 16), np.float32)
    for jj in range(4):
        for g in range(4):
            sinkT[0:64, jj * 4 + g] = sinks[8 * jj + g]
            sinkT[64:128, jj * 4 + g] = sinks[8 * jj + 4 + g]
    memT = _fm(inp['mem_prompt'][0]).reshape(128, NCH * 256)
    return dict(ws=ws_common, percore=percore, gv=gv, sinkT=sinkT, memT=memT)


def core_inputs(inp, cm, c):
    xp = inp['x_prompt'][0]
    xs = inp['x_sample'][:, 0, :]
    tok = np.zeros((T, D), np.float32)
    lo = c * NMAIN - HALO
    a = max(lo, 0)
    tok[a - lo:SAMP0] = xp[a:(c + 1) * NMAIN]
    tok[SAMP0:] = xs[NSAMP * c:NSAMP * (c + 1)]
    cs, icnt, mask = _const_tables(c)
    st = inp['state_pool'][0, NSAMP * c:NSAMP * (c + 1)]
    stT = np.ascontiguousarray(st.reshape(NSAMP, 15, NCH, 128).transpose(3, 2, 0, 1)).reshape(128, NCH * NSAMP * 15)
    ws = cm['ws'].copy()
    _fill_percore(ws, cm['percore'], inp, c)
    return {
        'xT': _fm(tok).reshape(128, NCH * T), 'memT': cm['memT'], 'stT': stT,
        'stN': np.ascontiguousarray(st).reshape(NSAMP, 15 * D),
        'ckN': np.ascontiguousarray(inp['cache_swa_k'][0, NSAMP * c:NSAMP * (c + 1)]).reshape(NSAMP, 128 * 512),
        'cvN': np.ascontiguousarray(inp['cache_swa_v'][0, NSAMP * c:NSAMP * (c + 1)]).reshape(NSAMP, 128 * 512),
        'gvec': cm['gv'].reshape(128, NGV * NCH), 'sinkT': cm['sinkT'], 'cs': cs, 'icnt': icnt, 'maskc': mask,
        'identc': np.eye(128, dtype=np.float32).astype(NPBF), 'wstream': ws,
    }


def kernel(**inp):
    inp = {k: np.asarray(v) for k, v in inp.items()}
    P = build_program()
    cm = prep_common(inp, P)
    in_maps = [core_inputs(inp, cm, c) for c in range(NCORES)]
    res = run_bass_kernel_spmd(P.nc, in_maps, core_ids=list(range(NCORES)))
    out = res.results
    f32 = np.float32
    y_prompt = np.zeros((1, 8192, D), f32)
    y_sample = np.zeros((128, 1, D), f32)
    pool_s = np.zeros((1, 128, 15, D), f32)
    k_s = np.zeros((1, 128, 128, 8, 64), f32)
    v_s = np.zeros((1, 128, 128, 8, 64), f32)
    for c in range(NCORES):
        o = out[c]
        yT = o['yT'].reshape(128, NCH, NOUT)
        yy = yT.transpose(2, 1, 0).reshape(NOUT, D)
        y_prompt[0, c * NMAIN:(c + 1) * NMAIN] = yy[:NMAIN]
        y_sample[NSAMP * c:NSAMP * (c + 1), 0] = yy[NMAIN:]
        sl = slice(NSAMP * c, NSAMP * (c + 1))
        pool_s[0, sl, 0:14] = o['pool_s_old'].reshape(NSAMP, 14, D)
        pool_s[0, sl, 14] = o['pool_s_new'].reshape(128, NCH, NSAMP).transpose(2, 1, 0).reshape(NSAMP, D)
        k_s[0, sl, 0:127] = o['swa_k_s_old'].reshape(NSAMP, 127, 8, 64)
        v_s[0, sl, 0:127] = o['swa_v_s_old'].reshape(NSAMP, 127, 8, 64)
        kn = o['swa_k_s_new'].reshape(2, 64, 4, NSAMP)
        k_s[0, sl, 127] = kn.transpose(3, 2, 0, 1).reshape(NSAMP, 8, 64)
        v_s[0, sl, 127] = o['swa_v_s_new'].reshape(NSAMP, 8, 64)
    o7 = out[NCORES - 1]
    pool_p = o7['pool_p'].reshape(128, NCH, 15).transpose(2, 1, 0).reshape(1, 1, 15, D).astype(f32)
    k_p = o7['swa_k_p'].reshape(2, 64, 4, 128).transpose(3, 2, 0, 1).reshape(1, 1, 128, 8, 64).astype(f32)
    v_p = o7['swa_v_p'].reshape(1, 1, 128, 8, 64).astype(f32)
    o0 = out[0]
    mem_k = o0['mem_k'].reshape(128, 2, 4, 256).transpose(1, 3, 2, 0).reshape(2, 1, 256, 4, 128).astype(f32)
    mem_v = o0['mem_v'].reshape(128, 2, 2, 512).transpose(1, 2, 0, 3).reshape(2, 1, 256, 4, 128).astype(f32)
    return (y_prompt, y_sample, np.ascontiguousarray(pool_p), pool_s, np.ascontiguousarray(k_p), np.ascontiguousarray(v_p),
            k_s, v_s, np.ascontiguousarray(mem_k), np.ascontiguousarray(mem_v))
```

```python
import math
from contextlib import ExitStack

import ml_dtypes
import numpy as np

import concourse.bass as bass
import concourse.mybir as mybir
from concourse.bass_utils import run_bass_kernel_spmd

F32 = mybir.dt.float32
BF16 = mybir.dt.bfloat16
AF = mybir.ActivationFunctionType
ALU = mybir.AluOpType
AX = mybir.AxisListType
NPBF = ml_dtypes.bfloat16

D = 2048
NCH = 16
DFF = 5632
NF = 44
T = 1184
HALO = 144
NMAIN = 1024
NSAMP = 16
SAMP0 = 1168
NOUT = 1040
R = 9
NCORES = 8
CT_ALL = [(0, 512), (512, 512), (1024, 160)]
CT_MAIN = [(144, 448), (592, 448), (1040, 144)]
SEGB = [0, 144, 512, 592, 1024, 1040, 1168, 1184]
GV = {'ffn1_0': 0, 'mix_0': 1, 'xq_0': 2, 'ffn2_0': 3, 'ffn1_1': 4, 'mix_1': 5, 'xq_1': 6, 'ffn2_1': 7,
      'final': 8, 'mem_0': 9, 'mem_1': 10, 'pscale': 11}
NGV = 12
EPS = 1e-6
PAST = 8192
POOLW = (2, 4, 8, 16)


class Buf:
    __slots__ = ('w', 'r', 'excl')

    def __init__(self, snap=None):
        self.w = None
        self.r = dict(snap) if snap else {}
        self.excl = False


class Seg:
    def __init__(self, bounds, snap=None):
        self.b = bounds
        self.bufs = [Buf(snap) for _ in bounds[:-1]]

    def get(self, lo, hi):
        return [self.bufs[i] for i in range(len(self.b) - 1) if self.b[i] < hi and self.b[i + 1] > lo]


class Eng:
    def __init__(self, name, h, key):
        self.name, self.h, self.key = name, h, key
        self.cnt = 0
        self.known = {}


class Prog:
    def __init__(self):
        nc = bass.Bass("TRN2", target_bir_lowering=False)
        self.nc = nc
        self.es = ExitStack()
        self.sems = []
        self.dval = {}
        self.slots = []
        self.uid = 0
        self.PE = Eng('PE', nc.tensor, self.new_sem('s_pe'))
        self.ACT = Eng('ACT', nc.scalar, self.new_sem('s_act'))
        self.DVE = Eng('DVE', nc.vector, self.new_sem('s_dve'))
        self.POOL = Eng('POOL', nc.gpsimd, None)
        self.SP = Eng('SP', nc.sync, None)
        self.rsem = [self.new_sem(f's_ring{i}') for i in range(R)]
        self.spsem = [self.new_sem(f's_sp{i}') for i in range(16)]
        self.sp_rr = 0
        self.ring_ptr = 0
        self.held = set()
        for k in self.rsem + self.spsem:
            self.dval[k] = 0

    def new_sem(self, name):
        h = self.es.enter_context(self.nc.semaphore(name))
        self.sems.append(h)
        return len(self.sems) - 1

    def dram(self, name, shape, dt, kind):
        return self.nc.dram_tensor(name, list(shape), dt, kind=kind).ap()

    def sb(self, name, shape, dt, es=None):
        self.uid += 1
        return (es or self.es).enter_context(self.nc.sbuf_tensor(f"{name}_{self.uid}", list(shape), dt))

    def snapshot(self):
        d = {}
        for E in (self.PE, self.ACT, self.DVE):
            if E.cnt > 0:
                d[E.key] = E.cnt
        for k in self.spsem:
            if self.dval[k] > 0:
                d[k] = self.dval[k]
        return d

    def _deps(self, E, Rb, Wb):
        need = {}

        def add(k, v):
            if need.get(k, 0) < v:
                need[k] = v
        for b in Rb:
            if b.w is not None:
                add(*b.w)
        for b in Wb:
            if b.w is not None:
                add(*b.w)
            for k, v in b.r.items():
                add(k, v)
        for k, v in need.items():
            if k == E.key:
                if E.name == 'PE':
                    continue
            if E.known.get(k, 0) >= v:
                continue
            E.h.wait_ge(self.sems[k], v)
            E.known[k] = v

    def _mark(self, tok, Rb, Wb):
        for b in Rb:
            if b.r.get(tok[0], 0) < tok[1]:
                b.r[tok[0]] = tok[1]
        for b in Wb:
            b.w = tok
            b.r = {}

    def op(self, E, fn, Rb=(), Wb=()):
        if E.name != 'PE':
            ex = [b for b in Rb if b.excl]
            if ex:
                Wb = list(Wb) + ex
                Rb = [b for b in Rb if not b.excl]
        self._deps(E, Rb, Wb)
        ins = fn()
        E.cnt += 1
        ins.then_inc(self.sems[E.key], 1)
        self._mark((E.key, E.cnt), Rb, Wb)

    def dma(self, Q, out, in_, Rb=(), Wb=(), semkey=None):
        self._deps(Q, Rb, Wb)
        if semkey is None:
            semkey = self.spsem[self.sp_rr % len(self.spsem)]
            self.sp_rr += 1
            if self.dval[semkey] > 0 and Q.known.get(semkey, 0) < self.dval[semkey]:
                Q.h.wait_ge(self.sems[semkey], self.dval[semkey])
                Q.known[semkey] = self.dval[semkey]
        self.dval[semkey] += 16
        Q.h.dma_start(out=out, in_=in_).then_inc(self.sems[semkey], 16)
        self._mark((semkey, self.dval[semkey]), Rb, Wb)

    def take_slot(self, desc, hold=False):
        s = len(self.slots)
        self.slots.append(desc)
        r = self.ring_ptr % R
        while r in self.held:
            r = (r + 1) % R
        self.ring_ptr = r + 1
        if hold:
            self.held.add(r)
        self.dma(self.POOL, out=self.ring[:, r, :], in_=self.wstream[s], Wb=(self.ringb[r],), semkey=self.rsem[r])
        return r

    def slot3(self, r, k):
        return self.ring[:, r, :].rearrange("p (k n) -> p k n", k=k)

    def PA(self, ci, n):
        if ci < 2:
            return self.pb[ci][:, 0:n], self.pbuf[ci].get(0, n)
        return self.pb[6][:, 0:n], self.pbuf[6].get(0, n)

    def PB(self, ci, n):
        if ci < 2:
            return self.pb[2 + ci][:, 0:n], self.pbuf[2 + ci].get(0, n)
        return self.pb[6][:, 160:160 + n], self.pbuf[6].get(0, 512)

    def bank(self, i, lo, hi):
        return self.pb[i][:, lo:hi], self.pbuf[i].get(lo, hi)

    def xb(self, k, c0, n):
        return self.xbufs[k].get(c0, c0 + n)

    def declare(self):
        nc = self.nc
        I, O = "ExternalInput", "ExternalOutput"
        self.d_xT = self.dram("xT", [128, NCH * T], F32, I)
        self.d_memT = self.dram("memT", [128, NCH * 256], F32, I)
        self.d_stT = self.dram("stT", [128, NCH * NSAMP * 15], F32, I)
        self.d_stN = self.dram("stN", [NSAMP, 15 * D], F32, I)
        self.d_ckN = self.dram("ckN", [NSAMP, 128 * 512], F32, I)
        self.d_cvN = self.dram("cvN", [NSAMP, 128 * 512], F32, I)
        self.d_gvec = self.dram("gvec", [128, NGV * NCH], F32, I)
        self.d_sinkT = self.dram("sinkT", [128, 16], F32, I)
        self.d_cs = self.dram("cs", [128, 2 * T], F32, I)
        self.d_icnt = self.dram("icnt", [128, 4 * T], F32, I)
        self.d_mask = self.dram("maskc", [128, 2 * 256], BF16, I)
        self.d_ident = self.dram("identc", [128, 128], BF16, I)
        self.NS_decl = None
        self.o_yT = self.dram("yT", [128, NCH * NOUT], F32, O)
        self.o_pool_p = self.dram("pool_p", [128, NCH * 15], F32, O)
        self.o_pool_sn = self.dram("pool_s_new", [128, NCH * NSAMP], F32, O)
        self.o_pool_so = self.dram("pool_s_old", [NSAMP, 14 * D], F32, O)
        self.o_kp = self.dram("swa_k_p", [128, 4 * 128], F32, O)
        self.o_vp = self.dram("swa_v_p", [128, 4 * 128], F32, O)
        self.o_ksn = self.dram("swa_k_s_new", [128, 4 * NSAMP], F32, O)
        self.o_vsn = self.dram("swa_v_s_new", [NSAMP, 4 * 128], F32, O)
        self.o_kso = self.dram("swa_k_s_old", [NSAMP, 127 * 512], F32, O)
        self.o_vso = self.dram("swa_v_s_old", [NSAMP, 127 * 512], F32, O)
        self.o_memk = self.dram("mem_k", [128, 2 * 4 * 256], F32, O)
        self.o_memv = self.dram("mem_v", [128, 2 * 2 * 512], F32, O)

        self.xT = self.sb("xT", [128, NCH, T], F32)
        self.hT = self.sb("hT", [128, NCH, T], BF16)
        self.ring = self.sb("ring", [128, R, 2048], BF16)
        self.KmT = self.sb("KmT", [128, 2, 4, 256], BF16)
        self.Vm = self.sb("Vm", [128, 2, 2, 512], BF16)
        self.gv = self.sb("gv", [128, NGV, NCH], F32)
        self.ones = self.sb("ones", [128, 192], BF16)
        self.onesf = self.sb("onesf", [128, 128], BF16)
        self.esink = self.sb("esink", [128, 16], F32)
        self.epsc = self.sb("epsc", [128, 1], F32)
        self.pb = [self.es.enter_context(nc.psum_tensor(f"pb{i}", [128, 512], F32)) for i in range(8)]
        self.pbuf = [Seg([0, 512]) for i in range(8)]
        for sg in self.pbuf:
            for b in sg.bufs:
                b.excl = True
        self.xbufs = [Seg(SEGB) for _ in range(NCH)]
        self.hTb = [Buf() for _ in range(NCH)]
        self.ringb = [Buf() for _ in range(R)]
        self.b_KmT = [Buf(), Buf()]
        self.b_Vm = [Buf(), Buf()]
        self.b_gv = Buf()
        self.b_ones = Buf()
        self.b_esink = Buf()
        self.pd_rr = 0

    def pdslot(self, n):
        i = (4, 5, 7)[self.pd_rr % 3]
        self.pd_rr += 1
        return self.bank(i, 0, n)

    def proj(self, dst, r, ctiles, src=None, srcb=None, fine=False):
        sv = self.slot3(r, 16)
        PE = self.PE.h
        src = self.hT if src is None else src
        srcb = self.hTb if srcb is None else srcb

        step = 2 if fine else NCH
        for k0 in range(0, NCH, step):
            def fn():
                ins = None
                for k in range(k0, k0 + step):
                    for ci, (c0, n) in enumerate(ctiles):
                        ins = PE.matmul(dst[ci][0], lhsT=sv[:, k, :], rhs=src[:, k, c0:c0 + n],
                                        start=(k == 0), stop=(k == NCH - 1))
                return ins
            self.op(self.PE, fn, Rb=[self.ringb[r]] + list(srcb[k0:k0 + step]), Wb=[b for d in dst for b in d[1]])

    def norm(self, gidx, ctiles, es, make_h=True):
        lo = ctiles[0][0]
        hi = ctiles[-1][0] + ctiles[-1][1]
        sq = self.sb("sq", [128, 2, T], BF16, es)
        rbc = self.sb("rbc", [128, T], F32, es)
        snap = self.snapshot()
        sqb = [Buf(snap), Buf(snap)]
        rbcb = Seg(SEGB, snap)
        A, V, PE = self.ACT.h, self.DVE.h, self.PE.h
        dst = [self.PA(ci, n) for ci, (c0, n) in enumerate(ctiles)]
        for k in range(NCH):
            p = k % 2
            if k % 2 == 0:
                self.op(self.ACT, lambda: A.activation(out=sq[:, p, lo:hi], in_=self.xT[:, k, lo:hi], func=AF.Square),
                        Rb=self.xb(k, lo, hi - lo), Wb=[sqb[p]])
            else:
                self.op(self.DVE, lambda: V.tensor_tensor(out=sq[:, p, lo:hi], in0=self.xT[:, k, lo:hi], in1=self.xT[:, k, lo:hi],
                                                          op=ALU.mult), Rb=self.xb(k, lo, hi - lo), Wb=[sqb[p]])

            def fn():
                ins = None
                for ci, (c0, n) in enumerate(ctiles):
                    ins = PE.matmul(dst[ci][0], lhsT=self.onesf[:, :], rhs=sq[:, p, c0:c0 + n],
                                    start=(k == 0), stop=(k == NCH - 1))
                return ins
            self.op(self.PE, fn, Rb=[sqb[p], self.b_ones], Wb=[b for d in dst for b in d[1]])
        for ci, (c0, n) in enumerate(ctiles):
            rb = rbcb.get(c0, c0 + n)
            self.op(self.ACT, lambda: A.activation(out=rbc[:, c0:c0 + n], in_=dst[ci][0], func=AF.Ln,
                                                   scale=1.0 / D, bias=self.epsc[:, 0:1]), Rb=dst[ci][1] + [self.b_ones], Wb=rb)
            self.op(self.ACT, lambda: A.activation(out=rbc[:, c0:c0 + n], in_=rbc[:, c0:c0 + n], func=AF.Exp, scale=-0.5),
                    Rb=rb, Wb=rb)
        if make_h:
            for k in range(NCH):
                self.op(self.DVE, lambda: V.scalar_tensor_tensor(
                    out=self.hT[:, k, lo:hi], in0=self.xT[:, k, lo:hi], scalar=self.gv[:, gidx, k:k + 1],
                    in1=rbc[:, lo:hi], op0=ALU.mult, op1=ALU.mult),
                    Rb=self.xb(k, lo, hi - lo) + rbcb.get(lo, hi) + [self.b_gv], Wb=[self.hTb[k]])
        return rbc, rbcb

    def resid_add(self, d, c0, n, ps, scale=None, scale_ap=None):
        V = self.DVE.h
        xs = self.xT[:, d, c0:c0 + n]
        sc = scale_ap if scale_ap is not None else scale
        rb = list(ps[1]) + self.xb(d, c0, n) + ([self.b_gv] if scale_ap is not None else [])
        self.op(self.DVE, lambda: V.scalar_tensor_tensor(out=xs, in0=ps[0], scalar=sc, in1=xs,
                                                         op0=ALU.mult, op1=ALU.add),
                Rb=rb, Wb=self.xb(d, c0, n))

    def phase_init(self):
        V = self.DVE.h
        self.dma(self.SP, out=self.gv[:, :, :], in_=self.d_gvec.rearrange("p (g k) -> p g k", g=NGV), Wb=[self.b_gv])
        self.dma(self.SP, out=self.esink[:, :], in_=self.d_sinkT[:, :], Wb=[self.b_esink])
        def fn():
            V.memset(self.ones[:, 0:64], 1.0)
            V.memset(self.ones[:, 64:128], 0.0)
            V.memset(self.ones[:, 128:192], 1.0)
            V.memset(self.epsc[:, :], EPS)
            return V.memset(self.onesf[:, :], 1.0)
        self.op(self.DVE, fn, Wb=[self.b_ones])
        self.op(self.ACT, lambda: self.ACT.h.activation(out=self.esink[:, :], in_=self.esink[:, :], func=AF.Exp),
                Rb=[self.b_esink], Wb=[self.b_esink])

    def phase_memkv(self):
        A, V, PE = self.ACT.h, self.DVE.h, self.PE.h
        with ExitStack() as es:
            snap = self.snapshot()
            memT = self.sb("memT", [128, NCH, 256], F32, es)
            hm = self.sb("hm", [128, NCH, 256], BF16, es)
            sq = self.sb("sqm", [128, 2, 256], BF16, es)
            rbc = self.sb("rbcm", [128, 256], F32, es)
            kst = self.sb("kst", [128, 2, 4, 256], F32, es)
            vst = self.sb("vst", [128, 2, 2, 512], F32, es)
            b_mem, b_rbc, b_kst, b_vst = Buf(snap), Buf(snap), Buf(snap), Buf(snap)
            sqb = [Buf(snap), Buf(snap)]
            hmb = [Buf(snap) for _ in range(NCH)]
            self.dma(self.SP, out=memT[:, :, :], in_=self.d_memT.rearrange("p (k m) -> p k m", k=NCH), Wb=[b_mem])
            xv = self.d_xT.rearrange("p (k t) -> p k t", k=NCH)
            for k in range(NCH):
                self.dma(self.SP, out=self.xT[:, k, :], in_=xv[:, k, :], Wb=self.xbufs[k].bufs)
            ps = self.bank(0, 0, 256)
            for k in range(NCH):
                p = k % 2
                self.op(self.ACT, lambda: A.activation(out=sq[:, p, :], in_=memT[:, k, :], func=AF.Square),
                        Rb=[b_mem], Wb=[sqb[p]])
                self.op(self.PE, lambda: PE.matmul(ps[0], lhsT=self.onesf[:, :], rhs=sq[:, p, :],
                                                   start=(k == 0), stop=(k == NCH - 1)),
                        Rb=[sqb[p], self.b_ones], Wb=ps[1])
            self.op(self.ACT, lambda: A.activation(out=rbc[:, :], in_=ps[0], func=AF.Sqrt, scale=1.0 / D, bias=EPS),
                    Rb=ps[1], Wb=[b_rbc])
            self.op(self.DVE, lambda: V.reciprocal(out=rbc[:, :], in_=rbc[:, :]), Rb=[b_rbc], Wb=[b_rbc])
            if DBG == 1:
                return
            for l in range(2):
                gi = GV[f'mem_{l}']
                for k in range(NCH):
                    self.op(self.DVE, lambda: V.scalar_tensor_tensor(
                        out=hm[:, k, :], in0=memT[:, k, :], scalar=self.gv[:, gi, k:k + 1], in1=rbc[:, :],
                        op0=ALU.mult, op1=ALU.mult), Rb=[b_mem, b_rbc, self.b_gv], Wb=[hmb[k]])
                for h in range(4):
                    r = self.take_slot(('col', 'w_xkv', l, h * 128))
                    sv = self.slot3(r, 16)
                    pk = self.bank(1 + (h % 2), 0, 256)

                    def fn():
                        ins = None
                        for k in range(NCH):
                            ins = PE.matmul(pk[0], lhsT=sv[:, k, :], rhs=hm[:, k, :], start=(k == 0), stop=(k == NCH - 1))
                        return ins
                    self.op(self.PE, fn, Rb=[self.ringb[r]] + hmb, Wb=pk[1])
                    self.op(self.ACT, lambda: A.activation(out=self.KmT[:, l, h, :], in_=pk[0], func=AF.Copy),
                            Rb=pk[1], Wb=[self.b_KmT[l]])
                    self.op(self.DVE, lambda: V.tensor_copy(out=kst[:, l, h, :], in_=pk[0]), Rb=pk[1], Wb=[b_kst])
                    if DBG == 2:
                        return
                pv = [self.bank(3, 0, 512), self.bank(4, 0, 512)]
                for h in range(4):
                    r = self.take_slot(('col', 'w_xkv', l, 512 + h * 128))
                    sv = self.slot3(r, 16)
                    for mb in range(2):
                        def fn():
                            ins = None
                            for k in range(NCH):
                                ins = PE.matmul(pv[mb][0][:, h * 128:(h + 1) * 128], lhsT=hm[:, k, mb * 128:(mb + 1) * 128],
                                                rhs=sv[:, k, :], start=(k == 0), stop=(k == NCH - 1))
                            return ins
                        self.op(self.PE, fn, Rb=[self.ringb[r]] + hmb, Wb=pv[mb][1])
                for mb in range(2):
                    self.op(self.ACT, lambda: A.activation(out=self.Vm[:, l, mb, :], in_=pv[mb][0], func=AF.Copy),
                            Rb=pv[mb][1], Wb=[self.b_Vm[l]])
                    self.op(self.DVE, lambda: V.tensor_copy(out=vst[:, l, mb, :], in_=pv[mb][0]), Rb=pv[mb][1], Wb=[b_vst])
            self.dma(self.SP, out=self.o_memk.rearrange("p (l h m) -> p l h m", l=2, h=4), in_=kst[:, :, :, :], Rb=[b_kst])
            self.dma(self.SP, out=self.o_memv.rearrange("p (l b n) -> p l b n", l=2, b=2), in_=vst[:, :, :, :], Rb=[b_vst])

    def phase_ffn(self, l, which, ctiles):
        A, V, PE = self.ACT.h, self.DVE.h, self.PE.h
        wgu, wdn = f'w_ffn{which}_gu', f'w_ffn{which}_dn'
        lo = ctiles[0][0]
        hi = ctiles[-1][0] + ctiles[-1][1]
        with ExitStack() as es:
            with ExitStack() as es2:
                self.norm(GV[f'ffn{which}_{l}'], ctiles, es2)
            snap = self.snapshot()
            act = self.sb("act", [128, 2, 4, T], BF16, es)
            sil = self.sb("sil", [128, T], F32, es)
            actb = [[[Buf(snap) for _ in range(3)] for _ in range(4)] for _ in range(2)]
            silb = [Buf(snap) for _ in range(3)]
            order = [2, 0, 1]
            pa = [self.PA(ci, n) for ci, (c0, n) in enumerate(ctiles)]
            pbb = [self.PB(ci, n) for ci, (c0, n) in enumerate(ctiles)]

            pending = []

            def drain(k):
                for _ in range(min(k, len(pending))):
                    pending.pop(0)()

            def gu(j):
                par, f = (j // 4) % 2, j % 4
                r = self.take_slot(('col', wgu, l, j * 128))
                self.proj(pa, r, ctiles, fine=(j == 0))

                for ci in order:
                    c0, n = ctiles[ci]
                    self.op(self.ACT, lambda: A.activation(out=sil[:, c0:c0 + n], in_=pa[ci][0], func=AF.Silu),
                            Rb=pa[ci][1], Wb=[silb[ci]])
                drain(6)
                r2 = self.take_slot(('col', wgu, l, DFF + j * 128))
                self.proj(pbb, r2, ctiles)

                for ci in order:
                    c0, n = ctiles[ci]
                    self.op(self.DVE, lambda: V.tensor_tensor(out=act[:, par, f, c0:c0 + n], in0=pbb[ci][0], in1=sil[:, c0:c0 + n],
                                                              op=ALU.mult), Rb=[silb[ci]] + pbb[ci][1], Wb=[actb[par][f][ci]])
                drain(6)

            def down_tasks(g):
                par = g % 2
                st = {}
                tasks = []
                ntask = NCH * len(ctiles)

                def mk(idx, d, ci):
                    def task():
                        if 'rs' not in st:
                            st['rs'] = [self.take_slot(('row', wdn, l, (4 * g + f) * 128), hold=True) for f in range(4)]
                        rs = st['rs']
                        c0, n = ctiles[ci]
                        ps = self.pdslot(n)

                        def fn():
                            ins = None
                            for f in range(4):
                                ins = PE.matmul(ps[0], lhsT=self.ring[:, rs[f], d * 128:(d + 1) * 128],
                                                rhs=act[:, par, f, c0:c0 + n], start=(f == 0), stop=(f == 3))
                            return ins
                        self.op(self.PE, fn, Rb=[self.ringb[x] for x in rs] + [actb[par][f][ci] for f in range(4)], Wb=ps[1])
                        self.resid_add(d, c0, n, ps, scale=0.5)
                        if idx == ntask - 1:
                            for x in rs:
                                self.held.discard(x)
                    return task
                i = 0
                for d in range(NCH):
                    for ci in range(len(ctiles)):
                        tasks.append(mk(i, d, ci))
                        i += 1
                return tasks
            NG = NF // 4
            for g in range(NG):
                for f in range(4):
                    gu(4 * g + f)
                drain(len(pending))
                pending.extend(down_tasks(g))
            drain(len(pending))

    def phase_pool(self):
        A, V, PE = self.ACT.h, self.DVE.h, self.PE.h
        ct = CT_ALL
        with ExitStack() as es:
            so = self.o_pool_so.rearrange("b (r d) -> b r d", r=14)
            si = self.d_stN.rearrange("b (r d) -> b r d", r=15)
            for r0 in range(0, 14, 4):
                r1 = min(r0 + 4, 14)
                self.dma(self.SP, out=so[:, r0:r1, :], in_=si[:, 1 + r0:1 + r1, :])
            for (o_, i_) in ((self.o_kso, self.d_ckN), (self.o_vso, self.d_cvN)):
                ov = o_.rearrange("b (r d) -> b r d", r=127)
                iv = i_.rearrange("b (r d) -> b r d", r=128)
                for r0 in range(0, 127, 16):
                    r1 = min(r0 + 16, 127)
                    self.dma(self.SP, out=ov[:, r0:r1, :], in_=iv[:, 1 + r0:1 + r1, :])

            rbc, rbcb = self.norm(GV['mix_0'], ct, es, make_h=False)
            snap = self.snapshot()
            tA = self.sb("tA", [128, T], F32, es)
            tB = self.sb("tB", [128, T], F32, es)
            tC = self.sb("tC", [128, T], F32, es)
            icn = self.sb("icn", [128, T], F32, es)
            stk = self.sb("stk", [128, 2, NSAMP, 15], F32, es)
            ssum = self.sb("ssum", [128, NSAMP], F32, es)
            pst = self.sb("pst", [128, NCH, 15], F32, es)
            sst = self.sb("sst", [128, NCH, NSAMP], F32, es)
            bA, bB, bC, b_icn, b_ss, b_pst, b_sst = (Buf(snap) for _ in range(7))
            b_stk = [Buf(snap), Buf(snap)]
            gi = GV['mix_0']
            stT = self.d_stT.rearrange("p (k b r) -> p k b r", k=NCH, b=NSAMP)
            icv = self.d_icnt.rearrange("p (g t) -> p g t", g=4)
            rall = rbcb.get(0, T)
            for k in range(NCH):
                g = k // 4
                w = POOLW[g]
                if k % 4 == 0:
                    self.dma(self.SP, out=icn[:, :], in_=icv[:, g, :], Wb=[b_icn])
                sp_ = k % 2
                self.dma(self.SP, out=stk[:, sp_, :, :], in_=stT[:, k, :, :], Wb=[b_stk[sp_]])
                self.op(self.DVE, lambda: V.scalar_tensor_tensor(
                    out=tA[:, :], in0=self.xT[:, k, :], scalar=self.gv[:, gi, k:k + 1], in1=rbc[:, :],
                    op0=ALU.mult, op1=ALU.mult), Rb=self.xb(k, 0, T) + rall + [self.b_gv], Wb=[bA])
                cur, curb = tA, bA
                oth = [(tB, bB), (tC, bC)]
                st, oi = 1, 0
                while st < w:
                    nt, nb = oth[oi]
                    lo_ = 2 * st - 1
                    self.op(self.DVE, lambda: V.tensor_tensor(out=nt[:, lo_:T], in0=cur[:, lo_:T], in1=cur[:, lo_ - st:T - st],
                                                              op=ALU.add), Rb=[curb], Wb=[nb])
                    cur, curb = nt, nb
                    oi ^= 1
                    st *= 2
                nt, nb = oth[oi]
                self.op(self.DVE, lambda: V.tensor_tensor(out=nt[:, 15:SAMP0], in0=cur[:, 15:SAMP0], in1=icn[:, 15:SAMP0],
                                                          op=ALU.mult), Rb=[curb, b_icn], Wb=[nb])
                self.op(self.DVE, lambda: V.tensor_tensor(out=self.hT[:, k, 15:SAMP0], in0=nt[:, 15:SAMP0], in1=tA[:, 15:SAMP0],
                                                          op=ALU.subtract), Rb=[nb, bA], Wb=[self.hTb[k]])
                self.op(self.DVE, lambda: V.tensor_reduce(out=ssum[:, :], in_=stk[:, sp_, :, 15 - (w - 1):15], axis=AX.X, op=ALU.add),
                        Rb=[b_stk[sp_]], Wb=[b_ss])
                self.op(self.DVE, lambda: V.tensor_tensor(out=ssum[:, :], in0=ssum[:, :], in1=tA[:, SAMP0:T], op=ALU.add),
                        Rb=[b_ss, bA], Wb=[b_ss])
                self.op(self.DVE, lambda: V.scalar_tensor_tensor(out=self.hT[:, k, SAMP0:T], in0=ssum[:, :], scalar=1.0 / w,
                                                                 in1=tA[:, SAMP0:T], op0=ALU.mult, op1=ALU.subtract),
                        Rb=[b_ss, bA], Wb=[self.hTb[k]])
                self.op(self.ACT, lambda: A.activation(out=pst[:, k, :], in_=tA[:, SAMP0 - 15:SAMP0], func=AF.Copy), Rb=[bA], Wb=[b_pst])
                self.op(self.ACT, lambda: A.activation(out=sst[:, k, :], in_=tA[:, SAMP0:T], func=AF.Copy), Rb=[bA], Wb=[b_sst])
            self.dma(self.SP, out=self.o_pool_p.rearrange("p (k r) -> p k r", k=NCH), in_=pst[:, :, :], Rb=[b_pst])
            self.dma(self.SP, out=self.o_pool_sn.rearrange("p (k b) -> p k b", k=NCH), in_=sst[:, :, :], Rb=[b_sst])
            gs = GV['pscale']
            for g in range(4):
                r = self.take_slot(('pool', g))
                sv = self.slot3(r, 4)
                for oc in range(4):
                    d = 4 * g + oc
                    for ci, (c0, n) in enumerate(ct):
                        ps = self.pdslot(n)

                        def fn():
                            ins = None
                            for kc in range(4):
                                ins = PE.matmul(ps[0], lhsT=sv[:, kc, oc * 128:(oc + 1) * 128], rhs=self.hT[:, 4 * g + kc, c0:c0 + n],
                                                start=(kc == 0), stop=(kc == 3))
                            return ins
                        self.op(self.PE, fn, Rb=[self.ringb[r]] + self.hTb[4 * g:4 * g + 4], Wb=ps[1])
                        self.resid_add(d, c0, n, ps, scale_ap=self.gv[:, gs, d:d + 1])

    def phase_xattn(self, l, ctiles):
        A, V, PE = self.ACT.h, self.DVE.h, self.PE.h
        sc = 1.0 / math.sqrt(128.0)
        with ExitStack() as es:
            with ExitStack() as es2:
                self.norm(GV[f'xq_{l}'], ctiles, es2)
            snap = self.snapshot()
            qT = self.sb("qT", [128, 4, T], BF16, es)
            oT = self.sb("oT", [128, 4, T], BF16, es)
            PT = self.sb("PT", [128, 2, 2, 512], BF16, es)
            rec = self.sb("rec", [128, 2, 512], F32, es)
            PTs = self.sb("PTs", [128, 128], BF16, es)
            recs = self.sb("recs", [128, 64], F32, es)
            qTb = [Buf(snap) for _ in range(4)]
            oTb = [Seg(SEGB, snap) for _ in range(4)]
            PTb = [[Buf(snap), Buf(snap)], [Buf(snap), Buf(snap)]]
            recb = [Buf(snap), Buf(snap)]
            b_PTs, b_recs = Buf(snap), Buf(snap)
            for h in range(4):
                r = self.take_slot(('col', 'w_xq', l, h * 128))
                dst = [(self.PA if h % 2 == 0 else self.PB)(ci, n) for ci, (c0, n) in enumerate(ctiles)]
                self.proj(dst, r, ctiles, fine=(h == 0))

                def fa():
                    ins = None
                    for ci, (c0, n) in enumerate(ctiles):
                        ins = A.activation(out=qT[:, h, c0:c0 + n], in_=dst[ci][0], func=AF.Copy)
                    return ins
                self.op(self.ACT, fa, Rb=[b for d in dst for b in d[1]], Wb=[qTb[h]])
            units = [(h, c0, n) for h in range(4) for (c0, n) in ctiles if c0 < SAMP0]

            def xs1(u):
                h, c0, n = units[u]
                par = u % 2
                pS = [self.bank(2 * par, 0, n), self.bank(2 * par + 1, 0, n)]

                def fs():
                    ins = None
                    for mb in range(2):
                        ins = PE.matmul(pS[mb][0], lhsT=self.KmT[:, l, h, mb * 128:(mb + 1) * 128], rhs=qT[:, h, c0:c0 + n],
                                        start=True, stop=True)
                    return ins
                self.op(self.PE, fs, Rb=[self.b_KmT[l], qTb[h]], Wb=pS[0][1] + pS[1][1])
                for mb in range(2):
                    self.op(self.ACT, lambda: A.activation(out=PT[:, par, mb, 0:n], in_=pS[mb][0], func=AF.Exp, scale=sc),
                            Rb=pS[mb][1], Wb=[PTb[par][mb]])

            def xs2(u):
                h, c0, n = units[u]
                par = u % 2
                po = self.bank(4 + par, 0, n)
                pd = self.bank(7 - par, 0, n)

                def fo():
                    ins = None
                    for mb in range(2):
                        ins = PE.matmul(po[0], lhsT=self.Vm[:, l, mb, h * 128:(h + 1) * 128], rhs=PT[:, par, mb, 0:n],
                                        start=(mb == 0), stop=(mb == 1))
                    for mb in range(2):
                        ins = PE.matmul(pd[0], lhsT=self.onesf[:, :], rhs=PT[:, par, mb, 0:n], start=(mb == 0), stop=(mb == 1))
                    return ins
                self.op(self.PE, fo, Rb=[self.b_Vm[l], self.b_ones] + PTb[par], Wb=po[1] + pd[1])
                self.op(self.ACT, lambda: A.activation(out=rec[:, par, 0:n], in_=pd[0], func=AF.Ln), Rb=pd[1], Wb=[recb[par]])
                self.op(self.ACT, lambda: A.activation(out=rec[:, par, 0:n], in_=rec[:, par, 0:n], func=AF.Exp, scale=-1.0),
                        Rb=[recb[par]], Wb=[recb[par]])
                self.op(self.DVE, lambda: V.tensor_tensor(out=oT[:, h, c0:c0 + n], in0=po[0], in1=rec[:, par, 0:n], op=ALU.mult),
                        Rb=po[1] + [recb[par]], Wb=oTb[h].get(c0, c0 + n))
            xs1(0)
            for u in range(len(units)):
                if u + 1 < len(units):
                    xs1(u + 1)
                xs2(u)
            pSs = self.bank(0, 0, 128)
            krs = [self.take_slot(('xk', l, bp)) for bp in range(8)]
            for bp in range(8):
                kv = self.ring[:, krs[bp], :].rearrange("p (b h m) -> p b h m", b=2, h=4)

                def fk():
                    ins = None
                    for bb in range(2):
                        b = 2 * bp + bb
                        for h in range(4):
                            for mb in range(2):
                                col = mb * 64 + h * 16 + b
                                ins = PE.matmul(pSs[0][:, col:col + 1], lhsT=kv[:, bb, h, mb * 128:(mb + 1) * 128],
                                                rhs=qT[:, h, SAMP0 + b:SAMP0 + b + 1], start=True, stop=True)
                    return ins
                self.op(self.PE, fk, Rb=[self.ringb[krs[bp]]] + qTb, Wb=pSs[1])
            self.op(self.ACT, lambda: A.activation(out=PTs[:, :], in_=pSs[0], func=AF.Exp, scale=sc), Rb=pSs[1], Wb=[b_PTs])
            pos_ = self.bank(1, 0, 64)
            pds_ = self.bank(2, 0, 64)
            vrs = [self.take_slot(('xv', l, bp)) for bp in range(8)]
            for bp in range(8):
                vv = self.ring[:, vrs[bp], :].rearrange("p (b m n) -> p b m n", b=2, m=2)

                def fv():
                    ins = None
                    for bb in range(2):
                        b = 2 * bp + bb
                        for h in range(4):
                            for mb in range(2):
                                col = mb * 64 + h * 16 + b
                                ins = PE.matmul(pos_[0][:, h * 16 + b:h * 16 + b + 1], lhsT=vv[:, bb, mb, h * 128:(h + 1) * 128],
                                                rhs=PTs[:, col:col + 1], start=(mb == 0), stop=(mb == 1))
                    return ins
                self.op(self.PE, fv, Rb=[self.ringb[vrs[bp]], b_PTs], Wb=pos_[1])

            def fd():
                ins = None
                for mb in range(2):
                    ins = PE.matmul(pds_[0], lhsT=self.onesf[:, :], rhs=PTs[:, mb * 64:(mb + 1) * 64], start=(mb == 0), stop=(mb == 1))
                return ins
            self.op(self.PE, fd, Rb=[b_PTs, self.b_ones], Wb=pds_[1])
            self.op(self.DVE, lambda: V.reciprocal(out=recs[:, :], in_=pds_[0]), Rb=pds_[1], Wb=[b_recs])
            self.op(self.DVE, lambda: V.tensor_tensor(out=oT[:, :, SAMP0:T], in0=pos_[0].rearrange("p (h b) -> p h b", h=4),
                                                      in1=recs[:, :].rearrange("p (h b) -> p h b", h=4), op=ALU.mult),
                    Rb=pos_[1] + [b_recs], Wb=[b for s in oTb for b in s.get(SAMP0, T)])
            rs = [self.take_slot(('row', 'w_xo', l, h * 128)) for h in range(4)]
            for d in range(NCH):
                for ci, (c0, n) in enumerate(ctiles):
                    ps = self.pdslot(n)

                    def fn():
                        ins = None
                        for h in range(4):
                            ins = PE.matmul(ps[0], lhsT=self.ring[:, rs[h], d * 128:(d + 1) * 128], rhs=oT[:, h, c0:c0 + n],
                                            start=(h == 0), stop=(h == 3))
                        return ins
                    self.op(self.PE, fn, Rb=[self.ringb[x] for x in rs] + [b for s in oTb for b in s.get(c0, c0 + n)], Wb=ps[1])
                    self.resid_add(d, c0, n, ps, scale=1.0)

    def phase_swa(self):
        A, V, PE = self.ACT.h, self.DVE.h, self.PE.h
        ct = CT_ALL
        sc = 0.125
        LAST0 = SAMP0 - 128
        with ExitStack() as es:
            with ExitStack() as es2:
                self.norm(GV['mix_1'], ct, es2)
            snap = self.snapshot()
            cs = self.sb("cs", [128, 2, T], F32, es)
            msk = self.sb("msk", [128, 2, 256], BF16, es)
            idn = self.sb("idn", [128, 128], BF16, es)
            kT = self.sb("kT", [128, T], BF16, es)
            Vp = self.sb("Vp", [128, 9, 192], BF16, es)
            qc = self.sb("qc", [128, 2, T], BF16, es)
            qs = self.sb("qs", [128, 4, NSAMP], BF16, es)
            oT = self.sb("oTs", [128, 4, T], BF16, es)
            r1 = self.sb("r1", [128, 2, 512], F32, es)
            r2 = self.sb("r2", [128, 2, 512], F32, es)
            PT = self.sb("PTw", [128, 2, 512], BF16, es)
            dn = self.sb("dn", [128, 2, 128], F32, es)
            PTs = self.sb("PTss", [128, 128], BF16, es)
            dns = self.sb("dns", [128, 64], F32, es)
            kpst = self.sb("kpst", [128, 4, 128], F32, es)
            vpst = self.sb("vpst", [128, 4, 128], F32, es)
            ksst = self.sb("ksst", [128, 4, NSAMP], F32, es)
            vsst = self.sb("vsst", [NSAMP, 4, 128], F32, es)
            vnb = self.sb("vnb", [NSAMP, 128], BF16, es)
            b_cs, b_msk, b_kT, b_Vp, b_qs, b_PTs, b_dns, b_kpst, b_vpst, b_ksst, b_vsst, b_vnb = (Buf(snap) for _ in range(12))
            qcb = [Buf(snap), Buf(snap)]
            oTb = [Seg(SEGB, snap) for _ in range(4)]
            r1b = [Buf(snap), Buf(snap)]
            r2b = [Buf(snap), Buf(snap)]
            PTb = [[Buf(snap), Buf(snap)], [Buf(snap), Buf(snap)]]
            dnb = [Buf(snap), Buf(snap)]
            self.dma(self.SP, out=cs[:, :, :], in_=self.d_cs.rearrange("p (a t) -> p a t", a=2), Wb=[b_cs])
            self.dma(self.SP, out=msk[:, :, :], in_=self.d_mask.rearrange("p (a t) -> p a t", a=2), Wb=[b_msk])
            self.dma(self.SP, out=idn[:, :], in_=self.d_ident[:, :], Wb=[b_msk])

            def fz():
                V.memset(Vp[:, :, :], 0.0)
                return V.memset(oT[:, :, :], 0.0)
            self.op(self.DVE, fz, Wb=[b_Vp] + [b for s in oTb for b in s.bufs])
            if DBG == 3:
                raise _Stop()
            self.rope_n = 0

            def rope(pn, ps_, outs):
                for ci, (c0, n) in enumerate(ct):
                    p = self.rope_n % 2
                    self.rope_n += 1
                    self.op(self.DVE, lambda: V.tensor_tensor(out=r1[:, p, 0:n], in0=pn[ci][0], in1=cs[:, 0, c0:c0 + n], op=ALU.mult),
                            Rb=pn[ci][1] + [b_cs], Wb=[r1b[p]])
                    self.op(self.DVE, lambda: V.tensor_tensor(out=r2[:, p, 0:n], in0=ps_[ci][0], in1=cs[:, 1, c0:c0 + n], op=ALU.mult),
                            Rb=ps_[ci][1] + [b_cs], Wb=[r2b[p]])
                    self.op(self.DVE, lambda: V.tensor_tensor(out=r1[:, p, 0:n], in0=r1[:, p, 0:n], in1=r2[:, p, 0:n], op=ALU.add),
                            Rb=[r1b[p], r2b[p]], Wb=[r1b[p]])
                    for (dst, bufs, lo_, hi_) in outs:
                        a, b = max(lo_, c0), min(hi_, c0 + n)
                        if a >= b:
                            continue
                        self.op(self.ACT, lambda: A.activation(out=dst(a, b), in_=r1[:, p, a - c0:b - c0], func=AF.Copy),
                                Rb=[r1b[p]], Wb=bufs)

            pa = [self.PA(ci, n) for ci, (c0, n) in enumerate(ct)]
            pbb = [self.PB(ci, n) for ci, (c0, n) in enumerate(ct)]
            unit = 0
            for jj in range(4):
                r = self.take_slot(('col', 'w_qkv', 0, ('k', jj, 0)))
                self.proj(pa, r, ct, fine=(jj == 0))
                r = self.take_slot(('col', 'w_qkv', 0, ('k', jj, 1)))
                self.proj(pbb, r, ct)
                rope(pa, pbb, [
                    (lambda a, b: kT[:, a:b], [b_kT], 0, T),
                    (lambda a, b: kpst[:, jj, a - LAST0:b - LAST0], [b_kpst], LAST0, SAMP0),
                    (lambda a, b: ksst[:, jj, a - SAMP0:b - SAMP0], [b_ksst], SAMP0, T),
                ])
                if DBG == 4:
                    raise _Stop()
                r = self.take_slot(('col', 'w_qkv', 0, ('v', jj)))
                sv = self.slot3(r, 16)
                pv = [self.bank(4, 0, 512), self.bank(5, 0, 512), self.bank(7, 0, 256)]

                def vdst(blk):
                    return pv[blk // 4][0][:, (blk % 4) * 128:(blk % 4) * 128 + 128]
                for blk in range(10):
                    c0 = 16 + 128 * blk if blk < 9 else SAMP0
                    m = 128 if blk < 9 else NSAMP

                    def fn():
                        ins = None
                        for k in range(NCH):
                            ins = PE.matmul(vdst(blk)[0:m, :], lhsT=self.hT[:, k, c0:c0 + m], rhs=sv[:, k, :],
                                            start=(k == 0), stop=(k == NCH - 1))
                        return ins
                    self.op(self.PE, fn, Rb=[self.ringb[r]] + self.hTb, Wb=pv[blk // 4][1])
                for grp in range(3):
                    nb = 4 if grp < 2 else 1
                    src = pv[grp][0][:, 0:nb * 128].rearrange("p (b j d) -> p b j d", b=nb, j=2)
                    dstv = Vp[:, 4 * grp:4 * grp + nb, :].rearrange("p b (j d) -> p b j d", j=3)[:, :, 0:3:2, :]
                    self.op(self.ACT, lambda: A.activation(out=dstv, in_=src, func=AF.Copy), Rb=pv[grp][1], Wb=[b_Vp])
                self.op(self.DVE, lambda: V.tensor_copy(out=vpst[:, jj, :], in_=vdst(8)), Rb=pv[2][1], Wb=[b_vpst])
                self.op(self.DVE, lambda: V.tensor_copy(out=vsst[:, jj, :], in_=vdst(9)[0:NSAMP, :]), Rb=pv[2][1], Wb=[b_vsst])
                self.op(self.DVE, lambda: V.tensor_copy(out=vnb[:, :], in_=vdst(9)[0:NSAMP, :]), Rb=pv[2][1], Wb=[b_vnb])
                if DBG == 5:
                    raise _Stop()
                for g in range(4):
                    qp = g % 2
                    r = self.take_slot(('col', 'w_qkv', 0, ('q', jj, g, 0)))
                    self.proj(pa, r, ct)
                    r = self.take_slot(('col', 'w_qkv', 0, ('q', jj, g, 1)))
                    self.proj(pbb, r, ct)
                    rope(pa, pbb, [
                        (lambda a, b: qc[:, qp, a:b], [qcb[qp]], 0, T),
                        (lambda a, b: qs[:, g, a - SAMP0:b - SAMP0], [b_qs], SAMP0, T),
                    ])
                    if DBG == 61:
                        raise _Stop()
                    cidx = jj * 4 + g

                    def stage1(n_):
                        par = n_ % 2
                        q0 = HALO + 128 * n_
                        pS2 = [self.bank(2 * par, 0, 256), self.bank(2 * par + 1, 0, 256)]
                        mv = 1 if n_ == 0 else 0

                        def fs():
                            ins = None
                            for hd in range(2):
                                ins = PE.matmul(pS2[hd][0], lhsT=idn[:, :], rhs=msk[:, mv, :], start=True, stop=False)
                            for kb in range(2):
                                for hd in range(2):
                                    b0 = 64 * hd
                                    k0 = 16 + 128 * (n_ + kb)
                                    ins = PE.matmul(pS2[hd][0][:, kb * 128:kb * 128 + 128], lhsT=kT[b0:b0 + 64, k0:k0 + 128],
                                                    rhs=qc[b0:b0 + 64, qp, q0:q0 + 128], start=False, stop=(kb == 1))
                            return ins
                        self.op(self.PE, fs, Rb=[b_kT, qcb[qp], b_msk], Wb=pS2[0][1] + pS2[1][1])
                        for hd in range(2):
                            self.op(self.ACT, lambda: A.activation(out=PT[:, par, hd * 256:hd * 256 + 256], in_=pS2[hd][0], func=AF.Exp, scale=sc),
                                    Rb=pS2[hd][1], Wb=[PTb[par][hd]])

                    def stage2(n_):
                        par = n_ % 2
                        q0 = HALO + 128 * n_
                        pod = self.bank(4 + par, 0, 256)

                        def fo():
                            ins = None
                            i = 0
                            for hd in range(2):
                                for kb in range(2):
                                    o0 = (hd * 2 + kb) * 128
                                    ins = PE.matmul(pod[0][:, 0:128], lhsT=Vp[:, n_ + kb, 64 * hd:64 * hd + 128], rhs=PT[:, par, o0:o0 + 128],
                                                    start=(i == 0), stop=(i == 3))
                                    i += 1
                            i = 0
                            for hd in range(2):
                                for kb in range(2):
                                    o0 = (hd * 2 + kb) * 128
                                    ins = PE.matmul(pod[0][:, 128:256], lhsT=self.ones[:, 64 * hd:64 * hd + 128], rhs=PT[:, par, o0:o0 + 128],
                                                    start=(i == 0), stop=(i == 3))
                                    i += 1
                            return ins
                        self.op(self.PE, fo, Rb=[b_Vp, self.b_ones] + PTb[par], Wb=pod[1])
                        self.op(self.ACT, lambda: A.activation(out=dn[:, par, :], in_=pod[0][:, 128:256], func=AF.Ln,
                                                               bias=self.esink[:, cidx:cidx + 1]), Rb=pod[1] + [self.b_esink], Wb=[dnb[par]])
                        self.op(self.ACT, lambda: A.activation(out=dn[:, par, :], in_=dn[:, par, :], func=AF.Exp, scale=-1.0),
                                Rb=[dnb[par]], Wb=[dnb[par]])
                        self.op(self.DVE, lambda: V.tensor_tensor(out=oT[:, g, q0:q0 + 128], in0=pod[0][:, 0:128], in1=dn[:, par, :], op=ALU.mult),
                                Rb=pod[1] + [dnb[par]], Wb=oTb[g].get(q0, q0 + 128))
                    stage1(0)
                    for n_ in range(8):
                        if n_ + 1 < 8:
                            stage1(n_ + 1)
                        stage2(n_)
                    if DBG == 6:
                        raise _Stop()
                rk = self.take_slot(('sk', jj))
                kv = self.ring[:, rk, :].rearrange("p (b s) -> p b s", b=NSAMP)
                self.op(self.DVE, lambda: V.tensor_copy(out=kv[:, :, 0], in_=kT[:, SAMP0:T]), Rb=[b_kT, self.ringb[rk]], Wb=[self.ringb[rk]])
                pSs2 = [self.bank(6, 0, 64), self.bank(7, 0, 64)]

                def fk():
                    ins = None
                    for g in range(4):
                        for b in range(NSAMP):
                            for hd in range(2):
                                b0 = 64 * hd
                                col = g * NSAMP + b
                                ins = PE.matmul(pSs2[hd][0][:, col:col + 1], lhsT=kv[b0:b0 + 64, b, :], rhs=qs[b0:b0 + 64, g, b:b + 1],
                                                start=True, stop=True)
                    return ins
                self.op(self.PE, fk, Rb=[self.ringb[rk], b_qs], Wb=pSs2[0][1] + pSs2[1][1])

                def fes():
                    ins = None
                    for hd in range(2):
                        ins = A.activation(out=PTs[:, hd * 64:hd * 64 + 64], in_=pSs2[hd][0], func=AF.Exp, scale=sc)
                    return ins
                self.op(self.ACT, fes, Rb=pSs2[0][1] + pSs2[1][1], Wb=[b_PTs])
                rvs = [self.take_slot(('sv', jj, hf)) for hf in range(2)]
                pos_ = self.bank(5, 0, 64)
                pds_ = self.bank(4, 0, 64)
                for hf in range(2):
                    vv = self.ring[:, rvs[hf], 0:8 * 192].rearrange("p (b n) -> p b n", b=8)
                    dst = self.ring[0:1, rvs[hf], 0:8 * 192].rearrange("p (b n) -> p b n", b=8)
                    for j2 in range(2):
                        self.dma(self.SP, out=dst[:, :, 128 * j2:128 * j2 + 64], in_=vnb[8 * hf:8 * hf + 8, 64 * j2:64 * j2 + 64],
                                 Rb=[b_vnb], Wb=[self.ringb[rvs[hf]]])

                    def fv():
                        ins = None
                        for g in range(4):
                            for b8 in range(8):
                                b = 8 * hf + b8
                                for hd in range(2):
                                    col = hd * 64 + g * NSAMP + b
                                    ins = PE.matmul(pos_[0][:, g * NSAMP + b:g * NSAMP + b + 1], lhsT=vv[:, b8, 64 * hd:64 * hd + 128],
                                                    rhs=PTs[:, col:col + 1], start=(hd == 0), stop=(hd == 1))
                        return ins
                    self.op(self.PE, fv, Rb=[self.ringb[rvs[hf]], b_PTs], Wb=pos_[1])

                def fd():
                    ins = None
                    for g in range(4):
                        for hd in range(2):
                            c_ = hd * 64 + g * NSAMP
                            ins = PE.matmul(pds_[0][:, g * NSAMP:(g + 1) * NSAMP], lhsT=self.ones[:, 64 * hd:64 * hd + 128],
                                            rhs=PTs[:, c_:c_ + NSAMP], start=(hd == 0), stop=(hd == 1))
                    return ins
                self.op(self.PE, fd, Rb=[b_PTs, self.b_ones], Wb=pds_[1])

                def fdn():
                    ins = None
                    for g in range(4):
                        cidx = jj * 4 + g
                        ins = V.tensor_scalar(out=dns[:, g * NSAMP:(g + 1) * NSAMP], in0=pds_[0][:, g * NSAMP:(g + 1) * NSAMP],
                                              scalar1=self.esink[:, cidx:cidx + 1], scalar2=None, op0=ALU.add)
                    return ins
                self.op(self.DVE, fdn, Rb=pds_[1] + [self.b_esink], Wb=[b_dns])
                self.op(self.DVE, lambda: V.reciprocal(out=dns[:, :], in_=dns[:, :]), Rb=[b_dns], Wb=[b_dns])
                self.op(self.DVE, lambda: V.tensor_tensor(out=oT[:, :, SAMP0:T], in0=pos_[0].rearrange("p (g b) -> p g b", g=4),
                                                          in1=dns[:, :].rearrange("p (g b) -> p g b", g=4), op=ALU.mult),
                        Rb=pos_[1] + [b_dns], Wb=[b for s in oTb for b in s.get(SAMP0, T)])
                if DBG == 7:
                    raise _Stop()
                rs = [self.take_slot(('rowsel', 'w_o', 0, (jj, g))) for g in range(4)]
                for d in range(NCH):
                    for ci, (c0, n) in enumerate(CT_MAIN):
                        ps = self.pdslot(n) if False else self.bank(6, 0, n) if False else self.pdslot_swa(n)

                        def fn():
                            ins = None
                            for g in range(4):
                                ins = PE.matmul(ps[0], lhsT=self.ring[:, rs[g], d * 128:(d + 1) * 128], rhs=oT[:, g, c0:c0 + n],
                                                start=(g == 0), stop=(g == 3))
                            return ins
                        self.op(self.PE, fn, Rb=[self.ringb[x] for x in rs] + [b for s in oTb for b in s.get(c0, c0 + n)], Wb=ps[1])
                        self.resid_add(d, c0, n, ps, scale=1.0)
            self.dma(self.SP, out=self.o_kp.rearrange("p (j t) -> p j t", j=4), in_=kpst[:, :, :], Rb=[b_kpst])
            self.dma(self.SP, out=self.o_vp.rearrange("p (j t) -> p j t", j=4), in_=vpst[:, :, :], Rb=[b_vpst])
            self.dma(self.SP, out=self.o_ksn.rearrange("p (j t) -> p j t", j=4), in_=ksst[:, :, :], Rb=[b_ksst])
            self.dma(self.SP, out=self.o_vsn.rearrange("p (j t) -> p j t", j=4), in_=vsst[:, :, :], Rb=[b_vsst])

    def pdslot_swa(self, n):
        i = (4, 5)[self.pd_rr % 2]
        self.pd_rr += 1
        return self.bank(i, 0, n)

    def phase_final(self):
        V = self.DVE.h
        ct = CT_MAIN
        with ExitStack() as es:
            rbc, rbcb = self.norm(GV['final'], ct, es, make_h=False)
            snap = self.snapshot()
            yst = self.sb("yst", [128, 2, NOUT], F32, es)
            yb = [Buf(snap), Buf(snap)]
            gi = GV['final']
            yv = self.o_yT.rearrange("p (k t) -> p k t", k=NCH)
            for k in range(NCH):
                p = k % 2
                self.op(self.DVE, lambda: V.scalar_tensor_tensor(
                    out=yst[:, p, :], in0=self.xT[:, k, HALO:T], scalar=self.gv[:, gi, k:k + 1], in1=rbc[:, HALO:T],
                    op0=ALU.mult, op1=ALU.mult), Rb=self.xb(k, HALO, NOUT) + rbcb.get(HALO, T) + [self.b_gv], Wb=[yb[p]])
                self.dma(self.SP, out=yv[:, k, :], in_=yst[:, p, :], Rb=[yb[p]])

    def finish(self):
        SP = self.SP
        for k in self.spsem:
            if self.dval[k] > 0 and SP.known.get(k, 0) < self.dval[k]:
                SP.h.wait_ge(self.sems[k], self.dval[k])
        for E in (self.PE, self.ACT, self.DVE):
            SP.h.wait_ge(self.sems[E.key], E.cnt)
        for k in self.rsem:
            if self.dval[k] > 0:
                self.POOL.h.wait_ge(self.sems[k], self.dval[k])


NS_MAX = 668
DBG = 0


class _Stop(Exception):
    pass
_CACHE = {}


def build_program(stop=None, ns=None):
    key = (stop, ns)
    if key in _CACHE:
        return _CACHE[key]
    P = Prog()
    P.declare()
    P.NS = ns if ns is not None else NS_MAX
    P.wstream = P.dram("wstream", [P.NS, 128, 2048], F32, "ExternalInput")
    phases = [P.phase_init, P.phase_memkv, lambda: P.phase_ffn(0, 1, CT_ALL), P.phase_pool,
              lambda: P.phase_xattn(0, CT_ALL), lambda: P.phase_ffn(0, 2, CT_ALL), lambda: P.phase_ffn(1, 1, CT_ALL),
              P.phase_swa, lambda: P.phase_xattn(1, CT_MAIN), lambda: P.phase_ffn(1, 2, CT_MAIN), P.phase_final]
    for i, ph in enumerate(phases):
        if stop is not None and i >= stop:
            break
        try:
            ph()
        except _Stop:
            break
    P.finish()
    assert len(P.slots) <= P.NS or stop is not None, len(P.slots)
    _CACHE[key] = P
    return P


def _col_tile(W, cols):
    return np.ascontiguousarray(W[:, cols].reshape(NCH, 128, 128).transpose(1, 0, 2)).reshape(128, 2048)


def _qkv_cols(spec):
    kind = spec[0]
    if kind == 'k':
        _, jj, sw = spec
        base = 2048 + jj * 128
        heads = [base, base + 64]
    elif kind == 'v':
        _, jj = spec
        return np.arange(2560 + jj * 128, 2560 + jj * 128 + 128)
    else:
        _, jj, g, sw = spec
        heads = [(8 * jj + g) * 64, (8 * jj + 4 + g) * 64]
    cols = []
    for hb in heads:
        idx = np.arange(64)
        if sw:
            idx = (idx + 32) % 64
        cols.append(hb + idx)
    return np.concatenate(cols)


def _common_slots(slots, inp):
    ws = np.zeros((max(len(slots), 1), 128, 2048), np.float32)
    percore = []
    for s, d in enumerate(slots):
        kind = d[0]
        if kind == 'col':
            _, name, l, c = d
            W = inp[name][l]
            cols = _qkv_cols(c) if name == 'w_qkv' else np.arange(c, c + 128)
            ws[s] = _col_tile(W, cols)
        elif kind == 'row':
            _, name, l, r0 = d
            ws[s] = inp[name][l][r0:r0 + 128, :]
        elif kind == 'rowsel':
            _, name, l, (jj, g) = d
            rows = np.concatenate([np.arange((8 * jj + g) * 64, (8 * jj + g) * 64 + 64),
                                   np.arange((8 * jj + 4 + g) * 64, (8 * jj + 4 + g) * 64 + 64)])
            ws[s] = inp[name][l][rows, :]
        elif kind == 'pool':
            g = d[1]
            ws[s] = inp['w_pool'][0][g].reshape(4, 128, 512).transpose(1, 0, 2).reshape(128, 2048)
        else:
            percore.append((s, d))
    return ws, percore


def _fill_percore(ws, percore, inp, c):
    b0 = NSAMP * c
    for s, d in percore:
        kind = d[0]
        if kind == 'xk':
            _, l, bp = d
            a = inp['cache_mem_k'][l, b0 + 2 * bp:b0 + 2 * bp + 2]
            ws[s] = a.transpose(3, 0, 2, 1).reshape(128, 2048)
        elif kind == 'xv':
            _, l, bp = d
            a = inp['cache_mem_v'][l, b0 + 2 * bp:b0 + 2 * bp + 2]
            ws[s] = a.reshape(2, 2, 128, 512).transpose(2, 0, 1, 3).reshape(128, 2048)
        elif kind == 'sk':
            _, jj = d
            a = inp['cache_swa_k'][0, b0:b0 + NSAMP, :, 2 * jj:2 * jj + 2, :]
            ws[s] = a.transpose(2, 3, 0, 1).reshape(128, 2048)
        elif kind == 'sv':
            _, jj, hf = d
            a = inp['cache_swa_v'][0, b0 + 8 * hf:b0 + 8 * hf + 8, :, 2 * jj:2 * jj + 2, :]
            t = np.zeros((128, 8, 192), np.float32)
            t[:, :, 0:64] = a[:, :, 0, :].transpose(1, 0, 2)
            t[:, :, 128:192] = a[:, :, 1, :].transpose(1, 0, 2)
            ws[s] = 0.0
            ws[s][:, 0:8 * 192] = t.reshape(128, 8 * 192)


def _fm(a):
    return np.ascontiguousarray(a.reshape(a.shape[0], NCH, 128).transpose(2, 1, 0))


def _const_tables(c):
    pos = np.zeros(T, np.float64)
    pos[:SAMP0] = c * NMAIN - HALO + np.arange(SAMP0)
    pos[SAMP0:] = PAST
    p = np.arange(128)
    dd = p % 64
    inv = (10000.0 ** (-(dd % 32).astype(np.float32) / 32.0)).astype(np.float32)
    ang = pos.astype(np.float32)[None, :] * inv[:, None]
    cs = np.zeros((128, 2, T), np.float32)
    cs[:, 0, :] = np.cos(ang)
    sn = np.sin(ang)
    cs[:, 1, :] = np.where((dd < 32)[:, None], -sn, sn)
    ic = np.ones((4, T), np.float32)
    for g, w in enumerate(POOLW):
        cnt = np.minimum(float(w), np.maximum(pos[:SAMP0] + 1.0, 1.0))
        ic[g, :SAMP0] = (1.0 / cnt).astype(np.float32)
    icnt = np.broadcast_to(ic[None], (128, 4, T)).reshape(128, 4 * T)
    k = np.arange(128)[:, None]
    q = np.arange(128)[None, :]
    m = np.zeros((128, 2, 2, 128), np.float32)
    m[:, 0, 0, :] = (k > q)
    m[:, 0, 1, :] = (k <= q)
    m[:, 1] = m[:, 0]
    if c == 0:
        m[:, 1, 0, :] = 0.0
    madd = np.where(m > 0.5, 0.0, -30000.0).astype(np.float32)
    return cs.reshape(128, 2 * T), np.ascontiguousarray(icnt), madd.reshape(128, 512).astype(NPBF)


def prep_common(inp, P):
    ws_common, percore = _common_slots(P.slots, inp)
    gv = np.zeros((128, NGV, NCH), np.float32)
    vecs = {'ffn1_0': inp['g_ffn1'][0], 'mix_0': inp['g_mix'][0], 'xq_0': inp['g_xq'][0], 'ffn2_0': inp['g_ffn2'][0],
            'ffn1_1': inp['g_ffn1'][1], 'mix_1': inp['g_mix'][1], 'xq_1': inp['g_xq'][1], 'ffn2_1': inp['g_ffn2'][1],
            'final': inp['g_final'], 'mem_0': inp['g_mem'][0], 'mem_1': inp['g_mem'][1], 'pscale': inp['pool_scale'][0]}
    for name, v in vecs.items():
        gv[:, GV[name], :] = v.reshape(NCH, 128).T
    sinks = inp['sinks'][0]
    sinkT = np.zeros((128, 16), np.float32)
    for jj in range(4):
        for g in range(4):
            sinkT[0:64, jj * 4 + g] = sinks[8 * jj + g]
            sinkT[64:128, jj * 4 + g] = sinks[8 * jj + 4 + g]
    memT = _fm(inp['mem_prompt'][0]).reshape(128, NCH * 256)
    return dict(ws=ws_common, percore=percore, gv=gv, sinkT=sinkT, memT=memT)


def core_inputs(inp, cm, c):
    xp = inp['x_prompt'][0]
    xs = inp['x_sample'][:, 0, :]
    tok = np.zeros((T, D), np.float32)
    lo = c * NMAIN - HALO
    a = max(lo, 0)
    tok[a - lo:SAMP0] = xp[a:(c + 1) * NMAIN]
    tok[SAMP0:] = xs[NSAMP * c:NSAMP * (c + 1)]
    cs, icnt, mask = _const_tables(c)
    st = inp['state_pool'][0, NSAMP * c:NSAMP * (c + 1)]
    stT = np.ascontiguousarray(st.reshape(NSAMP, 15, NCH, 128).transpose(3, 2, 0, 1)).reshape(128, NCH * NSAMP * 15)
    ws = cm['ws'].copy()
    _fill_percore(ws, cm['percore'], inp, c)
    return {
        'xT': _fm(tok).reshape(128, NCH * T), 'memT': cm['memT'], 'stT': stT,
        'stN': np.ascontiguousarray(st).reshape(NSAMP, 15 * D),
        'ckN': np.ascontiguousarray(inp['cache_swa_k'][0, NSAMP * c:NSAMP * (c + 1)]).reshape(NSAMP, 128 * 512),
        'cvN': np.ascontiguousarray(inp['cache_swa_v'][0, NSAMP * c:NSAMP * (c + 1)]).reshape(NSAMP, 128 * 512),
        'gvec': cm['gv'].reshape(128, NGV * NCH), 'sinkT': cm['sinkT'], 'cs': cs, 'icnt': icnt, 'maskc': mask,
        'identc': np.eye(128, dtype=np.float32).astype(NPBF), 'wstream': ws,
    }


def kernel(**inp):
    inp = {k: np.asarray(v) for k, v in inp.items()}
    P = build_program()
    cm = prep_common(inp, P)
    in_maps = [core_inputs(inp, cm, c) for c in range(NCORES)]
    res = run_bass_kernel_spmd(P.nc, in_maps, core_ids=list(range(NCORES)))
    out = res.results
    f32 = np.float32
    y_prompt = np.zeros((1, 8192, D), f32)
    y_sample = np.zeros((128, 1, D), f32)
    pool_s = np.zeros((1, 128, 15, D), f32)
    k_s = np.zeros((1, 128, 128, 8, 64), f32)
    v_s = np.zeros((1, 128, 128, 8, 64), f32)
    for c in range(NCORES):
        o = out[c]
        yT = o['yT'].reshape(128, NCH, NOUT)
        yy = yT.transpose(2, 1, 0).reshape(NOUT, D)
        y_prompt[0, c * NMAIN:(c + 1) * NMAIN] = yy[:NMAIN]
        y_sample[NSAMP * c:NSAMP * (c + 1), 0] = yy[NMAIN:]
        sl = slice(NSAMP * c, NSAMP * (c + 1))
        pool_s[0, sl, 0:14] = o['pool_s_old'].reshape(NSAMP, 14, D)
        pool_s[0, sl, 14] = o['pool_s_new'].reshape(128, NCH, NSAMP).transpose(2, 1, 0).reshape(NSAMP, D)
        k_s[0, sl, 0:127] = o['swa_k_s_old'].reshape(NSAMP, 127, 8, 64)
        v_s[0, sl, 0:127] = o['swa_v_s_old'].reshape(NSAMP, 127, 8, 64)
        kn = o['swa_k_s_new'].reshape(2, 64, 4, NSAMP)
        k_s[0, sl, 127] = kn.transpose(3, 2, 0, 1).reshape(NSAMP, 8, 64)
        v_s[0, sl, 127] = o['swa_v_s_new'].reshape(NSAMP, 8, 64)
    o7 = out[NCORES - 1]
    pool_p = o7['pool_p'].reshape(128, NCH, 15).transpose(2, 1, 0).reshape(1, 1, 15, D).astype(f32)
    k_p = o7['swa_k_p'].reshape(2, 64, 4, 128).transpose(3, 2, 0, 1).reshape(1, 1, 128, 8, 64).astype(f32)
    v_p = o7['swa_v_p'].reshape(1, 1, 128, 8, 64).astype(f32)
    o0 = out[0]
    mem_k = o0['mem_k'].reshape(128, 2, 4, 256).transpose(1, 3, 2, 0).reshape(2, 1, 256, 4, 128).astype(f32)
    mem_v = o0['mem_v'].reshape(128, 2, 2, 512).transpose(1, 2, 0, 3).reshape(2, 1, 256, 4, 128).astype(f32)
    return (y_prompt, y_sample, np.ascontiguousarray(pool_p), pool_s, np.ascontiguousarray(k_p), np.ascontiguousarray(v_p),
            k_s, v_s, np.ascontiguousarray(mem_k), np.ascontiguousarray(mem_v))
```

```python
import math
from contextlib import ExitStack

import ml_dtypes
import numpy as np

import concourse.bass as bass
import concourse.mybir as mybir
from concourse.bass_utils import run_bass_kernel_spmd

F32 = mybir.dt.float32
BF16 = mybir.dt.bfloat16
AF = mybir.ActivationFunctionType
ALU = mybir.AluOpType
AX = mybir.AxisListType
NPBF = ml_dtypes.bfloat16

D = 2048
NCH = 16
DFF = 5632
NF = 44
T = 1184
HALO = 144
NMAIN = 1024
NSAMP = 16
SAMP0 = 1168
NOUT = 1040
R = 9
NCORES = 8
CT_ALL = [(0, 512), (512, 512), (1024, 160)]
CT_MAIN = [(144, 448), (592, 448), (1040, 144)]
SEGB = [0, 144, 512, 592, 1024, 1040, 1168, 1184]
GV = {'ffn1_0': 0, 'mix_0': 1, 'xq_0': 2, 'ffn2_0': 3, 'ffn1_1': 4, 'mix_1': 5, 'xq_1': 6, 'ffn2_1': 7,
      'final': 8, 'mem_0': 9, 'mem_1': 10, 'pscale': 11}
NGV = 12
EPS = 1e-6
PAST = 8192
POOLW = (2, 4, 8, 16)


class Buf:
    __slots__ = ('w', 'r', 'excl')

    def __init__(self, snap=None):
        self.w = None
        self.r = dict(snap) if snap else {}
        self.excl = False


class Seg:
    def __init__(self, bounds, snap=None):
        self.b = bounds
        self.bufs = [Buf(snap) for _ in bounds[:-1]]

    def get(self, lo, hi):
        return [self.bufs[i] for i in range(len(self.b) - 1) if self.b[i] < hi and self.b[i + 1] > lo]


class Eng:
    def __init__(self, name, h, key):
        self.name, self.h, self.key = name, h, key
        self.cnt = 0
        self.known = {}


class Prog:
    def __init__(self):
        nc = bass.Bass("TRN2", target_bir_lowering=False)
        self.nc = nc
        self.es = ExitStack()
        self.sems = []
        self.dval = {}
        self.slots = []
        self.uid = 0
        self.PE = Eng('PE', nc.tensor, self.new_sem('s_pe'))
        self.ACT = Eng('ACT', nc.scalar, self.new_sem('s_act'))
        self.DVE = Eng('DVE', nc.vector, self.new_sem('s_dve'))
        self.POOL = Eng('POOL', nc.gpsimd, None)
        self.SP = Eng('SP', nc.sync, None)
        self.rsem = [self.new_sem(f's_ring{i}') for i in range(R)]
        self.spsem = [self.new_sem(f's_sp{i}') for i in range(16)]
        self.sp_rr = 0
        self.ring_ptr = 0
        self.held = set()
        for k in self.rsem + self.spsem:
            self.dval[k] = 0

    def new_sem(self, name):
        h = self.es.enter_context(self.nc.semaphore(name))
        self.sems.append(h)
        return len(self.sems) - 1

    def dram(self, name, shape, dt, kind):
        return self.nc.dram_tensor(name, list(shape), dt, kind=kind).ap()

    def sb(self, name, shape, dt, es=None):
        self.uid += 1
        return (es or self.es).enter_context(self.nc.sbuf_tensor(f"{name}_{self.uid}", list(shape), dt))

    def snapshot(self):
        d = {}
        for E in (self.PE, self.ACT, self.DVE):
            if E.cnt > 0:
                d[E.key] = E.cnt
        for k in self.spsem:
            if self.dval[k] > 0:
                d[k] = self.dval[k]
        return d

    def _deps(self, E, Rb, Wb):
        need = {}

        def add(k, v):
            if need.get(k, 0) < v:
                need[k] = v
        for b in Rb:
            if b.w is not None:
                add(*b.w)
        for b in Wb:
            if b.w is not None:
                add(*b.w)
            for k, v in b.r.items():
                add(k, v)
        for k, v in need.items():
            if k == E.key:
                if E.name == 'PE':
                    continue
            if E.known.get(k, 0) >= v:
                continue
            E.h.wait_ge(self.sems[k], v)
            E.known[k] = v

    def _mark(self, tok, Rb, Wb):
        for b in Rb:
            if b.r.get(tok[0], 0) < tok[1]:
                b.r[tok[0]] = tok[1]
        for b in Wb:
            b.w = tok
            b.r = {}

    def op(self, E, fn, Rb=(), Wb=()):
        if E.name != 'PE':
            ex = [b for b in Rb if b.excl]
            if ex:
                Wb = list(Wb) + ex
                Rb = [b for b in Rb if not b.excl]
        self._deps(E, Rb, Wb)
        ins = fn()
        E.cnt += 1
        ins.then_inc(self.sems[E.key], 1)
        self._mark((E.key, E.cnt), Rb, Wb)

    def dma(self, Q, out, in_, Rb=(), Wb=(), semkey=None):
        self._deps(Q, Rb, Wb)
        if semkey is None:
            semkey = self.spsem[self.sp_rr % len(self.spsem)]
            self.sp_rr += 1
            if self.dval[semkey] > 0 and Q.known.get(semkey, 0) < self.dval[semkey]:
                Q.h.wait_ge(self.sems[semkey], self.dval[semkey])
                Q.known[semkey] = self.dval[semkey]
        self.dval[semkey] += 16
        Q.h.dma_start(out=out, in_=in_).then_inc(self.sems[semkey], 16)
        self._mark((semkey, self.dval[semkey]), Rb, Wb)

    def take_slot(self, desc, hold=False):
        s = len(self.slots)
        self.slots.append(desc)
        r = self.ring_ptr % R
        while r in self.held:
            r = (r + 1) % R
        self.ring_ptr = r + 1
        if hold:
            self.held.add(r)
        self.dma(self.POOL, out=self.ring[:, r, :], in_=self.wstream[s], Wb=(self.ringb[r],), semkey=self.rsem[r])
        return r

    def slot3(self, r, k):
        return self.ring[:, r, :].rearrange("p (k n) -> p k n", k=k)

    def PA(self, ci, n):
        if ci < 2:
            return self.pb[ci][:, 0:n], self.pbuf[ci].get(0, n)
        return self.pb[6][:, 0:n], self.pbuf[6].get(0, n)

    def PB(self, ci, n):
        if ci < 2:
            return self.pb[2 + ci][:, 0:n], self.pbuf[2 + ci].get(0, n)
        return self.pb[6][:, 160:160 + n], self.pbuf[6].get(0, 512)

    def bank(self, i, lo, hi):
        return self.pb[i][:, lo:hi], self.pbuf[i].get(lo, hi)

    def xb(self, k, c0, n):
        return self.xbufs[k].get(c0, c0 + n)

    def declare(self):
        nc = self.nc
        I, O = "ExternalInput", "ExternalOutput"
        self.d_xT = self.dram("xT", [128, NCH * T], F32, I)
        self.d_memT = self.dram("memT", [128, NCH * 256], F32, I)
        self.d_stT = self.dram("stT", [128, NCH * NSAMP * 15], F32, I)
        self.d_stN = self.dram("stN", [NSAMP, 15 * D], F32, I)
        self.d_ckN = self.dram("ckN", [NSAMP, 128 * 512], F32, I)
        self.d_cvN = self.dram("cvN", [NSAMP, 128 * 512], F32, I)
        self.d_gvec = self.dram("gvec", [128, NGV * NCH], F32, I)
        self.d_sinkT = self.dram("sinkT", [128, 16], F32, I)
        self.d_cs = self.dram("cs", [128, 2 * T], F32, I)
        self.d_icnt = self.dram("icnt", [128, 4 * T], F32, I)
        self.d_mask = self.dram("maskc", [128, 2 * 256], BF16, I)
        self.d_ident = self.dram("identc", [128, 128], BF16, I)
        self.NS_decl = None
        self.o_yT = self.dram("yT", [128, NCH * NOUT], F32, O)
        self.o_pool_p = self.dram("pool_p", [128, NCH * 15], F32, O)
        self.o_pool_sn = self.dram("pool_s_new", [128, NCH * NSAMP], F32, O)
        self.o_pool_so = self.dram("pool_s_old", [NSAMP, 14 * D], F32, O)
        self.o_kp = self.dram("swa_k_p", [128, 4 * 128], F32, O)
        self.o_vp = self.dram("swa_v_p", [128, 4 * 128], F32, O)
        self.o_ksn = self.dram("swa_k_s_new", [128, 4 * NSAMP], F32, O)
        self.o_vsn = self.dram("swa_v_s_new", [NSAMP, 4 * 128], F32, O)
        self.o_kso = self.dram("swa_k_s_old", [NSAMP, 127 * 512], F32, O)
        self.o_vso = self.dram("swa_v_s_old", [NSAMP, 127 * 512], F32, O)
        self.o_memk = self.dram("mem_k", [128, 2 * 4 * 256], F32, O)
        self.o_memv = self.dram("mem_v", [128, 2 * 2 * 512], F32, O)

        self.xT = self.sb("xT", [128, NCH, T], F32)
        self.hT = self.sb("hT", [128, NCH, T], BF16)
        self.ring = self.sb("ring", [128, R, 2048], BF16)
        self.KmT = self.sb("KmT", [128, 2, 4, 256], BF16)
        self.Vm = self.sb("Vm", [128, 2, 2, 512], BF16)
        self.gv = self.sb("gv", [128, NGV, NCH], F32)
        self.ones = self.sb("ones", [128, 192], BF16)
        self.onesf = self.sb("onesf", [128, 128], BF16)
        self.esink = self.sb("esink", [128, 16], F32)
        self.epsc = self.sb("epsc", [128, 1], F32)
        self.pb = [self.es.enter_context(nc.psum_tensor(f"pb{i}", [128, 512], F32)) for i in range(8)]
        self.pbuf = [Seg([0, 512]) for i in range(8)]
        for sg in self.pbuf:
            for b in sg.bufs:
                b.excl = True
        self.xbufs = [Seg(SEGB) for _ in range(NCH)]
        self.hTb = [Buf() for _ in range(NCH)]
        self.ringb = [Buf() for _ in range(R)]
        self.b_KmT = [Buf(), Buf()]
        self.b_Vm = [Buf(), Buf()]
        self.b_gv = Buf()
        self.b_ones = Buf()
        self.b_esink = Buf()
        self.pd_rr = 0

    def pdslot(self, n):
        i = (4, 5, 7)[self.pd_rr % 3]
        self.pd_rr += 1
        return self.bank(i, 0, n)

    def proj(self, dst, r, ctiles, src=None, srcb=None, fine=False):
        sv = self.slot3(r, 16)
        PE = self.PE.h
        src = self.hT if src is None else src
        srcb = self.hTb if srcb is None else srcb

        step = 2 if fine else NCH
        for k0 in range(0, NCH, step):
            def fn():
                ins = None
                for k in range(k0, k0 + step):
                    for ci, (c0, n) in enumerate(ctiles):
                        ins = PE.matmul(dst[ci][0], lhsT=sv[:, k, :], rhs=src[:, k, c0:c0 + n],
                                        start=(k == 0), stop=(k == NCH - 1))
                return ins
            self.op(self.PE, fn, Rb=[self.ringb[r]] + list(srcb[k0:k0 + step]), Wb=[b for d in dst for b in d[1]])

    def norm(self, gidx, ctiles, es, make_h=True):
        lo = ctiles[0][0]
        hi = ctiles[-1][0] + ctiles[-1][1]
        sq = self.sb("sq", [128, 2, T], BF16, es)
        rbc = self.sb("rbc", [128, T], F32, es)
        snap = self.snapshot()
        sqb = [Buf(snap), Buf(snap)]
        rbcb = Seg(SEGB, snap)
        A, V, PE = self.ACT.h, self.DVE.h, self.PE.h
        dst = [self.PA(ci, n) for ci, (c0, n) in enumerate(ctiles)]
        for k in range(NCH):
            p = k % 2
            if k % 2 == 0:
                self.op(self.ACT, lambda: A.activation(out=sq[:, p, lo:hi], in_=self.xT[:, k, lo:hi], func=AF.Square),
                        Rb=self.xb(k, lo, hi - lo), Wb=[sqb[p]])
            else:
                self.op(self.DVE, lambda: V.tensor_tensor(out=sq[:, p, lo:hi], in0=self.xT[:, k, lo:hi], in1=self.xT[:, k, lo:hi],
                                                          op=ALU.mult), Rb=self.xb(k, lo, hi - lo), Wb=[sqb[p]])

            def fn():
                ins = None
                for ci, (c0, n) in enumerate(ctiles):
                    ins = PE.matmul(dst[ci][0], lhsT=self.onesf[:, :], rhs=sq[:, p, c0:c0 + n],
                                    start=(k == 0), stop=(k == NCH - 1))
                return ins
            self.op(self.PE, fn, Rb=[sqb[p], self.b_ones], Wb=[b for d in dst for b in d[1]])
        for ci, (c0, n) in enumerate(ctiles):
            rb = rbcb.get(c0, c0 + n)
            self.op(self.ACT, lambda: A.activation(out=rbc[:, c0:c0 + n], in_=dst[ci][0], func=AF.Ln,
                                                   scale=1.0 / D, bias=self.epsc[:, 0:1]), Rb=dst[ci][1] + [self.b_ones], Wb=rb)
            self.op(self.ACT, lambda: A.activation(out=rbc[:, c0:c0 + n], in_=rbc[:, c0:c0 + n], func=AF.Exp, scale=-0.5),
                    Rb=rb, Wb=rb)
        if make_h:
            for k in range(NCH):
                self.op(self.DVE, lambda: V.scalar_tensor_tensor(
                    out=self.hT[:, k, lo:hi], in0=self.xT[:, k, lo:hi], scalar=self.gv[:, gidx, k:k + 1],
                    in1=rbc[:, lo:hi], op0=ALU.mult, op1=ALU.mult),
                    Rb=self.xb(k, lo, hi - lo) + rbcb.get(lo, hi) + [self.b_gv], Wb=[self.hTb[k]])
        return rbc, rbcb

    def resid_add(self, d, c0, n, ps, scale=None, scale_ap=None):
        V = self.DVE.h
        xs = self.xT[:, d, c0:c0 + n]
        sc = scale_ap if scale_ap is not None else scale
        rb = list(ps[1]) + self.xb(d, c0, n) + ([self.b_gv] if scale_ap is not None else [])
        self.op(self.DVE, lambda: V.scalar_tensor_tensor(out=xs, in0=ps[0], scalar=sc, in1=xs,
                                                         op0=ALU.mult, op1=ALU.add),
                Rb=rb, Wb=self.xb(d, c0, n))

    def phase_init(self):
        V = self.DVE.h
        self.dma(self.SP, out=self.gv[:, :, :], in_=self.d_gvec.rearrange("p (g k) -> p g k", g=NGV), Wb=[self.b_gv])
        self.dma(self.SP, out=self.esink[:, :], in_=self.d_sinkT[:, :], Wb=[self.b_esink])
        def fn():
            V.memset(self.ones[:, 0:64], 1.0)
            V.memset(self.ones[:, 64:128], 0.0)
            V.memset(self.ones[:, 128:192], 1.0)
            V.memset(self.epsc[:, :], EPS)
            return V.memset(self.onesf[:, :], 1.0)
        self.op(self.DVE, fn, Wb=[self.b_ones])
        self.op(self.ACT, lambda: self.ACT.h.activation(out=self.esink[:, :], in_=self.esink[:, :], func=AF.Exp),
                Rb=[self.b_esink], Wb=[self.b_esink])

    def phase_memkv(self):
        A, V, PE = self.ACT.h, self.DVE.h, self.PE.h
        with ExitStack() as es:
            snap = self.snapshot()
            memT = self.sb("memT", [128, NCH, 256], F32, es)
            hm = self.sb("hm", [128, NCH, 256], BF16, es)
            sq = self.sb("sqm", [128, 2, 256], BF16, es)
            rbc = self.sb("rbcm", [128, 256], F32, es)
            kst = self.sb("kst", [128, 2, 4, 256], F32, es)
            vst = self.sb("vst", [128, 2, 2, 512], F32, es)
            b_mem, b_rbc, b_kst, b_vst = Buf(snap), Buf(snap), Buf(snap), Buf(snap)
            sqb = [Buf(snap), Buf(snap)]
            hmb = [Buf(snap) for _ in range(NCH)]
            self.dma(self.SP, out=memT[:, :, :], in_=self.d_memT.rearrange("p (k m) -> p k m", k=NCH), Wb=[b_mem])
            xv = self.d_xT.rearrange("p (k t) -> p k t", k=NCH)
            for k in range(NCH):
                self.dma(self.SP, out=self.xT[:, k, :], in_=xv[:, k, :], Wb=self.xbufs[k].bufs)
            ps = self.bank(0, 0, 256)
            for k in range(NCH):
                p = k % 2
                self.op(self.ACT, lambda: A.activation(out=sq[:, p, :], in_=memT[:, k, :], func=AF.Square),
                        Rb=[b_mem], Wb=[sqb[p]])
                self.op(self.PE, lambda: PE.matmul(ps[0], lhsT=self.onesf[:, :], rhs=sq[:, p, :],
                                                   start=(k == 0), stop=(k == NCH - 1)),
                        Rb=[sqb[p], self.b_ones], Wb=ps[1])
            self.op(self.ACT, lambda: A.activation(out=rbc[:, :], in_=ps[0], func=AF.Sqrt, scale=1.0 / D, bias=EPS),
                    Rb=ps[1], Wb=[b_rbc])
            self.op(self.DVE, lambda: V.reciprocal(out=rbc[:, :], in_=rbc[:, :]), Rb=[b_rbc], Wb=[b_rbc])
            if DBG == 1:
                return
            for l in range(2):
                gi = GV[f'mem_{l}']
                for k in range(NCH):
                    self.op(self.DVE, lambda: V.scalar_tensor_tensor(
                        out=hm[:, k, :], in0=memT[:, k, :], scalar=self.gv[:, gi, k:k + 1], in1=rbc[:, :],
                        op0=ALU.mult, op1=ALU.mult), Rb=[b_mem, b_rbc, self.b_gv], Wb=[hmb[k]])
                for h in range(4):
                    r = self.take_slot(('col', 'w_xkv', l, h * 128))
                    sv = self.slot3(r, 16)
                    pk = self.bank(1 + (h % 2), 0, 256)

                    def fn():
                        ins = None
                        for k in range(NCH):
                            ins = PE.matmul(pk[0], lhsT=sv[:, k, :], rhs=hm[:, k, :], start=(k == 0), stop=(k == NCH - 1))
                        return ins
                    self.op(self.PE, fn, Rb=[self.ringb[r]] + hmb, Wb=pk[1])
                    self.op(self.ACT, lambda: A.activation(out=self.KmT[:, l, h, :], in_=pk[0], func=AF.Copy),
                            Rb=pk[1], Wb=[self.b_KmT[l]])
                    self.op(self.DVE, lambda: V.tensor_copy(out=kst[:, l, h, :], in_=pk[0]), Rb=pk[1], Wb=[b_kst])
                    if DBG == 2:
                        return
                pv = [self.bank(3, 0, 512), self.bank(4, 0, 512)]
                for h in range(4):
                    r = self.take_slot(('col', 'w_xkv', l, 512 + h * 128))
                    sv = self.slot3(r, 16)
                    for mb in range(2):
                        def fn():
                            ins = None
                            for k in range(NCH):
                                ins = PE.matmul(pv[mb][0][:, h * 128:(h + 1) * 128], lhsT=hm[:, k, mb * 128:(mb + 1) * 128],
                                                rhs=sv[:, k, :], start=(k == 0), stop=(k == NCH - 1))
                            return ins
                        self.op(self.PE, fn, Rb=[self.ringb[r]] + hmb, Wb=pv[mb][1])
                for mb in range(2):
                    self.op(self.ACT, lambda: A.activation(out=self.Vm[:, l, mb, :], in_=pv[mb][0], func=AF.Copy),
                            Rb=pv[mb][1], Wb=[self.b_Vm[l]])
                    self.op(self.DVE, lambda: V.tensor_copy(out=vst[:, l, mb, :], in_=pv[mb][0]), Rb=pv[mb][1], Wb=[b_vst])
            self.dma(self.SP, out=self.o_memk.rearrange("p (l h m) -> p l h m", l=2, h=4), in_=kst[:, :, :, :], Rb=[b_kst])
            self.dma(self.SP, out=self.o_memv.rearrange("p (l b n) -> p l b n", l=2, b=2), in_=vst[:, :, :, :], Rb=[b_vst])

    def phase_ffn(self, l, which, ctiles):
        A, V, PE = self.ACT.h, self.DVE.h, self.PE.h
        wgu, wdn = f'w_ffn{which}_gu', f'w_ffn{which}_dn'
        lo = ctiles[0][0]
        hi = ctiles[-1][0] + ctiles[-1][1]
        with ExitStack() as es:
            with ExitStack() as es2:
                self.norm(GV[f'ffn{which}_{l}'], ctiles, es2)
            snap = self.snapshot()
            act = self.sb("act", [128, 2, 4, T], BF16, es)
            sil = self.sb("sil", [128, T], F32, es)
            actb = [[[Buf(snap) for _ in range(3)] for _ in range(4)] for _ in range(2)]
            silb = [Buf(snap) for _ in range(3)]
            order = [2, 0, 1]
            pa = [self.PA(ci, n) for ci, (c0, n) in enumerate(ctiles)]
            pbb = [self.PB(ci, n) for ci, (c0, n) in enumerate(ctiles)]

            pending = []

            def drain(k):
                for _ in range(min(k, len(pending))):
                    pending.pop(0)()

            def gu(j):
                par, f = (j // 4) % 2, j % 4
                r = self.take_slot(('col', wgu, l, j * 128))
                self.proj(pa, r, ctiles, fine=(j == 0))

                for ci in order:
                    c0, n = ctiles[ci]
                    self.op(self.ACT, lambda: A.activation(out=sil[:, c0:c0 + n], in_=pa[ci][0], func=AF.Silu),
                            Rb=pa[ci][1], Wb=[silb[ci]])
                drain(6)
                r2 = self.take_slot(('col', wgu, l, DFF + j * 128))
                self.proj(pbb, r2, ctiles)

                for ci in order:
                    c0, n = ctiles[ci]
                    self.op(self.DVE, lambda: V.tensor_tensor(out=act[:, par, f, c0:c0 + n], in0=pbb[ci][0], in1=sil[:, c0:c0 + n],
                                                              op=ALU.mult), Rb=[silb[ci]] + pbb[ci][1], Wb=[actb[par][f][ci]])
                drain(6)

            def down_tasks(g):
                par = g % 2
                st = {}
                tasks = []
                ntask = NCH * len(ctiles)

                def mk(idx, d, ci):
                    def task():
                        if 'rs' not in st:
                            st['rs'] = [self.take_slot(('row', wdn, l, (4 * g + f) * 128), hold=True) for f in range(4)]
                        rs = st['rs']
                        c0, n = ctiles[ci]
                        ps = self.pdslot(n)

                        def fn():
                            ins = None
                            for f in range(4):
                                ins = PE.matmul(ps[0], lhsT=self.ring[:, rs[f], d * 128:(d + 1) * 128],
                                                rhs=act[:, par, f, c0:c0 + n], start=(f == 0), stop=(f == 3))
                            return ins
                        self.op(self.PE, fn, Rb=[self.ringb[x] for x in rs] + [actb[par][f][ci] for f in range(4)], Wb=ps[1])
                        self.resid_add(d, c0, n, ps, scale=0.5)
                        if idx == ntask - 1:
                            for x in rs:
                                self.held.discard(x)
                    return task
                i = 0
                for d in range(NCH):
                    for ci in range(len(ctiles)):
                        tasks.append(mk(i, d, ci))
                        i += 1
                return tasks
            NG = NF // 4
            for g in range(NG):
                for f in range(4):
                    gu(4 * g + f)
                drain(len(pending))
                pending.extend(down_tasks(g))
            drain(len(pending))

    def phase_pool(self):
        A, V, PE = self.ACT.h, self.DVE.h, self.PE.h
        ct = CT_ALL
        with ExitStack() as es:
            rbc, rbcb = self.norm(GV['mix_0'], ct, es, make_h=False)
            snap = self.snapshot()
            tA = self.sb("tA", [128, T], F32, es)
            tB = self.sb("tB", [128, T], F32, es)
            tC = self.sb("tC", [128, T], F32, es)
            icn = self.sb("icn", [128, T], F32, es)
            stk = self.sb("stk", [128, 2, NSAMP, 15], F32, es)
            ssum = self.sb("ssum", [128, NSAMP], F32, es)
            pst = self.sb("pst", [128, NCH, 15], F32, es)
            sst = self.sb("sst", [128, NCH, NSAMP], F32, es)
            bA, bB, bC, b_icn, b_ss, b_pst, b_sst = (Buf(snap) for _ in range(7))
            b_stk = [Buf(snap), Buf(snap)]
            gi = GV['mix_0']
            stT = self.d_stT.rearrange("p (k b r) -> p k b r", k=NCH, b=NSAMP)
            icv = self.d_icnt.rearrange("p (g t) -> p g t", g=4)
            rall = rbcb.get(0, T)
            for k in range(NCH):
                g = k // 4
                w = POOLW[g]
                if k % 4 == 0:
                    self.dma(self.SP, out=icn[:, :], in_=icv[:, g, :], Wb=[b_icn])
                sp_ = k % 2
                self.dma(self.SP, out=stk[:, sp_, :, :], in_=stT[:, k, :, :], Wb=[b_stk[sp_]])
                self.op(self.DVE, lambda: V.scalar_tensor_tensor(
                    out=tA[:, :], in0=self.xT[:, k, :], scalar=self.gv[:, gi, k:k + 1], in1=rbc[:, :],
                    op0=ALU.mult, op1=ALU.mult), Rb=self.xb(k, 0, T) + rall + [self.b_gv], Wb=[bA])
                cur, curb = tA, bA
                oth = [(tB, bB), (tC, bC)]
                st, oi = 1, 0
                while st < w:
                    nt, nb = oth[oi]
                    lo_ = 2 * st - 1
                    self.op(self.DVE, lambda: V.tensor_tensor(out=nt[:, lo_:T], in0=cur[:, lo_:T], in1=cur[:, lo_ - st:T - st],
                                                              op=ALU.add), Rb=[curb], Wb=[nb])
                    cur, curb = nt, nb
                    oi ^= 1
                    st *= 2
                nt, nb = oth[oi]
                self.op(self.DVE, lambda: V.tensor_tensor(out=nt[:, 15:SAMP0], in0=cur[:, 15:SAMP0], in1=icn[:, 15:SAMP0],
                                                          op=ALU.mult), Rb=[curb, b_icn], Wb=[nb])
                self.op(self.DVE, lambda: V.tensor_tensor(out=self.hT[:, k, 15:SAMP0], in0=nt[:, 15:SAMP0], in1=tA[:, 15:SAMP0],
                                                          op=ALU.subtract), Rb=[nb, bA], Wb=[self.hTb[k]])
                self.op(self.DVE, lambda: V.tensor_reduce(out=ssum[:, :], in_=stk[:, sp_, :, 15 - (w - 1):15], axis=AX.X, op=ALU.add),
                        Rb=[b_stk[sp_]], Wb=[b_ss])
                self.op(self.DVE, lambda: V.tensor_tensor(out=ssum[:, :], in0=ssum[:, :], in1=tA[:, SAMP0:T], op=ALU.add),
                        Rb=[b_ss, bA], Wb=[b_ss])
                self.op(self.DVE, lambda: V.scalar_tensor_tensor(out=self.hT[:, k, SAMP0:T], in0=ssum[:, :], scalar=1.0 / w,
                                                                 in1=tA[:, SAMP0:T], op0=ALU.mult, op1=ALU.subtract),
                        Rb=[b_ss, bA], Wb=[self.hTb[k]])
                self.op(self.ACT, lambda: A.activation(out=pst[:, k, :], in_=tA[:, SAMP0 - 15:SAMP0], func=AF.Copy), Rb=[bA], Wb=[b_pst])
                self.op(self.ACT, lambda: A.activation(out=sst[:, k, :], in_=tA[:, SAMP0:T], func=AF.Copy), Rb=[bA], Wb=[b_sst])
            self.dma(self.SP, out=self.o_pool_p.rearrange("p (k r) -> p k r", k=NCH), in_=pst[:, :, :], Rb=[b_pst])
            self.dma(self.SP, out=self.o_pool_sn.rearrange("p (k b) -> p k b", k=NCH), in_=sst[:, :, :], Rb=[b_sst])
            so = self.o_pool_so.rearrange("b (r d) -> b r d", r=14)
            si = self.d_stN.rearrange("b (r d) -> b r d", r=15)
            for r0 in range(0, 14, 4):
                r1 = min(r0 + 4, 14)
                self.dma(self.SP, out=so[:, r0:r1, :], in_=si[:, 1 + r0:1 + r1, :])
            for (o_, i_) in ((self.o_kso, self.d_ckN), (self.o_vso, self.d_cvN)):
                ov = o_.rearrange("b (r d) -> b r d", r=127)
                iv = i_.rearrange("b (r d) -> b r d", r=128)
                for r0 in range(0, 127, 16):
                    r1 = min(r0 + 16, 127)
                    self.dma(self.SP, out=ov[:, r0:r1, :], in_=iv[:, 1 + r0:1 + r1, :])

            gs = GV['pscale']
            for g in range(4):
                r = self.take_slot(('pool', g))
                sv = self.slot3(r, 4)
                for oc in range(4):
                    d = 4 * g + oc
                    for ci, (c0, n) in enumerate(ct):
                        ps = self.pdslot(n)

                        def fn():
                            ins = None
                            for kc in range(4):
                                ins = PE.matmul(ps[0], lhsT=sv[:, kc, oc * 128:(oc + 1) * 128], rhs=self.hT[:, 4 * g + kc, c0:c0 + n],
                                                start=(kc == 0), stop=(kc == 3))
                            return ins
                        self.op(self.PE, fn, Rb=[self.ringb[r]] + self.hTb[4 * g:4 * g + 4], Wb=ps[1])
                        self.resid_add(d, c0, n, ps, scale_ap=self.gv[:, gs, d:d + 1])

    def phase_xattn(self, l, ctiles):
        A, V, PE = self.ACT.h, self.DVE.h, self.PE.h
        sc = 1.0 / math.sqrt(128.0)
        with ExitStack() as es:
            with ExitStack() as es2:
                self.norm(GV[f'xq_{l}'], ctiles, es2)
            snap = self.snapshot()
            qT = self.sb("qT", [128, 4, T], BF16, es)
            oT = self.sb("oT", [128, 4, T], BF16, es)
            PT = self.sb("PT", [128, 2, 2, 512], BF16, es)
            rec = self.sb("rec", [128, 2, 512], F32, es)
            PTs = self.sb("PTs", [128, 128], BF16, es)
            recs = self.sb("recs", [128, 64], F32, es)
            qTb = [Buf(snap) for _ in range(4)]
            oTb = [Seg(SEGB, snap) for _ in range(4)]
            PTb = [[Buf(snap), Buf(snap)], [Buf(snap), Buf(snap)]]
            recb = [Buf(snap), Buf(snap)]
            b_PTs, b_recs = Buf(snap), Buf(snap)
            for h in range(4):
                r = self.take_slot(('col', 'w_xq', l, h * 128))
                dst = [(self.PA if h % 2 == 0 else self.PB)(ci, n) for ci, (c0, n) in enumerate(ctiles)]
                self.proj(dst, r, ctiles, fine=(h == 0))

                def fa():
                    ins = None
                    for ci, (c0, n) in enumerate(ctiles):
                        ins = A.activation(out=qT[:, h, c0:c0 + n], in_=dst[ci][0], func=AF.Copy)
                    return ins
                self.op(self.ACT, fa, Rb=[b for d in dst for b in d[1]], Wb=[qTb[h]])
            units = [(h, c0, n) for h in range(4) for (c0, n) in ctiles if c0 < SAMP0]

            def xs1(u):
                h, c0, n = units[u]
                par = u % 2
                pS = [self.bank(2 * par, 0, n), self.bank(2 * par + 1, 0, n)]

                def fs():
                    ins = None
                    for mb in range(2):
                        ins = PE.matmul(pS[mb][0], lhsT=self.KmT[:, l, h, mb * 128:(mb + 1) * 128], rhs=qT[:, h, c0:c0 + n],
                                        start=True, stop=True)
                    return ins
                self.op(self.PE, fs, Rb=[self.b_KmT[l], qTb[h]], Wb=pS[0][1] + pS[1][1])
                for mb in range(2):
                    self.op(self.ACT, lambda: A.activation(out=PT[:, par, mb, 0:n], in_=pS[mb][0], func=AF.Exp, scale=sc),
                            Rb=pS[mb][1], Wb=[PTb[par][mb]])

            def xs2(u):
                h, c0, n = units[u]
                par = u % 2
                po = self.bank(4 + par, 0, n)
                pd = self.bank(7 - par, 0, n)

                def fo():
                    ins = None
                    for mb in range(2):
                        ins = PE.matmul(po[0], lhsT=self.Vm[:, l, mb, h * 128:(h + 1) * 128], rhs=PT[:, par, mb, 0:n],
                                        start=(mb == 0), stop=(mb == 1))
                    for mb in range(2):
                        ins = PE.matmul(pd[0], lhsT=self.onesf[:, :], rhs=PT[:, par, mb, 0:n], start=(mb == 0), stop=(mb == 1))
                    return ins
                self.op(self.PE, fo, Rb=[self.b_Vm[l], self.b_ones] + PTb[par], Wb=po[1] + pd[1])
                self.op(self.ACT, lambda: A.activation(out=rec[:, par, 0:n], in_=pd[0], func=AF.Ln), Rb=pd[1], Wb=[recb[par]])
                self.op(self.ACT, lambda: A.activation(out=rec[:, par, 0:n], in_=rec[:, par, 0:n], func=AF.Exp, scale=-1.0),
                        Rb=[recb[par]], Wb=[recb[par]])
                self.op(self.DVE, lambda: V.tensor_tensor(out=oT[:, h, c0:c0 + n], in0=po[0], in1=rec[:, par, 0:n], op=ALU.mult),
                        Rb=po[1] + [recb[par]], Wb=oTb[h].get(c0, c0 + n))
            xs1(0)
            for u in range(len(units)):
                if u + 1 < len(units):
                    xs1(u + 1)
                xs2(u)
            pSs = self.bank(0, 0, 128)
            krs = [self.take_slot(('xk', l, bp)) for bp in range(8)]
            for bp in range(8):
                kv = self.ring[:, krs[bp], :].rearrange("p (b h m) -> p b h m", b=2, h=4)

                def fk():
                    ins = None
                    for bb in range(2):
                        b = 2 * bp + bb
                        for h in range(4):
                            for mb in range(2):
                                col = mb * 64 + h * 16 + b
                                ins = PE.matmul(pSs[0][:, col:col + 1], lhsT=kv[:, bb, h, mb * 128:(mb + 1) * 128],
                                                rhs=qT[:, h, SAMP0 + b:SAMP0 + b + 1], start=True, stop=True)
                    return ins
                self.op(self.PE, fk, Rb=[self.ringb[krs[bp]]] + qTb, Wb=pSs[1])
            self.op(self.ACT, lambda: A.activation(out=PTs[:, :], in_=pSs[0], func=AF.Exp, scale=sc), Rb=pSs[1], Wb=[b_PTs])
            pos_ = self.bank(1, 0, 64)
            pds_ = self.bank(2, 0, 64)
            vrs = [self.take_slot(('xv', l, bp)) for bp in range(8)]
            for bp in range(8):
                vv = self.ring[:, vrs[bp], :].rearrange("p (b m n) -> p b m n", b=2, m=2)

                def fv():
                    ins = None
                    for bb in range(2):
                        b = 2 * bp + bb
                        for h in range(4):
                            for mb in range(2):
                                col = mb * 64 + h * 16 + b
                                ins = PE.matmul(pos_[0][:, h * 16 + b:h * 16 + b + 1], lhsT=vv[:, bb, mb, h * 128:(h + 1) * 128],
                                                rhs=PTs[:, col:col + 1], start=(mb == 0), stop=(mb == 1))
                    return ins
                self.op(self.PE, fv, Rb=[self.ringb[vrs[bp]], b_PTs], Wb=pos_[1])

            def fd():
                ins = None
                for mb in range(2):
                    ins = PE.matmul(pds_[0], lhsT=self.onesf[:, :], rhs=PTs[:, mb * 64:(mb + 1) * 64], start=(mb == 0), stop=(mb == 1))
                return ins
            self.op(self.PE, fd, Rb=[b_PTs, self.b_ones], Wb=pds_[1])
            self.op(self.DVE, lambda: V.reciprocal(out=recs[:, :], in_=pds_[0]), Rb=pds_[1], Wb=[b_recs])
            self.op(self.DVE, lambda: V.tensor_tensor(out=oT[:, :, SAMP0:T], in0=pos_[0].rearrange("p (h b) -> p h b", h=4),
                                                      in1=recs[:, :].rearrange("p (h b) -> p h b", h=4), op=ALU.mult),
                    Rb=pos_[1] + [b_recs], Wb=[b for s in oTb for b in s.get(SAMP0, T)])
            rs = [self.take_slot(('row', 'w_xo', l, h * 128)) for h in range(4)]
            for d in range(NCH):
                for ci, (c0, n) in enumerate(ctiles):
                    ps = self.pdslot(n)

                    def fn():
                        ins = None
                        for h in range(4):
                            ins = PE.matmul(ps[0], lhsT=self.ring[:, rs[h], d * 128:(d + 1) * 128], rhs=oT[:, h, c0:c0 + n],
                                            start=(h == 0), stop=(h == 3))
                        return ins
                    self.op(self.PE, fn, Rb=[self.ringb[x] for x in rs] + [b for s in oTb for b in s.get(c0, c0 + n)], Wb=ps[1])
                    self.resid_add(d, c0, n, ps, scale=1.0)

    def phase_swa(self):
        A, V, PE = self.ACT.h, self.DVE.h, self.PE.h
        ct = CT_ALL
        sc = 0.125
        LAST0 = SAMP0 - 128
        with ExitStack() as es:
            with ExitStack() as es2:
                self.norm(GV['mix_1'], ct, es2)
            snap = self.snapshot()
            cs = self.sb("cs", [128, 2, T], F32, es)
            msk = self.sb("msk", [128, 2, 256], BF16, es)
            idn = self.sb("idn", [128, 128], BF16, es)
            kT = self.sb("kT", [128, T], BF16, es)
            Vp = self.sb("Vp", [128, 9, 192], BF16, es)
            qc = self.sb("qc", [128, 2, T], BF16, es)
            qs = self.sb("qs", [128, 4, NSAMP], BF16, es)
            oT = self.sb("oTs", [128, 4, T], BF16, es)
            r1 = self.sb("r1", [128, 2, 512], F32, es)
            r2 = self.sb("r2", [128, 2, 512], F32, es)
            PT = self.sb("PTw", [128, 2, 512], BF16, es)
            dn = self.sb("dn", [128, 2, 128], F32, es)
            PTs = self.sb("PTss", [128, 128], BF16, es)
            dns = self.sb("dns", [128, 64], F32, es)
            kpst = self.sb("kpst", [128, 4, 128], F32, es)
            vpst = self.sb("vpst", [128, 4, 128], F32, es)
            ksst = self.sb("ksst", [128, 4, NSAMP], F32, es)
            vsst = self.sb("vsst", [NSAMP, 4, 128], F32, es)
            vnb = self.sb("vnb", [NSAMP, 128], BF16, es)
            b_cs, b_msk, b_kT, b_Vp, b_qs, b_PTs, b_dns, b_kpst, b_vpst, b_ksst, b_vsst, b_vnb = (Buf(snap) for _ in range(12))
            qcb = [Buf(snap), Buf(snap)]
            oTb = [Seg(SEGB, snap) for _ in range(4)]
            r1b = [Buf(snap), Buf(snap)]
            r2b = [Buf(snap), Buf(snap)]
            PTb = [[Buf(snap), Buf(snap)], [Buf(snap), Buf(snap)]]
            dnb = [Buf(snap), Buf(snap)]
            self.dma(self.SP, out=cs[:, :, :], in_=self.d_cs.rearrange("p (a t) -> p a t", a=2), Wb=[b_cs])
            self.dma(self.SP, out=msk[:, :, :], in_=self.d_mask.rearrange("p (a t) -> p a t", a=2), Wb=[b_msk])
            self.dma(self.SP, out=idn[:, :], in_=self.d_ident[:, :], Wb=[b_msk])

            def fz():
                V.memset(Vp[:, :, :], 0.0)
                return V.memset(oT[:, :, :], 0.0)
            self.op(self.DVE, fz, Wb=[b_Vp] + [b for s in oTb for b in s.bufs])
            if DBG == 3:
                raise _Stop()
            self.rope_n = 0

            def rope(pn, ps_, outs):
                for ci, (c0, n) in enumerate(ct):
                    p = self.rope_n % 2
                    self.rope_n += 1
                    self.op(self.DVE, lambda: V.tensor_tensor(out=r1[:, p, 0:n], in0=pn[ci][0], in1=cs[:, 0, c0:c0 + n], op=ALU.mult),
                            Rb=pn[ci][1] + [b_cs], Wb=[r1b[p]])
                    self.op(self.DVE, lambda: V.tensor_tensor(out=r2[:, p, 0:n], in0=ps_[ci][0], in1=cs[:, 1, c0:c0 + n], op=ALU.mult),
                            Rb=ps_[ci][1] + [b_cs], Wb=[r2b[p]])
                    self.op(self.DVE, lambda: V.tensor_tensor(out=r1[:, p, 0:n], in0=r1[:, p, 0:n], in1=r2[:, p, 0:n], op=ALU.add),
                            Rb=[r1b[p], r2b[p]], Wb=[r1b[p]])
                    for (dst, bufs, lo_, hi_) in outs:
                        a, b = max(lo_, c0), min(hi_, c0 + n)
                        if a >= b:
                            continue
                        self.op(self.ACT, lambda: A.activation(out=dst(a, b), in_=r1[:, p, a - c0:b - c0], func=AF.Copy),
                                Rb=[r1b[p]], Wb=bufs)

            pa = [self.PA(ci, n) for ci, (c0, n) in enumerate(ct)]
            pbb = [self.PB(ci, n) for ci, (c0, n) in enumerate(ct)]
            unit = 0
            for jj in range(4):
                r = self.take_slot(('col', 'w_qkv', 0, ('k', jj, 0)))
                self.proj(pa, r, ct, fine=(jj == 0))
                r = self.take_slot(('col', 'w_qkv', 0, ('k', jj, 1)))
                self.proj(pbb, r, ct)
                rope(pa, pbb, [
                    (lambda a, b: kT[:, a:b], [b_kT], 0, T),
                    (lambda a, b: kpst[:, jj, a - LAST0:b - LAST0], [b_kpst], LAST0, SAMP0),
                    (lambda a, b: ksst[:, jj, a - SAMP0:b - SAMP0], [b_ksst], SAMP0, T),
                ])
                if DBG == 4:
                    raise _Stop()
                r = self.take_slot(('col', 'w_qkv', 0, ('v', jj)))
                sv = self.slot3(r, 16)
                pv = [self.bank(4, 0, 512), self.bank(5, 0, 512), self.bank(7, 0, 256)]

                def vdst(blk):
                    return pv[blk // 4][0][:, (blk % 4) * 128:(blk % 4) * 128 + 128]
                for blk in range(10):
                    c0 = 16 + 128 * blk if blk < 9 else SAMP0
                    m = 128 if blk < 9 else NSAMP

                    def fn():
                        ins = None
                        for k in range(NCH):
                            ins = PE.matmul(vdst(blk)[0:m, :], lhsT=self.hT[:, k, c0:c0 + m], rhs=sv[:, k, :],
                                            start=(k == 0), stop=(k == NCH - 1))
                        return ins
                    self.op(self.PE, fn, Rb=[self.ringb[r]] + self.hTb, Wb=pv[blk // 4][1])
                for grp in range(3):
                    nb = 4 if grp < 2 else 1
                    src = pv[grp][0][:, 0:nb * 128].rearrange("p (b j d) -> p b j d", b=nb, j=2)
                    dstv = Vp[:, 4 * grp:4 * grp + nb, :].rearrange("p b (j d) -> p b j d", j=3)[:, :, 0:3:2, :]
                    self.op(self.ACT, lambda: A.activation(out=dstv, in_=src, func=AF.Copy), Rb=pv[grp][1], Wb=[b_Vp])
                self.op(self.DVE, lambda: V.tensor_copy(out=vpst[:, jj, :], in_=vdst(8)), Rb=pv[2][1], Wb=[b_vpst])
                self.op(self.DVE, lambda: V.tensor_copy(out=vsst[:, jj, :], in_=vdst(9)[0:NSAMP, :]), Rb=pv[2][1], Wb=[b_vsst])
                self.op(self.DVE, lambda: V.tensor_copy(out=vnb[:, :], in_=vdst(9)[0:NSAMP, :]), Rb=pv[2][1], Wb=[b_vnb])
                if DBG == 5:
                    raise _Stop()
                for g in range(4):
                    qp = g % 2
                    r = self.take_slot(('col', 'w_qkv', 0, ('q', jj, g, 0)))
                    self.proj(pa, r, ct)
                    r = self.take_slot(('col', 'w_qkv', 0, ('q', jj, g, 1)))
                    self.proj(pbb, r, ct)
                    rope(pa, pbb, [
                        (lambda a, b: qc[:, qp, a:b], [qcb[qp]], 0, T),
                        (lambda a, b: qs[:, g, a - SAMP0:b - SAMP0], [b_qs], SAMP0, T),
                    ])
                    if DBG == 61:
                        raise _Stop()
                    cidx = jj * 4 + g

                    def stage1(n_):
                        par = n_ % 2
                        q0 = HALO + 128 * n_
                        pS2 = [self.bank(2 * par, 0, 256), self.bank(2 * par + 1, 0, 256)]
                        mv = 1 if n_ == 0 else 0

                        def fs():
                            ins = None
                            for hd in range(2):
                                ins = PE.matmul(pS2[hd][0], lhsT=idn[:, :], rhs=msk[:, mv, :], start=True, stop=False)
                            for kb in range(2):
                                for hd in range(2):
                                    b0 = 64 * hd
                                    k0 = 16 + 128 * (n_ + kb)
                                    ins = PE.matmul(pS2[hd][0][:, kb * 128:kb * 128 + 128], lhsT=kT[b0:b0 + 64, k0:k0 + 128],
                                                    rhs=qc[b0:b0 + 64, qp, q0:q0 + 128], start=False, stop=(kb == 1))
                            return ins
                        self.op(self.PE, fs, Rb=[b_kT, qcb[qp], b_msk], Wb=pS2[0][1] + pS2[1][1])
                        for hd in range(2):
                            self.op(self.ACT, lambda: A.activation(out=PT[:, par, hd * 256:hd * 256 + 256], in_=pS2[hd][0], func=AF.Exp, scale=sc),
                                    Rb=pS2[hd][1], Wb=[PTb[par][hd]])

                    def stage2(n_):
                        par = n_ % 2
                        q0 = HALO + 128 * n_
                        pod = self.bank(4 + par, 0, 256)

                        def fo():
                            ins = None
                            i = 0
                            for hd in range(2):
                                for kb in range(2):
                                    o0 = (hd * 2 + kb) * 128
                                    ins = PE.matmul(pod[0][:, 0:128], lhsT=Vp[:, n_ + kb, 64 * hd:64 * hd + 128], rhs=PT[:, par, o0:o0 + 128],
                                                    start=(i == 0), stop=(i == 3))
                                    i += 1
                            i = 0
                            for hd in range(2):
                                for kb in range(2):
                                    o0 = (hd * 2 + kb) * 128
                                    ins = PE.matmul(pod[0][:, 128:256], lhsT=self.ones[:, 64 * hd:64 * hd + 128], rhs=PT[:, par, o0:o0 + 128],
                                                    start=(i == 0), stop=(i == 3))
                                    i += 1
                            return ins
                        self.op(self.PE, fo, Rb=[b_Vp, self.b_ones] + PTb[par], Wb=pod[1])
                        self.op(self.ACT, lambda: A.activation(out=dn[:, par, :], in_=pod[0][:, 128:256], func=AF.Ln,
                                                               bias=self.esink[:, cidx:cidx + 1]), Rb=pod[1] + [self.b_esink], Wb=[dnb[par]])
                        self.op(self.ACT, lambda: A.activation(out=dn[:, par, :], in_=dn[:, par, :], func=AF.Exp, scale=-1.0),
                                Rb=[dnb[par]], Wb=[dnb[par]])
                        self.op(self.DVE, lambda: V.tensor_tensor(out=oT[:, g, q0:q0 + 128], in0=pod[0][:, 0:128], in1=dn[:, par, :], op=ALU.mult),
                                Rb=pod[1] + [dnb[par]], Wb=oTb[g].get(q0, q0 + 128))
                    stage1(0)
                    for n_ in range(8):
                        if n_ + 1 < 8:
                            stage1(n_ + 1)
                        stage2(n_)
                    if DBG == 6:
                        raise _Stop()
                rk = self.take_slot(('sk', jj))
                kv = self.ring[:, rk, :].rearrange("p (b s) -> p b s", b=NSAMP)
                self.op(self.DVE, lambda: V.tensor_copy(out=kv[:, :, 0], in_=kT[:, SAMP0:T]), Rb=[b_kT, self.ringb[rk]], Wb=[self.ringb[rk]])
                pSs2 = [self.bank(6, 0, 64), self.bank(7, 0, 64)]

                def fk():
                    ins = None
                    for g in range(4):
                        for b in range(NSAMP):
                            for hd in range(2):
                                b0 = 64 * hd
                                col = g * NSAMP + b
                                ins = PE.matmul(pSs2[hd][0][:, col:col + 1], lhsT=kv[b0:b0 + 64, b, :], rhs=qs[b0:b0 + 64, g, b:b + 1],
                                                start=True, stop=True)
                    return ins
                self.op(self.PE, fk, Rb=[self.ringb[rk], b_qs], Wb=pSs2[0][1] + pSs2[1][1])

                def fes():
                    ins = None
                    for hd in range(2):
                        ins = A.activation(out=PTs[:, hd * 64:hd * 64 + 64], in_=pSs2[hd][0], func=AF.Exp, scale=sc)
                    return ins
                self.op(self.ACT, fes, Rb=pSs2[0][1] + pSs2[1][1], Wb=[b_PTs])
                rvs = [self.take_slot(('sv', jj, hf)) for hf in range(2)]
                pos_ = self.bank(5, 0, 64)
                pds_ = self.bank(4, 0, 64)
                for hf in range(2):
                    vv = self.ring[:, rvs[hf], 0:8 * 192].rearrange("p (b n) -> p b n", b=8)
                    dst = self.ring[0:1, rvs[hf], 0:8 * 192].rearrange("p (b n) -> p b n", b=8)
                    for j2 in range(2):
                        self.dma(self.SP, out=dst[:, :, 128 * j2:128 * j2 + 64], in_=vnb[8 * hf:8 * hf + 8, 64 * j2:64 * j2 + 64],
                                 Rb=[b_vnb], Wb=[self.ringb[rvs[hf]]])

                    def fv():
                        ins = None
                        for g in range(4):
                            for b8 in range(8):
                                b = 8 * hf + b8
                                for hd in range(2):
                                    col = hd * 64 + g * NSAMP + b
                                    ins = PE.matmul(pos_[0][:, g * NSAMP + b:g * NSAMP + b + 1], lhsT=vv[:, b8, 64 * hd:64 * hd + 128],
                                                    rhs=PTs[:, col:col + 1], start=(hd == 0), stop=(hd == 1))
                        return ins
                    self.op(self.PE, fv, Rb=[self.ringb[rvs[hf]], b_PTs], Wb=pos_[1])

                def fd():
                    ins = None
                    for g in range(4):
                        for hd in range(2):
                            c_ = hd * 64 + g * NSAMP
                            ins = PE.matmul(pds_[0][:, g * NSAMP:(g + 1) * NSAMP], lhsT=self.ones[:, 64 * hd:64 * hd + 128],
                                            rhs=PTs[:, c_:c_ + NSAMP], start=(hd == 0), stop=(hd == 1))
                    return ins
                self.op(self.PE, fd, Rb=[b_PTs, self.b_ones], Wb=pds_[1])

                def fdn():
                    ins = None
                    for g in range(4):
                        cidx = jj * 4 + g
                        ins = V.tensor_scalar(out=dns[:, g * NSAMP:(g + 1) * NSAMP], in0=pds_[0][:, g * NSAMP:(g + 1) * NSAMP],
                                              scalar1=self.esink[:, cidx:cidx + 1], scalar2=None, op0=ALU.add)
                    return ins
                self.op(self.DVE, fdn, Rb=pds_[1] + [self.b_esink], Wb=[b_dns])
                self.op(self.DVE, lambda: V.reciprocal(out=dns[:, :], in_=dns[:, :]), Rb=[b_dns], Wb=[b_dns])
                self.op(self.DVE, lambda: V.tensor_tensor(out=oT[:, :, SAMP0:T], in0=pos_[0].rearrange("p (g b) -> p g b", g=4),
                                                          in1=dns[:, :].rearrange("p (g b) -> p g b", g=4), op=ALU.mult),
                        Rb=pos_[1] + [b_dns], Wb=[b for s in oTb for b in s.get(SAMP0, T)])
                if DBG == 7:
                    raise _Stop()
                rs = [self.take_slot(('rowsel', 'w_o', 0, (jj, g))) for g in range(4)]
                for d in range(NCH):
                    for ci, (c0, n) in enumerate(CT_MAIN):
                        ps = self.pdslot(n) if False else self.bank(6, 0, n) if False else self.pdslot_swa(n)

                        def fn():
                            ins = None
                            for g in range(4):
                                ins = PE.matmul(ps[0], lhsT=self.ring[:, rs[g], d * 128:(d + 1) * 128], rhs=oT[:, g, c0:c0 + n],
                                                start=(g == 0), stop=(g == 3))
                            return ins
                        self.op(self.PE, fn, Rb=[self.ringb[x] for x in rs] + [b for s in oTb for b in s.get(c0, c0 + n)], Wb=ps[1])
                        self.resid_add(d, c0, n, ps, scale=1.0)
            self.dma(self.SP, out=self.o_kp.rearrange("p (j t) -> p j t", j=4), in_=kpst[:, :, :], Rb=[b_kpst])
            self.dma(self.SP, out=self.o_vp.rearrange("p (j t) -> p j t", j=4), in_=vpst[:, :, :], Rb=[b_vpst])
            self.dma(self.SP, out=self.o_ksn.rearrange("p (j t) -> p j t", j=4), in_=ksst[:, :, :], Rb=[b_ksst])
            self.dma(self.SP, out=self.o_vsn.rearrange("p (j t) -> p j t", j=4), in_=vsst[:, :, :], Rb=[b_vsst])

    def pdslot_swa(self, n):
        i = (4, 5)[self.pd_rr % 2]
        self.pd_rr += 1
        return self.bank(i, 0, n)

    def phase_final(self):
        V = self.DVE.h
        ct = CT_MAIN
        with ExitStack() as es:
            rbc, rbcb = self.norm(GV['final'], ct, es, make_h=False)
            snap = self.snapshot()
            yst = self.sb("yst", [128, 2, NOUT], F32, es)
            yb = [Buf(snap), Buf(snap)]
            gi = GV['final']
            yv = self.o_yT.rearrange("p (k t) -> p k t", k=NCH)
            for k in range(NCH):
                p = k % 2
                self.op(self.DVE, lambda: V.scalar_tensor_tensor(
                    out=yst[:, p, :], in0=self.xT[:, k, HALO:T], scalar=self.gv[:, gi, k:k + 1], in1=rbc[:, HALO:T],
                    op0=ALU.mult, op1=ALU.mult), Rb=self.xb(k, HALO, NOUT) + rbcb.get(HALO, T) + [self.b_gv], Wb=[yb[p]])
                self.dma(self.SP, out=yv[:, k, :], in_=yst[:, p, :], Rb=[yb[p]])

    def finish(self):
        SP = self.SP
        for k in self.spsem:
            if self.dval[k] > 0 and SP.known.get(k, 0) < self.dval[k]:
                SP.h.wait_ge(self.sems[k], self.dval[k])
        for E in (self.PE, self.ACT, self.DVE):
            SP.h.wait_ge(self.sems[E.key], E.cnt)
        for k in self.rsem:
            if self.dval[k] > 0:
                self.POOL.h.wait_ge(self.sems[k], self.dval[k])


NS_MAX = 668
DBG = 0


class _Stop(Exception):
    pass
_CACHE = {}


def build_program(stop=None, ns=None):
    key = (stop, ns)
    if key in _CACHE:
        return _CACHE[key]
    P = Prog()
    P.declare()
    P.NS = ns if ns is not None else NS_MAX
    P.wstream = P.dram("wstream", [P.NS, 128, 2048], F32, "ExternalInput")
    phases = [P.phase_init, P.phase_memkv, lambda: P.phase_ffn(0, 1, CT_ALL), P.phase_pool,
              lambda: P.phase_xattn(0, CT_ALL), lambda: P.phase_ffn(0, 2, CT_ALL), lambda: P.phase_ffn(1, 1, CT_ALL),
              P.phase_swa, lambda: P.phase_xattn(1, CT_MAIN), lambda: P.phase_ffn(1, 2, CT_MAIN), P.phase_final]
    for i, ph in enumerate(phases):
        if stop is not None and i >= stop:
            break
        try:
            ph()
        except _Stop:
            break
    P.finish()
    assert len(P.slots) <= P.NS or stop is not None, len(P.slots)
    _CACHE[key] = P
    return P


def _col_tile(W, cols):
    return np.ascontiguousarray(W[:, cols].reshape(NCH, 128, 128).transpose(1, 0, 2)).reshape(128, 2048)


def _qkv_cols(spec):
    kind = spec[0]
    if kind == 'k':
        _, jj, sw = spec
        base = 2048 + jj * 128
        heads = [base, base + 64]
    elif kind == 'v':
        _, jj = spec
        return np.arange(2560 + jj * 128, 2560 + jj * 128 + 128)
    else:
        _, jj, g, sw = spec
        heads = [(8 * jj + g) * 64, (8 * jj + 4 + g) * 64]
    cols = []
    for hb in heads:
        idx = np.arange(64)
        if sw:
            idx = (idx + 32) % 64
        cols.append(hb + idx)
    return np.concatenate(cols)


def _common_slots(slots, inp):
    ws = np.zeros((max(len(slots), 1), 128, 2048), np.float32)
    percore = []
    for s, d in enumerate(slots):
        kind = d[0]
        if kind == 'col':
            _, name, l, c = d
            W = inp[name][l]
            cols = _qkv_cols(c) if name == 'w_qkv' else np.arange(c, c + 128)
            ws[s] = _col_tile(W, cols)
        elif kind == 'row':
            _, name, l, r0 = d
            ws[s] = inp[name][l][r0:r0 + 128, :]
        elif kind == 'rowsel':
            _, name, l, (jj, g) = d
            rows = np.concatenate([np.arange((8 * jj + g) * 64, (8 * jj + g) * 64 + 64),
                                   np.arange((8 * jj + 4 + g) * 64, (8 * jj + 4 + g) * 64 + 64)])
            ws[s] = inp[name][l][rows, :]
        elif kind == 'pool':
            g = d[1]
            ws[s] = inp['w_pool'][0][g].reshape(4, 128, 512).transpose(1, 0, 2).reshape(128, 2048)
        else:
            percore.append((s, d))
    return ws, percore


def _fill_percore(ws, percore, inp, c):
    b0 = NSAMP * c
    for s, d in percore:
        kind = d[0]
        if kind == 'xk':
            _, l, bp = d
            a = inp['cache_mem_k'][l, b0 + 2 * bp:b0 + 2 * bp + 2]
            ws[s] = a.transpose(3, 0, 2, 1).reshape(128, 2048)
        elif kind == 'xv':
            _, l, bp = d
            a = inp['cache_mem_v'][l, b0 + 2 * bp:b0 + 2 * bp + 2]
            ws[s] = a.reshape(2, 2, 128, 512).transpose(2, 0, 1, 3).reshape(128, 2048)
        elif kind == 'sk':
            _, jj = d
            a = inp['cache_swa_k'][0, b0:b0 + NSAMP, :, 2 * jj:2 * jj + 2, :]
            ws[s] = a.transpose(2, 3, 0, 1).reshape(128, 2048)
        elif kind == 'sv':
            _, jj, hf = d
            a = inp['cache_swa_v'][0, b0 + 8 * hf:b0 + 8 * hf + 8, :, 2 * jj:2 * jj + 2, :]
            t = np.zeros((128, 8, 192), np.float32)
            t[:, :, 0:64] = a[:, :, 0, :].transpose(1, 0, 2)
            t[:, :, 128:192] = a[:, :, 1, :].transpose(1, 0, 2)
            ws[s] = 0.0
            ws[s][:, 0:8 * 192] = t.reshape(128, 8 * 192)


def _fm(a):
    return np.ascontiguousarray(a.reshape(a.shape[0], NCH, 128).transpose(2, 1, 0))


def _const_tables(c):
    pos = np.zeros(T, np.float64)
    pos[:SAMP0] = c * NMAIN - HALO + np.arange(SAMP0)
    pos[SAMP0:] = PAST
    p = np.arange(128)
    dd = p % 64
    inv = (10000.0 ** (-(dd % 32).astype(np.float32) / 32.0)).astype(np.float32)
    ang = pos.astype(np.float32)[None, :] * inv[:, None]
    cs = np.zeros((128, 2, T), np.float32)
    cs[:, 0, :] = np.cos(ang)
    sn = np.sin(ang)
    cs[:, 1, :] = np.where((dd < 32)[:, None], -sn, sn)
    ic = np.ones((4, T), np.float32)
    for g, w in enumerate(POOLW):
        cnt = np.minimum(float(w), np.maximum(pos[:SAMP0] + 1.0, 1.0))
        ic[g, :SAMP0] = (1.0 / cnt).astype(np.float32)
    icnt = np.broadcast_to(ic[None], (128, 4, T)).reshape(128, 4 * T)
    k = np.arange(128)[:, None]
    q = np.arange(128)[None, :]
    m = np.zeros((128, 2, 2, 128), np.float32)
    m[:, 0, 0, :] = (k > q)
    m[:, 0, 1, :] = (k <= q)
    m[:, 1] = m[:, 0]
    if c == 0:
        m[:, 1, 0, :] = 0.0
    madd = np.where(m > 0.5, 0.0, -30000.0).astype(np.float32)
    return cs.reshape(128, 2 * T), np.ascontiguousarray(icnt), madd.reshape(128, 512).astype(NPBF)


def prep_common(inp, P):
    ws_common, percore = _common_slots(P.slots, inp)
    gv = np.zeros((128, NGV, NCH), np.float32)
    vecs = {'ffn1_0': inp['g_ffn1'][0], 'mix_0': inp['g_mix'][0], 'xq_0': inp['g_xq'][0], 'ffn2_0': inp['g_ffn2'][0],
            'ffn1_1': inp['g_ffn1'][1], 'mix_1': inp['g_mix'][1], 'xq_1': inp['g_xq'][1], 'ffn2_1': inp['g_ffn2'][1],
            'final': inp['g_final'], 'mem_0': inp['g_mem'][0], 'mem_1': inp['g_mem'][1], 'pscale': inp['pool_scale'][0]}
    for name, v in vecs.items():
        gv[:, GV[name], :] = v.reshape(NCH, 128).T
    sinks = inp['sinks'][0]
    sinkT = np.zeros((128, 16), np.float32)
    for jj in range(4):
        for g in range(4):
            sinkT[0:64, jj * 4 + g] = sinks[8 * jj + g]
            sinkT[64:128, jj * 4 + g] = sinks[8 * jj + 4 + g]
    memT = _fm(inp['mem_prompt'][0]).reshape(128, NCH * 256)
    return dict(ws=ws_common, percore=percore, gv=gv, sinkT=sinkT, memT=memT)


def core_inputs(inp, cm, c):
    xp = inp['x_prompt'][0]
    xs = inp['x_sample'][:, 0, :]
    tok = np.zeros((T, D), np.float32)
    lo = c * NMAIN - HALO
    a = max(lo, 0)
    tok[a - lo:SAMP0] = xp[a:(c + 1) * NMAIN]
    tok[SAMP0:] = xs[NSAMP * c:NSAMP * (c + 1)]
    cs, icnt, mask = _const_tables(c)
    st = inp['state_pool'][0, NSAMP * c:NSAMP * (c + 1)]
    stT = np.ascontiguousarray(st.reshape(NSAMP, 15, NCH, 128).transpose(3, 2, 0, 1)).reshape(128, NCH * NSAMP * 15)
    ws = cm['ws'].copy()
    _fill_percore(ws, cm['percore'], inp, c)
    return {
        'xT': _fm(tok).reshape(128, NCH * T), 'memT': cm['memT'], 'stT': stT,
        'stN': np.ascontiguousarray(st).reshape(NSAMP, 15 * D),
        'ckN': np.ascontiguousarray(inp['cache_swa_k'][0, NSAMP * c:NSAMP * (c + 1)]).reshape(NSAMP, 128 * 512),
        'cvN': np.ascontiguousarray(inp['cache_swa_v'][0, NSAMP * c:NSAMP * (c + 1)]).reshape(NSAMP, 128 * 512),
        'gvec': cm['gv'].reshape(128, NGV * NCH), 'sinkT': cm['sinkT'], 'cs': cs, 'icnt': icnt, 'maskc': mask,
        'identc': np.eye(128, dtype=np.float32).astype(NPBF), 'wstream': ws,
    }


def kernel(**inp):
    inp = {k: np.asarray(v) for k, v in inp.items()}
    P = build_program()
    cm = prep_common(inp, P)
    in_maps = [core_inputs(inp, cm, c) for c in range(NCORES)]
    res = run_bass_kernel_spmd(P.nc, in_maps, core_ids=list(range(NCORES)))
    out = res.results
    f32 = np.float32
    y_prompt = np.zeros((1, 8192, D), f32)
    y_sample = np.zeros((128, 1, D), f32)
    pool_s = np.zeros((1, 128, 15, D), f32)
    k_s = np.zeros((1, 128, 128, 8, 64), f32)
    v_s = np.zeros((1, 128, 128, 8, 64), f32)
    for c in range(NCORES):
        o = out[c]
        yT = o['yT'].reshape(128, NCH, NOUT)
        yy = yT.transpose(2, 1, 0).reshape(NOUT, D)
        y_prompt[0, c * NMAIN:(c + 1) * NMAIN] = yy[:NMAIN]
        y_sample[NSAMP * c:NSAMP * (c + 1), 0] = yy[NMAIN:]
        sl = slice(NSAMP * c, NSAMP * (c + 1))
        pool_s[0, sl, 0:14] = o['pool_s_old'].reshape(NSAMP, 14, D)
        pool_s[0, sl, 14] = o['pool_s_new'].reshape(128, NCH, NSAMP).transpose(2, 1, 0).reshape(NSAMP, D)
        k_s[0, sl, 0:127] = o['swa_k_s_old'].reshape(NSAMP, 127, 8, 64)
        v_s[0, sl, 0:127] = o['swa_v_s_old'].reshape(NSAMP, 127, 8, 64)
        kn = o['swa_k_s_new'].reshape(2, 64, 4, NSAMP)
        k_s[0, sl, 127] = kn.transpose(3, 2, 0, 1).reshape(NSAMP, 8, 64)
        v_s[0, sl, 127] = o['swa_v_s_new'].reshape(NSAMP, 8, 64)
    o7 = out[NCORES - 1]
    pool_p = o7['pool_p'].reshape(128, NCH, 15).transpose(2, 1, 0).reshape(1, 1, 15, D).astype(f32)
    k_p = o7['swa_k_p'].reshape(2, 64, 4, 128).transpose(3, 2, 0, 1).reshape(1, 1, 128, 8, 64).astype(f32)
    v_p = o7['swa_v_p'].reshape(1, 1, 128, 8, 64).astype(f32)
    o0 = out[0]
    mem_k = o0['mem_k'].reshape(128, 2, 4, 256).transpose(1, 3, 2, 0).reshape(2, 1, 256, 4, 128).astype(f32)
    mem_v = o0['mem_v'].reshape(128, 2, 2, 512).transpose(1, 2, 0, 3).reshape(2, 1, 256, 4, 128).astype(f32)
    return (y_prompt, y_sample, np.ascontiguousarray(pool_p), pool_s, np.ascontiguousarray(k_p), np.ascontiguousarray(v_p),
            k_s, v_s, np.ascontiguousarray(mem_k), np.ascontiguousarray(mem_v))
```
